# Optimizing a Trainium2 kernel written in Bass

```python
import math
import jax, jax.numpy as jnp
from jax import lax
import numpy as np

D_MODEL = 1024
BATCH = 8
SEQ = 4096
DEPTH = 4

N_A_LAYERS = DEPTH // 2
N_B_LAYERS = DEPTH - N_A_LAYERS
D_FF = 4 * D_MODEL
NORM_EPS = 1e-6
NEG = -1e30

DIFF_HEAD_DIM = 64
DIFF_HEADS = D_MODEL // (2 * DIFF_HEAD_DIM)
Q_BLOCK = 128

REL_BUCKETS = 32
REL_MAX_DIST = 128
REL_HEADS = 2 * DIFF_HEADS

NSA_HEAD_DIM = 64
NSA_HEADS = D_MODEL // NSA_HEAD_DIM
NSA_KV_HEADS = 4
NSA_GROUP = NSA_HEADS // NSA_KV_HEADS
CMP_LEN = 32
CMP_STRIDE = 16
CMP_HIDDEN = 256
SLC_LEN = 64
SLC_TOPK = 16
SLC_FORCED_LOCAL = 2
FORCE_BONUS = 1e4
WINDOW = 512
NSA_Q_BLOCK = 32
NSA_IN_COLS = NSA_HEADS * NSA_HEAD_DIM + 3 * NSA_HEADS
KV_COLS = 6 * NSA_KV_HEADS * NSA_HEAD_DIM

kernel_name = "yoco_diffattn_nsa_hybrid"


def rms_norm(x, g):
    xf = x.astype(jnp.float32)
    y = xf * lax.rsqrt(jnp.mean(xf * xf, axis=-1, keepdims=True) + NORM_EPS)
    return (y * g.astype(jnp.float32)).astype(x.dtype)


def modulate(h, shift, scale):
    return h * (1 + scale) + shift


def t5_bucket(dist):
    n = jnp.maximum(dist, 0)
    max_exact = REL_BUCKETS // 2
    nf = jnp.maximum(n, 1).astype(jnp.float32)
    large = max_exact + (jnp.log(nf / max_exact) / math.log(REL_MAX_DIST / max_exact)
                         * (REL_BUCKETS - max_exact)).astype(jnp.int32)
    large = jnp.minimum(large, REL_BUCKETS - 1)
    return jnp.where(n < max_exact, n, large)


def squared_relu_mlp(h, w1, w2):
    return jnp.square(jax.nn.relu(h @ w1)) @ w2


def diff_attention(h, w_in, w_out, lam, subln, rel_bias, layer_idx):
    B, T, _ = h.shape
    H, d = DIFF_HEADS, DIFF_HEAD_DIM
    q, k, v = jnp.split(h @ w_in, 3, axis=-1)
    q = q.reshape(B, T, H, 2, d)
    k = k.reshape(B, T, H, 2, d)
    v = v.reshape(B, T, H, 2 * d)
    lam_init = 0.8 - 0.6 * math.exp(-0.3 * layer_idx)
    lf = lam.astype(jnp.float32)
    lam_full = jnp.exp(jnp.sum(lf[0] * lf[1])) - jnp.exp(jnp.sum(lf[2] * lf[3])) + lam_init
    bias_tab = rel_bias.reshape(REL_BUCKETS, H, 2).astype(jnp.float32)
    k_pos = jnp.arange(T)
    scale = d ** -0.5
    nblk = T // Q_BLOCK
    qb = q.reshape(B, nblk, Q_BLOCK, H, 2, d).transpose(1, 0, 2, 3, 4, 5)

    def block(args):
        qi, blk = args
        q_pos = blk * Q_BLOCK + jnp.arange(Q_BLOCK)
        dist = q_pos[:, None] - k_pos[None, :]
        bias = bias_tab[t5_bucket(dist)].transpose(2, 3, 0, 1)
        s = jnp.einsum('bqhmd,bkhmd->bhmqk', qi, k,
                       preferred_element_type=jnp.float32) * scale + bias[None]
        s = jnp.where((dist >= 0)[None, None, None], s, NEG)
        p = jax.nn.softmax(s, axis=-1)
        a = p[:, :, 0] - lam_full * p[:, :, 1]
        return jnp.einsum('bhqk,bkhe->bqhe', a.astype(v.dtype), v)

    o = lax.map(block, (qb, jnp.arange(nblk)))
    o = o.transpose(1, 0, 2, 3, 4).reshape(B, T, H, 2 * d)
    o = rms_norm(o, subln) * (1 - lam_init)
    return o.reshape(B, T, H * 2 * d) @ w_out


def nsa_shared_kv(h, w_kv, cmp_pos, cmp_w1, cmp_w2):
    B, T, _ = h.shape
    G, d = NSA_KV_HEADS, NSA_HEAD_DIM
    kv = (h @ w_kv).reshape(B, T, 6, G, d).transpose(2, 0, 3, 1, 4)
    n_cmp = (T - CMP_LEN) // CMP_STRIDE + 1
    idx = jnp.arange(n_cmp)[:, None] * CMP_STRIDE + jnp.arange(CMP_LEN)[None, :]
    blocks = kv[0:2][:, :, :, idx] + cmp_pos[:, None, None, None]
    flat = blocks.reshape(2, B, G, n_cmp, CMP_LEN * d)
    hid = jax.nn.gelu(jnp.einsum('sbgnf,sfh->sbgnh', flat, cmp_w1))
    cmp = jnp.einsum('sbgnh,shd->sbgnd', hid, cmp_w2)
    return (cmp[0], cmp[1], kv[2], kv[3], kv[4], kv[5])


def nsa_attention(h, w_in, w_out, rel_bias, k_cmp, v_cmp, k_slc, v_slc, k_win, v_win):
    B, T, _ = h.shape
    H, G, R, d = NSA_HEADS, NSA_KV_HEADS, NSA_GROUP, NSA_HEAD_DIM
    proj = h @ w_in
    q = proj[..., :H * d].reshape(B, T, G, R, d)
    gates = jax.nn.sigmoid(proj[..., H * d:].astype(jnp.float32)).reshape(B, T, G, R, 3)
    scale = d ** -0.5
    bias_tab = rel_bias.reshape(REL_BUCKETS, G, R).astype(jnp.float32)
    bias_tab_g = bias_tab.transpose(1, 0, 2)
    n_cmp = k_cmp.shape[2]
    n_slc = T // SLC_LEN
    top_k = min(SLC_TOPK, n_slc)
    cmp_start = jnp.arange(n_cmp) * CMP_STRIDE
    cmp_end = cmp_start + CMP_LEN - 1
    slc_start = jnp.arange(n_slc) * SLC_LEN
    overlap = ((cmp_start[:, None] < slc_start[None, :] + SLC_LEN)
               & (cmp_end[:, None] >= slc_start[None, :])).astype(jnp.float32)
    k_slc_blk = k_slc.reshape(B, G, n_slc, SLC_LEN, d)
    v_slc_blk = v_slc.reshape(B, G, n_slc, SLC_LEN, d)
    k_win_pad = jnp.pad(k_win, ((0, 0), (0, 0), (WINDOW, 0), (0, 0)))
    v_win_pad = jnp.pad(v_win, ((0, 0), (0, 0), (WINDOW, 0), (0, 0)))
    b_idx = jnp.arange(B)[:, None, None, None]
    g_idx = jnp.arange(G)[None, :, None, None]
    j_idx = jnp.arange(n_slc)
    in_offs = jnp.arange(SLC_LEN)
    nblk = T // NSA_Q_BLOCK
    qb = q.reshape(B, nblk, NSA_Q_BLOCK, G, R, d).transpose(1, 0, 2, 3, 4, 5)
    gb = gates.reshape(B, nblk, NSA_Q_BLOCK, G, R, 3).transpose(1, 0, 2, 3, 4, 5)

    def block(args):
        qi, gi, blk = args
        q_pos = blk * NSA_Q_BLOCK + jnp.arange(NSA_Q_BLOCK)
        s_c = jnp.einsum('bqgrd,bgnd->bgrqn', qi, k_cmp,
                         preferred_element_type=jnp.float32) * scale
        valid_c = cmp_end[None, :] <= q_pos[:, None]
        p_c = jax.nn.softmax(jnp.where(valid_c, s_c, NEG), axis=-1) * valid_c
        o_c = jnp.einsum('bgrqn,bgnd->bqgrd', p_c.astype(v_cmp.dtype), v_cmp)
        imp = jnp.einsum('bgrqn,nj->bgqj', p_c, overlap)
        q_blk = q_pos // SLC_LEN
        forced = (j_idx[None, :] == 0) | ((j_idx[None, :] <= q_blk[:, None])
                                          & (j_idx[None, :] > q_blk[:, None] - SLC_FORCED_LOCAL))
        imp = jnp.where(forced[None, None], FORCE_BONUS, imp)
        imp = jnp.where((j_idx[None, :] <= q_blk[:, None])[None, None], imp, NEG)
        _, sel = lax.top_k(imp, top_k)
        ks = k_slc_blk[b_idx, g_idx, sel].reshape(B, G, NSA_Q_BLOCK, top_k * SLC_LEN, d)
        vs = v_slc_blk[b_idx, g_idx, sel].reshape(B, G, NSA_Q_BLOCK, top_k * SLC_LEN, d)
        pos_s = (sel[..., None] * SLC_LEN + in_offs).reshape(B, G, NSA_Q_BLOCK, top_k * SLC_LEN)
        dist_s = q_pos[None, None, :, None] - pos_s
        bias_s = bias_tab_g[g_idx, t5_bucket(dist_s)].transpose(0, 1, 4, 2, 3)
        s_s = jnp.einsum('bqgrd,bgqkd->bgrqk', qi, ks,
                         preferred_element_type=jnp.float32) * scale + bias_s
        s_s = jnp.where((dist_s >= 0)[:, :, None], s_s, NEG)
        p_s = jax.nn.softmax(s_s, axis=-1)
        o_s = jnp.einsum('bgrqk,bgqkd->bqgrd', p_s.astype(vs.dtype), vs)
        start = blk * NSA_Q_BLOCK
        kw = lax.dynamic_slice_in_dim(k_win_pad, start, WINDOW + NSA_Q_BLOCK, axis=2)
        vw = lax.dynamic_slice_in_dim(v_win_pad, start, WINDOW + NSA_Q_BLOCK, axis=2)
        k_pos = start - WINDOW + jnp.arange(WINDOW + NSA_Q_BLOCK)
        dist_w = q_pos[:, None] - k_pos[None, :]
        valid_w = (dist_w >= 0) & (dist_w < WINDOW) & (k_pos[None, :] >= 0)
        bias_w = bias_tab[t5_bucket(dist_w)].transpose(2, 3, 0, 1)
        s_w = jnp.einsum('bqgrd,bgkd->bgrqk', qi, kw,
                         preferred_element_type=jnp.float32) * scale + bias_w[None]
        p_w = jax.nn.softmax(jnp.where(valid_w[None, None, None], s_w, NEG), axis=-1)
        o_w = jnp.einsum('bgrqk,bgkd->bqgrd', p_w.astype(vw.dtype), vw)
        o = gi[..., 0:1] * o_c + gi[..., 1:2] * o_s + gi[..., 2:3] * o_w
        return o.astype(qi.dtype)

    o = lax.map(block, (qb, gb, jnp.arange(nblk)))
    o = o.transpose(1, 0, 2, 3, 4, 5).reshape(B, T, H * d)
    return o @ w_out


def setup_inputs(seed: int = 0) -> dict:
    key = jax.random.key(seed)
    ks = jax.random.split(key, 24)
    f32 = jnp.float32
    D, d_a, d_n = D_MODEL, DIFF_HEAD_DIM, NSA_HEAD_DIM
    nrm = lambda k, shape, s: jax.random.normal(k, shape, f32) * s
    gain = lambda k, shape: 1.0 + 0.02 * jax.random.normal(k, shape, f32)
    return {
        "x": nrm(ks[0], (BATCH, SEQ, D), 1.0),
        "c": nrm(ks[1], (BATCH, D), 1.0),
        "rel_bias": nrm(ks[2], (REL_BUCKETS, REL_HEADS), 0.5),
        "ada_w": nrm(ks[3], (DEPTH, D, 6 * D), D ** -0.5),
        "ada_b": nrm(ks[4], (DEPTH, 6 * D), 0.02),
        "attn_norm": gain(ks[5], (DEPTH, D)),
        "mlp_norm": gain(ks[6], (DEPTH, D)),
        "mlp_w1": nrm(ks[7], (DEPTH, D, D_FF), D ** -0.5),
        "mlp_w2": nrm(ks[8], (DEPTH, D_FF, D), D_FF ** -0.5),
        "a_w_in": nrm(ks[9], (N_A_LAYERS, D, 3 * D), D ** -0.5),
        "a_w_out": nrm(ks[10], (N_A_LAYERS, D, D), D ** -0.5),
        "a_lambda": nrm(ks[11], (N_A_LAYERS, 4, d_a), 0.1),
        "a_subln": gain(ks[12], (N_A_LAYERS, 2 * d_a)),
        "kv_ada_w": nrm(ks[13], (D, 2 * D), D ** -0.5),
        "kv_ada_b": nrm(ks[14], (2 * D,), 0.02),
        "kv_norm": gain(ks[15], (D,)),
        "w_kv": nrm(ks[16], (D, KV_COLS), D ** -0.5),
        "cmp_pos": nrm(ks[17], (2, CMP_LEN, d_n), 0.1),
        "cmp_w1": nrm(ks[18], (2, CMP_LEN * d_n, CMP_HIDDEN), (CMP_LEN * d_n) ** -0.5),
        "cmp_w2": nrm(ks[19], (2, CMP_HIDDEN, d_n), CMP_HIDDEN ** -0.5),
        "b_w_in": nrm(ks[20], (N_B_LAYERS, D, NSA_IN_COLS), D ** -0.5),
        "b_w_out": nrm(ks[21], (N_B_LAYERS, D, D), D ** -0.5),
        "final_norm": gain(ks[22], (D,)),
    }


def reference(x, c, rel_bias, ada_w, ada_b, attn_norm, mlp_norm, mlp_w1, mlp_w2,
              a_w_in, a_w_out, a_lambda, a_subln, kv_ada_w, kv_ada_b, kv_norm, w_kv,
              cmp_pos, cmp_w1, cmp_w2, b_w_in, b_w_out, final_norm):
    c_act = jax.nn.silu(c)
    shared = None
    for layer in range(DEPTH):
        mod = (c_act @ ada_w[layer] + ada_b[layer])[:, None, :]
        sh_a, sc_a, gt_a, sh_m, sc_m, gt_m = jnp.split(mod, 6, axis=-1)
        h = modulate(rms_norm(x, attn_norm[layer]), sh_a, sc_a)
        if layer < N_A_LAYERS:
            mix = diff_attention(h, a_w_in[layer], a_w_out[layer], a_lambda[layer],
                                 a_subln[layer], rel_bias, layer)
        else:
            i = layer - N_A_LAYERS
            mix = nsa_attention(h, b_w_in[i], b_w_out[i], rel_bias, *shared)
        x = x + gt_a * mix
        h = modulate(rms_norm(x, mlp_norm[layer]), sh_m, sc_m)
        x = x + gt_m * squared_relu_mlp(h, mlp_w1[layer], mlp_w2[layer])
        if layer == N_A_LAYERS - 1:
            kv_mod = (c_act @ kv_ada_w + kv_ada_b)[:, None, :]
            sh_kv, sc_kv = jnp.split(kv_mod, 2, axis=-1)
            h_kv = modulate(rms_norm(x, kv_norm), sh_kv, sc_kv)
            shared = nsa_shared_kv(h_kv, w_kv, cmp_pos, cmp_w1, cmp_w2)
    return rms_norm(x, final_norm)
```

```python
import math
from contextlib import ExitStack

import numpy as np
import ml_dtypes

import concourse.bass as bass
import concourse.mybir as mybir
from concourse.bass_utils import run_bass_kernel_spmd

F32 = mybir.dt.float32
BF16 = mybir.dt.bfloat16
AF = mybir.ActivationFunctionType
ALU = mybir.AluOpType
AX = mybir.AxisListType

D = 1024
DFF = 4096
NL = 4
EPS = 1e-6
NEG = -1e30
P = 128


class Buf:
    __slots__ = ("name", "w", "r")

    def __init__(self, name):
        self.name = name
        self.w = []
        self.r = []


class Op:
    __slots__ = ("eng", "fn", "deps", "dma", "slot", "target", "val", "waits")

    def __init__(self, eng, fn, deps, dma):
        self.eng = eng
        self.fn = fn
        self.deps = deps
        self.dma = dma
        self.slot = None
        self.target = False
        self.val = 0
        self.waits = []


class Prog:
    ENGS = ("pe", "act", "dve", "pool", "sp")
    NSLOT = {"sp": 24, "pool": 8, "act": 4}

    def __init__(self, nc):
        self.nc = nc
        self.ops = []
        self.rr = {q: 0 for q in self.NSLOT}
        self.slot_last = {}
        self.last_on_eng = {}

    def _reduce(self, ids):
        best = {}
        out = set()
        for i in ids:
            o = self.ops[i]
            if o.dma:
                out.add(i)
            else:
                if o.eng not in best or best[o.eng] < i:
                    best[o.eng] = i
        out.update(best.values())
        return out

    def _mk(self, eng, fn, reads, writes, dma):
        deps = set()
        for b in reads:
            deps.update(b.w)
        for b in writes:
            deps.update(b.w)
            deps.update(b.r)
        gid = len(self.ops)
        op = Op(eng, fn, self._reduce(deps), dma)
        if dma:
            s = self.rr[eng]
            self.rr[eng] = (s + 1) % self.NSLOT[eng]
            op.slot = (eng, s)
            prev = self.slot_last.get(op.slot)
            if prev is not None:
                op.deps.add(prev)
            self.slot_last[op.slot] = gid
        self.ops.append(op)
        wset = set(id(b) for b in writes)
        for b in writes:
            b.w = [gid]
            b.r = []
        for b in reads:
            if id(b) not in wset:
                b.r.append(gid)
                if len(b.r) > 12:
                    b.r = list(self._reduce(b.r))
        self.last_on_eng[eng if not dma else ("dma", gid)] = gid
        return gid

    def op(self, eng, fn, reads=(), writes=()):
        return self._mk(eng, fn, reads, writes, False)

    def dma(self, q, fn, reads=(), writes=()):
        return self._mk(q, fn, reads, writes, True)

    def barrier(self):
        allb = Buf("barrier")
        ids = [i for i, o in enumerate(self.ops)]
        last = {}
        dmas = []
        for i in range(len(self.ops) - 1, -1, -1):
            o = self.ops[i]
            if o.dma:
                if o.slot not in last:
                    last[o.slot] = i
                    dmas.append(i)
            elif o.eng not in last:
                last[o.eng] = i
                dmas.append(i)
        allb.w = dmas
        nc = self.nc
        z = self._bar_tiles
        self.op("pe", lambda e: e.matmul(z["ps"][0:1, 0:2], z["bf"][0:1, 0:1], z["bf"][0:1, 0:2],
                                          start=True, stop=True), reads=[allb], writes=[z["b_pe"]])
        self.op("act", lambda e: e.activation(out=z["a"][0:1, 0:1], in_=z["src"][0:1, 0:1], func=AF.Copy),
                reads=[allb], writes=[z["b_act"]])
        self.op("dve", lambda e: e.tensor_copy(out=z["v"][0:1, 0:1], in_=z["src"][0:1, 0:1]),
                reads=[allb], writes=[z["b_dve"]])
        self.op("pool", lambda e: e.tensor_copy(out=z["g"][0:1, 0:1], in_=z["src"][0:1, 0:1]),
                reads=[allb], writes=[z["b_pool"]])
        self.dma("sp", lambda e: e.dma_start(out=z["s"][0:1, 0:1], in_=z["src"][0:1, 0:1]),
                 reads=[allb], writes=[z["b_sp"]])

    def emit(self, stack):
        nc = self.nc
        ops = self.ops
        comp = ("pe", "act", "dve", "pool")
        sems = {e: stack.enter_context(nc.semaphore("sem_" + e)) for e in comp}
        dsem = {}
        for q, n in self.NSLOT.items():
            for s in range(n):
                dsem[(q, s)] = stack.enter_context(nc.semaphore("dsem_%s_%d" % (q, s)))
        for o in ops:
            for d in o.deps:
                t = ops[d]
                if t.dma:
                    continue
                if o.eng == "pe" and t.eng == "pe" and not o.dma:
                    continue
                t.target = True
        cnt = {e: 0 for e in comp}
        dcnt = {k: 0 for k in dsem}
        for o in ops:
            if o.dma:
                dcnt[o.slot] += 16
                o.val = dcnt[o.slot]
            else:
                if o.target:
                    cnt[o.eng] += 1
                o.val = cnt[o.eng]
        waited = {e: {} for e in self.ENGS}
        for o in ops:
            w = waited[o.eng]
            for d in sorted(o.deps):
                t = ops[d]
                if t.dma:
                    key = ("d", t.slot)
                    sem = dsem[t.slot]
                else:
                    if o.eng == "pe" and t.eng == "pe" and not o.dma:
                        continue
                    key = ("e", t.eng)
                    sem = sems[t.eng]
                if w.get(key, 0) < t.val:
                    w[key] = t.val
                    o.waits.append((key, sem, t.val))
            if len(o.waits) > 1:
                m = {}
                for key, sem, v in o.waits:
                    if key not in m or m[key][1] < v:
                        m[key] = (sem, v)
                o.waits = [(k, s, v) for k, (s, v) in m.items()]
        streams = {e: [] for e in self.ENGS}
        for o in ops:
            streams[o.eng].append(o)
        final = [(dsem[k], v) for k, v in dcnt.items() if v > 0]

        def run(eng_name, e):
            for o in streams[eng_name]:
                for _, sem, v in o.waits:
                    e.wait_ge(sem, v)
                ins = o.fn(e)
                if o.dma:
                    ins.then_inc(dsem[o.slot], 16)
                elif o.target:
                    ins.then_inc(sems[o.eng], 1)
            if eng_name == "sp":
                for sem, v in final:
                    e.wait_ge(sem, v)

        with nc.Block() as block:
            @block.tensor
            def _(e):
                run("pe", e)

            @block.scalar
            def _(e):
                run("act", e)

            @block.vector
            def _(e):
                run("dve", e)

            @block.gpsimd
            def _(e):
                run("pool", e)

            @block.sync
            def _(e):
                run("sp", e)


def _t5_bucket_np(dist):
    n = np.maximum(dist, 0)
    nf = np.maximum(n, 1).astype(np.float32)
    large = 16 + (np.log(nf / np.float32(16)) / np.float32(math.log(8.0)) * np.float32(16)).astype(np.int32)
    large = np.minimum(large, 31)
    return np.where(n < 16, n, large)


def make_consts(T):
    NT = T // 128
    c = {}
    oh = np.zeros((33, 384), np.float32)
    for i in range(384):
        dist = i - 127
        if dist < 0:
            oh[32, i] = 1.0
        else:
            oh[int(_t5_bucket_np(np.array(dist))), i] += 1.0
            oh[31, i] -= 1.0
    c["c_oh"] = oh
    ki = np.arange(128)[:, None]
    qi = np.arange(128)[None, :]
    c["c_w4"] = np.where(ki > qi, 0.0, NEG).astype(np.float32)
    c["c_ident"] = np.eye(128, dtype=np.float32)
    ex = np.zeros((64, NT, 128), np.float32)
    for kt in range(NT):
        for k in range(128):
            ex[2 * kt + k // 64, kt, k] = 1.0
    c["c_ex"] = ex
    n_cmp = (T - 32) // 16 + 1
    n_slc = T // 64
    n = np.arange(256)
    cs = n * 16
    ce = cs + 31
    ss = np.arange(n_slc) * 64
    ov = ((cs[:, None] < ss[None, :] + 64) & (ce[:, None] >= ss[None, :]) & (n[:, None] < n_cmp)).astype(np.float32)
    ovp = np.zeros((256, 64), np.float32)
    ovp[:, :n_slc] = ov
    c["c_ov"] = np.ascontiguousarray(ovp.reshape(2, 128, 64).transpose(1, 0, 2))
    q = np.arange(T)
    mc = ((ce[:, None] <= q[None, :]) & (n[:, None] < n_cmp)).astype(np.float32)
    c["c_maskc"] = np.ascontiguousarray(mc.reshape(2, 128, T).transpose(1, 0, 2))
    j = np.arange(64)[None, :]
    qb = (q // 64)[:, None]
    forced = (j == 0) | ((j <= qb) & (j > qb - 2))
    valid = (j <= qb) & (j < n_slc)
    keep = (~forced & valid).astype(np.float32)
    add = np.where(valid, np.where(forced, 1e4, 0.0), NEG).astype(np.float32)
    c["c_keep"] = np.ascontiguousarray(keep.reshape(NT, 128, 64).transpose(1, 0, 2))
    c["c_add"] = np.ascontiguousarray(add.reshape(NT, 128, 64).transpose(1, 0, 2))
    return c


class Builder:
    def __init__(self, T, layers=NL, debug=False):
        self.T = T
        self.NT = T // 128
        self.NC = T // 512
        self.layers = layers
        self.debug = debug
        self.nc = bass.Bass("TRN2", target_bir_lowering=False)
        self.pg = Prog(self.nc)
        self.gstack = ExitStack()
        self.pstack = None
        self.uid = 0

    def dram_in(self, name, shape, dt=F32):
        return self.nc.dram_tensor(name, list(shape), dt, kind="ExternalInput").ap()

    def dram_out(self, name, shape, dt=F32):
        return self.nc.dram_tensor(name, list(shape), dt, kind="ExternalOutput").ap()

    def dram(self, name, shape, dt):
        return self.nc.dram_tensor(name, list(shape), dt).ap()

    def sb(self, shape, dt, persistent=False, name=None):
        self.uid += 1
        st = self.gstack if persistent else self.pstack
        return st.enter_context(self.nc.sbuf_tensor("%s_%d" % (name or "t", self.uid), list(shape), dt))

    def B(self, name="b"):
        self.uid += 1
        return Buf("%s%d" % (name, self.uid))

    def phase_begin(self):
        self.pstack = ExitStack()

    def phase_end(self):
        self.pg.barrier()
        self.pstack.close()
        self.pstack = None

    def mm(self, out, lhsT, rhs, start, stop, reads, writes):
        self.pg.op("pe", lambda e: e.matmul(out, lhsT, rhs, start=start, stop=stop), reads, writes)

    def act(self, out, in_, func, reads, writes, bias=None, scale=None):
        kw = {}
        if bias is not None:
            kw["bias"] = bias
        if scale is not None:
            kw["scale"] = scale
        self.pg.op("act", lambda e: e.activation(out=out, in_=in_, func=func, **kw), reads, writes)

    def tt(self, out, in0, in1, op, reads, writes, eng="dve"):
        self.pg.op(eng, lambda e: e.tensor_tensor(out=out, in0=in0, in1=in1, op=op), reads, writes)

    def ts(self, out, in0, s1, s2, op0, op1, reads, writes, eng="dve"):
        if op1 is None:
            self.pg.op(eng, lambda e: e.tensor_scalar(out=out, in0=in0, scalar1=s1, scalar2=None, op0=op0),
                       reads, writes)
        else:
            self.pg.op(eng, lambda e: e.tensor_scalar(out=out, in0=in0, scalar1=s1, scalar2=s2, op0=op0, op1=op1),
                       reads, writes)

    def stt(self, out, in0, scalar, in1, op0, op1, reads, writes):
        self.pg.op("dve", lambda e: e.scalar_tensor_tensor(out=out, in0=in0, scalar=scalar, in1=in1,
                                                          op0=op0, op1=op1), reads, writes)

    def cp(self, out, in_, reads, writes, eng="dve"):
        self.pg.op(eng, lambda e: e.tensor_copy(out=out, in_=in_), reads, writes)

    def ld(self, out, in_, reads, writes, q="sp"):
        self.pg.dma(q, lambda e: e.dma_start(out=out, in_=in_), reads, writes)

    def build(self):
        nc, pg, T, NT, NC = self.nc, self.pg, self.T, self.NT, self.NC
        g = self.gstack
        I = {}
        I["xT"] = self.dram_in("xT", [D, T])
        I["cT"] = self.dram_in("cT", [P, 8])
        I["rel_bias"] = self.dram_in("rel_bias", [32, 16])
        I["ada_w"] = self.dram_in("ada_w", [NL, D, 6 * D])
        I["adab"] = self.dram_in("adab", [NL, P, 48])
        I["an"] = self.dram_in("an", [P, NL, 8])
        I["mn"] = self.dram_in("mn", [P, NL, 8])
        I["mlp_w1"] = self.dram_in("mlp_w1", [NL, D, DFF])
        I["mlp_w2"] = self.dram_in("mlp_w2", [NL, DFF, D])
        I["a_w_in"] = self.dram_in("a_w_in", [2, D, 3 * D])
        I["a_w_out"] = self.dram_in("a_w_out", [2, D, D])
        I["a_lambda"] = self.dram_in("a_lambda", [2, 256])
        I["a_subln"] = self.dram_in("a_subln", [P, 2])
        I["kv_ada_w"] = self.dram_in("kv_ada_w", [D, 2 * D])
        I["kvadab"] = self.dram_in("kvadab", [P, 16])
        I["kvn"] = self.dram_in("kvn", [P, 8])
        I["w_kv"] = self.dram_in("w_kv", [D, 1536])
        I["cmp_posT"] = self.dram_in("cmp_posT", [2, 64, 32])
        I["cmp_w1"] = self.dram_in("cmp_w1", [2, 2048, 256])
        I["cmp_w2"] = self.dram_in("cmp_w2", [2, 256, 64])
        I["b_w_in"] = self.dram_in("b_w_in", [2, D, 1072])
        I["b_w_out"] = self.dram_in("b_w_out", [2, D, D])
        I["fnorm"] = self.dram_in("fnorm", [P, 8])
        I["c_oh"] = self.dram_in("c_oh", [33, 384])
        I["c_w4"] = self.dram_in("c_w4", [P, 128])
        I["c_ident"] = self.dram_in("c_ident", [P, 128])
        I["c_ex"] = self.dram_in("c_ex", [64, NT, 128])
        I["c_ov"] = self.dram_in("c_ov", [P, 2, 64])
        I["c_maskc"] = self.dram_in("c_maskc", [P, 2, T])
        I["c_keep"] = self.dram_in("c_keep", [P, NT, 64])
        I["c_add"] = self.dram_in("c_add", [P, NT, 64])
        self.I = I
        outT = self.dram_out("outT", [D, T])
        S = {}
        S["xT"] = self.dram("s_xT", [D, T], F32)
        S["qT"] = self.dram("s_qT", [D, T], BF16)
        S["kT"] = self.dram("s_kT", [D, T], BF16)
        S["vtok"] = self.dram("s_vtok", [T, D], BF16)
        S["oT"] = self.dram("s_oT", [D, T], BF16)
        S["kvT"] = self.dram("s_kvT", [1536, T], BF16)
        S["vslc"] = self.dram("s_vslc", [T, 256], BF16)
        S["vwin"] = self.dram("s_vwin", [T, 256], BF16)
        S["gT"] = self.dram("s_gT", [48, T], F32)
        S["tT"] = self.dram("s_tT", [16, 384], F32)
        S["d0"] = self.dram("s_d0", [P, 16, 128], F32)
        S["d1"] = self.dram("s_d1", [P, 16, 128], F32)
        self.S = S
        SB = {k: [self.B(k) for _ in range(NC)] for k in ("xT", "qT", "kT", "vtok", "oT", "kvT", "vslc", "vwin", "gT")}
        for k in ("tT", "d0", "d1"):
            SB[k] = [self.B(k)]
        self.SB = SB
        ps = [g.enter_context(nc.psum_tensor("ps%d" % i, [P, 512], F32)) for i in range(7)]
        psb = g.enter_context(nc.psum_tensor("psbf", [P, 1024], BF16))
        self.ps = ps
        self.psb = psb
        self.psB = [self.B("ps") for _ in range(8)]
        K = {}
        K["ones_bf"] = self.sb([P, 128], BF16, True, "ones")
        K["onesD"] = self.sb([P, 128], BF16, True, "onesD")
        K["onesH"] = self.sb([P, 128], BF16, True, "onesH")
        K["ones32"] = self.sb([P, 128], F32, True, "ones32")
        K["ident"] = self.sb([P, 128], BF16, True, "ident")
        K["cact"] = self.sb([P, 8], F32, True, "cact")
        K["mod"] = self.sb([P, NL, 48], F32, True, "mod")
        K["kvmod"] = self.sb([P, 16], F32, True, "kvmod")
        K["g1"] = self.sb([P, NL, 8], F32, True, "g1")
        K["g2"] = self.sb([P, NL, 8], F32, True, "g2")
        K["gkv"] = self.sb([P, 8], F32, True, "gkv")
        K["fn"] = self.sb([P, 8], F32, True, "fn")
        K["zero8"] = self.sb([P, 8], F32, True, "zero8")
        K["b31"] = self.sb([P, 16], F32, True, "b31")
        K["lamneg"] = self.sb([P, 2], F32, True, "lamneg")
        K["subg"] = self.sb([P, 2], F32, True, "subg")
        K["kcmpT"] = self.sb([64, 4, 256], BF16, True, "kcmpT")
        K["vcmp"] = self.sb([P, 4, 2, 64], BF16, True, "vcmp")
        K["bar"] = self.sb([P, 16], F32, True, "bar")
        K["barbf"] = self.sb([P, 4], BF16, True, "barbf")
        self.K = K
        KB = {k: self.B(k) for k in K}
        self.KB = KB
        pg._bar_tiles = dict(ps=ps[6], bf=K["barbf"], src=K["bar"][:, 0:1], a=K["bar"][:, 1:2], v=K["bar"][:, 2:3],
                             g=K["bar"][:, 3:4], s=K["bar"][:, 4:5], b_pe=self.psB[6], b_act=self.B(), b_dve=self.B(),
                             b_pool=self.B(), b_sp=self.B())

        self.phase_setup()
        for l in range(self.layers):
            if l < 2:
                self.phase_a_proj(l)
                self.phase_a_attn(l)
                wo = I["a_w_out"][l]
            else:
                self.phase_b_proj(l)
                self.phase_b_attn(l)
                wo = I["b_w_out"][l - 2]
            self.phase_outproj(l, wo)
            self.phase_mlp(l)
            if l == 1:
                self.phase_kv()
                self.phase_cmp()
        self.phase_final(outT)
        if self.debug:
            dbg = {}
            for k in self.debug:
                t = S[k]
                o = self.dram_out("dbg_" + k, list(t.shape), t.dtype)
                self.ld(o, t, reads=SB[k], writes=[self.B()])
        pg.emit(g)
        return nc

    def phase_setup(self):
        nc, pg, I, K, KB, S, SB = self.nc, self.pg, self.I, self.K, self.KB, self.S, self.SB
        ps, psB = self.ps, self.psB
        self.phase_begin()
        pg.op("dve", lambda e: e.memset(K["bar"][:], 0.0), [], [KB["bar"]])
        pg.op("dve", lambda e: e.memset(K["barbf"][:], 0.0), [], [KB["barbf"]])
        pg.op("dve", lambda e: e.memset(K["ones_bf"][:], 1.0), [], [KB["ones_bf"]])
        pg.op("dve", lambda e: e.memset(K["onesD"][:], 1.0 / 1024), [], [KB["onesD"]])
        pg.op("dve", lambda e: e.memset(K["onesH"][:], 1.0 / 128), [], [KB["onesH"]])
        pg.op("dve", lambda e: e.memset(K["ones32"][:], 1.0), [], [KB["ones32"]])
        pg.op("dve", lambda e: e.memset(K["zero8"][:], 0.0), [], [KB["zero8"]])
        pg.op("dve", lambda e: e.memset(K["vcmp"][:], 0.0), [], [KB["vcmp"]])
        pg.op("dve", lambda e: e.memset(K["kcmpT"][:], 0.0), [], [KB["kcmpT"]])
        self.ld(K["ident"][:], I["c_ident"][:, :], [], [KB["ident"]], q="pool")
        self.ld(K["fn"][:], I["fnorm"][:, :], [], [KB["fn"]])
        for c in range(self.NC):
            cs = slice(c * 512, (c + 1) * 512)
            self.ld(S["xT"][:, cs], I["xT"][:, cs], [], [SB["xT"][c]])
        craw = self.sb([P, 8], F32)
        b_craw = self.B()
        self.ld(craw[:], I["cT"][:, :], [], [b_craw])
        csig = self.sb([P, 8], F32)
        b_csig = self.B()
        self.act(csig[:], craw[:], AF.Sigmoid, [b_craw], [b_csig])
        self.tt(K["cact"][:], craw[:], csig[:], ALU.mult, [b_craw, b_csig], [KB["cact"]])
        wt = [self.sb([P, 8, 512], F32) for _ in range(2)]
        wtB = [self.B() for _ in range(2)]
        adab = self.sb([P, NL, 48], F32)
        b_adab = self.B()
        self.ld(adab[:], I["adab"].rearrange("l p j -> p l j"), [], [b_adab])
        kvadab = self.sb([P, 16], F32)
        b_kvadab = self.B()
        self.ld(kvadab[:], I["kvadab"][:, :], [], [b_kvadab])
        blk = 0
        jobs = [(I["ada_w"][l], 12, l) for l in range(NL)] + [(I["kv_ada_w"], 4, None)]
        for (wsrc, nblk, l) in jobs:
            pacc = ps[0]
            for bi in range(nblk):
                w = wt[blk % 2]
                wb = wtB[blk % 2]
                blk += 1
                self.ld(w[:], wsrc.rearrange("(kc p) n -> p kc n", p=P)[:, :, bi * 512:(bi + 1) * 512], [], [wb])
                for jj in range(4):
                    j = bi * 4 + jj
                    for kc in range(8):
                        self.mm(pacc[:, j:j + 1], w[:, kc, jj * 128:(jj + 1) * 128], K["cact"][:, kc:kc + 1],
                                kc == 0, kc == 7, [wb, KB["cact"]], [psB[0]])
            if l is not None:
                self.tt(K["mod"][:, l, :], pacc[:, 0:48], adab[:, l, :], ALU.add, [psB[0], b_adab], [KB["mod"]])
            else:
                self.tt(K["kvmod"][:], pacc[:, 0:16], kvadab[:], ALU.add, [psB[0], b_kvadab], [KB["kvmod"]])
        an = self.sb([P, NL, 8], F32)
        mn = self.sb([P, NL, 8], F32)
        kvn = self.sb([P, 8], F32)
        b_n = self.B()
        self.ld(an[:], I["an"][:, :, :], [], [b_n])
        b_n2 = self.B()
        self.ld(mn[:], I["mn"][:, :, :], [], [b_n2])
        b_n3 = self.B()
        self.ld(kvn[:], I["kvn"][:, :], [], [b_n3])
        tmp = self.sb([P, NL, 8], F32)
        b_tmp = self.B()
        for (dst, kb, nrm, nb, lo) in ((K["g1"], KB["g1"], an, b_n, 8), (K["g2"], KB["g2"], mn, b_n2, 32)):
            self.tt(tmp[:], K["mod"][:, :, lo:lo + 8], nrm[:], ALU.mult, [KB["mod"], nb], [b_tmp])
            self.tt(dst[:], tmp[:], nrm[:], ALU.add, [b_tmp, nb], [kb])
        tmp2 = self.sb([P, 8], F32)
        b_tmp2 = self.B()
        self.tt(tmp2[:], K["kvmod"][:, 8:16], kvn[:], ALU.mult, [KB["kvmod"], b_n3], [b_tmp2])
        self.tt(K["gkv"][:], tmp2[:], kvn[:], ALU.add, [b_tmp2, b_n3], [KB["gkv"]])
        tab = self.sb([33, 16], F32)
        b_tab = self.B()
        pg.op("dve", lambda e: e.memset(tab[32:33, :], NEG), [], [b_tab])
        b_tab2 = self.B()
        self.ld(tab[0:32, :], I["rel_bias"][:, :], [b_tab], [b_tab2])
        oh = self.sb([33, 384], F32)
        b_oh = self.B()
        self.ld(oh[:], I["c_oh"][:, :], [], [b_oh])
        self.mm(ps[1][0:16, 0:384], tab[:, :], oh[:, :], True, True, [b_tab, b_tab2, b_oh], [psB[1]])
        tsb = self.sb([16, 384], F32)
        b_tsb = self.B()
        self.cp(tsb[:], ps[1][0:16, 0:384], [psB[1]], [b_tsb])
        self.ld(S["tT"][:, :], tsb[:], [b_tsb], SB["tT"])
        for k in range(128):
            self.ld(S["d0"][k:k + 1, :, :], S["tT"][:, 127 - k:255 - k].rearrange("(o m) q -> o m q", o=1),
                    SB["tT"], [self.B()], q=("sp" if k % 2 == 0 else "act"))
            self.ld(S["d1"][k:k + 1, :, :], S["tT"][:, 255 - k:383 - k].rearrange("(o m) q -> o m q", o=1),
                    SB["tT"], [self.B()], q=("sp" if k % 2 == 0 else "act"))
        self.ld(K["b31"][:], bass.AP(I["rel_bias"].tensor, 31 * 16, [[0, P], [1, 16]]), [], [KB["b31"]])
        lam = self.sb([P, 2, 256], F32)
        b_lam = self.B()
        self.ld(lam[:], bass.AP(I["a_lambda"].tensor, 0, [[0, P], [256, 2], [1, 256]]), [], [b_lam])
        sub = self.sb([P, 2], F32)
        b_sub = self.B()
        self.ld(sub[:], I["a_subln"][:, :], [], [b_sub])
        prod = self.sb([P, 2, 2, 64], F32)
        b_prod = self.B()
        red = self.sb([P, 4], F32)
        b_red = self.B()
        for l in range(2):
            for i in range(2):
                self.tt(prod[:, l, i, :], lam[:, l, (2 * i) * 64:(2 * i + 1) * 64],
                        lam[:, l, (2 * i + 1) * 64:(2 * i + 2) * 64], ALU.mult, [b_lam], [b_prod])
        pg.op("dve", lambda e: e.tensor_reduce(out=red[:], in_=prod[:].rearrange("p l i d -> p (l i) d"),
                                               axis=AX.X, op=ALU.add), [b_prod], [b_red])
        ered = self.sb([P, 4], F32)
        b_ered = self.B()
        self.act(ered[:], red[:], AF.Exp, [b_red], [b_ered])
        for l in range(2):
            lam_init = 0.8 - 0.6 * math.exp(-0.3 * l)
            self.tt(K["lamneg"][:, l:l + 1], ered[:, 2 * l + 1:2 * l + 2], ered[:, 2 * l:2 * l + 1], ALU.subtract,
                    [b_ered], [KB["lamneg"]])
            self.ts(K["lamneg"][:, l:l + 1], K["lamneg"][:, l:l + 1], -lam_init, None, ALU.add, None,
                    [KB["lamneg"]], [KB["lamneg"]])
            self.ts(K["subg"][:, l:l + 1], sub[:, l:l + 1], 1.0 - lam_init, None, ALU.mult, None,
                    [b_sub], [KB["subg"]])
        self.phase_end()

    def norm_mod(self, xt, xb, N, gvec, shvec, gB, hout, hB, sq, sqB, rstd, rB, psi, tout=None, tB=None):
        K, KB, ps, psB = self.K, self.KB, self.ps, self.psB
        if tout is None:
            tout, tB = xt, xb
        self.act(sq[:, :, 0:N], xt[:, :, 0:N], AF.Square, [xb], [sqB])
        for j in range(8):
            self.mm(ps[psi][:, 0:N], K["onesD"][:, :], sq[:, j, 0:N], j == 0, j == 7, [KB["onesD"], sqB], [psB[psi]])
        self.act(rstd[:, 0:N], ps[psi][:, 0:N], AF.Ln, [psB[psi], self.KB["bar"]], [rB], bias=self.eps_ap)
        self.act(rstd[:, 0:N], rstd[:, 0:N], AF.Exp, [rB], [rB], scale=-0.5)
        for j in range(8):
            self.tt(tout[:, j, 0:N], xt[:, j, 0:N], rstd[:, 0:N], ALU.mult, [xb, rB], [tB])
        for j in range(8):
            self.act(hout[:, j, 0:N], tout[:, j, 0:N], AF.Identity, [tB, gB], [hB],
                     bias=shvec[:, j:j + 1], scale=gvec[:, j:j + 1])

    @property
    def eps_ap(self):
        if not hasattr(self, "_eps_done"):
            self._eps_done = True
            K, KB = self.K, self.KB
            self.pg.op("dve", lambda e: e.memset(K["bar"][:, 5:6], EPS), [], [KB["bar"]])
        return self.K["bar"][:, 5:6]

    def load_w_bf16(self, dst, dstB, src_view, ncols, blk=512):
        nb = (ncols + blk - 1) // blk
        for i in range(nb):
            a, b = i * blk, min(ncols, (i + 1) * blk)
            self.ld(dst[:, :, a:b], src_view[:, :, a:b], [], [dstB[i]], q="pool")

    def phase_a_proj(self, l):
        I, K, KB, S, SB, ps, psB = self.I, self.K, self.KB, self.S, self.SB, self.ps, self.psB
        NC = self.NC
        self.phase_begin()
        w = self.sb([P, 8, 3072], BF16)
        wB = [self.B() for _ in range(6)]
        self.load_w_bf16(w, wB, I["a_w_in"][l].rearrange("(kc p) n -> p kc n", p=P), 3072)
        xt = [self.sb([P, 8, 512], F32) for _ in range(2)]
        xB = [self.B() for _ in range(2)]
        sq = self.sb([P, 8, 512], BF16)
        sqB = self.B()
        rstd = self.sb([P, 512], F32)
        rB = self.B()
        h = [self.sb([P, 8, 512], BF16) for _ in range(2)]
        hB = [self.B() for _ in range(2)]
        qst = [self.sb([P, 16, 512], BF16) for _ in range(2)]
        qB = [self.B() for _ in range(2)]
        kB = [self.B() for _ in range(2)]
        vst = [self.sb([P, 4, 1024], BF16) for _ in range(2)]
        vB = [self.B() for _ in range(2)]
        xv = S["xT"].rearrange("(j p) t -> p j t", p=P)
        ring = 0
        for c in range(NC):
            cs = slice(c * 512, (c + 1) * 512)
            x_, xb_ = xt[c % 2], xB[c % 2]
            self.ld(x_[:], xv[:, :, cs], [SB["xT"][c]], [xb_])
            h_, hb_ = h[c % 2], hB[c % 2]
            self.norm_mod(x_, xb_, 512, K["g1"][:, l, :], K["mod"][:, l, 0:8], KB["g1"], h_, hb_, sq, sqB, rstd, rB, 2)
            q_, qb_, kb_ = qst[c % 2], qB[c % 2], kB[c % 2]
            for m in range(16):
                pi = ring % 2
                ring += 1
                for kc in range(8):
                    self.mm(ps[pi][:, :], w[:, kc, m * 128:(m + 1) * 128], h_[:, kc, :], kc == 0, kc == 7,
                            [wB[m // 4], hb_], [psB[pi]])
                if m < 8:
                    self.act(q_[:, m, :], ps[pi][:, :], AF.Copy, [psB[pi]], [qb_], scale=0.125)
                else:
                    self.cp(q_[:, m, :], ps[pi][:, :], [psB[pi]], [kb_])
            self.ld(S["qT"].rearrange("(m p) t -> p m t", p=P)[:, :, cs], q_[:, 0:8, :], [qb_], [SB["qT"][c]])
            self.ld(S["kT"].rearrange("(m p) t -> p m t", p=P)[:, :, cs], q_[:, 8:16, :], [kb_], [SB["kT"][c]])
            v_, vb_ = vst[c % 2], vB[c % 2]
            for tt in range(4):
                for half in range(2):
                    pi = ring % 2
                    ring += 1
                    for kc in range(8):
                        self.mm(ps[pi][:, :], h_[:, kc, tt * 128:(tt + 1) * 128],
                                w[:, kc, 2048 + half * 512:2048 + (half + 1) * 512], kc == 0, kc == 7,
                                [wB[4 + half], hb_], [psB[pi]])
                    self.cp(v_[:, tt, half * 512:(half + 1) * 512], ps[pi][:, :], [psB[pi]], [vb_],
                            eng=("dve" if half == 0 else "act_copy"))
            self.ld(S["vtok"].rearrange("(tt p) e -> p tt e", p=P)[:, c * 4:(c + 1) * 4, :], v_[:], [vb_],
                    [SB["vtok"][c]])
        self.phase_end()

    def attn_tiles(self, tiles, stageA, stageB):
        if not tiles:
            return
        stageA(tiles[0], 0)
        for i, t in enumerate(tiles):
            if i + 1 < len(tiles):
                stageA(tiles[i + 1], i + 1)
            stageB(t, i)

    def load_bias_tiles(self):
        S, SB = self.S, self.SB
        d0 = self.sb([P, 16, 128], F32)
        d1 = self.sb([P, 16, 128], F32)
        w4 = self.sb([P, 128], F32)
        bd = self.B()
        self.ld(d0[:], S["d0"][:, :, :], SB["d0"], [bd])
        bd1 = self.B()
        self.ld(d1[:], S["d1"][:, :, :], SB["d1"], [bd1])
        bw = self.B()
        self.ld(w4[:], self.I["c_w4"][:, :], [], [bw])
        return d0, d1, w4, [bd, bd1, bw]

    def phase_a_attn(self, l):
        I, K, KB, S, SB, ps, psB = self.I, self.K, self.KB, self.S, self.SB, self.ps, self.psB
        NC, NT, T = self.NC, self.NT, self.T
        self.phase_begin()
        d0, d1, w4, dB = self.load_bias_tiles()
        qh = [self.sb([P, T], BF16) for _ in range(2)]
        kh = [self.sb([P, T], BF16) for _ in range(2)]
        vh = [self.sb([P, NT, 128], BF16) for _ in range(2)]
        lB = [[self.B() for _ in range(3)] for _ in range(2)]
        Pt = [self.sb([P, 512], BF16) for _ in range(3)]
        PB = [self.B() for _ in range(3)]
        r0 = self.sb([P, 512], F32)
        r1 = self.sb([P, 512], F32)
        t0 = self.sb([P, 512], F32)
        t1 = self.sb([P, 512], F32)
        osq = self.sb([P, 512], BF16)
        ost = [self.sb([P, 512], BF16) for _ in range(2)]
        bb = {k: self.B() for k in ("r0", "r1", "t0", "t1", "osq", "rs")}
        ostB = [self.B(), self.B()]
        rs = self.sb([P, 512], F32)
        ctr = {"s": 0, "p": 0, "o": 0}
        for h in range(8):
            q_, k_, v_ = qh[h % 2], kh[h % 2], vh[h % 2]
            lb = lB[h % 2]
            self.ld(q_[:], S["qT"][h * 128:(h + 1) * 128, :], SB["qT"], [lb[0]])
            self.ld(k_[:], S["kT"][h * 128:(h + 1) * 128, :], SB["kT"], [lb[1]])
            self.ld(v_[:], S["vtok"].rearrange("(kt p) e -> p kt e", p=P)[:, :, h * 128:(h + 1) * 128], SB["vtok"],
                    [lb[2]])
            for qc in range(NC):
                for m in range(2):
                    hm = h * 2 + m
                    o_ps, o_b = ps[2 + m], psB[2 + m]
                    L_ps, L_b = ps[4 + m], psB[4 + m]
                    tiles = list(range(0, 4 * qc + 4))
                    nk = len(tiles)
                    st = {}

                    def stageA(kt, i, m=m, q_=q_, k_=k_, lb=lb, st=st, qc=qc):
                        si = ctr["s"] % 2
                        ctr["s"] += 1
                        st[kt] = si
                        c0 = max(0, kt - 4 * qc) * 128
                        self.mm(ps[si][:, c0:512], k_[m * 64:(m + 1) * 64, kt * 128:(kt + 1) * 128],
                                q_[m * 64:(m + 1) * 64, qc * 512 + c0:(qc + 1) * 512], True, True,
                                [lb[0], lb[1]], [psB[si]])

                    def stageB(kt, i, m=m, hm=hm, v_=v_, lb=lb, st=st, qc=qc, o_ps=o_ps, o_b=o_b, L_ps=L_ps,
                               L_b=L_b, nk=nk):
                        si = st[kt]
                        c0 = max(0, kt - 4 * qc) * 128
                        for ii in range(4):
                            delta = 4 * qc + ii - kt
                            if delta == 0 or delta == 1:
                                dd = d0 if delta == 0 else d1
                                self.tt(ps[si][:, ii * 128:(ii + 1) * 128], ps[si][:, ii * 128:(ii + 1) * 128],
                                        dd[:, hm, :], ALU.add, [psB[si]] + dB, [psB[si]])
                        pi = ctr["p"] % 3
                        ctr["p"] += 1
                        self.act(Pt[pi][:, c0:512], ps[si][:, c0:512], AF.Exp, [psB[si], KB["b31"]], [PB[pi]],
                                 bias=K["b31"][:, hm:hm + 1])
                        self.mm(o_ps[:, c0:512], v_[:, kt, :], Pt[pi][:, c0:512], i == 0, i == nk - 1,
                                [lb[2], PB[pi]], [o_b])
                        self.mm(L_ps[:, c0:512], K["ones_bf"][:, :], Pt[pi][:, c0:512], i == 0, i == nk - 1,
                                [KB["ones_bf"], PB[pi]], [L_b])

                    self.attn_tiles(tiles, stageA, stageB)
                for m, (rr, tt_) in enumerate(((r0, t0), (r1, t1))):
                    rb, tb = bb["r%d" % m], bb["t%d" % m]
                    self.act(rr[:], ps[4 + m][:, :], AF.Ln, [psB[4 + m]], [rb])
                    self.act(rr[:], rr[:], AF.Exp, [rb], [rb], scale=-1.0)
                    self.tt(tt_[:], ps[2 + m][:, :], rr[:], ALU.mult, [psB[2 + m], rb], [tb])
                self.stt(t0[:], t1[:], K["lamneg"][:, l:l + 1], t0[:], ALU.mult, ALU.add,
                         [bb["t0"], bb["t1"], KB["lamneg"]], [bb["t0"]])
                self.act(osq[:], t0[:], AF.Square, [bb["t0"]], [bb["osq"]])
                self.mm(ps[6][:, :], K["onesH"][:, :], osq[:], True, True, [KB["onesH"], bb["osq"]], [psB[6]])
                self.act(rs[:], ps[6][:, :], AF.Ln, [psB[6], KB["bar"]], [bb["rs"]], bias=self.eps_ap)
                self.act(rs[:], rs[:], AF.Exp, [bb["rs"]], [bb["rs"]], scale=-0.5)
                self.tt(t0[:], t0[:], rs[:], ALU.mult, [bb["t0"], bb["rs"]], [bb["t0"]])
                oi = ctr["o"] % 2
                ctr["o"] += 1
                self.act(ost[oi][:], t0[:], AF.Identity, [bb["t0"], KB["subg"]], [ostB[oi]], scale=K["subg"][:, l:l + 1])
                self.ld(S["oT"][h * 128:(h + 1) * 128, qc * 512:(qc + 1) * 512], ost[oi][:], [ostB[oi]],
                        [SB["oT"][qc]])
        self.phase_end()

    def phase_outproj(self, l, wo_src):
        I, K, KB, S, SB, ps, psB = self.I, self.K, self.KB, self.S, self.SB, self.ps, self.psB
        NC = self.NC
        self.phase_begin()
        w = self.sb([P, 8, 1024], BF16)
        wB = [self.B() for _ in range(2)]
        self.load_w_bf16(w, wB, wo_src.rearrange("(kc p) n -> p kc n", p=P), 1024)
        xt = [self.sb([P, 8, 512], F32) for _ in range(2)]
        xB = [self.B() for _ in range(2)]
        ot = [self.sb([P, 8, 512], BF16) for _ in range(2)]
        oB = [self.B() for _ in range(2)]
        xv = S["xT"].rearrange("(j p) t -> p j t", p=P)
        ov = S["oT"].rearrange("(j p) t -> p j t", p=P)
        ring = 0
        for c in range(NC):
            cs = slice(c * 512, (c + 1) * 512)
            x_, xb_ = xt[c % 2], xB[c % 2]
            o_, ob_ = ot[c % 2], oB[c % 2]
            self.ld(x_[:], xv[:, :, cs], [SB["xT"][c]], [xb_])
            self.ld(o_[:], ov[:, :, cs], [SB["oT"][c]], [ob_])
            for j in range(8):
                pi = ring % 2
                ring += 1
                for hc in range(8):
                    self.mm(ps[pi][:, :], w[:, hc, j * 128:(j + 1) * 128], o_[:, hc, :], hc == 0, hc == 7,
                            [wB[j // 4], ob_], [psB[pi]])
                self.stt(x_[:, j, :], ps[pi][:, :], K["mod"][:, l, 16 + j:17 + j], x_[:, j, :], ALU.mult, ALU.add,
                         [psB[pi], xb_, KB["mod"]], [xb_])
            self.ld(xv[:, :, cs], x_[:], [xb_], [SB["xT"][c]])
        self.phase_end()

    def phase_mlp(self, l):
        I, K, KB, S, SB, ps, psB = self.I, self.K, self.KB, self.S, self.SB, self.ps, self.psB
        T = self.T
        N = 256
        self.phase_begin()
        w1 = self.sb([P, 8, DFF], BF16)
        w1B = [self.B() for _ in range(8)]
        w2 = self.sb([P, 32, D], BF16)
        w2B = [self.B() for _ in range(8)]
        self.load_w_bf16(w1, w1B, I["mlp_w1"][l].rearrange("(kc p) n -> p kc n", p=P), DFF)
        v2 = I["mlp_w2"][l].rearrange("(f p) n -> p f n", p=P)
        for i in range(8):
            self.ld(w2[:, i * 4:(i + 1) * 4, :], v2[:, i * 4:(i + 1) * 4, :], [], [w2B[i]], q="pool")
        xt = [self.sb([P, 8, N], F32) for _ in range(2)]
        xB = [self.B() for _ in range(2)]
        tt_ = self.sb([P, 8, N], F32)
        tB = self.B()
        sq = self.sb([P, 8, N], BF16)
        sqB = self.B()
        rstd = self.sb([P, N], F32)
        rB = self.B()
        h = self.sb([P, 8, N], BF16)
        hB = self.B()
        hid = self.sb([P, 32, N], BF16)
        hidB = self.B()
        r32 = [self.sb([P, N], F32) for _ in range(2)]
        r32B = [self.B() for _ in range(2)]
        xv = S["xT"].rearrange("(j p) t -> p j t", p=P)
        ring = 0
        for c in range(T // N):
            cs = slice(c * N, (c + 1) * N)
            sbx = SB["xT"][(c * N) // 512]
            x_, xb_ = xt[c % 2], xB[c % 2]
            self.ld(x_[:], xv[:, :, cs], [sbx], [xb_])
            self.norm_mod(x_, xb_, N, K["g2"][:, l, :], K["mod"][:, l, 24:32], KB["g2"], h, hB, sq, sqB, rstd, rB, 2,
                          tout=tt_, tB=tB)
            for f in range(32):
                pi = ring % 2
                ring += 1
                for kc in range(8):
                    self.mm(ps[pi][:, 0:N], w1[:, kc, f * 128:(f + 1) * 128], h[:, kc, :], kc == 0, kc == 7,
                            [w1B[f // 4], hB], [psB[pi]])
                ri = f % 2
                self.act(r32[ri][:], ps[pi][:, 0:N], AF.Relu, [psB[pi]], [r32B[ri]])
                self.tt(hid[:, f, :], r32[ri][:], r32[ri][:], ALU.mult, [r32B[ri]], [hidB])
            for j in range(8):
                pi = 3 + (ring % 2)
                ring += 1
                for f in range(32):
                    self.mm(ps[pi][:, 0:N], w2[:, f, j * 128:(j + 1) * 128], hid[:, f, :], f == 0, f == 31,
                            [w2B[f // 4], hidB], [psB[pi]])
                self.stt(x_[:, j, :], ps[pi][:, 0:N], K["mod"][:, l, 40 + j:41 + j], x_[:, j, :], ALU.mult, ALU.add,
                         [psB[pi], xb_, KB["mod"]], [xb_])
            self.ld(xv[:, :, cs], x_[:], [xb_], [sbx])
        self.phase_end()

    def phase_final(self, outT):
        I, K, KB, S, SB, ps, psB = self.I, self.K, self.KB, self.S, self.SB, self.ps, self.psB
        self.phase_begin()
        xt = [self.sb([P, 8, 512], F32) for _ in range(2)]
        xB = [self.B() for _ in range(2)]
        yt = [self.sb([P, 8, 512], F32) for _ in range(2)]
        yB = [self.B() for _ in range(2)]
        sq = self.sb([P, 8, 512], BF16)
        sqB = self.B()
        rstd = self.sb([P, 512], F32)
        rB = self.B()
        xv = S["xT"].rearrange("(j p) t -> p j t", p=P)
        ov = outT.rearrange("(j p) t -> p j t", p=P)
        for c in range(self.NC):
            cs = slice(c * 512, (c + 1) * 512)
            x_, xb_ = xt[c % 2], xB[c % 2]
            self.ld(x_[:], xv[:, :, cs], [SB["xT"][c]], [xb_])
            self.norm_mod(x_, xb_, 512, K["fn"], K["zero8"], KB["fn"], yt[c % 2], yB[c % 2], sq, sqB, rstd, rB, 2)
            self.ld(ov[:, :, cs], yt[c % 2][:], [yB[c % 2]], [self.B()])
        self.phase_end()

    def phase_kv(self):
        I, K, KB, S, SB, ps, psB = self.I, self.K, self.KB, self.S, self.SB, self.ps, self.psB
        NC = self.NC
        self.phase_begin()
        w = self.sb([P, 8, 1536], BF16)
        wB = [self.B() for _ in range(3)]
        self.load_w_bf16(w, wB, I["w_kv"].rearrange("(kc p) n -> p kc n", p=P), 1536)
        xt = [self.sb([P, 8, 512], F32) for _ in range(2)]
        xB = [self.B() for _ in range(2)]
        sq = self.sb([P, 8, 512], BF16)
        sqB = self.B()
        rstd = self.sb([P, 512], F32)
        rB = self.B()
        h = [self.sb([P, 8, 512], BF16) for _ in range(2)]
        hB = [self.B() for _ in range(2)]
        kst = [self.sb([P, 12, 512], BF16) for _ in range(2)]
        kB = [self.B() for _ in range(2)]
        vst = [self.sb([P, 4, 2, 256], BF16) for _ in range(2)]
        vB = [self.B() for _ in range(2)]
        xv = S["xT"].rearrange("(j p) t -> p j t", p=P)
        ring = 0
        for c in range(NC):
            cs = slice(c * 512, (c + 1) * 512)
            x_, xb_ = xt[c % 2], xB[c % 2]
            self.ld(x_[:], xv[:, :, cs], [SB["xT"][c]], [xb_])
            h_, hb_ = h[c % 2], hB[c % 2]
            self.norm_mod(x_, xb_, 512, K["gkv"], K["kvmod"][:, 0:8], KB["gkv"], h_, hb_, sq, sqB, rstd, rB, 2)
            k_, kb_ = kst[c % 2], kB[c % 2]
            for m in range(12):
                pi = ring % 2
                ring += 1
                for kc in range(8):
                    self.mm(ps[pi][:, :], w[:, kc, m * 128:(m + 1) * 128], h_[:, kc, :], kc == 0, kc == 7,
                            [wB[m // 4], hb_], [psB[pi]])
                self.cp(k_[:, m, :], ps[pi][:, :], [psB[pi]], [kb_], eng=("dve" if m % 2 == 0 else "act_copy"))
            self.ld(S["kvT"].rearrange("(m p) t -> p m t", p=P)[:, :, cs], k_[:], [kb_], [SB["kvT"][c]])
            v_, vb_ = vst[c % 2], vB[c % 2]
            for tt in range(4):
                pi = ring % 2
                ring += 1
                for si, s0 in enumerate((768, 1280)):
                    for kc in range(8):
                        self.mm(ps[pi][:, si * 256:(si + 1) * 256], h_[:, kc, tt * 128:(tt + 1) * 128],
                                w[:, kc, s0:s0 + 256], kc == 0, kc == 7, [wB[s0 // 512], hb_], [psB[pi]])
                self.cp(v_[:, tt, :, :], ps[pi][:, :].rearrange("p (s e) -> p s e", s=2), [psB[pi]], [vb_])
            self.ld(S["vslc"].rearrange("(tt p) e -> p tt e", p=P)[:, c * 4:(c + 1) * 4, :], v_[:, :, 0, :], [vb_],
                    [SB["vslc"][c]])
            self.ld(S["vwin"].rearrange("(tt p) e -> p tt e", p=P)[:, c * 4:(c + 1) * 4, :], v_[:, :, 1, :], [vb_],
                    [SB["vwin"][c]])
        self.phase_end()

    def phase_cmp(self):
        I, K, KB, S, SB, ps, psB = self.I, self.K, self.KB, self.S, self.SB, self.ps, self.psB
        T = self.T
        ncmp = T // 16 - 1
        self.phase_begin()
        src = [self.sb([64, T], BF16) for _ in range(2)]
        srcB = [self.B() for _ in range(2)]
        w1r = self.sb([64, 32, 256], BF16)
        w2 = self.sb([P, 2, 64], BF16)
        posT = self.sb([64, 32], BF16)
        hidT = self.sb([P, 2, 256], BF16)
        hidB = self.B()
        pre = self.sb([P, 256], F32)
        u = self.sb([P, 256], F32)
        bias = self.sb([P, 2], F32)
        bB = {k: self.B() for k in ("pre", "u", "bias")}
        pg = self.pg
        pg.op("dve", lambda e: e.memset(hidT[:], 0.0), [], [hidB])
        it = 0
        wb = [self.B(), self.B(), self.B()]
        for s in range(2):
            self.ld(w1r[:], I["cmp_w1"][s].rearrange("(t d) h -> d t h", d=64), [], [wb[0]], q="pool")
            self.ld(w2[:], I["cmp_w2"][s].rearrange("(hc p) d -> p hc d", p=P), [], [wb[1]], q="pool")
            self.ld(posT[:], I["cmp_posT"][s], [], [wb[2]], q="pool")
            for hc in range(2):
                for t in range(32):
                    self.mm(ps[6][:, hc:hc + 1], w1r[:, t, hc * 128:(hc + 1) * 128], posT[:, t:t + 1], t == 0, t == 31,
                            [wb[0], wb[2]], [psB[6]])
            self.cp(bias[:], ps[6][:, 0:2], [psB[6]], [bB["bias"]])
            for g in range(4):
                sr, srb = src[it % 2], srcB[it % 2]
                it += 1
                r0 = s * 256 + g * 64
                self.ld(sr[:], S["kvT"][r0:r0 + 64, :], SB["kvT"], [srb])
                for hc in range(2):
                    for t in range(32):
                        self.mm(ps[hc][:, 0:ncmp], w1r[:, t, hc * 128:(hc + 1) * 128],
                                sr[:, t:t + 16 * (ncmp - 1) + 1:16], t == 0, t == 31, [wb[0], srb], [psB[hc]])
                    self.act(pre[:, 0:ncmp], ps[hc][:, 0:ncmp], AF.Identity, [psB[hc], bB["bias"]], [bB["pre"]],
                             bias=bias[:, hc:hc + 1])
                    self.tt(u[:, 0:ncmp], pre[:, 0:ncmp], pre[:, 0:ncmp], ALU.mult, [bB["pre"]], [bB["u"]])
                    self.ts(u[:, 0:ncmp], u[:, 0:ncmp], 0.044715, 1.0, ALU.mult, ALU.add, [bB["u"]], [bB["u"]])
                    self.tt(u[:, 0:ncmp], u[:, 0:ncmp], pre[:, 0:ncmp], ALU.mult, [bB["u"], bB["pre"]], [bB["u"]])
                    self.act(u[:, 0:ncmp], u[:, 0:ncmp], AF.Sigmoid, [bB["u"]], [bB["u"]],
                             scale=2.0 * math.sqrt(2.0 / math.pi))
                    self.tt(hidT[:, hc, 0:ncmp], u[:, 0:ncmp], pre[:, 0:ncmp], ALU.mult, [bB["u"], bB["pre"]],
                            [hidB])
                if s == 0:
                    for hc in range(2):
                        self.mm(ps[2][0:64, 0:ncmp], w2[:, hc, :], hidT[:, hc, 0:ncmp], hc == 0, hc == 1,
                                [wb[1], hidB], [psB[2]])
                    self.cp(K["kcmpT"][:, g, 0:ncmp], ps[2][0:64, 0:ncmp], [psB[2]], [KB["kcmpT"]])
                else:
                    for nt in range(2):
                        nn = min(128, ncmp - nt * 128)
                        if nn <= 0:
                            continue
                        for hc in range(2):
                            self.mm(ps[3][0:nn, nt * 64:(nt + 1) * 64], hidT[:, hc, nt * 128:nt * 128 + nn],
                                    w2[:, hc, :], hc == 0, hc == 1, [wb[1], hidB], [psB[3]])
                        self.cp(K["vcmp"][0:nn, g, nt, :], ps[3][0:nn, nt * 64:(nt + 1) * 64], [psB[3]],
                                [KB["vcmp"]])
        self.phase_end()

    def phase_b_proj(self, l):
        I, K, KB, S, SB, ps, psB = self.I, self.K, self.KB, self.S, self.SB, self.ps, self.psB
        NC = self.NC
        self.phase_begin()
        w = self.sb([P, 8, 1072], BF16)
        wB = [self.B() for _ in range(3)]
        self.load_w_bf16(w, wB, I["b_w_in"][l - 2].rearrange("(kc p) n -> p kc n", p=P), 1072)
        xt = [self.sb([P, 8, 512], F32) for _ in range(2)]
        xB = [self.B() for _ in range(2)]
        sq = self.sb([P, 8, 512], BF16)
        sqB = self.B()
        rstd = self.sb([P, 512], F32)
        rB = self.B()
        h = [self.sb([P, 8, 512], BF16) for _ in range(2)]
        hB = [self.B() for _ in range(2)]
        qst = [self.sb([P, 8, 512], BF16) for _ in range(2)]
        qB = [self.B() for _ in range(2)]
        gst = [self.sb([48, 512], F32) for _ in range(2)]
        gB = [self.B() for _ in range(2)]
        xv = S["xT"].rearrange("(j p) t -> p j t", p=P)
        ring = 0
        for c in range(NC):
            cs = slice(c * 512, (c + 1) * 512)
            x_, xb_ = xt[c % 2], xB[c % 2]
            self.ld(x_[:], xv[:, :, cs], [SB["xT"][c]], [xb_])
            h_, hb_ = h[c % 2], hB[c % 2]
            self.norm_mod(x_, xb_, 512, K["g1"][:, l, :], K["mod"][:, l, 0:8], KB["g1"], h_, hb_, sq, sqB, rstd, rB, 2)
            q_, qb_ = qst[c % 2], qB[c % 2]
            for m in range(8):
                pi = ring % 2
                ring += 1
                for kc in range(8):
                    self.mm(ps[pi][:, :], w[:, kc, m * 128:(m + 1) * 128], h_[:, kc, :], kc == 0, kc == 7,
                            [wB[m // 4], hb_], [psB[pi]])
                self.act(q_[:, m, :], ps[pi][:, :], AF.Copy, [psB[pi]], [qb_], scale=0.125)
            self.ld(S["qT"].rearrange("(m p) t -> p m t", p=P)[:, :, cs], q_[:], [qb_], [SB["qT"][c]])
            pi = ring % 2
            ring += 1
            for kc in range(8):
                self.mm(ps[pi][0:48, :], w[:, kc, 1024:1072], h_[:, kc, :], kc == 0, kc == 7, [wB[2], hb_], [psB[pi]])
            self.act(gst[c % 2][:], ps[pi][0:48, :], AF.Sigmoid, [psB[pi]], [gB[c % 2]])
            self.ld(S["gT"][:, cs], gst[c % 2][:], [gB[c % 2]], [SB["gT"][c]])
        self.phase_end()

    def phase_b_attn(self, l):
        I, K, KB, S, SB, ps, psB = self.I, self.K, self.KB, self.S, self.SB, self.ps, self.psB
        psb = self.psb
        NC, NT, T = self.NC, self.NT, self.T
        pg = self.pg
        self.phase_begin()
        d0, d1, w4, dB = self.load_bias_tiles()
        ex = self.sb([64, NT, 128], BF16)
        exB = self.B()
        self.ld(ex[:], I["c_ex"][:, :, :], [], [exB], q="pool")
        ov = self.sb([P, 2, 64], F32)
        ovB = self.B()
        self.ld(ov[:], I["c_ov"][:, :, :], [], [ovB])
        maskc = self.sb([P, 2, T], BF16)
        mcB = self.B()
        self.ld(maskc[:], I["c_maskc"][:, :, :], [], [mcB], q="pool")
        keep = self.sb([P, NT, 64], BF16)
        addm = self.sb([P, NT, 64], BF16)
        kaB = [self.B(), self.B()]
        self.ld(keep[:], I["c_keep"][:, :, :], [], [kaB[0]], q="pool")
        self.ld(addm[:], I["c_add"][:, :, :], [], [kaB[1]], q="pool")
        ksl = [self.sb([64, T], BF16) for _ in range(1)]
        kwn = [self.sb([64, T], BF16) for _ in range(1)]
        vsl = [self.sb([P, NT, 64], BF16) for _ in range(1)]
        vwn = [self.sb([P, NT, 64], BF16) for _ in range(1)]
        gB_ = [[self.B() for _ in range(4)] for _ in range(1)]
        qg = [self.sb([64, 4, 512], BF16) for _ in range(2)]
        qgB = [self.B() for _ in range(2)]
        gb = self.sb([64, 12, 512], F32)
        gbB = self.B()
        pc32 = [self.sb([P, 512], F32) for _ in range(2)]
        pn32 = [self.sb([P, 512], F32) for _ in range(2)]
        pn16 = [self.sb([P, 512], BF16) for _ in range(2)]
        pcB = [self.B() for _ in range(2)]
        pnB = [self.B() for _ in range(2)]
        pn16B = [self.B() for _ in range(2)]
        rl = self.sb([P, 512], F32)
        rlB = self.B()
        oc = self.sb([64, 4, 512], F32)
        ocB = [self.B() for _ in range(4)]
        impv = self.sb([P, 64], F32)
        impv2 = self.sb([P, 64], F32)
        m8a = self.sb([P, 8], F32)
        m8b = self.sb([P, 8], F32)
        msel = self.sb([P, 4, 64], BF16)
        tkB = {k: self.B() for k in ("impv", "impv2", "m8a", "m8b", "msel")}
        mT = self.sb([64, 512], BF16)
        mTB = self.B()
        mall = self.sb([P, NT, 512], BF16)
        mallB = [self.B() for _ in range(NT)]
        Pt = [self.sb([P, 512], BF16) for _ in range(3)]
        PB = [self.B() for _ in range(3)]
        rr = self.sb([64, 512], F32)
        rrB = self.B()
        acc = self.sb([64, 512], F32)
        accB = self.B()
        tmp = self.sb([64, 512], F32)
        tmpB = self.B()
        ost = [self.sb([64, 4, 512], BF16) for _ in range(2)]
        ostB = [self.B() for _ in range(2)]
        ctr = {"s": 0, "p": 0}
        it = 0
        for g in range(4):
            gi = 0
            r0 = g * 64
            self.ld(ksl[gi][:], S["kvT"][512 + r0:512 + r0 + 64, :], SB["kvT"], [gB_[gi][0]])
            self.ld(kwn[gi][:], S["kvT"][1024 + r0:1024 + r0 + 64, :], SB["kvT"], [gB_[gi][1]])
            self.ld(vsl[gi][:], S["vslc"].rearrange("(kt p) e -> p kt e", p=P)[:, :, r0:r0 + 64], SB["vslc"],
                    [gB_[gi][2]])
            self.ld(vwn[gi][:], S["vwin"].rearrange("(kt p) e -> p kt e", p=P)[:, :, r0:r0 + 64], SB["vwin"],
                    [gB_[gi][3]])
            for qc in range(NC):
                cs = slice(qc * 512, (qc + 1) * 512)
                q_, qb_ = qg[it % 2], qgB[it % 2]
                o_st, o_stB = ost[it % 2], ostB[it % 2]
                it += 1
                self.ld(q_[:], S["qT"].rearrange("(h d) t -> d h t", d=64)[:, g * 4:(g + 1) * 4, cs], [SB["qT"][qc]],
                        [qb_])
                self.ld(gb[:], bass.AP(S["gT"].tensor, g * 12 * T + qc * 512, [[0, 64], [T, 12], [1, 512]]),
                        [SB["gT"][qc]], [gbB])
                for r in range(4):
                    for nt in range(2):
                        si = ctr["s"] % 2
                        ctr["s"] += 1
                        self.mm(ps[si][:, :], K["kcmpT"][:, g, nt * 128:(nt + 1) * 128], q_[:, r, :], True, True,
                                [KB["kcmpT"], qb_], [psB[si]])
                        self.act(pc32[nt][:], ps[si][:, :], AF.Exp, [psB[si]], [pcB[nt]])
                        self.tt(pc32[nt][:], pc32[nt][:], maskc[:, nt, cs], ALU.mult, [pcB[nt], mcB], [pcB[nt]])
                    for nt in range(2):
                        self.mm(ps[4][:, :], K["ones32"][:, :], pc32[nt][:], nt == 0, nt == 1,
                                [KB["ones32"], pcB[nt]], [psB[4]])
                    self.ts(rl[:], ps[4][:, :], 1e-18, None, ALU.max, None, [psB[4]], [rlB])
                    self.act(rl[:], rl[:], AF.Ln, [rlB], [rlB])
                    self.act(rl[:], rl[:], AF.Exp, [rlB], [rlB], scale=-1.0)
                    for nt in range(2):
                        self.tt(pn32[nt][:], pc32[nt][:], rl[:], ALU.mult, [pcB[nt], rlB], [pnB[nt]])
                        self.cp(pn16[nt][:], pn32[nt][:], [pnB[nt]], [pn16B[nt]], eng="pool")
                    for nt in range(2):
                        self.mm(ps[2][0:64, :], K["vcmp"][:, g, nt, :], pn16[nt][:], nt == 0, nt == 1,
                                [KB["vcmp"], pn16B[nt]], [psB[2]])
                    self.cp(oc[:, r, :], ps[2][0:64, :], [psB[2]], [ocB[r]], eng="act_copy")
                    for nt in range(2):
                        for i in range(4):
                            first = (r == 0 and nt == 0 and i == 0)
                            last = (r == 3 and nt == 1 and i == 3)
                            self.mm(ps[5][:, i * 64:(i + 1) * 64], pn32[nt][:, i * 128:(i + 1) * 128], ov[:, nt, :],
                                    first, last, [pnB[nt], ovB], [psB[5]])
                for i in range(4):
                    qb = qc * 4 + i
                    self.tt(impv[:], ps[5][:, i * 64:(i + 1) * 64], keep[:, qb, :], ALU.mult, [psB[5], kaB[0]],
                            [tkB["impv"]])
                    self.tt(impv[:], impv[:], addm[:, qb, :], ALU.add, [tkB["impv"], kaB[1]], [tkB["impv"]])
                    pg.op("dve", lambda e: e.max(out=m8a[:], in_=impv[:]), [tkB["impv"]], [tkB["m8a"]])
                    pg.op("dve", lambda e: e.match_replace(out=impv2[:], in_to_replace=m8a[:], in_values=impv[:],
                                                           imm_value=-3.0e38),
                          [tkB["impv"], tkB["m8a"]], [tkB["impv2"]])
                    pg.op("dve", lambda e: e.max(out=m8b[:], in_=impv2[:]), [tkB["impv2"]], [tkB["m8b"]])
                    self.ts(msel[:, i, :], impv[:], m8b[:, 7:8], None, ALU.is_ge, None, [tkB["impv"], tkB["m8b"]],
                            [tkB["msel"]])
                for i in range(4):
                    pg.op("pe", lambda e, i=i: e.transpose(out=psb[0:64, i * 128:(i + 1) * 128], in_=msel[:, i, :],
                                                           identity=K["ident"][:, :]),
                          [tkB["msel"], KB["ident"]], [psB[7]])
                self.cp(mT[:], psb[0:64, 0:512], [psB[7]], [mTB])
                nk = 4 * qc + 4
                for kt in range(nk):
                    c0 = max(0, kt - 4 * qc) * 128
                    self.mm(ps[6][:, c0:512], ex[:, kt, :], mT[:, c0:512], True, True, [exB, mTB], [psB[6]])
                    self.cp(mall[:, kt, c0:512], ps[6][:, c0:512], [psB[6]], [mallB[kt]],
                            eng=("dve" if kt % 2 == 0 else "act_copy"))
                for r in range(4):
                    h = g * 4 + r

                    def run_branch(kT_, kB_, vT_, vB_, tiles, sel, r=r, h=h, q_=q_, qb_=qb_, qc=qc):
                        st = {}
                        n = len(tiles)

                        def rng(kt):
                            c0 = max(0, kt - 4 * qc) * 128
                            c1 = 512 if sel else min(4, kt + 5 - 4 * qc) * 128
                            return c0, c1

                        def stageA(kt, i):
                            si = ctr["s"] % 2
                            ctr["s"] += 1
                            st[kt] = si
                            c0, c1 = rng(kt)
                            self.mm(ps[si][:, c0:c1], kT_[:, kt * 128:(kt + 1) * 128], q_[:, r, c0:c1], True, True,
                                    [kB_, qb_], [psB[si]])

                        def stageB(kt, i):
                            si = st[kt]
                            c0, c1 = rng(kt)
                            for ii in range(c0 // 128, c1 // 128):
                                delta = 4 * qc + ii - kt
                                dd = None
                                if delta == 0:
                                    dd = d0[:, h, :]
                                elif delta == 1:
                                    dd = d1[:, h, :]
                                elif delta == 4 and not sel:
                                    dd = w4[:, :]
                                if dd is not None:
                                    self.tt(ps[si][:, ii * 128:(ii + 1) * 128], ps[si][:, ii * 128:(ii + 1) * 128],
                                            dd, ALU.add, [psB[si]] + dB, [psB[si]])
                            pi = ctr["p"] % 3
                            ctr["p"] += 1
                            self.act(Pt[pi][:, c0:c1], ps[si][:, c0:c1], AF.Exp, [psB[si], KB["b31"]], [PB[pi]],
                                     bias=K["b31"][:, h:h + 1])
                            if sel:
                                self.tt(Pt[pi][:, c0:c1], Pt[pi][:, c0:c1], mall[:, kt, c0:c1], ALU.mult,
                                        [PB[pi], mallB[kt]], [PB[pi]])
                            self.mm(ps[2][0:64, c0:c1], vT_[:, kt, :], Pt[pi][:, c0:c1], i == 0, i == n - 1,
                                    [vB_, PB[pi]], [psB[2]])
                            self.mm(ps[3][0:64, c0:c1], K["ones_bf"][:, 0:64], Pt[pi][:, c0:c1], i == 0, i == n - 1,
                                    [KB["ones_bf"], PB[pi]], [psB[3]])

                        self.attn_tiles(tiles, stageA, stageB)
                        self.act(rr[:], ps[3][0:64, :], AF.Ln, [psB[3]], [rrB])
                        self.act(rr[:], rr[:], AF.Exp, [rrB], [rrB], scale=-1.0)
                        self.tt(tmp[:], ps[2][0:64, :], rr[:], ALU.mult, [psB[2], rrB], [tmpB])

                    self.tt(acc[:], oc[:, r, :], gb[:, r * 3 + 0, :], ALU.mult, [ocB[r], gbB], [accB])
                    run_branch(ksl[gi], gB_[gi][0], vsl[gi], gB_[gi][2], list(range(0, 4 * qc + 4)), True)
                    self.tt(tmp[:], tmp[:], gb[:, r * 3 + 1, :], ALU.mult, [tmpB, gbB], [tmpB])
                    self.tt(acc[:], acc[:], tmp[:], ALU.add, [accB, tmpB], [accB])
                    run_branch(kwn[gi], gB_[gi][1], vwn[gi], gB_[gi][3], list(range(max(0, 4 * qc - 4), 4 * qc + 4)),
                               False)
                    self.tt(tmp[:], tmp[:], gb[:, r * 3 + 2, :], ALU.mult, [tmpB, gbB], [tmpB])
                    self.tt(o_st[:, r, :], acc[:], tmp[:], ALU.add, [accB, tmpB], [o_stB])
                self.ld(S["oT"].rearrange("(h d) t -> d h t", d=64)[:, g * 4:(g + 1) * 4, cs], o_st[:], [o_stB],
                        [SB["oT"][qc]])
        self.phase_end()


_orig_op = Prog.op


def _op(self, eng, fn, reads=(), writes=()):
    if eng == "act_copy":
        return _orig_op(self, "act", fn, reads, writes)
    return _orig_op(self, eng, fn, reads, writes)


Prog.op = _op
_orig_cp = Builder.cp


def _cp(self, out, in_, reads, writes, eng="dve"):
    if eng == "act_copy":
        self.pg.op("act", lambda e: e.activation(out=out, in_=in_, func=AF.Copy), reads, writes)
    else:
        _orig_cp(self, out, in_, reads, writes, eng)


Builder.cp = _cp


def col8(v):
    v = np.asarray(v, np.float32)
    return np.ascontiguousarray(np.moveaxis(v.reshape(v.shape[:-1] + (v.shape[-1] // 128, 128)), -1, 0))


def make_in_maps(inputs, T):
    x = np.asarray(inputs["x"], np.float32)
    B = x.shape[0]
    shared = {}
    f = lambda k: np.ascontiguousarray(np.asarray(inputs[k], np.float32))
    shared["rel_bias"] = f("rel_bias")
    shared["ada_w"] = f("ada_w")
    shared["adab"] = np.ascontiguousarray(f("ada_b").reshape(NL, 48, 128).transpose(0, 2, 1))
    shared["an"] = col8(f("attn_norm"))
    shared["mn"] = col8(f("mlp_norm"))
    shared["mlp_w1"] = f("mlp_w1")
    shared["mlp_w2"] = f("mlp_w2")
    shared["a_w_in"] = f("a_w_in")
    shared["a_w_out"] = f("a_w_out")
    shared["a_lambda"] = f("a_lambda").reshape(2, 256)
    shared["a_subln"] = np.ascontiguousarray(f("a_subln").T)
    shared["kv_ada_w"] = f("kv_ada_w")
    shared["kvadab"] = np.ascontiguousarray(f("kv_ada_b").reshape(16, 128).T)
    shared["kvn"] = col8(f("kv_norm"))
    shared["w_kv"] = f("w_kv")
    shared["cmp_posT"] = np.ascontiguousarray(f("cmp_pos").transpose(0, 2, 1))
    shared["cmp_w1"] = f("cmp_w1")
    shared["cmp_w2"] = f("cmp_w2")
    shared["b_w_in"] = f("b_w_in")
    shared["b_w_out"] = f("b_w_out")
    shared["fnorm"] = col8(f("final_norm"))
    shared.update(make_consts(T))
    maps = []
    c = np.asarray(inputs["c"], np.float32)
    for b in range(B):
        m = dict(shared)
        m["xT"] = np.ascontiguousarray(x[b].T)
        m["cT"] = np.ascontiguousarray(c[b].reshape(8, 128).T)
        maps.append(m)
    return maps


_CACHE = {}


def run(inputs, T, layers=NL, debug=None):
    key = (T, layers, tuple(debug) if debug else None)
    if key not in _CACHE:
        _CACHE[key] = Builder(T, layers, debug).build()
    nc = _CACHE[key]
    maps = make_in_maps(inputs, T)
    res = run_bass_kernel_spmd(nc, maps, core_ids=list(range(len(maps))))
    return res.results


def kernel(**inputs):
    T = int(np.asarray(inputs["x"]).shape[1])
    results = run(inputs, T)
    out = np.stack([np.ascontiguousarray(r["outT"].T) for r in results], axis=0)
    return out.astype(np.float32)
```

```python
import math
from contextlib import ExitStack

import numpy as np
import ml_dtypes

import concourse.bass as bass
import concourse.mybir as mybir
from concourse.bass_utils import run_bass_kernel_spmd

F32 = mybir.dt.float32
BF16 = mybir.dt.bfloat16
AF = mybir.ActivationFunctionType
ALU = mybir.AluOpType
AX = mybir.AxisListType

D = 1024
DFF = 4096
NL = 4
EPS = 1e-6
NEG = -1e30
P = 128


class Buf:
    __slots__ = ("name", "w", "r")

    def __init__(self, name):
        self.name = name
        self.w = []
        self.r = []


class Op:
    __slots__ = ("eng", "fn", "deps", "dma", "slot", "target", "val", "waits")

    def __init__(self, eng, fn, deps, dma):
        self.eng = eng
        self.fn = fn
        self.deps = deps
        self.dma = dma
        self.slot = None
        self.target = False
        self.val = 0
        self.waits = []


class Prog:
    ENGS = ("pe", "act", "dve", "pool", "sp")
    NSLOT = {"sp": 24, "pool": 8, "act": 4}

    def __init__(self, nc):
        self.nc = nc
        self.ops = []
        self.rr = {q: 0 for q in self.NSLOT}
        self.slot_last = {}
        self.last_on_eng = {}

    def _reduce(self, ids):
        best = {}
        out = set()
        for i in ids:
            o = self.ops[i]
            if o.dma:
                out.add(i)
            else:
                if o.eng not in best or best[o.eng] < i:
                    best[o.eng] = i
        out.update(best.values())
        return out

    def _mk(self, eng, fn, reads, writes, dma):
        deps = set()
        for b in reads:
            deps.update(b.w)
        for b in writes:
            deps.update(b.w)
            deps.update(b.r)
        gid = len(self.ops)
        op = Op(eng, fn, self._reduce(deps), dma)
        if dma:
            s = self.rr[eng]
            self.rr[eng] = (s + 1) % self.NSLOT[eng]
            op.slot = (eng, s)
            prev = self.slot_last.get(op.slot)
            if prev is not None:
                op.deps.add(prev)
            self.slot_last[op.slot] = gid
        self.ops.append(op)
        wset = set(id(b) for b in writes)
        for b in writes:
            b.w = [gid]
            b.r = []
        for b in reads:
            if id(b) not in wset:
                b.r.append(gid)
                if len(b.r) > 12:
                    b.r = list(self._reduce(b.r))
        self.last_on_eng[eng if not dma else ("dma", gid)] = gid
        return gid

    def op(self, eng, fn, reads=(), writes=()):
        return self._mk(eng, fn, reads, writes, False)

    def dma(self, q, fn, reads=(), writes=()):
        return self._mk(q, fn, reads, writes, True)

    def barrier(self):
        allb = Buf("barrier")
        ids = [i for i, o in enumerate(self.ops)]
        last = {}
        dmas = []
        for i in range(len(self.ops) - 1, -1, -1):
            o = self.ops[i]
            if o.dma:
                if o.slot not in last:
                    last[o.slot] = i
                    dmas.append(i)
            elif o.eng not in last:
                last[o.eng] = i
                dmas.append(i)
        allb.w = dmas
        nc = self.nc
        z = self._bar_tiles
        self.op("pe", lambda e: e.matmul(z["ps"][0:1, 0:2], z["bf"][0:1, 0:1], z["bf"][0:1, 0:2],
                                          start=True, stop=True), reads=[allb], writes=[z["b_pe"]])
        self.op("act", lambda e: e.activation(out=z["a"][0:1, 0:1], in_=z["src"][0:1, 0:1], func=AF.Copy),
                reads=[allb], writes=[z["b_act"]])
        self.op("dve", lambda e: e.tensor_copy(out=z["v"][0:1, 0:1], in_=z["src"][0:1, 0:1]),
                reads=[allb], writes=[z["b_dve"]])
        self.op("pool", lambda e: e.tensor_copy(out=z["g"][0:1, 0:1], in_=z["src"][0:1, 0:1]),
                reads=[allb], writes=[z["b_pool"]])
        self.dma("sp", lambda e: e.dma_start(out=z["s"][0:1, 0:1], in_=z["src"][0:1, 0:1]),
                 reads=[allb], writes=[z["b_sp"]])

    def emit(self, stack):
        nc = self.nc
        ops = self.ops
        comp = ("pe", "act", "dve", "pool")
        sems = {e: stack.enter_context(nc.semaphore("sem_" + e)) for e in comp}
        dsem = {}
        for q, n in self.NSLOT.items():
            for s in range(n):
                dsem[(q, s)] = stack.enter_context(nc.semaphore("dsem_%s_%d" % (q, s)))
        for o in ops:
            for d in o.deps:
                t = ops[d]
                if t.dma:
                    continue
                if o.eng == "pe" and t.eng == "pe" and not o.dma:
                    continue
                t.target = True
        cnt = {e: 0 for e in comp}
        dcnt = {k: 0 for k in dsem}
        for o in ops:
            if o.dma:
                dcnt[o.slot] += 16
                o.val = dcnt[o.slot]
            else:
                if o.target:
                    cnt[o.eng] += 1
                o.val = cnt[o.eng]
        waited = {e: {} for e in self.ENGS}
        for o in ops:
            w = waited[o.eng]
            for d in sorted(o.deps):
                t = ops[d]
                if t.dma:
                    key = ("d", t.slot)
                    sem = dsem[t.slot]
                else:
                    if o.eng == "pe" and t.eng == "pe" and not o.dma:
                        continue
                    key = ("e", t.eng)
                    sem = sems[t.eng]
                if w.get(key, 0) < t.val:
                    w[key] = t.val
                    o.waits.append((key, sem, t.val))
            if len(o.waits) > 1:
                m = {}
                for key, sem, v in o.waits:
                    if key not in m or m[key][1] < v:
                        m[key] = (sem, v)
                o.waits = [(k, s, v) for k, (s, v) in m.items()]
        streams = {e: [] for e in self.ENGS}
        for o in ops:
            streams[o.eng].append(o)
        final = [(dsem[k], v) for k, v in dcnt.items() if v > 0]

        def run(eng_name, e):
            for o in streams[eng_name]:
                for _, sem, v in o.waits:
                    e.wait_ge(sem, v)
                ins = o.fn(e)
                if o.dma:
                    ins.then_inc(dsem[o.slot], 16)
                elif o.target:
                    ins.then_inc(sems[o.eng], 1)
            if eng_name == "sp":
                for sem, v in final:
                    e.wait_ge(sem, v)

        with nc.Block() as block:
            @block.tensor
            def _(e):
                run("pe", e)

            @block.scalar
            def _(e):
                run("act", e)

            @block.vector
            def _(e):
                run("dve", e)

            @block.gpsimd
            def _(e):
                run("pool", e)

            @block.sync
            def _(e):
                run("sp", e)


def _t5_bucket_np(dist):
    n = np.maximum(dist, 0)
    nf = np.maximum(n, 1).astype(np.float32)
    large = 16 + (np.log(nf / np.float32(16)) / np.float32(math.log(8.0)) * np.float32(16)).astype(np.int32)
    large = np.minimum(large, 31)
    return np.where(n < 16, n, large)


def make_consts(T):
    NT = T // 128
    c = {}
    oh = np.zeros((33, 384), np.float32)
    for i in range(384):
        dist = i - 127
        if dist < 0:
            oh[32, i] = 1.0
        else:
            oh[int(_t5_bucket_np(np.array(dist))), i] += 1.0
            oh[31, i] -= 1.0
    c["c_oh"] = oh
    ki = np.arange(128)[:, None]
    qi = np.arange(128)[None, :]
    c["c_w4"] = np.where(ki > qi, 0.0, NEG).astype(np.float32)
    c["c_ident"] = np.eye(128, dtype=np.float32)
    ex = np.zeros((64, NT, 128), np.float32)
    for kt in range(NT):
        for k in range(128):
            ex[2 * kt + k // 64, kt, k] = 1.0
    c["c_ex"] = ex
    n_cmp = (T - 32) // 16 + 1
    n_slc = T // 64
    n = np.arange(256)
    cs = n * 16
    ce = cs + 31
    ss = np.arange(n_slc) * 64
    ov = ((cs[:, None] < ss[None, :] + 64) & (ce[:, None] >= ss[None, :]) & (n[:, None] < n_cmp)).astype(np.float32)
    ovp = np.zeros((256, 64), np.float32)
    ovp[:, :n_slc] = ov
    c["c_ov"] = np.ascontiguousarray(ovp.reshape(2, 128, 64).transpose(1, 0, 2))
    q = np.arange(T)
    mc = ((ce[:, None] <= q[None, :]) & (n[:, None] < n_cmp)).astype(np.float32)
    c["c_maskc"] = np.ascontiguousarray(mc.reshape(2, 128, T).transpose(1, 0, 2))
    j = np.arange(64)[None, :]
    qb = (q // 64)[:, None]
    forced = (j == 0) | ((j <= qb) & (j > qb - 2))
    valid = (j <= qb) & (j < n_slc)
    keep = (~forced & valid).astype(np.float32)
    add = np.where(valid, np.where(forced, 1e4, 0.0), NEG).astype(np.float32)
    c["c_keep"] = np.ascontiguousarray(keep.reshape(NT, 128, 64).transpose(1, 0, 2))
    c["c_add"] = np.ascontiguousarray(add.reshape(NT, 128, 64).transpose(1, 0, 2))
    return c


class Builder:
    def __init__(self, T, layers=NL, debug=False):
        self.T = T
        self.NT = T // 128
        self.NC = T // 512
        self.layers = layers
        self.debug = debug
        self.nc = bass.Bass("TRN2", target_bir_lowering=False)
        self.pg = Prog(self.nc)
        self.gstack = ExitStack()
        self.pstack = None
        self.uid = 0

    def dram_in(self, name, shape, dt=F32):
        return self.nc.dram_tensor(name, list(shape), dt, kind="ExternalInput").ap()

    def dram_out(self, name, shape, dt=F32):
        return self.nc.dram_tensor(name, list(shape), dt, kind="ExternalOutput").ap()

    def dram(self, name, shape, dt):
        return self.nc.dram_tensor(name, list(shape), dt).ap()

    def sb(self, shape, dt, persistent=False, name=None):
        self.uid += 1
        st = self.gstack if persistent else self.pstack
        return st.enter_context(self.nc.sbuf_tensor("%s_%d" % (name or "t", self.uid), list(shape), dt))

    def B(self, name="b"):
        self.uid += 1
        return Buf("%s%d" % (name, self.uid))

    def phase_begin(self):
        self.pstack = ExitStack()

    def phase_end(self):
        self.pg.barrier()
        self.pstack.close()
        self.pstack = None

    def mm(self, out, lhsT, rhs, start, stop, reads, writes):
        self.pg.op("pe", lambda e: e.matmul(out, lhsT, rhs, start=start, stop=stop), reads, writes)

    def act(self, out, in_, func, reads, writes, bias=None, scale=None):
        kw = {}
        if bias is not None:
            kw["bias"] = bias
        if scale is not None:
            kw["scale"] = scale
        self.pg.op("act", lambda e: e.activation(out=out, in_=in_, func=func, **kw), reads, writes)

    def tt(self, out, in0, in1, op, reads, writes, eng="dve"):
        self.pg.op(eng, lambda e: e.tensor_tensor(out=out, in0=in0, in1=in1, op=op), reads, writes)

    def ts(self, out, in0, s1, s2, op0, op1, reads, writes, eng="dve"):
        if op1 is None:
            self.pg.op(eng, lambda e: e.tensor_scalar(out=out, in0=in0, scalar1=s1, scalar2=None, op0=op0),
                       reads, writes)
        else:
            self.pg.op(eng, lambda e: e.tensor_scalar(out=out, in0=in0, scalar1=s1, scalar2=s2, op0=op0, op1=op1),
                       reads, writes)

    def stt(self, out, in0, scalar, in1, op0, op1, reads, writes):
        self.pg.op("dve", lambda e: e.scalar_tensor_tensor(out=out, in0=in0, scalar=scalar, in1=in1,
                                                          op0=op0, op1=op1), reads, writes)

    def cp(self, out, in_, reads, writes, eng="dve"):
        self.pg.op(eng, lambda e: e.tensor_copy(out=out, in_=in_), reads, writes)

    def ld(self, out, in_, reads, writes, q="sp"):
        self.pg.dma(q, lambda e: e.dma_start(out=out, in_=in_), reads, writes)

    def build(self):
        nc, pg, T, NT, NC = self.nc, self.pg, self.T, self.NT, self.NC
        g = self.gstack
        I = {}
        I["xT"] = self.dram_in("xT", [D, T])
        I["cT"] = self.dram_in("cT", [P, 8])
        I["rel_bias"] = self.dram_in("rel_bias", [32, 16])
        I["ada_w"] = self.dram_in("ada_w", [NL, D, 6 * D])
        I["adab"] = self.dram_in("adab", [NL, P, 48])
        I["an"] = self.dram_in("an", [P, NL, 8])
        I["mn"] = self.dram_in("mn", [P, NL, 8])
        I["mlp_w1"] = self.dram_in("mlp_w1", [NL, D, DFF])
        I["mlp_w2"] = self.dram_in("mlp_w2", [NL, DFF, D])
        I["a_w_in"] = self.dram_in("a_w_in", [2, D, 3 * D])
        I["a_w_out"] = self.dram_in("a_w_out", [2, D, D])
        I["a_lambda"] = self.dram_in("a_lambda", [2, 256])
        I["a_subln"] = self.dram_in("a_subln", [P, 2])
        I["kv_ada_w"] = self.dram_in("kv_ada_w", [D, 2 * D])
        I["kvadab"] = self.dram_in("kvadab", [P, 16])
        I["kvn"] = self.dram_in("kvn", [P, 8])
        I["w_kv"] = self.dram_in("w_kv", [D, 1536])
        I["cmp_posT"] = self.dram_in("cmp_posT", [2, 64, 32])
        I["cmp_w1"] = self.dram_in("cmp_w1", [2, 2048, 256])
        I["cmp_w2"] = self.dram_in("cmp_w2", [2, 256, 64])
        I["b_w_in"] = self.dram_in("b_w_in", [2, D, 1072])
        I["b_w_out"] = self.dram_in("b_w_out", [2, D, D])
        I["fnorm"] = self.dram_in("fnorm", [P, 8])
        I["c_oh"] = self.dram_in("c_oh", [33, 384])
        I["c_w4"] = self.dram_in("c_w4", [P, 128])
        I["c_ident"] = self.dram_in("c_ident", [P, 128])
        I["c_ex"] = self.dram_in("c_ex", [64, NT, 128])
        I["c_ov"] = self.dram_in("c_ov", [P, 2, 64])
        I["c_maskc"] = self.dram_in("c_maskc", [P, 2, T])
        I["c_keep"] = self.dram_in("c_keep", [P, NT, 64])
        I["c_add"] = self.dram_in("c_add", [P, NT, 64])
        self.I = I
        outT = self.dram_out("outT", [D, T])
        S = {}
        S["xT"] = self.dram("s_xT", [D, T], F32)
        S["qT"] = self.dram("s_qT", [D, T], BF16)
        S["kT"] = self.dram("s_kT", [D, T], BF16)
        S["vtok"] = self.dram("s_vtok", [T, D], BF16)
        S["oT"] = self.dram("s_oT", [D, T], BF16)
        S["kvT"] = self.dram("s_kvT", [1536, T], BF16)
        S["vslc"] = self.dram("s_vslc", [T, 256], BF16)
        S["vwin"] = self.dram("s_vwin", [T, 256], BF16)
        S["gT"] = self.dram("s_gT", [48, T], F32)
        S["tT"] = self.dram("s_tT", [16, 384], F32)
        S["d0"] = self.dram("s_d0", [P, 16, 128], F32)
        S["d1"] = self.dram("s_d1", [P, 16, 128], F32)
        self.S = S
        SB = {k: [self.B(k) for _ in range(NC)] for k in ("xT", "qT", "kT", "vtok", "oT", "kvT", "vslc", "vwin", "gT")}
        for k in ("tT", "d0", "d1"):
            SB[k] = [self.B(k)]
        self.SB = SB
        ps = [g.enter_context(nc.psum_tensor("ps%d" % i, [P, 512], F32)) for i in range(8)]
        self.ps = ps
        self.psB = [self.B("ps") for _ in range(8)]
        K = {}
        K["ones_bf"] = self.sb([P, 128], BF16, True, "ones")
        K["onesD"] = self.sb([P, 128], BF16, True, "onesD")
        K["onesH"] = self.sb([P, 128], BF16, True, "onesH")
        K["ones32"] = self.sb([P, 128], F32, True, "ones32")
        K["ident"] = self.sb([P, 128], F32, True, "ident")
        K["cact"] = self.sb([P, 8], F32, True, "cact")
        K["mod"] = self.sb([P, NL, 48], F32, True, "mod")
        K["kvmod"] = self.sb([P, 16], F32, True, "kvmod")
        K["g1"] = self.sb([P, NL, 8], F32, True, "g1")
        K["g2"] = self.sb([P, NL, 8], F32, True, "g2")
        K["gkv"] = self.sb([P, 8], F32, True, "gkv")
        K["fn"] = self.sb([P, 8], F32, True, "fn")
        K["zero8"] = self.sb([P, 8], F32, True, "zero8")
        K["b31"] = self.sb([P, 16], F32, True, "b31")
        K["lamneg"] = self.sb([P, 2], F32, True, "lamneg")
        K["subg"] = self.sb([P, 2], F32, True, "subg")
        K["kcmpT"] = self.sb([64, 4, 256], BF16, True, "kcmpT")
        K["vcmp"] = self.sb([P, 4, 2, 64], BF16, True, "vcmp")
        K["bar"] = self.sb([P, 16], F32, True, "bar")
        K["barbf"] = self.sb([P, 4], BF16, True, "barbf")
        self.K = K
        KB = {k: self.B(k) for k in K}
        self.KB = KB
        pg._bar_tiles = dict(ps=ps[6], bf=K["barbf"], src=K["bar"][:, 0:1], a=K["bar"][:, 1:2], v=K["bar"][:, 2:3],
                             g=K["bar"][:, 3:4], s=K["bar"][:, 4:5], b_pe=self.psB[6], b_act=self.B(), b_dve=self.B(),
                             b_pool=self.B(), b_sp=self.B())

        self.phase_setup()
        for l in range(self.layers):
            if l < 2:
                self.phase_a_proj(l)
                self.phase_a_attn(l)
                wo = I["a_w_out"][l]
            else:
                self.phase_b_proj(l)
                self.phase_b_attn(l)
                wo = I["b_w_out"][l - 2]
            self.phase_outproj(l, wo)
            self.phase_mlp(l)
            if l == 1:
                self.phase_kv()
                self.phase_cmp()
        self.phase_final(outT)
        if self.debug:
            dbg = {}
            for k in self.debug:
                t = S[k]
                o = self.dram_out("dbg_" + k, list(t.shape), t.dtype)
                self.ld(o, t, reads=SB[k], writes=[self.B()])
        pg.emit(g)
        return nc

    def phase_setup(self):
        nc, pg, I, K, KB, S, SB = self.nc, self.pg, self.I, self.K, self.KB, self.S, self.SB
        ps, psB = self.ps, self.psB
        self.phase_begin()
        pg.op("dve", lambda e: e.memset(K["bar"][:], 0.0), [], [KB["bar"]])
        pg.op("dve", lambda e: e.memset(K["barbf"][:], 0.0), [], [KB["barbf"]])
        pg.op("dve", lambda e: e.memset(K["ones_bf"][:], 1.0), [], [KB["ones_bf"]])
        pg.op("dve", lambda e: e.memset(K["onesD"][:], 1.0 / 1024), [], [KB["onesD"]])
        pg.op("dve", lambda e: e.memset(K["onesH"][:], 1.0 / 128), [], [KB["onesH"]])
        pg.op("dve", lambda e: e.memset(K["ones32"][:], 1.0), [], [KB["ones32"]])
        pg.op("dve", lambda e: e.memset(K["zero8"][:], 0.0), [], [KB["zero8"]])
        pg.op("dve", lambda e: e.memset(K["vcmp"][:], 0.0), [], [KB["vcmp"]])
        pg.op("dve", lambda e: e.memset(K["kcmpT"][:], 0.0), [], [KB["kcmpT"]])
        self.ld(K["ident"][:], I["c_ident"][:, :], [], [KB["ident"]])
        self.ld(K["fn"][:], I["fnorm"][:, :], [], [KB["fn"]])
        for c in range(self.NC):
            cs = slice(c * 512, (c + 1) * 512)
            self.ld(S["xT"][:, cs], I["xT"][:, cs], [], [SB["xT"][c]])
        craw = self.sb([P, 8], F32)
        b_craw = self.B()
        self.ld(craw[:], I["cT"][:, :], [], [b_craw])
        csig = self.sb([P, 8], F32)
        b_csig = self.B()
        self.act(csig[:], craw[:], AF.Sigmoid, [b_craw], [b_csig])
        self.tt(K["cact"][:], craw[:], csig[:], ALU.mult, [b_craw, b_csig], [KB["cact"]])
        wt = [self.sb([P, 8, 512], F32) for _ in range(2)]
        wtB = [self.B() for _ in range(2)]
        adab = self.sb([P, NL, 48], F32)
        b_adab = self.B()
        self.ld(adab[:], I["adab"].rearrange("l p j -> p l j"), [], [b_adab])
        kvadab = self.sb([P, 16], F32)
        b_kvadab = self.B()
        self.ld(kvadab[:], I["kvadab"][:, :], [], [b_kvadab])
        blk = 0
        jobs = [(I["ada_w"][l], 12, l) for l in range(NL)] + [(I["kv_ada_w"], 4, None)]
        for (wsrc, nblk, l) in jobs:
            pacc = ps[0]
            for bi in range(nblk):
                w = wt[blk % 2]
                wb = wtB[blk % 2]
                blk += 1
                self.ld(w[:], wsrc.rearrange("(kc p) n -> p kc n", p=P)[:, :, bi * 512:(bi + 1) * 512], [], [wb])
                for jj in range(4):
                    j = bi * 4 + jj
                    for kc in range(8):
                        self.mm(pacc[:, j:j + 1], w[:, kc, jj * 128:(jj + 1) * 128], K["cact"][:, kc:kc + 1],
                                kc == 0, kc == 7, [wb, KB["cact"]], [psB[0]])
            if l is not None:
                self.tt(K["mod"][:, l, :], pacc[:, 0:48], adab[:, l, :], ALU.add, [psB[0], b_adab], [KB["mod"]])
            else:
                self.tt(K["kvmod"][:], pacc[:, 0:16], kvadab[:], ALU.add, [psB[0], b_kvadab], [KB["kvmod"]])
        an = self.sb([P, NL, 8], F32)
        mn = self.sb([P, NL, 8], F32)
        kvn = self.sb([P, 8], F32)
        b_n = self.B()
        self.ld(an[:], I["an"][:, :, :], [], [b_n])
        b_n2 = self.B()
        self.ld(mn[:], I["mn"][:, :, :], [], [b_n2])
        b_n3 = self.B()
        self.ld(kvn[:], I["kvn"][:, :], [], [b_n3])
        tmp = self.sb([P, NL, 8], F32)
        b_tmp = self.B()
        for (dst, kb, nrm, nb, lo) in ((K["g1"], KB["g1"], an, b_n, 8), (K["g2"], KB["g2"], mn, b_n2, 32)):
            self.tt(tmp[:], K["mod"][:, :, lo:lo + 8], nrm[:], ALU.mult, [KB["mod"], nb], [b_tmp])
            self.tt(dst[:], tmp[:], nrm[:], ALU.add, [b_tmp, nb], [kb])
        tmp2 = self.sb([P, 8], F32)
        b_tmp2 = self.B()
        self.tt(tmp2[:], K["kvmod"][:, 8:16], kvn[:], ALU.mult, [KB["kvmod"], b_n3], [b_tmp2])
        self.tt(K["gkv"][:], tmp2[:], kvn[:], ALU.add, [b_tmp2, b_n3], [KB["gkv"]])
        tab = self.sb([33, 16], F32)
        b_tab = self.B()
        pg.op("dve", lambda e: e.memset(tab[32:33, :], NEG), [], [b_tab])
        b_tab2 = self.B()
        self.ld(tab[0:32, :], I["rel_bias"][:, :], [b_tab], [b_tab2])
        oh = self.sb([33, 384], F32)
        b_oh = self.B()
        self.ld(oh[:], I["c_oh"][:, :], [], [b_oh])
        self.mm(ps[1][0:16, 0:384], tab[:, :], oh[:, :], True, True, [b_tab, b_tab2, b_oh], [psB[1]])
        tsb = self.sb([16, 384], F32)
        b_tsb = self.B()
        self.cp(tsb[:], ps[1][0:16, 0:384], [psB[1]], [b_tsb])
        self.ld(S["tT"][:, :], tsb[:], [b_tsb], SB["tT"])
        for k in range(128):
            self.ld(S["d0"][k:k + 1, :, :], S["tT"][:, 127 - k:255 - k].rearrange("(o m) q -> o m q", o=1),
                    SB["tT"], [self.B()], q=("sp" if k % 2 == 0 else "act"))
            self.ld(S["d1"][k:k + 1, :, :], S["tT"][:, 255 - k:383 - k].rearrange("(o m) q -> o m q", o=1),
                    SB["tT"], [self.B()], q=("sp" if k % 2 == 0 else "act"))
        self.ld(K["b31"][:], bass.AP(I["rel_bias"].tensor, 31 * 16, [[0, P], [1, 16]]), [], [KB["b31"]])
        lam = self.sb([P, 2, 256], F32)
        b_lam = self.B()
        self.ld(lam[:], bass.AP(I["a_lambda"].tensor, 0, [[0, P], [256, 2], [1, 256]]), [], [b_lam])
        sub = self.sb([P, 2], F32)
        b_sub = self.B()
        self.ld(sub[:], I["a_subln"][:, :], [], [b_sub])
        prod = self.sb([P, 2, 2, 64], F32)
        b_prod = self.B()
        red = self.sb([P, 4], F32)
        b_red = self.B()
        for l in range(2):
            for i in range(2):
                self.tt(prod[:, l, i, :], lam[:, l, (2 * i) * 64:(2 * i + 1) * 64],
                        lam[:, l, (2 * i + 1) * 64:(2 * i + 2) * 64], ALU.mult, [b_lam], [b_prod])
        pg.op("dve", lambda e: e.tensor_reduce(out=red[:], in_=prod[:].rearrange("p l i d -> p (l i) d"),
                                               axis=AX.X, op=ALU.add), [b_prod], [b_red])
        ered = self.sb([P, 4], F32)
        b_ered = self.B()
        self.act(ered[:], red[:], AF.Exp, [b_red], [b_ered])
        for l in range(2):
            lam_init = 0.8 - 0.6 * math.exp(-0.3 * l)
            self.tt(K["lamneg"][:, l:l + 1], ered[:, 2 * l + 1:2 * l + 2], ered[:, 2 * l:2 * l + 1], ALU.subtract,
                    [b_ered], [KB["lamneg"]])
            self.ts(K["lamneg"][:, l:l + 1], K["lamneg"][:, l:l + 1], -lam_init, None, ALU.add, None,
                    [KB["lamneg"]], [KB["lamneg"]])
            self.ts(K["subg"][:, l:l + 1], sub[:, l:l + 1], 1.0 - lam_init, None, ALU.mult, None,
                    [b_sub], [KB["subg"]])
        self.phase_end()

    def norm_mod(self, xt, xb, N, gvec, shvec, gB, hout, hB, sq, sqB, rstd, rB, psi, tout=None, tB=None):
        K, KB, ps, psB = self.K, self.KB, self.ps, self.psB
        for j in range(8):
            s_, sb_ = sq[j % len(sq)], sqB[j % len(sq)]
            self.act(s_[:, 0:N], xt[:, j, 0:N], AF.Square, [xb], [sb_])
            self.mm(ps[psi][:, 0:N], K["onesD"][:, :], s_[:, 0:N], j == 0, j == 7, [KB["onesD"], sb_], [psB[psi]])
        self.act(rstd[:, 0:N], ps[psi][:, 0:N], AF.Ln, [psB[psi], self.KB["bar"]], [rB], bias=self.eps_ap)
        self.act(rstd[:, 0:N], rstd[:, 0:N], AF.Exp, [rB], [rB], scale=-0.5)
        for j in range(8):
            t_, tb_ = tout[j % len(tout)], tB[j % len(tout)]
            self.tt(t_[:, 0:N], xt[:, j, 0:N], rstd[:, 0:N], ALU.mult, [xb, rB], [tb_])
            self.act(hout[:, j, 0:N], t_[:, 0:N], AF.Identity, [tb_, gB], [hB],
                     bias=shvec[:, j:j + 1], scale=gvec[:, j:j + 1])

    def norm_rings(self, N=512):
        sq = [self.sb([P, N], BF16) for _ in range(2)]
        tt_ = [self.sb([P, N], F32) for _ in range(2)]
        return sq, [self.B() for _ in range(2)], tt_, [self.B() for _ in range(2)]

    @property
    def eps_ap(self):
        if not hasattr(self, "_eps_done"):
            self._eps_done = True
            K, KB = self.K, self.KB
            self.pg.op("dve", lambda e: e.memset(K["bar"][:, 5:6], EPS), [], [KB["bar"]])
        return self.K["bar"][:, 5:6]

    def load_w_bf16(self, dst, dstB, src_view, ncols, blk=512):
        nb = (ncols + blk - 1) // blk
        for i in range(nb):
            a, b = i * blk, min(ncols, (i + 1) * blk)
            self.ld(dst[:, :, a:b], src_view[:, :, a:b], [], [dstB[i]], q="pool")

    def phase_a_proj(self, l):
        I, K, KB, S, SB, ps, psB = self.I, self.K, self.KB, self.S, self.SB, self.ps, self.psB
        NC = self.NC
        self.phase_begin()
        w = self.sb([P, 8, 3072], BF16)
        wB = [self.B() for _ in range(6)]
        self.load_w_bf16(w, wB, I["a_w_in"][l].rearrange("(kc p) n -> p kc n", p=P), 3072)
        xt = [self.sb([P, 8, 512], F32) for _ in range(2)]
        xB = [self.B() for _ in range(2)]
        sq, sqB, tr, trB = self.norm_rings(512)
        rstd = self.sb([P, 512], F32)
        rB = self.B()
        h = [self.sb([P, 8, 512], BF16) for _ in range(2)]
        hB = [self.B() for _ in range(2)]
        qst = [self.sb([P, 16, 512], BF16) for _ in range(2)]
        qB = [self.B() for _ in range(2)]
        kB = [self.B() for _ in range(2)]
        vst = [self.sb([P, 4, 1024], BF16) for _ in range(2)]
        vB = [self.B() for _ in range(2)]
        xv = S["xT"].rearrange("(j p) t -> p j t", p=P)
        ring = 0
        for c in range(NC):
            cs = slice(c * 512, (c + 1) * 512)
            x_, xb_ = xt[c % 2], xB[c % 2]
            self.ld(x_[:], xv[:, :, cs], [SB["xT"][c]], [xb_])
            h_, hb_ = h[c % 2], hB[c % 2]
            self.norm_mod(x_, xb_, 512, K["g1"][:, l, :], K["mod"][:, l, 0:8], KB["g1"], h_, hb_, sq, sqB, rstd, rB, 2, tout=tr, tB=trB)
            q_, qb_, kb_ = qst[c % 2], qB[c % 2], kB[c % 2]
            for m in range(16):
                pi = ring % 2
                ring += 1
                for kc in range(8):
                    self.mm(ps[pi][:, :], w[:, kc, m * 128:(m + 1) * 128], h_[:, kc, :], kc == 0, kc == 7,
                            [wB[m // 4], hb_], [psB[pi]])
                if m < 8:
                    self.act(q_[:, m, :], ps[pi][:, :], AF.Copy, [psB[pi]], [qb_], scale=0.125)
                else:
                    self.cp(q_[:, m, :], ps[pi][:, :], [psB[pi]], [kb_])
            self.ld(S["qT"].rearrange("(m p) t -> p m t", p=P)[:, :, cs], q_[:, 0:8, :], [qb_], [SB["qT"][c]])
            self.ld(S["kT"].rearrange("(m p) t -> p m t", p=P)[:, :, cs], q_[:, 8:16, :], [kb_], [SB["kT"][c]])
            v_, vb_ = vst[c % 2], vB[c % 2]
            for tt in range(4):
                for half in range(2):
                    pi = ring % 2
                    ring += 1
                    for kc in range(8):
                        self.mm(ps[pi][:, :], h_[:, kc, tt * 128:(tt + 1) * 128],
                                w[:, kc, 2048 + half * 512:2048 + (half + 1) * 512], kc == 0, kc == 7,
                                [wB[4 + half], hb_], [psB[pi]])
                    self.cp(v_[:, tt, half * 512:(half + 1) * 512], ps[pi][:, :], [psB[pi]], [vb_],
                            eng=("dve" if half == 0 else "act_copy"))
            self.ld(S["vtok"].rearrange("(tt p) e -> p tt e", p=P)[:, c * 4:(c + 1) * 4, :], v_[:], [vb_],
                    [SB["vtok"][c]])
        self.phase_end()

    def attn_tiles(self, tiles, stageA, stageB, depth=1):
        n = len(tiles)
        for i in range(min(depth, n)):
            stageA(tiles[i], i)
        for i, t in enumerate(tiles):
            if i + depth < n:
                stageA(tiles[i + depth], i + depth)
            stageB(t, i)

    def load_bias_tiles(self):
        S, SB = self.S, self.SB
        d0 = self.sb([P, 16, 128], F32)
        d1 = self.sb([P, 16, 128], F32)
        w4 = self.sb([P, 128], F32)
        bd = self.B()
        self.ld(d0[:], S["d0"][:, :, :], SB["d0"], [bd])
        bd1 = self.B()
        self.ld(d1[:], S["d1"][:, :, :], SB["d1"], [bd1])
        bw = self.B()
        self.ld(w4[:], self.I["c_w4"][:, :], [], [bw])
        return d0, d1, w4, [bd, bd1, bw]

    def phase_a_attn(self, l):
        I, K, KB, S, SB, ps, psB = self.I, self.K, self.KB, self.S, self.SB, self.ps, self.psB
        NC, NT, T = self.NC, self.NT, self.T
        self.phase_begin()
        d0, d1, w4, dB = self.load_bias_tiles()
        qh = [self.sb([P, T], BF16) for _ in range(2)]
        kh = [self.sb([P, T], BF16) for _ in range(2)]
        vh = [self.sb([P, NT, 128], BF16) for _ in range(2)]
        lB = [[self.B() for _ in range(3)] for _ in range(2)]
        NPT = 8
        Pt = [self.sb([P, 512], BF16) for _ in range(NPT)]
        PB = [self.B() for _ in range(NPT)]
        accL = [[self.sb([P, 512], F32) for _ in range(2)] for _ in range(2)]
        accB = [[self.B() for _ in range(2)] for _ in range(2)]
        r0 = self.sb([P, 512], F32)
        r1 = self.sb([P, 512], F32)
        t0 = self.sb([P, 512], F32)
        t1 = self.sb([P, 512], F32)
        osq = self.sb([P, 512], BF16)
        ost = [self.sb([P, 512], BF16) for _ in range(2)]
        bb = {k: self.B() for k in ("r0", "r1", "t0", "t1", "osq", "rs")}
        ostB = [self.B(), self.B()]
        rs = self.sb([P, 512], F32)
        pairs = [(0, 1), (4, 5), (6, 7)]
        ctr = {"s": 0, "p": 0, "o": 0, "a": 0}
        for h in range(8):
            q_, k_, v_ = qh[h % 2], kh[h % 2], vh[h % 2]
            lb = lB[h % 2]
            self.ld(q_[:], S["qT"][h * 128:(h + 1) * 128, :], SB["qT"], [lb[0]])
            self.ld(k_[:], S["kT"][h * 128:(h + 1) * 128, :], SB["kT"], [lb[1]])
            self.ld(v_[:], S["vtok"].rearrange("(kt p) e -> p kt e", p=P)[:, :, h * 128:(h + 1) * 128], SB["vtok"],
                    [lb[2]])
            for qc in range(NC):
                tiles = list(range(0, 4 * qc + 4))
                nk = len(tiles)
                st = {}
                ai = ctr["a"] % 2
                ctr["a"] += 1
                aL, aB = accL[ai], accB[ai]

                def stageA(kt, i, q_=q_, k_=k_, lb=lb, st=st, qc=qc):
                    pr = pairs[ctr["s"] % 3]
                    ctr["s"] += 1
                    st[kt] = pr
                    c0 = max(0, kt - 4 * qc) * 128
                    for m in range(2):
                        self.mm(ps[pr[m]][:, c0:512], k_[m * 64:(m + 1) * 64, kt * 128:(kt + 1) * 128],
                                q_[m * 64:(m + 1) * 64, qc * 512 + c0:(qc + 1) * 512], True, True,
                                [lb[0], lb[1]], [psB[pr[m]]])

                def stageB(kt, i, h=h, v_=v_, lb=lb, st=st, qc=qc, nk=nk, aL=aL, aB=aB):
                    pr = st[kt]
                    c0 = max(0, kt - 4 * qc) * 128
                    for m in range(2):
                        hm = h * 2 + m
                        si = pr[m]
                        for ii in range(4):
                            delta = 4 * qc + ii - kt
                            if delta == 0 or delta == 1:
                                dd = d0 if delta == 0 else d1
                                self.tt(ps[si][:, ii * 128:(ii + 1) * 128], ps[si][:, ii * 128:(ii + 1) * 128],
                                        dd[:, hm, :], ALU.add, [psB[si]] + dB, [psB[si]])
                    pis = []
                    for m in range(2):
                        hm = h * 2 + m
                        si = pr[m]
                        pi = ctr["p"] % NPT
                        ctr["p"] += 1
                        pis.append(pi)
                        self.act(Pt[pi][:, c0:512], ps[si][:, c0:512], AF.Exp, [psB[si], KB["b31"]], [PB[pi]],
                                 bias=K["b31"][:, hm:hm + 1])
                    for m in range(2):
                        pi = pis[m]
                        eng = "pool" if m == 0 else "dve"
                        if i == 0:
                            self.cp(aL[m][:, c0:512], Pt[pi][:, c0:512], [PB[pi]], [aB[m]], eng=eng)
                        else:
                            self.tt(aL[m][:, c0:512], aL[m][:, c0:512], Pt[pi][:, c0:512], ALU.add,
                                    [PB[pi], aB[m]], [aB[m]], eng=eng)
                    for m in range(2):
                        pi = pis[m]
                        self.mm(ps[2 + m][:, c0:512], v_[:, kt, :], Pt[pi][:, c0:512], i == 0, i == nk - 1,
                                [lb[2], PB[pi]], [psB[2 + m]])

                self.attn_tiles(tiles, stageA, stageB, depth=2)
                for m, (rr, tt_) in enumerate(((r0, t0), (r1, t1))):
                    rb, tb = bb["r%d" % m], bb["t%d" % m]
                    self.mm(ps[m][:, :], K["ones32"][:, :], aL[m][:, :], True, True, [KB["ones32"], aB[m]], [psB[m]])
                    self.act(rr[:], ps[m][:, :], AF.Ln, [psB[m]], [rb])
                    self.act(rr[:], rr[:], AF.Exp, [rb], [rb], scale=-1.0)
                    self.tt(tt_[:], ps[2 + m][:, :], rr[:], ALU.mult, [psB[2 + m], rb], [tb])
                self.stt(t0[:], t1[:], K["lamneg"][:, l:l + 1], t0[:], ALU.mult, ALU.add,
                         [bb["t0"], bb["t1"], KB["lamneg"]], [bb["t0"]])
                self.act(osq[:], t0[:], AF.Square, [bb["t0"]], [bb["osq"]])
                self.mm(ps[4][:, :], K["onesH"][:, :], osq[:], True, True, [KB["onesH"], bb["osq"]], [psB[4]])
                self.act(rs[:], ps[4][:, :], AF.Ln, [psB[4], KB["bar"]], [bb["rs"]], bias=self.eps_ap)
                self.act(rs[:], rs[:], AF.Exp, [bb["rs"]], [bb["rs"]], scale=-0.5)
                self.tt(t0[:], t0[:], rs[:], ALU.mult, [bb["t0"], bb["rs"]], [bb["t0"]])
                oi = ctr["o"] % 2
                ctr["o"] += 1
                self.act(ost[oi][:], t0[:], AF.Identity, [bb["t0"], KB["subg"]], [ostB[oi]],
                         scale=K["subg"][:, l:l + 1])
                self.ld(S["oT"][h * 128:(h + 1) * 128, qc * 512:(qc + 1) * 512], ost[oi][:], [ostB[oi]],
                        [SB["oT"][qc]])
        self.phase_end()

    def phase_outproj(self, l, wo_src):
        I, K, KB, S, SB, ps, psB = self.I, self.K, self.KB, self.S, self.SB, self.ps, self.psB
        NC = self.NC
        self.phase_begin()
        w = self.sb([P, 8, 1024], BF16)
        wB = [self.B() for _ in range(2)]
        self.load_w_bf16(w, wB, wo_src.rearrange("(kc p) n -> p kc n", p=P), 1024)
        xt = [self.sb([P, 8, 512], F32) for _ in range(2)]
        xB = [self.B() for _ in range(2)]
        ot = [self.sb([P, 8, 512], BF16) for _ in range(2)]
        oB = [self.B() for _ in range(2)]
        xv = S["xT"].rearrange("(j p) t -> p j t", p=P)
        ov = S["oT"].rearrange("(j p) t -> p j t", p=P)
        ring = 0
        for c in range(NC):
            cs = slice(c * 512, (c + 1) * 512)
            x_, xb_ = xt[c % 2], xB[c % 2]
            o_, ob_ = ot[c % 2], oB[c % 2]
            self.ld(x_[:], xv[:, :, cs], [SB["xT"][c]], [xb_])
            self.ld(o_[:], ov[:, :, cs], [SB["oT"][c]], [ob_])
            for j in range(8):
                pi = ring % 2
                ring += 1
                for hc in range(8):
                    self.mm(ps[pi][:, :], w[:, hc, j * 128:(j + 1) * 128], o_[:, hc, :], hc == 0, hc == 7,
                            [wB[j // 4], ob_], [psB[pi]])
                self.stt(x_[:, j, :], ps[pi][:, :], K["mod"][:, l, 16 + j:17 + j], x_[:, j, :], ALU.mult, ALU.add,
                         [psB[pi], xb_, KB["mod"]], [xb_])
            self.ld(xv[:, :, cs], x_[:], [xb_], [SB["xT"][c]])
        self.phase_end()

    def phase_mlp(self, l):
        I, K, KB, S, SB, ps, psB = self.I, self.K, self.KB, self.S, self.SB, self.ps, self.psB
        T = self.T
        N = 512
        self.phase_begin()
        w1 = self.sb([P, 8, DFF], BF16)
        w1B = [self.B() for _ in range(8)]
        w2 = self.sb([P, 32, D], BF16)
        w2B = [self.B() for _ in range(8)]
        self.load_w_bf16(w1, w1B, I["mlp_w1"][l].rearrange("(kc p) n -> p kc n", p=P), DFF)
        v2 = I["mlp_w2"][l].rearrange("(f p) n -> p f n", p=P)
        for i in range(8):
            self.ld(w2[:, i * 4:(i + 1) * 4, :], v2[:, i * 4:(i + 1) * 4, :], [], [w2B[i]], q="pool")
        xt = self.sb([P, 8, N], F32)
        xB = self.B()
        sq, sqB, tr, trB = self.norm_rings(N)
        rstd = self.sb([P, N], F32)
        rB = self.B()
        h = self.sb([P, 8, N], BF16)
        hB = self.B()
        hid = self.sb([P, 32, N], BF16)
        hidB = [self.B() for _ in range(8)]
        r32 = [self.sb([P, N], F32) for _ in range(2)]
        r32B = [self.B() for _ in range(2)]
        xv = S["xT"].rearrange("(j p) t -> p j t", p=P)
        ring = 0
        for c in range(T // N):
            cs = slice(c * N, (c + 1) * N)
            sbx = SB["xT"][c]
            x_, xb_ = xt, xB
            self.ld(x_[:], xv[:, :, cs], [sbx], [xb_])
            self.norm_mod(x_, xb_, N, K["g2"][:, l, :], K["mod"][:, l, 24:32], KB["g2"], h, hB, sq, sqB, rstd, rB, 2,
                          tout=tr, tB=trB)
            for f in range(32):
                pi = ring % 2
                ring += 1
                for kc in range(8):
                    self.mm(ps[pi][:, 0:N], w1[:, kc, f * 128:(f + 1) * 128], h[:, kc, :], kc == 0, kc == 7,
                            [w1B[f // 4], hB], [psB[pi]])
                ri = f % 2
                self.act(r32[ri][:], ps[pi][:, 0:N], AF.Relu, [psB[pi]], [r32B[ri]])
                self.tt(hid[:, f, :], r32[ri][:], r32[ri][:], ALU.mult, [r32B[ri]], [hidB[f // 4]])
            for j in range(8):
                pi = 3 + (ring % 2)
                ring += 1
                for f in range(32):
                    self.mm(ps[pi][:, 0:N], w2[:, f, j * 128:(j + 1) * 128], hid[:, f, :], f == 0, f == 31,
                            [w2B[f // 4], hidB[f // 4]], [psB[pi]])
                self.stt(x_[:, j, :], ps[pi][:, 0:N], K["mod"][:, l, 40 + j:41 + j], x_[:, j, :], ALU.mult, ALU.add,
                         [psB[pi], xb_, KB["mod"]], [xb_])
            self.ld(xv[:, :, cs], x_[:], [xb_], [sbx])
        self.phase_end()

    def phase_final(self, outT):
        I, K, KB, S, SB, ps, psB = self.I, self.K, self.KB, self.S, self.SB, self.ps, self.psB
        self.phase_begin()
        xt = [self.sb([P, 8, 512], F32) for _ in range(2)]
        xB = [self.B() for _ in range(2)]
        yt = [self.sb([P, 8, 512], F32) for _ in range(2)]
        yB = [self.B() for _ in range(2)]
        sq, sqB, tr, trB = self.norm_rings(512)
        rstd = self.sb([P, 512], F32)
        rB = self.B()
        xv = S["xT"].rearrange("(j p) t -> p j t", p=P)
        ov = outT.rearrange("(j p) t -> p j t", p=P)
        for c in range(self.NC):
            cs = slice(c * 512, (c + 1) * 512)
            x_, xb_ = xt[c % 2], xB[c % 2]
            self.ld(x_[:], xv[:, :, cs], [SB["xT"][c]], [xb_])
            self.norm_mod(x_, xb_, 512, K["fn"], K["zero8"], KB["fn"], yt[c % 2], yB[c % 2], sq, sqB, rstd, rB, 2, tout=tr, tB=trB)
            self.ld(ov[:, :, cs], yt[c % 2][:], [yB[c % 2]], [self.B()])
        self.phase_end()

    def phase_kv(self):
        I, K, KB, S, SB, ps, psB = self.I, self.K, self.KB, self.S, self.SB, self.ps, self.psB
        NC = self.NC
        self.phase_begin()
        w = self.sb([P, 8, 1536], BF16)
        wB = [self.B() for _ in range(3)]
        self.load_w_bf16(w, wB, I["w_kv"].rearrange("(kc p) n -> p kc n", p=P), 1536)
        xt = [self.sb([P, 8, 512], F32) for _ in range(2)]
        xB = [self.B() for _ in range(2)]
        sq, sqB, tr, trB = self.norm_rings(512)
        rstd = self.sb([P, 512], F32)
        rB = self.B()
        h = [self.sb([P, 8, 512], BF16) for _ in range(2)]
        hB = [self.B() for _ in range(2)]
        kst = [self.sb([P, 12, 512], BF16) for _ in range(2)]
        kB = [self.B() for _ in range(2)]
        vst = [self.sb([P, 4, 2, 256], BF16) for _ in range(2)]
        vB = [self.B() for _ in range(2)]
        xv = S["xT"].rearrange("(j p) t -> p j t", p=P)
        ring = 0
        for c in range(NC):
            cs = slice(c * 512, (c + 1) * 512)
            x_, xb_ = xt[c % 2], xB[c % 2]
            self.ld(x_[:], xv[:, :, cs], [SB["xT"][c]], [xb_])
            h_, hb_ = h[c % 2], hB[c % 2]
            self.norm_mod(x_, xb_, 512, K["gkv"], K["kvmod"][:, 0:8], KB["gkv"], h_, hb_, sq, sqB, rstd, rB, 2, tout=tr, tB=trB)
            k_, kb_ = kst[c % 2], kB[c % 2]
            for m in range(12):
                pi = ring % 2
                ring += 1
                for kc in range(8):
                    self.mm(ps[pi][:, :], w[:, kc, m * 128:(m + 1) * 128], h_[:, kc, :], kc == 0, kc == 7,
                            [wB[m // 4], hb_], [psB[pi]])
                self.cp(k_[:, m, :], ps[pi][:, :], [psB[pi]], [kb_], eng=("dve" if m % 2 == 0 else "act_copy"))
            self.ld(S["kvT"].rearrange("(m p) t -> p m t", p=P)[:, :, cs], k_[:], [kb_], [SB["kvT"][c]])
            v_, vb_ = vst[c % 2], vB[c % 2]
            for tt in range(4):
                pi = ring % 2
                ring += 1
                for si, s0 in enumerate((768, 1280)):
                    for kc in range(8):
                        self.mm(ps[pi][:, si * 256:(si + 1) * 256], h_[:, kc, tt * 128:(tt + 1) * 128],
                                w[:, kc, s0:s0 + 256], kc == 0, kc == 7, [wB[s0 // 512], hb_], [psB[pi]])
                self.cp(v_[:, tt, :, :], ps[pi][:, :].rearrange("p (s e) -> p s e", s=2), [psB[pi]], [vb_])
            self.ld(S["vslc"].rearrange("(tt p) e -> p tt e", p=P)[:, c * 4:(c + 1) * 4, :], v_[:, :, 0, :], [vb_],
                    [SB["vslc"][c]])
            self.ld(S["vwin"].rearrange("(tt p) e -> p tt e", p=P)[:, c * 4:(c + 1) * 4, :], v_[:, :, 1, :], [vb_],
                    [SB["vwin"][c]])
        self.phase_end()

    def phase_cmp(self):
        I, K, KB, S, SB, ps, psB = self.I, self.K, self.KB, self.S, self.SB, self.ps, self.psB
        T = self.T
        ncmp = T // 16 - 1
        self.phase_begin()
        src = [self.sb([64, T], BF16) for _ in range(2)]
        srcB = [self.B() for _ in range(2)]
        w1r = self.sb([64, 32, 256], BF16)
        w2 = self.sb([P, 2, 64], BF16)
        posT = self.sb([64, 32], BF16)
        hidT = self.sb([P, 2, 256], BF16)
        hidB = self.B()
        pre = self.sb([P, 256], F32)
        u = self.sb([P, 256], F32)
        bias = self.sb([P, 2], F32)
        bB = {k: self.B() for k in ("pre", "u", "bias")}
        pg = self.pg
        pg.op("dve", lambda e: e.memset(hidT[:], 0.0), [], [hidB])
        it = 0
        wb = [self.B(), self.B(), self.B()]
        for s in range(2):
            self.ld(w1r[:], I["cmp_w1"][s].rearrange("(t d) h -> d t h", d=64), [], [wb[0]], q="pool")
            self.ld(w2[:], I["cmp_w2"][s].rearrange("(hc p) d -> p hc d", p=P), [], [wb[1]], q="pool")
            self.ld(posT[:], I["cmp_posT"][s], [], [wb[2]], q="pool")
            for hc in range(2):
                for t in range(32):
                    self.mm(ps[6][:, hc:hc + 1], w1r[:, t, hc * 128:(hc + 1) * 128], posT[:, t:t + 1], t == 0, t == 31,
                            [wb[0], wb[2]], [psB[6]])
            self.cp(bias[:], ps[6][:, 0:2], [psB[6]], [bB["bias"]])
            for g in range(4):
                sr, srb = src[it % 2], srcB[it % 2]
                it += 1
                r0 = s * 256 + g * 64
                self.ld(sr[:], S["kvT"][r0:r0 + 64, :], SB["kvT"], [srb])
                for hc in range(2):
                    for t in range(32):
                        self.mm(ps[hc][:, 0:ncmp], w1r[:, t, hc * 128:(hc + 1) * 128],
                                sr[:, t:t + 16 * (ncmp - 1) + 1:16], t == 0, t == 31, [wb[0], srb], [psB[hc]])
                    self.act(pre[:, 0:ncmp], ps[hc][:, 0:ncmp], AF.Identity, [psB[hc], bB["bias"]], [bB["pre"]],
                             bias=bias[:, hc:hc + 1])
                    self.tt(u[:, 0:ncmp], pre[:, 0:ncmp], pre[:, 0:ncmp], ALU.mult, [bB["pre"]], [bB["u"]])
                    self.ts(u[:, 0:ncmp], u[:, 0:ncmp], 0.044715, 1.0, ALU.mult, ALU.add, [bB["u"]], [bB["u"]])
                    self.tt(u[:, 0:ncmp], u[:, 0:ncmp], pre[:, 0:ncmp], ALU.mult, [bB["u"], bB["pre"]], [bB["u"]])
                    self.act(u[:, 0:ncmp], u[:, 0:ncmp], AF.Sigmoid, [bB["u"]], [bB["u"]],
                             scale=2.0 * math.sqrt(2.0 / math.pi))
                    self.tt(hidT[:, hc, 0:ncmp], u[:, 0:ncmp], pre[:, 0:ncmp], ALU.mult, [bB["u"], bB["pre"]],
                            [hidB])
                if s == 0:
                    for hc in range(2):
                        self.mm(ps[2][0:64, 0:ncmp], w2[:, hc, :], hidT[:, hc, 0:ncmp], hc == 0, hc == 1,
                                [wb[1], hidB], [psB[2]])
                    self.cp(K["kcmpT"][:, g, 0:ncmp], ps[2][0:64, 0:ncmp], [psB[2]], [KB["kcmpT"]])
                else:
                    for nt in range(2):
                        nn = min(128, ncmp - nt * 128)
                        if nn <= 0:
                            continue
                        for hc in range(2):
                            self.mm(ps[3][0:nn, nt * 64:(nt + 1) * 64], hidT[:, hc, nt * 128:nt * 128 + nn],
                                    w2[:, hc, :], hc == 0, hc == 1, [wb[1], hidB], [psB[3]])
                        self.cp(K["vcmp"][0:nn, g, nt, :], ps[3][0:nn, nt * 64:(nt + 1) * 64], [psB[3]],
                                [KB["vcmp"]])
        self.phase_end()

    def phase_b_proj(self, l):
        I, K, KB, S, SB, ps, psB = self.I, self.K, self.KB, self.S, self.SB, self.ps, self.psB
        NC = self.NC
        self.phase_begin()
        w = self.sb([P, 8, 1072], BF16)
        wB = [self.B() for _ in range(3)]
        self.load_w_bf16(w, wB, I["b_w_in"][l - 2].rearrange("(kc p) n -> p kc n", p=P), 1072)
        xt = [self.sb([P, 8, 512], F32) for _ in range(2)]
        xB = [self.B() for _ in range(2)]
        sq, sqB, tr, trB = self.norm_rings(512)
        rstd = self.sb([P, 512], F32)
        rB = self.B()
        h = [self.sb([P, 8, 512], BF16) for _ in range(2)]
        hB = [self.B() for _ in range(2)]
        qst = [self.sb([P, 8, 512], BF16) for _ in range(2)]
        qB = [self.B() for _ in range(2)]
        gst = [self.sb([48, 512], F32) for _ in range(2)]
        gB = [self.B() for _ in range(2)]
        xv = S["xT"].rearrange("(j p) t -> p j t", p=P)
        ring = 0
        for c in range(NC):
            cs = slice(c * 512, (c + 1) * 512)
            x_, xb_ = xt[c % 2], xB[c % 2]
            self.ld(x_[:], xv[:, :, cs], [SB["xT"][c]], [xb_])
            h_, hb_ = h[c % 2], hB[c % 2]
            self.norm_mod(x_, xb_, 512, K["g1"][:, l, :], K["mod"][:, l, 0:8], KB["g1"], h_, hb_, sq, sqB, rstd, rB, 2, tout=tr, tB=trB)
            q_, qb_ = qst[c % 2], qB[c % 2]
            for m in range(8):
                pi = ring % 2
                ring += 1
                for kc in range(8):
                    self.mm(ps[pi][:, :], w[:, kc, m * 128:(m + 1) * 128], h_[:, kc, :], kc == 0, kc == 7,
                            [wB[m // 4], hb_], [psB[pi]])
                self.act(q_[:, m, :], ps[pi][:, :], AF.Copy, [psB[pi]], [qb_], scale=0.125)
            self.ld(S["qT"].rearrange("(m p) t -> p m t", p=P)[:, :, cs], q_[:], [qb_], [SB["qT"][c]])
            pi = ring % 2
            ring += 1
            for kc in range(8):
                self.mm(ps[pi][0:48, :], w[:, kc, 1024:1072], h_[:, kc, :], kc == 0, kc == 7, [wB[2], hb_], [psB[pi]])
            self.act(gst[c % 2][:], ps[pi][0:48, :], AF.Sigmoid, [psB[pi]], [gB[c % 2]])
            self.ld(S["gT"][:, cs], gst[c % 2][:], [gB[c % 2]], [SB["gT"][c]])
        self.phase_end()

    def phase_b_attn(self, l):
        I, K, KB, S, SB, ps, psB = self.I, self.K, self.KB, self.S, self.SB, self.ps, self.psB
        NC, NT, T = self.NC, self.NT, self.T
        pg = self.pg
        self.phase_begin()
        d0, d1, w4, dB = self.load_bias_tiles()
        ex = self.sb([64, NT, 128], BF16)
        exB = self.B()
        self.ld(ex[:], I["c_ex"][:, :, :], [], [exB], q="pool")
        ov = self.sb([P, 2, 64], F32)
        ovB = self.B()
        self.ld(ov[:], I["c_ov"][:, :, :], [], [ovB])
        maskc = self.sb([P, 2, T], BF16)
        mcB = self.B()
        self.ld(maskc[:], I["c_maskc"][:, :, :], [], [mcB], q="pool")
        keep = self.sb([P, NT, 64], BF16)
        addm = self.sb([P, NT, 64], BF16)
        kaB = [self.B(), self.B()]
        self.ld(keep[:], I["c_keep"][:, :, :], [], [kaB[0]], q="pool")
        self.ld(addm[:], I["c_add"][:, :, :], [], [kaB[1]], q="pool")
        ksl = [self.sb([64, T], BF16) for _ in range(1)]
        kwn = [self.sb([64, T], BF16) for _ in range(1)]
        vsl = [self.sb([P, NT, 65], BF16) for _ in range(1)]
        vwn = [self.sb([P, NT, 65], BF16) for _ in range(1)]
        gB_ = [[self.B() for _ in range(4)] for _ in range(1)]
        pg.op("pool", lambda e: e.memset(vsl[0][:], 1.0), [], [gB_[0][2]])
        pg.op("pool", lambda e: e.memset(vwn[0][:], 1.0), [], [gB_[0][3]])
        lrow = self.sb([65, 512], F32)
        lrowB = self.B()
        qg = [self.sb([64, 4, 512], BF16) for _ in range(2)]
        qgB = [self.B() for _ in range(2)]
        gb = self.sb([64, 12, 512], F32)
        gbB = self.B()
        pc32 = [self.sb([P, 512], F32) for _ in range(2)]
        pn32 = [self.sb([P, 512], F32) for _ in range(2)]
        pn16 = [self.sb([P, 512], BF16) for _ in range(2)]
        pcB = [self.B() for _ in range(2)]
        pnB = [self.B() for _ in range(2)]
        pn16B = [self.B() for _ in range(2)]
        rl = self.sb([P, 512], F32)
        rlB = self.B()
        oc = self.sb([64, 4, 512], F32)
        ocB = [self.B() for _ in range(4)]
        impv = self.sb([P, 64], F32)
        impv2 = self.sb([P, 64], F32)
        m8a = self.sb([P, 8], F32)
        m8b = self.sb([P, 8], F32)
        msel = self.sb([P, 4, 64], F32)
        tkB = {k: self.B() for k in ("impv", "impv2", "m8a", "m8b", "msel")}
        mT = self.sb([64, 512], BF16)
        mTB = self.B()
        mall = self.sb([P, NT, 512], BF16)
        mallB = [self.B() for _ in range(NT)]
        NPT = 5
        SR = [0, 1, 4, 5]
        Pt = [self.sb([P, 512], BF16) for _ in range(NPT)]
        PB = [self.B() for _ in range(NPT)]
        rr = self.sb([64, 512], F32)
        rrB = self.B()
        acc = self.sb([64, 512], F32)
        accB = self.B()
        tmp = self.sb([64, 512], F32)
        tmpB = self.B()
        ost = [self.sb([64, 4, 512], BF16) for _ in range(2)]
        ostB = [self.B() for _ in range(2)]
        ctr = {"s": 0, "p": 0}
        it = 0
        for g in range(4):
            gi = 0
            r0 = g * 64
            self.ld(ksl[gi][:], S["kvT"][512 + r0:512 + r0 + 64, :], SB["kvT"], [gB_[gi][0]])
            self.ld(kwn[gi][:], S["kvT"][1024 + r0:1024 + r0 + 64, :], SB["kvT"], [gB_[gi][1]])
            self.ld(vsl[gi][:, :, 0:64], S["vslc"].rearrange("(kt p) e -> p kt e", p=P)[:, :, r0:r0 + 64], SB["vslc"],
                    [gB_[gi][2]])
            self.ld(vwn[gi][:, :, 0:64], S["vwin"].rearrange("(kt p) e -> p kt e", p=P)[:, :, r0:r0 + 64], SB["vwin"],
                    [gB_[gi][3]])
            for qc in range(NC):
                cs = slice(qc * 512, (qc + 1) * 512)
                q_, qb_ = qg[it % 2], qgB[it % 2]
                o_st, o_stB = ost[it % 2], ostB[it % 2]
                it += 1
                self.ld(q_[:], S["qT"].rearrange("(h d) t -> d h t", d=64)[:, g * 4:(g + 1) * 4, cs], [SB["qT"][qc]],
                        [qb_])
                self.ld(gb[:], bass.AP(S["gT"].tensor, g * 12 * T + qc * 512, [[0, 64], [T, 12], [1, 512]]),
                        [SB["gT"][qc]], [gbB])
                for r in range(4):
                    for nt in range(2):
                        si = ctr["s"] % 2
                        ctr["s"] += 1
                        self.mm(ps[si][:, :], K["kcmpT"][:, g, nt * 128:(nt + 1) * 128], q_[:, r, :], True, True,
                                [KB["kcmpT"], qb_], [psB[si]])
                        self.act(pc32[nt][:], ps[si][:, :], AF.Exp, [psB[si]], [pcB[nt]])
                        self.tt(pc32[nt][:], pc32[nt][:], maskc[:, nt, cs], ALU.mult, [pcB[nt], mcB], [pcB[nt]])
                    for nt in range(2):
                        self.mm(ps[4][:, :], K["ones32"][:, :], pc32[nt][:], nt == 0, nt == 1,
                                [KB["ones32"], pcB[nt]], [psB[4]])
                    self.ts(rl[:], ps[4][:, :], 1e-18, None, ALU.max, None, [psB[4]], [rlB])
                    self.act(rl[:], rl[:], AF.Ln, [rlB], [rlB])
                    self.act(rl[:], rl[:], AF.Exp, [rlB], [rlB], scale=-1.0)
                    for nt in range(2):
                        self.tt(pn32[nt][:], pc32[nt][:], rl[:], ALU.mult, [pcB[nt], rlB], [pnB[nt]])
                        self.cp(pn16[nt][:], pn32[nt][:], [pnB[nt]], [pn16B[nt]], eng="pool")
                    for nt in range(2):
                        self.mm(ps[2][0:64, :], K["vcmp"][:, g, nt, :], pn16[nt][:], nt == 0, nt == 1,
                                [KB["vcmp"], pn16B[nt]], [psB[2]])
                    self.cp(oc[:, r, :], ps[2][0:64, :], [psB[2]], [ocB[r]], eng="act_copy")
                    for nt in range(2):
                        for i in range(4):
                            first = (r == 0 and nt == 0 and i == 0)
                            last = (r == 3 and nt == 1 and i == 3)
                            self.mm(ps[5][:, i * 64:(i + 1) * 64], pn32[nt][:, i * 128:(i + 1) * 128], ov[:, nt, :],
                                    first, last, [pnB[nt], ovB], [psB[5]])
                for i in range(4):
                    qb = qc * 4 + i
                    self.tt(impv[:], ps[5][:, i * 64:(i + 1) * 64], keep[:, qb, :], ALU.mult, [psB[5], kaB[0]],
                            [tkB["impv"]])
                    self.tt(impv[:], impv[:], addm[:, qb, :], ALU.add, [tkB["impv"], kaB[1]], [tkB["impv"]])
                    pg.op("dve", lambda e: e.max(out=m8a[:], in_=impv[:]), [tkB["impv"]], [tkB["m8a"]])
                    pg.op("dve", lambda e: e.match_replace(out=impv2[:], in_to_replace=m8a[:], in_values=impv[:],
                                                           imm_value=-3.0e38),
                          [tkB["impv"], tkB["m8a"]], [tkB["impv2"]])
                    pg.op("dve", lambda e: e.max(out=m8b[:], in_=impv2[:]), [tkB["impv2"]], [tkB["m8b"]])
                    self.ts(msel[:, i, :], impv[:], m8b[:, 7:8], None, ALU.is_ge, None, [tkB["impv"], tkB["m8b"]],
                            [tkB["msel"]])
                for i in range(4):
                    pg.op("pe", lambda e, i=i: e.transpose(out=ps[7][0:64, i * 128:(i + 1) * 128], in_=msel[:, i, :],
                                                           identity=K["ident"][:, :]),
                          [tkB["msel"], KB["ident"]], [psB[7]])
                self.cp(mT[:], ps[7][0:64, 0:512], [psB[7]], [mTB])
                nk = 4 * qc + 4
                for kt in range(nk):
                    c0 = max(0, kt - 4 * qc) * 128
                    self.mm(ps[6][:, c0:512], ex[:, kt, :], mT[:, c0:512], True, True, [exB, mTB], [psB[6]])
                    self.cp(mall[:, kt, c0:512], ps[6][:, c0:512], [psB[6]], [mallB[kt]],
                            eng=("dve" if kt % 2 == 0 else "act_copy"))
                for r in range(4):
                    h = g * 4 + r

                    def run_branch(kT_, kB_, vT_, vB_, tiles, sel, r=r, h=h, q_=q_, qb_=qb_, qc=qc):
                        st = {}
                        n = len(tiles)

                        def rng(kt):
                            c0 = max(0, kt - 4 * qc) * 128
                            c1 = 512 if sel else min(4, kt + 5 - 4 * qc) * 128
                            return c0, c1

                        def stageA(kt, i):
                            si = SR[ctr["s"] % 4]
                            ctr["s"] += 1
                            st[kt] = si
                            c0, c1 = rng(kt)
                            self.mm(ps[si][:, c0:c1], kT_[:, kt * 128:(kt + 1) * 128], q_[:, r, c0:c1], True, True,
                                    [kB_, qb_], [psB[si]])

                        def stageB(kt, i):
                            si = st[kt]
                            c0, c1 = rng(kt)
                            for ii in range(c0 // 128, c1 // 128):
                                delta = 4 * qc + ii - kt
                                dd = None
                                if delta == 0:
                                    dd = d0[:, h, :]
                                elif delta == 1:
                                    dd = d1[:, h, :]
                                elif delta == 4 and not sel:
                                    dd = w4[:, :]
                                if dd is not None:
                                    self.tt(ps[si][:, ii * 128:(ii + 1) * 128], ps[si][:, ii * 128:(ii + 1) * 128],
                                            dd, ALU.add, [psB[si]] + dB, [psB[si]])
                            pi = ctr["p"] % NPT
                            ctr["p"] += 1
                            self.act(Pt[pi][:, c0:c1], ps[si][:, c0:c1], AF.Exp, [psB[si], KB["b31"]], [PB[pi]],
                                     bias=K["b31"][:, h:h + 1])
                            if sel:
                                self.tt(Pt[pi][:, c0:c1], Pt[pi][:, c0:c1], mall[:, kt, c0:c1], ALU.mult,
                                        [PB[pi], mallB[kt]], [PB[pi]])
                            self.mm(ps[2][0:65, c0:c1], vT_[:, kt, :], Pt[pi][:, c0:c1], i == 0, i == n - 1,
                                    [vB_, PB[pi]], [psB[2]])

                        self.attn_tiles(tiles, stageA, stageB, depth=3)
                        self.act(lrow[64:65, :], ps[2][64:65, :], AF.Ln, [psB[2]], [lrowB])
                        self.act(lrow[64:65, :], lrow[64:65, :], AF.Exp, [lrowB], [lrowB], scale=-1.0)
                        self.mm(ps[3][0:64, :], K["ones32"][64:65, 0:64], lrow[64:65, :], True, True,
                                [KB["ones32"], lrowB], [psB[3]])
                        self.cp(rr[:], ps[3][0:64, :], [psB[3]], [rrB], eng="act_copy")
                        self.tt(tmp[:], ps[2][0:64, :], rr[:], ALU.mult, [psB[2], rrB], [tmpB])

                    self.tt(acc[:], oc[:, r, :], gb[:, r * 3 + 0, :], ALU.mult, [ocB[r], gbB], [accB])
                    run_branch(ksl[gi], gB_[gi][0], vsl[gi], gB_[gi][2], list(range(0, 4 * qc + 4)), True)
                    self.tt(tmp[:], tmp[:], gb[:, r * 3 + 1, :], ALU.mult, [tmpB, gbB], [tmpB])
                    self.tt(acc[:], acc[:], tmp[:], ALU.add, [accB, tmpB], [accB])
                    run_branch(kwn[gi], gB_[gi][1], vwn[gi], gB_[gi][3], list(range(max(0, 4 * qc - 4), 4 * qc + 4)),
                               False)
                    self.tt(tmp[:], tmp[:], gb[:, r * 3 + 2, :], ALU.mult, [tmpB, gbB], [tmpB])
                    self.tt(o_st[:, r, :], acc[:], tmp[:], ALU.add, [accB, tmpB], [o_stB])
                self.ld(S["oT"].rearrange("(h d) t -> d h t", d=64)[:, g * 4:(g + 1) * 4, cs], o_st[:], [o_stB],
                        [SB["oT"][qc]])
        self.phase_end()


_orig_op = Prog.op


def _op(self, eng, fn, reads=(), writes=()):
    if eng == "act_copy":
        return _orig_op(self, "act", fn, reads, writes)
    return _orig_op(self, eng, fn, reads, writes)


Prog.op = _op
_orig_cp = Builder.cp


def _cp(self, out, in_, reads, writes, eng="dve"):
    if eng == "act_copy":
        self.pg.op("act", lambda e: e.activation(out=out, in_=in_, func=AF.Copy), reads, writes)
    else:
        _orig_cp(self, out, in_, reads, writes, eng)


Builder.cp = _cp


def col8(v):
    v = np.asarray(v, np.float32)
    return np.ascontiguousarray(np.moveaxis(v.reshape(v.shape[:-1] + (v.shape[-1] // 128, 128)), -1, 0))


def make_in_maps(inputs, T):
    x = np.asarray(inputs["x"], np.float32)
    B = x.shape[0]
    shared = {}
    f = lambda k: np.ascontiguousarray(np.asarray(inputs[k], np.float32))
    shared["rel_bias"] = f("rel_bias")
    shared["ada_w"] = f("ada_w")
    shared["adab"] = np.ascontiguousarray(f("ada_b").reshape(NL, 48, 128).transpose(0, 2, 1))
    shared["an"] = col8(f("attn_norm"))
    shared["mn"] = col8(f("mlp_norm"))
    shared["mlp_w1"] = f("mlp_w1")
    shared["mlp_w2"] = f("mlp_w2")
    shared["a_w_in"] = f("a_w_in")
    shared["a_w_out"] = f("a_w_out")
    shared["a_lambda"] = f("a_lambda").reshape(2, 256)
    shared["a_subln"] = np.ascontiguousarray(f("a_subln").T)
    shared["kv_ada_w"] = f("kv_ada_w")
    shared["kvadab"] = np.ascontiguousarray(f("kv_ada_b").reshape(16, 128).T)
    shared["kvn"] = col8(f("kv_norm"))
    shared["w_kv"] = f("w_kv")
    shared["cmp_posT"] = np.ascontiguousarray(f("cmp_pos").transpose(0, 2, 1))
    shared["cmp_w1"] = f("cmp_w1")
    shared["cmp_w2"] = f("cmp_w2")
    shared["b_w_in"] = f("b_w_in")
    shared["b_w_out"] = f("b_w_out")
    shared["fnorm"] = col8(f("final_norm"))
    shared.update(make_consts(T))
    maps = []
    c = np.asarray(inputs["c"], np.float32)
    for b in range(B):
        m = dict(shared)
        m["xT"] = np.ascontiguousarray(x[b].T)
        m["cT"] = np.ascontiguousarray(c[b].reshape(8, 128).T)
        maps.append(m)
    return maps


_CACHE = {}


def run(inputs, T, layers=NL, debug=None):
    key = (T, layers, tuple(debug) if debug else None)
    if key not in _CACHE:
        _CACHE[key] = Builder(T, layers, debug).build()
    nc = _CACHE[key]
    maps = make_in_maps(inputs, T)
    res = run_bass_kernel_spmd(nc, maps, core_ids=list(range(len(maps))))
    return res.results


def kernel(**inputs):
    T = int(np.asarray(inputs["x"]).shape[1])
    results = run(inputs, T)
    out = np.stack([np.ascontiguousarray(r["outT"].T) for r in results], axis=0)
    return out.astype(np.float32)
```

```python
import math
from contextlib import ExitStack

import numpy as np
import ml_dtypes

import concourse.bass as bass
import concourse.mybir as mybir
from concourse.bass_utils import run_bass_kernel_spmd

F32 = mybir.dt.float32
BF16 = mybir.dt.bfloat16
AF = mybir.ActivationFunctionType
ALU = mybir.AluOpType
AX = mybir.AxisListType

D = 1024
DFF = 4096
NL = 4
EPS = 1e-6
NEG = -1e30
P = 128


class Buf:
    __slots__ = ("name", "w", "r")

    def __init__(self, name):
        self.name = name
        self.w = []
        self.r = []


class Op:
    __slots__ = ("eng", "fn", "deps", "dma", "slot", "target", "val", "waits")

    def __init__(self, eng, fn, deps, dma):
        self.eng = eng
        self.fn = fn
        self.deps = deps
        self.dma = dma
        self.slot = None
        self.target = False
        self.val = 0
        self.waits = []


class Prog:
    ENGS = ("pe", "act", "dve", "pool", "sp")
    NSLOT = {"sp": 24, "pool": 8, "act": 4}

    def __init__(self, nc):
        self.nc = nc
        self.ops = []
        self.rr = {q: 0 for q in self.NSLOT}
        self.slot_last = {}
        self.last_on_eng = {}

    def _reduce(self, ids):
        best = {}
        out = set()
        for i in ids:
            o = self.ops[i]
            if o.dma:
                out.add(i)
            else:
                if o.eng not in best or best[o.eng] < i:
                    best[o.eng] = i
        out.update(best.values())
        return out

    def _mk(self, eng, fn, reads, writes, dma):
        deps = set()
        for b in reads:
            deps.update(b.w)
        for b in writes:
            deps.update(b.w)
            deps.update(b.r)
        gid = len(self.ops)
        op = Op(eng, fn, self._reduce(deps), dma)
        if dma:
            s = self.rr[eng]
            self.rr[eng] = (s + 1) % self.NSLOT[eng]
            op.slot = (eng, s)
            prev = self.slot_last.get(op.slot)
            if prev is not None:
                op.deps.add(prev)
            self.slot_last[op.slot] = gid
        self.ops.append(op)
        wset = set(id(b) for b in writes)
        for b in writes:
            b.w = [gid]
            b.r = []
        for b in reads:
            if id(b) not in wset:
                b.r.append(gid)
                if len(b.r) > 12:
                    b.r = list(self._reduce(b.r))
        self.last_on_eng[eng if not dma else ("dma", gid)] = gid
        return gid

    def op(self, eng, fn, reads=(), writes=()):
        return self._mk(eng, fn, reads, writes, False)

    def dma(self, q, fn, reads=(), writes=()):
        return self._mk(q, fn, reads, writes, True)

    def barrier(self):
        allb = Buf("barrier")
        ids = [i for i, o in enumerate(self.ops)]
        last = {}
        dmas = []
        for i in range(len(self.ops) - 1, -1, -1):
            o = self.ops[i]
            if o.dma:
                if o.slot not in last:
                    last[o.slot] = i
                    dmas.append(i)
            elif o.eng not in last:
                last[o.eng] = i
                dmas.append(i)
        allb.w = dmas
        nc = self.nc
        z = self._bar_tiles
        self.op("pe", lambda e: e.matmul(z["ps"][0:1, 0:2], z["bf"][0:1, 0:1], z["bf"][0:1, 0:2],
                                          start=True, stop=True), reads=[allb], writes=[z["b_pe"]])
        self.op("act", lambda e: e.activation(out=z["a"][0:1, 0:1], in_=z["src"][0:1, 0:1], func=AF.Copy),
                reads=[allb], writes=[z["b_act"]])
        self.op("dve", lambda e: e.tensor_copy(out=z["v"][0:1, 0:1], in_=z["src"][0:1, 0:1]),
                reads=[allb], writes=[z["b_dve"]])
        self.op("pool", lambda e: e.tensor_copy(out=z["g"][0:1, 0:1], in_=z["src"][0:1, 0:1]),
                reads=[allb], writes=[z["b_pool"]])
        self.dma("sp", lambda e: e.dma_start(out=z["s"][0:1, 0:1], in_=z["src"][0:1, 0:1]),
                 reads=[allb], writes=[z["b_sp"]])

    def emit(self, stack):
        nc = self.nc
        ops = self.ops
        comp = ("pe", "act", "dve", "pool")
        sems = {e: stack.enter_context(nc.semaphore("sem_" + e)) for e in comp}
        dsem = {}
        for q, n in self.NSLOT.items():
            for s in range(n):
                dsem[(q, s)] = stack.enter_context(nc.semaphore("dsem_%s_%d" % (q, s)))
        for o in ops:
            for d in o.deps:
                t = ops[d]
                if t.dma:
                    continue
                if o.eng == "pe" and t.eng == "pe" and not o.dma:
                    continue
                t.target = True
        cnt = {e: 0 for e in comp}
        dcnt = {k: 0 for k in dsem}
        for o in ops:
            if o.dma:
                dcnt[o.slot] += 16
                o.val = dcnt[o.slot]
            else:
                if o.target:
                    cnt[o.eng] += 1
                o.val = cnt[o.eng]
        known = {e: {} for e in self.ENGS}
        clocks = {}
        for gid, o in enumerate(ops):
            kn = known[o.eng]
            m = {}
            for d in sorted(o.deps):
                t = ops[d]
                if t.dma:
                    key = ("d", t.slot)
                    sem = dsem[t.slot]
                else:
                    if o.eng == "pe" and t.eng == "pe" and not o.dma:
                        continue
                    key = ("e", t.eng)
                    sem = sems[t.eng]
                if kn.get(key, 0) >= t.val:
                    continue
                if key not in m or m[key][1] < t.val:
                    m[key] = (sem, t.val, d)
            for key, (sem, v, d) in sorted(m.items(), key=lambda kv: -kv[1][2]):
                if kn.get(key, 0) >= v:
                    continue
                o.waits.append((key, sem, v))
                for k2, v2 in clocks[d].items():
                    if kn.get(k2, 0) < v2:
                        kn[k2] = v2
            if o.dma or o.target:
                c = dict(kn)
                if o.dma:
                    c[("d", o.slot)] = o.val
                else:
                    k = ("e", o.eng)
                    if c.get(k, 0) < o.val:
                        c[k] = o.val
                clocks[gid] = c
        streams = {e: [] for e in self.ENGS}
        for o in ops:
            streams[o.eng].append(o)
        final = [(dsem[k], v) for k, v in dcnt.items() if v > 0]

        def run(eng_name, e):
            for o in streams[eng_name]:
                for _, sem, v in o.waits:
                    e.wait_ge(sem, v)
                ins = o.fn(e)
                if o.dma:
                    ins.then_inc(dsem[o.slot], 16)
                elif o.target:
                    ins.then_inc(sems[o.eng], 1)
            if eng_name == "sp":
                for sem, v in final:
                    e.wait_ge(sem, v)

        with nc.Block() as block:
            @block.tensor
            def _(e):
                run("pe", e)

            @block.scalar
            def _(e):
                run("act", e)

            @block.vector
            def _(e):
                run("dve", e)

            @block.gpsimd
            def _(e):
                run("pool", e)

            @block.sync
            def _(e):
                run("sp", e)


def _t5_bucket_np(dist):
    n = np.maximum(dist, 0)
    nf = np.maximum(n, 1).astype(np.float32)
    large = 16 + (np.log(nf / np.float32(16)) / np.float32(math.log(8.0)) * np.float32(16)).astype(np.int32)
    large = np.minimum(large, 31)
    return np.where(n < 16, n, large)


def make_consts(T):
    NT = T // 128
    c = {}
    oh = np.zeros((33, 384), np.float32)
    for i in range(384):
        dist = i - 127
        if dist < 0:
            oh[32, i] = 1.0
        else:
            oh[int(_t5_bucket_np(np.array(dist))), i] += 1.0
            oh[31, i] -= 1.0
    c["c_oh"] = oh
    ki = np.arange(128)[:, None]
    qi = np.arange(128)[None, :]
    c["c_w4"] = np.where(ki > qi, 0.0, NEG).astype(np.float32)
    c["c_ident"] = np.eye(128, dtype=np.float32)
    ex = np.zeros((64, NT, 128), np.float32)
    for kt in range(NT):
        for k in range(128):
            ex[2 * kt + k // 64, kt, k] = 1.0
    c["c_ex"] = ex
    n_cmp = (T - 32) // 16 + 1
    n_slc = T // 64
    n = np.arange(256)
    cs = n * 16
    ce = cs + 31
    ss = np.arange(n_slc) * 64
    ov = ((cs[:, None] < ss[None, :] + 64) & (ce[:, None] >= ss[None, :]) & (n[:, None] < n_cmp)).astype(np.float32)
    ovp = np.zeros((256, 64), np.float32)
    ovp[:, :n_slc] = ov
    c["c_ov"] = np.ascontiguousarray(ovp.reshape(2, 128, 64).transpose(1, 0, 2))
    q = np.arange(T)
    mc = ((ce[:, None] <= q[None, :]) & (n[:, None] < n_cmp)).astype(np.float32)
    c["c_maskc"] = np.ascontiguousarray(mc.reshape(2, 128, T).transpose(1, 0, 2))
    j = np.arange(64)[None, :]
    qb = (q // 64)[:, None]
    forced = (j == 0) | ((j <= qb) & (j > qb - 2))
    valid = (j <= qb) & (j < n_slc)
    keep = (~forced & valid).astype(np.float32)
    add = np.where(valid, np.where(forced, 1e4, 0.0), NEG).astype(np.float32)
    c["c_keep"] = np.ascontiguousarray(keep.reshape(NT, 128, 64).transpose(1, 0, 2))
    c["c_add"] = np.ascontiguousarray(add.reshape(NT, 128, 64).transpose(1, 0, 2))
    return c


class Builder:
    def __init__(self, T, layers=NL, debug=False):
        self.T = T
        self.NT = T // 128
        self.NC = T // 512
        self.layers = layers
        self.debug = debug
        self.nc = bass.Bass("TRN2", target_bir_lowering=False)
        self.pg = Prog(self.nc)
        self.gstack = ExitStack()
        self.pstack = None
        self.uid = 0

    def dram_in(self, name, shape, dt=F32):
        return self.nc.dram_tensor(name, list(shape), dt, kind="ExternalInput").ap()

    def dram_out(self, name, shape, dt=F32):
        return self.nc.dram_tensor(name, list(shape), dt, kind="ExternalOutput").ap()

    def dram(self, name, shape, dt):
        return self.nc.dram_tensor(name, list(shape), dt).ap()

    def sb(self, shape, dt, persistent=False, name=None):
        self.uid += 1
        st = self.gstack if persistent else self.pstack
        return st.enter_context(self.nc.sbuf_tensor("%s_%d" % (name or "t", self.uid), list(shape), dt))

    def B(self, name="b"):
        self.uid += 1
        return Buf("%s%d" % (name, self.uid))

    def phase_begin(self):
        self.pstack = ExitStack()

    def phase_end(self):
        self.pg.barrier()
        self.pstack.close()
        self.pstack = None

    def mm(self, out, lhsT, rhs, start, stop, reads, writes):
        self.pg.op("pe", lambda e: e.matmul(out, lhsT, rhs, start=start, stop=stop), reads, writes)

    def act(self, out, in_, func, reads, writes, bias=None, scale=None):
        kw = {}
        if bias is not None:
            kw["bias"] = bias
        if scale is not None:
            kw["scale"] = scale
        self.pg.op("act", lambda e: e.activation(out=out, in_=in_, func=func, **kw), reads, writes)

    def tt(self, out, in0, in1, op, reads, writes, eng="dve"):
        self.pg.op(eng, lambda e: e.tensor_tensor(out=out, in0=in0, in1=in1, op=op), reads, writes)

    def ts(self, out, in0, s1, s2, op0, op1, reads, writes, eng="dve"):
        if op1 is None:
            self.pg.op(eng, lambda e: e.tensor_scalar(out=out, in0=in0, scalar1=s1, scalar2=None, op0=op0),
                       reads, writes)
        else:
            self.pg.op(eng, lambda e: e.tensor_scalar(out=out, in0=in0, scalar1=s1, scalar2=s2, op0=op0, op1=op1),
                       reads, writes)

    def stt(self, out, in0, scalar, in1, op0, op1, reads, writes):
        self.pg.op("dve", lambda e: e.scalar_tensor_tensor(out=out, in0=in0, scalar=scalar, in1=in1,
                                                          op0=op0, op1=op1), reads, writes)

    def cp(self, out, in_, reads, writes, eng="dve"):
        self.pg.op(eng, lambda e: e.tensor_copy(out=out, in_=in_), reads, writes)

    def ld(self, out, in_, reads, writes, q="sp"):
        self.pg.dma(q, lambda e: e.dma_start(out=out, in_=in_), reads, writes)

    def build(self):
        nc, pg, T, NT, NC = self.nc, self.pg, self.T, self.NT, self.NC
        g = self.gstack
        I = {}
        I["xT"] = self.dram_in("xT", [D, T])
        I["cT"] = self.dram_in("cT", [P, 8])
        I["rel_bias"] = self.dram_in("rel_bias", [32, 16])
        I["ada_w"] = self.dram_in("ada_w", [NL, D, 6 * D])
        I["adab"] = self.dram_in("adab", [NL, P, 48])
        I["an"] = self.dram_in("an", [P, NL, 8])
        I["mn"] = self.dram_in("mn", [P, NL, 8])
        I["mlp_w1"] = self.dram_in("mlp_w1", [NL, D, DFF])
        I["mlp_w2"] = self.dram_in("mlp_w2", [NL, DFF, D])
        I["a_w_in"] = self.dram_in("a_w_in", [2, D, 3 * D])
        I["a_w_out"] = self.dram_in("a_w_out", [2, D, D])
        I["a_lambda"] = self.dram_in("a_lambda", [2, 256])
        I["a_subln"] = self.dram_in("a_subln", [P, 2])
        I["kv_ada_w"] = self.dram_in("kv_ada_w", [D, 2 * D])
        I["kvadab"] = self.dram_in("kvadab", [P, 16])
        I["kvn"] = self.dram_in("kvn", [P, 8])
        I["w_kv"] = self.dram_in("w_kv", [D, 1536])
        I["cmp_posT"] = self.dram_in("cmp_posT", [2, 64, 32])
        I["cmp_w1"] = self.dram_in("cmp_w1", [2, 2048, 256])
        I["cmp_w2"] = self.dram_in("cmp_w2", [2, 256, 64])
        I["b_w_in"] = self.dram_in("b_w_in", [2, D, 1072])
        I["b_w_out"] = self.dram_in("b_w_out", [2, D, D])
        I["fnorm"] = self.dram_in("fnorm", [P, 8])
        I["c_oh"] = self.dram_in("c_oh", [33, 384])
        I["c_w4"] = self.dram_in("c_w4", [P, 128])
        I["c_ident"] = self.dram_in("c_ident", [P, 128])
        I["c_ex"] = self.dram_in("c_ex", [64, NT, 128])
        I["c_ov"] = self.dram_in("c_ov", [P, 2, 64])
        I["c_maskc"] = self.dram_in("c_maskc", [P, 2, T])
        I["c_keep"] = self.dram_in("c_keep", [P, NT, 64])
        I["c_add"] = self.dram_in("c_add", [P, NT, 64])
        self.I = I
        outT = self.dram_out("outT", [D, T])
        S = {}
        S["xT"] = self.dram("s_xT", [D, T], F32)
        S["qT"] = self.dram("s_qT", [D, T], BF16)
        S["kT"] = self.dram("s_kT", [D, T], BF16)
        S["vtok"] = self.dram("s_vtok", [T, D], BF16)
        S["oT"] = self.dram("s_oT", [D, T], BF16)
        S["kvT"] = self.dram("s_kvT", [1536, T], BF16)
        S["vslc"] = self.dram("s_vslc", [T, 256], BF16)
        S["vwin"] = self.dram("s_vwin", [T, 256], BF16)
        S["gT"] = self.dram("s_gT", [48, T], F32)
        S["tT"] = self.dram("s_tT", [16, 384], F32)
        S["d0"] = self.dram("s_d0", [P, 16, 128], F32)
        S["d1"] = self.dram("s_d1", [P, 16, 128], F32)
        self.S = S
        SB = {k: [self.B(k) for _ in range(NC)] for k in ("xT", "qT", "kT", "vtok", "oT", "kvT", "vslc", "vwin", "gT")}
        for k in ("tT", "d0", "d1"):
            SB[k] = [self.B(k)]
        self.SB = SB
        PS = g.enter_context(nc.psum_tensor("psall", [P, 4096], F32))
        self.PS = PS
        ps = [PS[:, i * 512:(i + 1) * 512] for i in range(8)]
        self.ps = ps
        self.psB = [self.B("ps") for _ in range(8)]
        K = {}
        K["ones_bf"] = self.sb([P, 128], BF16, True, "ones")
        K["onesD"] = self.sb([P, 128], BF16, True, "onesD")
        K["onesH"] = self.sb([P, 128], BF16, True, "onesH")
        K["ones32"] = self.sb([P, 128], F32, True, "ones32")
        K["ident"] = self.sb([P, 128], F32, True, "ident")
        K["cact"] = self.sb([P, 8], F32, True, "cact")
        K["mod"] = self.sb([P, NL, 48], F32, True, "mod")
        K["kvmod"] = self.sb([P, 16], F32, True, "kvmod")
        K["g1"] = self.sb([P, NL, 8], F32, True, "g1")
        K["g2"] = self.sb([P, NL, 8], F32, True, "g2")
        K["gkv"] = self.sb([P, 8], F32, True, "gkv")
        K["fn"] = self.sb([P, 8], F32, True, "fn")
        K["zero8"] = self.sb([P, 8], F32, True, "zero8")
        K["b31"] = self.sb([P, 16], F32, True, "b31")
        K["lamneg"] = self.sb([P, 2], F32, True, "lamneg")
        K["subg"] = self.sb([P, 2], F32, True, "subg")
        K["kcmpT"] = self.sb([P, 4, 256], BF16, True, "kcmpT")
        K["vcmp"] = self.sb([P, 4, 2, 128], BF16, True, "vcmp")
        K["sel64"] = self.sb([P, 128], F32, True, "sel64")
        K["bar"] = self.sb([P, 16], F32, True, "bar")
        K["barbf"] = self.sb([P, 4], BF16, True, "barbf")
        self.K = K
        KB = {k: self.B(k) for k in K}
        self.KB = KB
        pg._bar_tiles = dict(ps=ps[6], bf=K["barbf"], src=K["bar"][:, 0:1], a=K["bar"][:, 1:2], v=K["bar"][:, 2:3],
                             g=K["bar"][:, 3:4], s=K["bar"][:, 4:5], b_pe=self.psB[6], b_act=self.B(), b_dve=self.B(),
                             b_pool=self.B(), b_sp=self.B())

        self.phase_setup()
        for l in range(self.layers):
            if l < 2:
                self.phase_a_proj(l)
                self.phase_a_attn(l)
                wo = I["a_w_out"][l]
            else:
                self.phase_b_proj(l)
                self.phase_b_attn(l)
                wo = I["b_w_out"][l - 2]
            self.phase_outproj(l, wo)
            self.phase_mlp(l)
            if l == 1:
                self.phase_kv()
                self.phase_cmp()
        self.phase_final(outT)
        if self.debug:
            dbg = {}
            for k in self.debug:
                t = S[k]
                o = self.dram_out("dbg_" + k, list(t.shape), t.dtype)
                self.ld(o, t, reads=SB[k], writes=[self.B()])
        pg.emit(g)
        return nc

    def phase_setup(self):
        nc, pg, I, K, KB, S, SB = self.nc, self.pg, self.I, self.K, self.KB, self.S, self.SB
        ps, psB = self.ps, self.psB
        self.phase_begin()
        pg.op("dve", lambda e: e.memset(K["bar"][:], 0.0), [], [KB["bar"]])
        pg.op("dve", lambda e: e.memset(K["barbf"][:], 0.0), [], [KB["barbf"]])
        pg.op("dve", lambda e: e.memset(K["ones_bf"][:], 1.0), [], [KB["ones_bf"]])
        pg.op("dve", lambda e: e.memset(K["onesD"][:], 1.0 / 1024), [], [KB["onesD"]])
        pg.op("dve", lambda e: e.memset(K["onesH"][:], 1.0 / 128), [], [KB["onesH"]])
        pg.op("dve", lambda e: e.memset(K["ones32"][:], 1.0), [], [KB["ones32"]])
        pg.op("dve", lambda e: e.memset(K["zero8"][:], 0.0), [], [KB["zero8"]])
        pg.op("dve", lambda e: e.memset(K["sel64"][:], 0.0), [], [KB["sel64"]])
        pg.op("dve", lambda e: e.memset(K["sel64"][64:65, :], 1.0), [KB["sel64"]], [KB["sel64"]])
        pg.op("dve", lambda e: e.memset(K["vcmp"][:], 0.0), [], [KB["vcmp"]])
        pg.op("dve", lambda e: e.memset(K["kcmpT"][:], 0.0), [], [KB["kcmpT"]])
        self.ld(K["ident"][:], I["c_ident"][:, :], [], [KB["ident"]])
        self.ld(K["fn"][:], I["fnorm"][:, :], [], [KB["fn"]])
        for c in range(self.NC):
            cs = slice(c * 512, (c + 1) * 512)
            self.ld(S["xT"][:, cs], I["xT"][:, cs], [], [SB["xT"][c]])
        craw = self.sb([P, 8], F32)
        b_craw = self.B()
        self.ld(craw[:], I["cT"][:, :], [], [b_craw])
        csig = self.sb([P, 8], F32)
        b_csig = self.B()
        self.act(csig[:], craw[:], AF.Sigmoid, [b_craw], [b_csig])
        self.tt(K["cact"][:], craw[:], csig[:], ALU.mult, [b_craw, b_csig], [KB["cact"]])
        wt = [self.sb([P, 8, 512], F32) for _ in range(2)]
        wtB = [self.B() for _ in range(2)]
        adab = self.sb([P, NL, 48], F32)
        b_adab = self.B()
        self.ld(adab[:], I["adab"].rearrange("l p j -> p l j"), [], [b_adab])
        kvadab = self.sb([P, 16], F32)
        b_kvadab = self.B()
        self.ld(kvadab[:], I["kvadab"][:, :], [], [b_kvadab])
        blk = 0
        jobs = [(I["ada_w"][l], 12, l) for l in range(NL)] + [(I["kv_ada_w"], 4, None)]
        for (wsrc, nblk, l) in jobs:
            pacc = ps[0]
            for bi in range(nblk):
                w = wt[blk % 2]
                wb = wtB[blk % 2]
                blk += 1
                self.ld(w[:], wsrc.rearrange("(kc p) n -> p kc n", p=P)[:, :, bi * 512:(bi + 1) * 512], [], [wb])
                for jj in range(4):
                    j = bi * 4 + jj
                    for kc in range(8):
                        self.mm(pacc[:, j:j + 1], w[:, kc, jj * 128:(jj + 1) * 128], K["cact"][:, kc:kc + 1],
                                kc == 0, kc == 7, [wb, KB["cact"]], [psB[0]])
            if l is not None:
                self.tt(K["mod"][:, l, :], pacc[:, 0:48], adab[:, l, :], ALU.add, [psB[0], b_adab], [KB["mod"]])
            else:
                self.tt(K["kvmod"][:], pacc[:, 0:16], kvadab[:], ALU.add, [psB[0], b_kvadab], [KB["kvmod"]])
        an = self.sb([P, NL, 8], F32)
        mn = self.sb([P, NL, 8], F32)
        kvn = self.sb([P, 8], F32)
        b_n = self.B()
        self.ld(an[:], I["an"][:, :, :], [], [b_n])
        b_n2 = self.B()
        self.ld(mn[:], I["mn"][:, :, :], [], [b_n2])
        b_n3 = self.B()
        self.ld(kvn[:], I["kvn"][:, :], [], [b_n3])
        tmp = self.sb([P, NL, 8], F32)
        b_tmp = self.B()
        for (dst, kb, nrm, nb, lo) in ((K["g1"], KB["g1"], an, b_n, 8), (K["g2"], KB["g2"], mn, b_n2, 32)):
            self.tt(tmp[:], K["mod"][:, :, lo:lo + 8], nrm[:], ALU.mult, [KB["mod"], nb], [b_tmp])
            self.tt(dst[:], tmp[:], nrm[:], ALU.add, [b_tmp, nb], [kb])
        tmp2 = self.sb([P, 8], F32)
        b_tmp2 = self.B()
        self.tt(tmp2[:], K["kvmod"][:, 8:16], kvn[:], ALU.mult, [KB["kvmod"], b_n3], [b_tmp2])
        self.tt(K["gkv"][:], tmp2[:], kvn[:], ALU.add, [b_tmp2, b_n3], [KB["gkv"]])
        tab = self.sb([33, 16], F32)
        b_tab = self.B()
        pg.op("dve", lambda e: e.memset(tab[32:33, :], NEG), [], [b_tab])
        b_tab2 = self.B()
        self.ld(tab[0:32, :], I["rel_bias"][:, :], [b_tab], [b_tab2])
        oh = self.sb([33, 384], F32)
        b_oh = self.B()
        self.ld(oh[:], I["c_oh"][:, :], [], [b_oh])
        self.mm(ps[1][0:16, 0:384], tab[:, :], oh[:, :], True, True, [b_tab, b_tab2, b_oh], [psB[1]])
        tsb = self.sb([16, 384], F32)
        b_tsb = self.B()
        self.cp(tsb[:], ps[1][0:16, 0:384], [psB[1]], [b_tsb])
        self.ld(S["tT"][:, :], tsb[:], [b_tsb], SB["tT"])
        for k in range(128):
            self.ld(S["d0"][k:k + 1, :, :], S["tT"][:, 127 - k:255 - k].rearrange("(o m) q -> o m q", o=1),
                    SB["tT"], [self.B()], q=("sp" if k % 2 == 0 else "act"))
            self.ld(S["d1"][k:k + 1, :, :], S["tT"][:, 255 - k:383 - k].rearrange("(o m) q -> o m q", o=1),
                    SB["tT"], [self.B()], q=("sp" if k % 2 == 0 else "act"))
        self.ld(K["b31"][:], bass.AP(I["rel_bias"].tensor, 31 * 16, [[0, P], [1, 16]]), [], [KB["b31"]])
        lam = self.sb([P, 2, 256], F32)
        b_lam = self.B()
        self.ld(lam[:], bass.AP(I["a_lambda"].tensor, 0, [[0, P], [256, 2], [1, 256]]), [], [b_lam])
        sub = self.sb([P, 2], F32)
        b_sub = self.B()
        self.ld(sub[:], I["a_subln"][:, :], [], [b_sub])
        prod = self.sb([P, 2, 2, 64], F32)
        b_prod = self.B()
        red = self.sb([P, 4], F32)
        b_red = self.B()
        for l in range(2):
            for i in range(2):
                self.tt(prod[:, l, i, :], lam[:, l, (2 * i) * 64:(2 * i + 1) * 64],
                        lam[:, l, (2 * i + 1) * 64:(2 * i + 2) * 64], ALU.mult, [b_lam], [b_prod])
        pg.op("dve", lambda e: e.tensor_reduce(out=red[:], in_=prod[:].rearrange("p l i d -> p (l i) d"),
                                               axis=AX.X, op=ALU.add), [b_prod], [b_red])
        ered = self.sb([P, 4], F32)
        b_ered = self.B()
        self.act(ered[:], red[:], AF.Exp, [b_red], [b_ered])
        for l in range(2):
            lam_init = 0.8 - 0.6 * math.exp(-0.3 * l)
            self.tt(K["lamneg"][:, l:l + 1], ered[:, 2 * l + 1:2 * l + 2], ered[:, 2 * l:2 * l + 1], ALU.subtract,
                    [b_ered], [KB["lamneg"]])
            self.ts(K["lamneg"][:, l:l + 1], K["lamneg"][:, l:l + 1], -lam_init, None, ALU.add, None,
                    [KB["lamneg"]], [KB["lamneg"]])
            self.ts(K["subg"][:, l:l + 1], sub[:, l:l + 1], 1.0 - lam_init, None, ALU.mult, None,
                    [b_sub], [KB["subg"]])
        self.phase_end()

    def norm_mod(self, xt, xb, N, gvec, shvec, gB, hout, hB, sq, sqB, rstd, rB, psi, tout=None, tB=None):
        K, KB, ps, psB = self.K, self.KB, self.ps, self.psB
        for j in range(8):
            s_, sb_ = sq[j % len(sq)], sqB[j % len(sq)]
            self.act(s_[:, 0:N], xt[:, j, 0:N], AF.Square, [xb], [sb_])
            self.mm(ps[psi][:, 0:N], K["onesD"][:, :], s_[:, 0:N], j == 0, j == 7, [KB["onesD"], sb_], [psB[psi]])
        self.act(rstd[:, 0:N], ps[psi][:, 0:N], AF.Ln, [psB[psi], self.KB["bar"]], [rB], bias=self.eps_ap)
        self.act(rstd[:, 0:N], rstd[:, 0:N], AF.Exp, [rB], [rB], scale=-0.5)
        for j in range(8):
            t_, tb_ = tout[j % len(tout)], tB[j % len(tout)]
            self.tt(t_[:, 0:N], xt[:, j, 0:N], rstd[:, 0:N], ALU.mult, [xb, rB], [tb_])
            self.act(hout[:, j, 0:N], t_[:, 0:N], AF.Identity, [tb_, gB], [hB],
                     bias=shvec[:, j:j + 1], scale=gvec[:, j:j + 1])

    def norm_rings(self, N=512):
        sq = [self.sb([P, N], BF16) for _ in range(2)]
        tt_ = [self.sb([P, N], F32) for _ in range(2)]
        return sq, [self.B() for _ in range(2)], tt_, [self.B() for _ in range(2)]

    @property
    def eps_ap(self):
        if not hasattr(self, "_eps_done"):
            self._eps_done = True
            K, KB = self.K, self.KB
            self.pg.op("dve", lambda e: e.memset(K["bar"][:, 5:6], EPS), [], [KB["bar"]])
        return self.K["bar"][:, 5:6]

    def load_w_bf16(self, dst, dstB, src_view, ncols, blk=512):
        nb = (ncols + blk - 1) // blk
        for i in range(nb):
            a, b = i * blk, min(ncols, (i + 1) * blk)
            self.ld(dst[:, :, a:b], src_view[:, :, a:b], [], [dstB[i]], q="pool")

    def phase_a_proj(self, l):
        I, K, KB, S, SB, ps, psB = self.I, self.K, self.KB, self.S, self.SB, self.ps, self.psB
        NC = self.NC
        self.phase_begin()
        w = self.sb([P, 8, 3072], BF16)
        wB = [self.B() for _ in range(6)]
        self.load_w_bf16(w, wB, I["a_w_in"][l].rearrange("(kc p) n -> p kc n", p=P), 3072)
        xt = [self.sb([P, 8, 512], F32) for _ in range(2)]
        xB = [self.B() for _ in range(2)]
        sq, sqB, tr, trB = self.norm_rings(512)
        rstd = self.sb([P, 512], F32)
        rB = self.B()
        h = [self.sb([P, 8, 512], BF16) for _ in range(2)]
        hB = [self.B() for _ in range(2)]
        qst = [self.sb([P, 16, 512], BF16) for _ in range(2)]
        qB = [self.B() for _ in range(2)]
        kB = [self.B() for _ in range(2)]
        vst = [self.sb([P, 4, 1024], BF16) for _ in range(2)]
        vB = [self.B() for _ in range(2)]
        xv = S["xT"].rearrange("(j p) t -> p j t", p=P)
        ring = 0
        for c in range(NC):
            cs = slice(c * 512, (c + 1) * 512)
            x_, xb_ = xt[c % 2], xB[c % 2]
            self.ld(x_[:], xv[:, :, cs], [SB["xT"][c]], [xb_])
            h_, hb_ = h[c % 2], hB[c % 2]
            self.norm_mod(x_, xb_, 512, K["g1"][:, l, :], K["mod"][:, l, 0:8], KB["g1"], h_, hb_, sq, sqB, rstd, rB, 2, tout=tr, tB=trB)
            q_, qb_, kb_ = qst[c % 2], qB[c % 2], kB[c % 2]
            for m in range(16):
                pi = ring % 2
                ring += 1
                for kc in range(8):
                    self.mm(ps[pi][:, :], w[:, kc, m * 128:(m + 1) * 128], h_[:, kc, :], kc == 0, kc == 7,
                            [wB[m // 4], hb_], [psB[pi]])
                if m < 8:
                    self.act(q_[:, m, :], ps[pi][:, :], AF.Copy, [psB[pi]], [qb_], scale=0.125)
                else:
                    self.cp(q_[:, m, :], ps[pi][:, :], [psB[pi]], [kb_])
            self.ld(S["qT"].rearrange("(m p) t -> p m t", p=P)[:, :, cs], q_[:, 0:8, :], [qb_], [SB["qT"][c]])
            self.ld(S["kT"].rearrange("(m p) t -> p m t", p=P)[:, :, cs], q_[:, 8:16, :], [kb_], [SB["kT"][c]])
            v_, vb_ = vst[c % 2], vB[c % 2]
            for tt in range(4):
                for half in range(2):
                    pi = ring % 2
                    ring += 1
                    for kc in range(8):
                        self.mm(ps[pi][:, :], h_[:, kc, tt * 128:(tt + 1) * 128],
                                w[:, kc, 2048 + half * 512:2048 + (half + 1) * 512], kc == 0, kc == 7,
                                [wB[4 + half], hb_], [psB[pi]])
                    self.cp(v_[:, tt, half * 512:(half + 1) * 512], ps[pi][:, :], [psB[pi]], [vb_],
                            eng=("dve" if half == 0 else "act_copy"))
            self.ld(S["vtok"].rearrange("(tt p) e -> p tt e", p=P)[:, c * 4:(c + 1) * 4, :], v_[:], [vb_],
                    [SB["vtok"][c]])
        self.phase_end()

    def attn_tiles(self, tiles, stageA, stageB, depth=1):
        n = len(tiles)
        for i in range(min(depth, n)):
            stageA(tiles[i], i)
        for i, t in enumerate(tiles):
            if i + depth < n:
                stageA(tiles[i + depth], i + depth)
            stageB(t, i)

    def load_bias_tiles(self):
        S, SB = self.S, self.SB
        d0 = self.sb([P, 16, 128], F32)
        d1 = self.sb([P, 16, 128], F32)
        w4 = self.sb([P, 128], F32)
        bd = self.B()
        self.ld(d0[:], S["d0"][:, :, :], SB["d0"], [bd])
        bd1 = self.B()
        self.ld(d1[:], S["d1"][:, :, :], SB["d1"], [bd1])
        bw = self.B()
        self.ld(w4[:], self.I["c_w4"][:, :], [], [bw])
        return d0, d1, w4, [bd, bd1, bw]

    def phase_a_attn(self, l):
        I, K, KB, S, SB, ps, psB, PS = self.I, self.K, self.KB, self.S, self.SB, self.ps, self.psB, self.PS
        NC, NT, T = self.NC, self.NT, self.T
        pg = self.pg
        self.phase_begin()
        d0, d1, w4, dB = self.load_bias_tiles()
        qh = [self.sb([P, T], BF16) for _ in range(2)]
        kA = [self.sb([P, T], BF16) for _ in range(2)]
        kBt = [self.sb([P, T], BF16) for _ in range(2)]
        vh = [self.sb([P, NT, 128], BF16) for _ in range(2)]
        lB = [[self.B() for _ in range(4)] for _ in range(2)]
        for i in range(2):
            pg.op("pool", lambda e, i=i: e.memset(kA[i][64:128, :], 0.0), [], [lB[i][1]])
            pg.op("pool", lambda e, i=i: e.memset(kBt[i][0:64, :], 0.0), [], [lB[i][3]])
        NPT = 4
        Pt = [self.sb([P, 2, 512], BF16) for _ in range(NPT)]
        PB = [self.B() for _ in range(NPT)]
        accL = [[self.sb([P, 512], F32) for _ in range(2)] for _ in range(2)]
        accB = [[self.B() for _ in range(2)] for _ in range(2)]
        r0 = self.sb([P, 512], F32)
        r1 = self.sb([P, 512], F32)
        t0 = self.sb([P, 512], F32)
        t1 = self.sb([P, 512], F32)
        osq = self.sb([P, 512], BF16)
        ost = [self.sb([P, 512], BF16) for _ in range(2)]
        bb = {k: self.B() for k in ("r0", "r1", "t0", "t1", "osq", "rs")}
        ostB = [self.B(), self.B()]
        rs = self.sb([P, 512], F32)
        pairs = [0, 4, 6]
        pairB = {0: self.B(), 4: self.B(), 6: self.B()}
        ctr = {"s": 0, "p": 0, "o": 0, "a": 0}
        for h in range(8):
            q_, ka_, kb_, v_ = qh[h % 2], kA[h % 2], kBt[h % 2], vh[h % 2]
            lb = lB[h % 2]
            self.ld(q_[:], S["qT"][h * 128:(h + 1) * 128, :], SB["qT"], [lb[0]])
            self.ld(ka_[0:64, :], S["kT"][h * 128:h * 128 + 64, :], SB["kT"], [lb[1]])
            self.ld(kb_[64:128, :], S["kT"][h * 128 + 64:(h + 1) * 128, :], SB["kT"], [lb[3]])
            self.ld(v_[:], S["vtok"].rearrange("(kt p) e -> p kt e", p=P)[:, :, h * 128:(h + 1) * 128], SB["vtok"],
                    [lb[2]])
            for qc in range(NC):
                tiles = list(range(0, 4 * qc + 4))
                nk = len(tiles)
                st = {}
                ai = ctr["a"] % 2
                ctr["a"] += 1
                aL, aB = accL[ai], accB[ai]

                def stageA(kt, i, h=h, q_=q_, ka_=ka_, kb_=kb_, lb=lb, st=st, qc=qc):
                    pb = pairs[ctr["s"] % 3]
                    ctr["s"] += 1
                    st[kt] = pb
                    c0 = max(0, kt - 4 * qc) * 128
                    for m in range(2):
                        hm = h * 2 + m
                        bank = ps[pb + m]
                        fixes = []
                        for ii in range(4):
                            delta = 4 * qc + ii - kt
                            if delta == 0 or delta == 1:
                                fixes.append((ii, d0 if delta == 0 else d1))
                        kk = ka_ if m == 0 else kb_
                        self.mm(bank[:, c0:512], kk[:, kt * 128:(kt + 1) * 128],
                                q_[:, qc * 512 + c0:(qc + 1) * 512], True, len(fixes) == 0,
                                [lb[0], lb[1], lb[3]], [pairB[pb]])
                        for fi, (ii, dd) in enumerate(fixes):
                            self.mm(bank[:, ii * 128:(ii + 1) * 128], K["ident"][:, :], dd[:, hm, :], False,
                                    fi == len(fixes) - 1, [KB["ident"]] + dB, [pairB[pb]])

                def stageB(kt, i, v_=v_, lb=lb, st=st, qc=qc, nk=nk, aL=aL, aB=aB):
                    pb = st[kt]
                    c0 = max(0, kt - 4 * qc) * 128
                    pi = ctr["p"] % NPT
                    ctr["p"] += 1
                    pv = PS[:, pb * 512:(pb + 2) * 512].rearrange("p (m c) -> p m c", m=2)
                    self.act(Pt[pi][:, :, c0:512], pv[:, :, c0:512], AF.Exp, [pairB[pb]], [PB[pi]])
                    for m in range(2):
                        eng = "pool" if m == 0 else "dve"
                        if i == 0:
                            self.cp(aL[m][:, c0:512], Pt[pi][:, m, c0:512], [PB[pi]], [aB[m]], eng=eng)
                        else:
                            self.tt(aL[m][:, c0:512], aL[m][:, c0:512], Pt[pi][:, m, c0:512], ALU.add,
                                    [PB[pi], aB[m]], [aB[m]], eng=eng)
                    for m in range(2):
                        self.mm(ps[2 + m][:, c0:512], v_[:, kt, :], Pt[pi][:, m, c0:512], i == 0, i == nk - 1,
                                [lb[2], PB[pi]], [psB[2 + m]])

                self.attn_tiles(tiles, stageA, stageB, depth=2)
                for m, (rr, tt_) in enumerate(((r0, t0), (r1, t1))):
                    rb, tb = bb["r%d" % m], bb["t%d" % m]
                    self.mm(ps[m][:, :], K["ones32"][:, :], aL[m][:, :], True, True, [KB["ones32"], aB[m]],
                            [pairB[0]])
                    self.act(rr[:], ps[m][:, :], AF.Ln, [pairB[0]], [rb])
                    self.act(rr[:], rr[:], AF.Exp, [rb], [rb], scale=-1.0)
                    self.tt(tt_[:], ps[2 + m][:, :], rr[:], ALU.mult, [psB[2 + m], rb], [tb])
                self.stt(t0[:], t1[:], K["lamneg"][:, l:l + 1], t0[:], ALU.mult, ALU.add,
                         [bb["t0"], bb["t1"], KB["lamneg"]], [bb["t0"]])
                self.act(osq[:], t0[:], AF.Square, [bb["t0"]], [bb["osq"]])
                self.mm(ps[4][:, :], K["onesH"][:, :], osq[:], True, True, [KB["onesH"], bb["osq"]], [pairB[4]])
                self.act(rs[:], ps[4][:, :], AF.Ln, [pairB[4], KB["bar"]], [bb["rs"]], bias=self.eps_ap)
                self.act(rs[:], rs[:], AF.Exp, [bb["rs"]], [bb["rs"]], scale=-0.5)
                self.tt(t0[:], t0[:], rs[:], ALU.mult, [bb["t0"], bb["rs"]], [bb["t0"]])
                oi = ctr["o"] % 2
                ctr["o"] += 1
                self.act(ost[oi][:], t0[:], AF.Identity, [bb["t0"], KB["subg"]], [ostB[oi]],
                         scale=K["subg"][:, l:l + 1])
                self.ld(S["oT"][h * 128:(h + 1) * 128, qc * 512:(qc + 1) * 512], ost[oi][:], [ostB[oi]],
                        [SB["oT"][qc]])
        self.phase_end()

    def phase_outproj(self, l, wo_src):
        I, K, KB, S, SB, ps, psB = self.I, self.K, self.KB, self.S, self.SB, self.ps, self.psB
        NC = self.NC
        self.phase_begin()
        w = self.sb([P, 8, 1024], BF16)
        wB = [self.B() for _ in range(2)]
        self.load_w_bf16(w, wB, wo_src.rearrange("(kc p) n -> p kc n", p=P), 1024)
        xt = [self.sb([P, 8, 512], F32) for _ in range(2)]
        xB = [self.B() for _ in range(2)]
        ot = [self.sb([P, 8, 512], BF16) for _ in range(2)]
        oB = [self.B() for _ in range(2)]
        xv = S["xT"].rearrange("(j p) t -> p j t", p=P)
        ov = S["oT"].rearrange("(j p) t -> p j t", p=P)
        ring = 0
        for c in range(NC):
            cs = slice(c * 512, (c + 1) * 512)
            x_, xb_ = xt[c % 2], xB[c % 2]
            o_, ob_ = ot[c % 2], oB[c % 2]
            self.ld(x_[:], xv[:, :, cs], [SB["xT"][c]], [xb_])
            self.ld(o_[:], ov[:, :, cs], [SB["oT"][c]], [ob_])
            for j in range(8):
                pi = ring % 2
                ring += 1
                for hc in range(8):
                    self.mm(ps[pi][:, :], w[:, hc, j * 128:(j + 1) * 128], o_[:, hc, :], hc == 0, hc == 7,
                            [wB[j // 4], ob_], [psB[pi]])
                self.stt(x_[:, j, :], ps[pi][:, :], K["mod"][:, l, 16 + j:17 + j], x_[:, j, :], ALU.mult, ALU.add,
                         [psB[pi], xb_, KB["mod"]], [xb_])
            self.ld(xv[:, :, cs], x_[:], [xb_], [SB["xT"][c]])
        self.phase_end()

    def phase_mlp(self, l):
        I, K, KB, S, SB, ps, psB = self.I, self.K, self.KB, self.S, self.SB, self.ps, self.psB
        T = self.T
        N = 512
        self.phase_begin()
        w1 = self.sb([P, 8, DFF], BF16)
        w1B = [self.B() for _ in range(8)]
        w2 = self.sb([P, 32, D], BF16)
        w2B = [self.B() for _ in range(8)]
        self.load_w_bf16(w1, w1B, I["mlp_w1"][l].rearrange("(kc p) n -> p kc n", p=P), DFF)
        v2 = I["mlp_w2"][l].rearrange("(f p) n -> p f n", p=P)
        for i in range(8):
            self.ld(w2[:, i * 4:(i + 1) * 4, :], v2[:, i * 4:(i + 1) * 4, :], [], [w2B[i]], q="pool")
        xt = self.sb([P, 8, N], F32)
        xB = self.B()
        sq, sqB, tr, trB = self.norm_rings(N)
        rstd = self.sb([P, N], F32)
        rB = self.B()
        h = self.sb([P, 8, N], BF16)
        hB = self.B()
        hid = self.sb([P, 32, N], BF16)
        hidB = [self.B() for _ in range(8)]
        r32 = [self.sb([P, N], F32) for _ in range(2)]
        r32B = [self.B() for _ in range(2)]
        xv = S["xT"].rearrange("(j p) t -> p j t", p=P)
        ring = 0
        for c in range(T // N):
            cs = slice(c * N, (c + 1) * N)
            sbx = SB["xT"][c]
            x_, xb_ = xt, xB
            self.ld(x_[:], xv[:, :, cs], [sbx], [xb_])
            self.norm_mod(x_, xb_, N, K["g2"][:, l, :], K["mod"][:, l, 24:32], KB["g2"], h, hB, sq, sqB, rstd, rB, 2,
                          tout=tr, tB=trB)
            for f in range(32):
                pi = ring % 2
                ring += 1
                for kc in range(8):
                    self.mm(ps[pi][:, 0:N], w1[:, kc, f * 128:(f + 1) * 128], h[:, kc, :], kc == 0, kc == 7,
                            [w1B[f // 4], hB], [psB[pi]])
                ri = f % 2
                self.act(r32[ri][:], ps[pi][:, 0:N], AF.Relu, [psB[pi]], [r32B[ri]])
                self.tt(hid[:, f, :], r32[ri][:], r32[ri][:], ALU.mult, [r32B[ri]], [hidB[f // 4]])
            for j in range(8):
                pi = 3 + (ring % 2)
                ring += 1
                for f in range(32):
                    self.mm(ps[pi][:, 0:N], w2[:, f, j * 128:(j + 1) * 128], hid[:, f, :], f == 0, f == 31,
                            [w2B[f // 4], hidB[f // 4]], [psB[pi]])
                self.stt(x_[:, j, :], ps[pi][:, 0:N], K["mod"][:, l, 40 + j:41 + j], x_[:, j, :], ALU.mult, ALU.add,
                         [psB[pi], xb_, KB["mod"]], [xb_])
            self.ld(xv[:, :, cs], x_[:], [xb_], [sbx])
        self.phase_end()

    def phase_final(self, outT):
        I, K, KB, S, SB, ps, psB = self.I, self.K, self.KB, self.S, self.SB, self.ps, self.psB
        self.phase_begin()
        xt = [self.sb([P, 8, 512], F32) for _ in range(2)]
        xB = [self.B() for _ in range(2)]
        yt = [self.sb([P, 8, 512], F32) for _ in range(2)]
        yB = [self.B() for _ in range(2)]
        sq, sqB, tr, trB = self.norm_rings(512)
        rstd = self.sb([P, 512], F32)
        rB = self.B()
        xv = S["xT"].rearrange("(j p) t -> p j t", p=P)
        ov = outT.rearrange("(j p) t -> p j t", p=P)
        for c in range(self.NC):
            cs = slice(c * 512, (c + 1) * 512)
            x_, xb_ = xt[c % 2], xB[c % 2]
            self.ld(x_[:], xv[:, :, cs], [SB["xT"][c]], [xb_])
            self.norm_mod(x_, xb_, 512, K["fn"], K["zero8"], KB["fn"], yt[c % 2], yB[c % 2], sq, sqB, rstd, rB, 2, tout=tr, tB=trB)
            self.ld(ov[:, :, cs], yt[c % 2][:], [yB[c % 2]], [self.B()])
        self.phase_end()

    def phase_kv(self):
        I, K, KB, S, SB, ps, psB = self.I, self.K, self.KB, self.S, self.SB, self.ps, self.psB
        NC = self.NC
        self.phase_begin()
        w = self.sb([P, 8, 1536], BF16)
        wB = [self.B() for _ in range(3)]
        self.load_w_bf16(w, wB, I["w_kv"].rearrange("(kc p) n -> p kc n", p=P), 1536)
        xt = [self.sb([P, 8, 512], F32) for _ in range(2)]
        xB = [self.B() for _ in range(2)]
        sq, sqB, tr, trB = self.norm_rings(512)
        rstd = self.sb([P, 512], F32)
        rB = self.B()
        h = [self.sb([P, 8, 512], BF16) for _ in range(2)]
        hB = [self.B() for _ in range(2)]
        kst = [self.sb([P, 12, 512], BF16) for _ in range(2)]
        kB = [self.B() for _ in range(2)]
        vst = [self.sb([P, 4, 2, 256], BF16) for _ in range(2)]
        vB = [self.B() for _ in range(2)]
        xv = S["xT"].rearrange("(j p) t -> p j t", p=P)
        ring = 0
        for c in range(NC):
            cs = slice(c * 512, (c + 1) * 512)
            x_, xb_ = xt[c % 2], xB[c % 2]
            self.ld(x_[:], xv[:, :, cs], [SB["xT"][c]], [xb_])
            h_, hb_ = h[c % 2], hB[c % 2]
            self.norm_mod(x_, xb_, 512, K["gkv"], K["kvmod"][:, 0:8], KB["gkv"], h_, hb_, sq, sqB, rstd, rB, 2, tout=tr, tB=trB)
            k_, kb_ = kst[c % 2], kB[c % 2]
            for m in range(12):
                pi = ring % 2
                ring += 1
                for kc in range(8):
                    self.mm(ps[pi][:, :], w[:, kc, m * 128:(m + 1) * 128], h_[:, kc, :], kc == 0, kc == 7,
                            [wB[m // 4], hb_], [psB[pi]])
                self.cp(k_[:, m, :], ps[pi][:, :], [psB[pi]], [kb_], eng=("dve" if m % 2 == 0 else "act_copy"))
            self.ld(S["kvT"].rearrange("(m p) t -> p m t", p=P)[:, :, cs], k_[:], [kb_], [SB["kvT"][c]])
            v_, vb_ = vst[c % 2], vB[c % 2]
            for tt in range(4):
                pi = ring % 2
                ring += 1
                for si, s0 in enumerate((768, 1280)):
                    for kc in range(8):
                        self.mm(ps[pi][:, si * 256:(si + 1) * 256], h_[:, kc, tt * 128:(tt + 1) * 128],
                                w[:, kc, s0:s0 + 256], kc == 0, kc == 7, [wB[s0 // 512], hb_], [psB[pi]])
                self.cp(v_[:, tt, :, :], ps[pi][:, :].rearrange("p (s e) -> p s e", s=2), [psB[pi]], [vb_])
            self.ld(S["vslc"].rearrange("(tt p) e -> p tt e", p=P)[:, c * 4:(c + 1) * 4, :], v_[:, :, 0, :], [vb_],
                    [SB["vslc"][c]])
            self.ld(S["vwin"].rearrange("(tt p) e -> p tt e", p=P)[:, c * 4:(c + 1) * 4, :], v_[:, :, 1, :], [vb_],
                    [SB["vwin"][c]])
        self.phase_end()

    def phase_cmp(self):
        I, K, KB, S, SB, ps, psB = self.I, self.K, self.KB, self.S, self.SB, self.ps, self.psB
        T = self.T
        ncmp = T // 16 - 1
        self.phase_begin()
        src = [self.sb([64, T], BF16) for _ in range(2)]
        srcB = [self.B() for _ in range(2)]
        w1r = self.sb([64, 32, 256], BF16)
        w2 = self.sb([P, 2, 64], BF16)
        posT = self.sb([64, 32], BF16)
        hidT = self.sb([P, 2, 256], BF16)
        hidB = self.B()
        pre = self.sb([P, 256], F32)
        u = self.sb([P, 256], F32)
        bias = self.sb([P, 2], F32)
        bB = {k: self.B() for k in ("pre", "u", "bias")}
        pg = self.pg
        pg.op("dve", lambda e: e.memset(hidT[:], 0.0), [], [hidB])
        it = 0
        wb = [self.B(), self.B(), self.B()]
        for s in range(2):
            self.ld(w1r[:], I["cmp_w1"][s].rearrange("(t d) h -> d t h", d=64), [], [wb[0]], q="pool")
            self.ld(w2[:], I["cmp_w2"][s].rearrange("(hc p) d -> p hc d", p=P), [], [wb[1]], q="pool")
            self.ld(posT[:], I["cmp_posT"][s], [], [wb[2]], q="pool")
            for hc in range(2):
                for t in range(32):
                    self.mm(ps[6][:, hc:hc + 1], w1r[:, t, hc * 128:(hc + 1) * 128], posT[:, t:t + 1], t == 0, t == 31,
                            [wb[0], wb[2]], [psB[6]])
            self.cp(bias[:], ps[6][:, 0:2], [psB[6]], [bB["bias"]])
            for g in range(4):
                sr, srb = src[it % 2], srcB[it % 2]
                it += 1
                r0 = s * 256 + g * 64
                self.ld(sr[:], S["kvT"][r0:r0 + 64, :], SB["kvT"], [srb])
                for hc in range(2):
                    for t in range(32):
                        self.mm(ps[hc][:, 0:ncmp], w1r[:, t, hc * 128:(hc + 1) * 128],
                                sr[:, t:t + 16 * (ncmp - 1) + 1:16], t == 0, t == 31, [wb[0], srb], [psB[hc]])
                    self.act(pre[:, 0:ncmp], ps[hc][:, 0:ncmp], AF.Identity, [psB[hc], bB["bias"]], [bB["pre"]],
                             bias=bias[:, hc:hc + 1])
                    self.tt(u[:, 0:ncmp], pre[:, 0:ncmp], pre[:, 0:ncmp], ALU.mult, [bB["pre"]], [bB["u"]])
                    self.ts(u[:, 0:ncmp], u[:, 0:ncmp], 0.044715, 1.0, ALU.mult, ALU.add, [bB["u"]], [bB["u"]])
                    self.tt(u[:, 0:ncmp], u[:, 0:ncmp], pre[:, 0:ncmp], ALU.mult, [bB["u"], bB["pre"]], [bB["u"]])
                    self.act(u[:, 0:ncmp], u[:, 0:ncmp], AF.Sigmoid, [bB["u"]], [bB["u"]],
                             scale=2.0 * math.sqrt(2.0 / math.pi))
                    self.tt(hidT[:, hc, 0:ncmp], u[:, 0:ncmp], pre[:, 0:ncmp], ALU.mult, [bB["u"], bB["pre"]],
                            [hidB])
                if s == 0:
                    for hc in range(2):
                        self.mm(ps[2][0:64, 0:ncmp], w2[:, hc, :], hidT[:, hc, 0:ncmp], hc == 0, hc == 1,
                                [wb[1], hidB], [psB[2]])
                    self.cp(K["kcmpT"][0:64, g, 0:ncmp], ps[2][0:64, 0:ncmp], [psB[2]], [KB["kcmpT"]])
                else:
                    for nt in range(2):
                        nn = min(128, ncmp - nt * 128)
                        if nn <= 0:
                            continue
                        for hc in range(2):
                            self.mm(ps[3][0:nn, nt * 64:(nt + 1) * 64], hidT[:, hc, nt * 128:nt * 128 + nn],
                                    w2[:, hc, :], hc == 0, hc == 1, [wb[1], hidB], [psB[3]])
                        self.cp(K["vcmp"][0:nn, g, nt, 0:64], ps[3][0:nn, nt * 64:(nt + 1) * 64], [psB[3]],
                                [KB["vcmp"]])
        self.phase_end()

    def phase_b_proj(self, l):
        I, K, KB, S, SB, ps, psB = self.I, self.K, self.KB, self.S, self.SB, self.ps, self.psB
        NC = self.NC
        self.phase_begin()
        w = self.sb([P, 8, 1072], BF16)
        wB = [self.B() for _ in range(3)]
        self.load_w_bf16(w, wB, I["b_w_in"][l - 2].rearrange("(kc p) n -> p kc n", p=P), 1072)
        xt = [self.sb([P, 8, 512], F32) for _ in range(2)]
        xB = [self.B() for _ in range(2)]
        sq, sqB, tr, trB = self.norm_rings(512)
        rstd = self.sb([P, 512], F32)
        rB = self.B()
        h = [self.sb([P, 8, 512], BF16) for _ in range(2)]
        hB = [self.B() for _ in range(2)]
        qst = [self.sb([P, 8, 512], BF16) for _ in range(2)]
        qB = [self.B() for _ in range(2)]
        gst = [self.sb([48, 512], F32) for _ in range(2)]
        gB = [self.B() for _ in range(2)]
        xv = S["xT"].rearrange("(j p) t -> p j t", p=P)
        ring = 0
        for c in range(NC):
            cs = slice(c * 512, (c + 1) * 512)
            x_, xb_ = xt[c % 2], xB[c % 2]
            self.ld(x_[:], xv[:, :, cs], [SB["xT"][c]], [xb_])
            h_, hb_ = h[c % 2], hB[c % 2]
            self.norm_mod(x_, xb_, 512, K["g1"][:, l, :], K["mod"][:, l, 0:8], KB["g1"], h_, hb_, sq, sqB, rstd, rB, 2, tout=tr, tB=trB)
            q_, qb_ = qst[c % 2], qB[c % 2]
            for m in range(8):
                pi = ring % 2
                ring += 1
                for kc in range(8):
                    self.mm(ps[pi][:, :], w[:, kc, m * 128:(m + 1) * 128], h_[:, kc, :], kc == 0, kc == 7,
                            [wB[m // 4], hb_], [psB[pi]])
                self.act(q_[:, m, :], ps[pi][:, :], AF.Copy, [psB[pi]], [qb_], scale=0.125)
            self.ld(S["qT"].rearrange("(m p) t -> p m t", p=P)[:, :, cs], q_[:], [qb_], [SB["qT"][c]])
            pi = ring % 2
            ring += 1
            for kc in range(8):
                self.mm(ps[pi][0:48, :], w[:, kc, 1024:1072], h_[:, kc, :], kc == 0, kc == 7, [wB[2], hb_], [psB[pi]])
            self.act(gst[c % 2][:], ps[pi][0:48, :], AF.Sigmoid, [psB[pi]], [gB[c % 2]])
            self.ld(S["gT"][:, cs], gst[c % 2][:], [gB[c % 2]], [SB["gT"][c]])
        self.phase_end()

    def phase_b_attn(self, l):
        I, K, KB, S, SB, ps, psB = self.I, self.K, self.KB, self.S, self.SB, self.ps, self.psB
        NC, NT, T = self.NC, self.NT, self.T
        pg = self.pg
        self.phase_begin()
        d0, d1, w4, dB = self.load_bias_tiles()
        ex = self.sb([P, NT, 128], BF16)
        exB = self.B()
        pg.op("pool", lambda e: e.memset(ex[64:128, :, :], 0.0), [], [exB])
        self.ld(ex[0:64, :, :], I["c_ex"][:, :, :], [exB], [exB], q="pool")
        ov = self.sb([P, 2, 64], F32)
        ovB = self.B()
        self.ld(ov[:], I["c_ov"][:, :, :], [], [ovB])
        maskc = self.sb([P, 2, T], BF16)
        mcB = self.B()
        self.ld(maskc[:], I["c_maskc"][:, :, :], [], [mcB], q="pool")
        keep = self.sb([P, NT, 64], BF16)
        addm = self.sb([P, NT, 64], BF16)
        kaB = [self.B(), self.B()]
        self.ld(keep[:], I["c_keep"][:, :, :], [], [kaB[0]], q="pool")
        self.ld(addm[:], I["c_add"][:, :, :], [], [kaB[1]], q="pool")
        ksl = self.sb([P, T], BF16)
        kwn = self.sb([P, T], BF16)
        vsl = self.sb([P, NT, 65], BF16)
        vwn = self.sb([P, NT, 65], BF16)
        gB_ = [self.B() for _ in range(4)]
        pg.op("pool", lambda e: e.memset(ksl[64:128, :], 0.0), [], [gB_[0]])
        pg.op("pool", lambda e: e.memset(kwn[64:128, :], 0.0), [], [gB_[1]])
        pg.op("pool", lambda e: e.memset(vsl[:], 1.0), [], [gB_[2]])
        pg.op("pool", lambda e: e.memset(vwn[:], 1.0), [], [gB_[3]])
        lfull = self.sb([P, 512], F32)
        lfB = self.B()
        pg.op("dve", lambda e: e.memset(lfull[:], 0.0), [], [lfB])
        qg = [self.sb([P, 4, 512], BF16) for _ in range(2)]
        qgB = [self.B() for _ in range(2)]
        for i in range(2):
            pg.op("pool", lambda e, i=i: e.memset(qg[i][64:128, :, :], 0.0), [], [qgB[i]])
        gb = self.sb([64, 12, 512], F32)
        gbB = self.B()
        pc32 = [self.sb([P, 512], F32) for _ in range(2)]
        pn32 = [self.sb([P, 512], F32) for _ in range(2)]
        pn16 = [self.sb([P, 512], BF16) for _ in range(2)]
        pcB = [self.B() for _ in range(2)]
        pnB = [self.B() for _ in range(2)]
        pn16B = [self.B() for _ in range(2)]
        rl = self.sb([P, 512], F32)
        rlB = self.B()
        oc = self.sb([64, 4, 512], F32)
        ocB = [self.B() for _ in range(4)]
        impv = self.sb([P, 64], F32)
        impv2 = self.sb([P, 64], F32)
        m8a = self.sb([P, 8], F32)
        m8b = self.sb([P, 8], F32)
        msel = self.sb([P, 4, 128], F32)
        tkB = {k: self.B() for k in ("impv", "impv2", "m8a", "m8b", "msel")}
        pg.op("dve", lambda e: e.memset(msel[:], 0.0), [], [tkB["msel"]])
        mT = self.sb([P, 512], BF16)
        mTB = self.B()
        NPT = 6
        SR = [0, 1, 4, 5, 6]
        Pt = [self.sb([P, 512], BF16) for _ in range(NPT)]
        PB = [self.B() for _ in range(NPT)]
        rr = self.sb([64, 512], F32)
        rrB = self.B()
        acc = self.sb([64, 512], F32)
        accB = self.B()
        tmp = self.sb([64, 512], F32)
        tmpB = self.B()
        ost = [self.sb([64, 4, 512], BF16) for _ in range(2)]
        ostB = [self.B() for _ in range(2)]
        ctr = {"s": 0, "p": 0}
        it = 0
        for g in range(4):
            r0 = g * 64
            self.ld(ksl[0:64, :], S["kvT"][512 + r0:512 + r0 + 64, :], SB["kvT"], [gB_[0]])
            self.ld(kwn[0:64, :], S["kvT"][1024 + r0:1024 + r0 + 64, :], SB["kvT"], [gB_[1]])
            self.ld(vsl[:, :, 0:64], S["vslc"].rearrange("(kt p) e -> p kt e", p=P)[:, :, r0:r0 + 64], SB["vslc"],
                    [gB_[2]])
            self.ld(vwn[:, :, 0:64], S["vwin"].rearrange("(kt p) e -> p kt e", p=P)[:, :, r0:r0 + 64], SB["vwin"],
                    [gB_[3]])
            for qc in range(NC):
                cs = slice(qc * 512, (qc + 1) * 512)
                q_, qb_ = qg[it % 2], qgB[it % 2]
                o_st, o_stB = ost[it % 2], ostB[it % 2]
                it += 1
                self.ld(q_[0:64, :, :], S["qT"].rearrange("(h d) t -> d h t", d=64)[:, g * 4:(g + 1) * 4, cs],
                        [SB["qT"][qc]], [qb_])
                self.ld(gb[:], bass.AP(S["gT"].tensor, g * 12 * T + qc * 512, [[0, 64], [T, 12], [1, 512]]),
                        [SB["gT"][qc]], [gbB])
                for r in range(4):
                    for nt in range(2):
                        si = ctr["s"] % 2
                        ctr["s"] += 1
                        self.mm(ps[si][:, :], K["kcmpT"][:, g, nt * 128:(nt + 1) * 128], q_[:, r, :], True, True,
                                [KB["kcmpT"], qb_], [psB[si]])
                        self.act(pc32[nt][:], ps[si][:, :], AF.Exp, [psB[si]], [pcB[nt]])
                        self.tt(pc32[nt][:], pc32[nt][:], maskc[:, nt, cs], ALU.mult, [pcB[nt], mcB], [pcB[nt]])
                    for nt in range(2):
                        self.mm(ps[7][:, :], K["ones32"][:, :], pc32[nt][:], nt == 0, nt == 1,
                                [KB["ones32"], pcB[nt]], [psB[7]])
                    self.ts(rl[:], ps[7][:, :], 1e-18, None, ALU.max, None, [psB[7]], [rlB])
                    self.act(rl[:], rl[:], AF.Ln, [rlB], [rlB])
                    self.act(rl[:], rl[:], AF.Exp, [rlB], [rlB], scale=-1.0)
                    for nt in range(2):
                        self.tt(pn32[nt][:], pc32[nt][:], rl[:], ALU.mult, [pcB[nt], rlB], [pnB[nt]])
                        self.cp(pn16[nt][:], pn32[nt][:], [pnB[nt]], [pn16B[nt]], eng="pool")
                    for nt in range(2):
                        self.mm(ps[2][:, :], K["vcmp"][:, g, nt, :], pn16[nt][:], nt == 0, nt == 1,
                                [KB["vcmp"], pn16B[nt]], [psB[2]])
                    self.cp(oc[:, r, :], ps[2][0:64, :], [psB[2]], [ocB[r]])
                    for nt in range(2):
                        for i in range(4):
                            first = (r == 0 and nt == 0 and i == 0)
                            last = (r == 3 and nt == 1 and i == 3)
                            self.mm(ps[3][:, i * 64:(i + 1) * 64], pn32[nt][:, i * 128:(i + 1) * 128], ov[:, nt, :],
                                    first, last, [pnB[nt], ovB], [psB[3]])
                for i in range(4):
                    qb = qc * 4 + i
                    self.tt(impv[:], ps[3][:, i * 64:(i + 1) * 64], keep[:, qb, :], ALU.mult, [psB[3], kaB[0]],
                            [tkB["impv"]])
                    self.tt(impv[:], impv[:], addm[:, qb, :], ALU.add, [tkB["impv"], kaB[1]], [tkB["impv"]])
                    pg.op("dve", lambda e: e.max(out=m8a[:], in_=impv[:]), [tkB["impv"]], [tkB["m8a"]])
                    pg.op("dve", lambda e: e.match_replace(out=impv2[:], in_to_replace=m8a[:], in_values=impv[:],
                                                           imm_value=-3.0e38),
                          [tkB["impv"], tkB["m8a"]], [tkB["impv2"]])
                    pg.op("dve", lambda e: e.max(out=m8b[:], in_=impv2[:]), [tkB["impv2"]], [tkB["m8b"]])
                    self.ts(msel[:, i, 0:64], impv[:], m8b[:, 7:8], None, ALU.is_ge, None,
                            [tkB["impv"], tkB["m8b"]], [tkB["msel"]])
                for i in range(4):
                    pg.op("pe", lambda e, i=i: e.transpose(out=ps[7][:, i * 128:(i + 1) * 128], in_=msel[:, i, :],
                                                           identity=K["ident"][:, :]),
                          [tkB["msel"], KB["ident"]], [psB[7]])
                self.ts(mT[:], ps[7][:, 0:512], -1.0, 30000.0, ALU.add, ALU.mult, [psB[7]], [mTB])
                for r in range(4):
                    h = g * 4 + r

                    def run_branch(kT_, kB_, vT_, vB_, tiles, sel, r=r, h=h, q_=q_, qb_=qb_, qc=qc):
                        st = {}
                        n = len(tiles)

                        def rng(kt):
                            c0 = max(0, kt - 4 * qc) * 128
                            c1 = 512 if sel else min(4, kt + 5 - 4 * qc) * 128
                            return c0, c1

                        def stageA(kt, i):
                            si = SR[ctr["s"] % len(SR)]
                            ctr["s"] += 1
                            st[kt] = si
                            c0, c1 = rng(kt)
                            extra = []
                            if sel:
                                extra.append((c0, c1, ex[:, kt, :], mT[:, c0:c1], [exB, mTB]))
                            for ii in range(c0 // 128, c1 // 128):
                                delta = 4 * qc + ii - kt
                                dd = None
                                if delta == 0:
                                    dd = d0[:, h, :]
                                elif delta == 1:
                                    dd = d1[:, h, :]
                                elif delta == 4 and not sel:
                                    dd = w4[:, :]
                                if dd is not None:
                                    extra.append((ii * 128, (ii + 1) * 128, K["ident"][:, :], dd, [KB["ident"]] + dB))
                            self.mm(ps[si][:, c0:c1], kT_[:, kt * 128:(kt + 1) * 128], q_[:, r, c0:c1], True,
                                    len(extra) == 0, [kB_, qb_], [psB[si]])
                            for xi, (a0, a1, lh, rh, rd) in enumerate(extra):
                                self.mm(ps[si][:, a0:a1], lh, rh, False, xi == len(extra) - 1, rd, [psB[si]])

                        def stageB(kt, i):
                            si = st[kt]
                            c0, c1 = rng(kt)
                            pi = ctr["p"] % NPT
                            ctr["p"] += 1
                            self.act(Pt[pi][:, c0:c1], ps[si][:, c0:c1], AF.Exp, [psB[si]], [PB[pi]])
                            self.mm(ps[2][0:65, c0:c1], vT_[:, kt, :], Pt[pi][:, c0:c1], i == 0, i == n - 1,
                                    [vB_, PB[pi]], [psB[2]])

                        self.attn_tiles(tiles, stageA, stageB, depth=4)
                        self.act(lfull[64:65, :], ps[2][64:65, :], AF.Ln, [psB[2]], [lfB])
                        self.act(lfull[64:65, :], lfull[64:65, :], AF.Exp, [lfB], [lfB], scale=-1.0)
                        self.mm(ps[3][:, :], K["sel64"][:, :], lfull[:, :], True, True, [KB["sel64"], lfB], [psB[3]])
                        self.cp(rr[:], ps[3][0:64, :], [psB[3]], [rrB])
                        self.tt(tmp[:], ps[2][0:64, :], rr[:], ALU.mult, [psB[2], rrB], [tmpB])

                    self.tt(acc[:], oc[:, r, :], gb[:, r * 3 + 0, :], ALU.mult, [ocB[r], gbB], [accB], eng="pool")
                    run_branch(ksl, gB_[0], vsl, gB_[2], list(range(0, 4 * qc + 4)), True)
                    self.tt(tmp[:], tmp[:], gb[:, r * 3 + 1, :], ALU.mult, [tmpB, gbB], [tmpB])
                    self.tt(acc[:], acc[:], tmp[:], ALU.add, [accB, tmpB], [accB])
                    run_branch(kwn, gB_[1], vwn, gB_[3], list(range(max(0, 4 * qc - 4), 4 * qc + 4)), False)
                    self.tt(tmp[:], tmp[:], gb[:, r * 3 + 2, :], ALU.mult, [tmpB, gbB], [tmpB])
                    self.tt(o_st[:, r, :], acc[:], tmp[:], ALU.add, [accB, tmpB], [o_stB])
                self.ld(S["oT"].rearrange("(h d) t -> d h t", d=64)[:, g * 4:(g + 1) * 4, cs], o_st[:], [o_stB],
                        [SB["oT"][qc]])
        self.phase_end()


_orig_op = Prog.op


def _op(self, eng, fn, reads=(), writes=()):
    if eng == "act_copy":
        return _orig_op(self, "act", fn, reads, writes)
    return _orig_op(self, eng, fn, reads, writes)


Prog.op = _op
_orig_cp = Builder.cp


def _cp(self, out, in_, reads, writes, eng="dve"):
    if eng == "act_copy":
        self.pg.op("act", lambda e: e.activation(out=out, in_=in_, func=AF.Copy), reads, writes)
    else:
        _orig_cp(self, out, in_, reads, writes, eng)


Builder.cp = _cp


def col8(v):
    v = np.asarray(v, np.float32)
    return np.ascontiguousarray(np.moveaxis(v.reshape(v.shape[:-1] + (v.shape[-1] // 128, 128)), -1, 0))


def make_in_maps(inputs, T):
    x = np.asarray(inputs["x"], np.float32)
    B = x.shape[0]
    shared = {}
    f = lambda k: np.ascontiguousarray(np.asarray(inputs[k], np.float32))
    shared["rel_bias"] = f("rel_bias")
    shared["ada_w"] = f("ada_w")
    shared["adab"] = np.ascontiguousarray(f("ada_b").reshape(NL, 48, 128).transpose(0, 2, 1))
    shared["an"] = col8(f("attn_norm"))
    shared["mn"] = col8(f("mlp_norm"))
    shared["mlp_w1"] = f("mlp_w1")
    shared["mlp_w2"] = f("mlp_w2")
    shared["a_w_in"] = f("a_w_in")
    shared["a_w_out"] = f("a_w_out")
    shared["a_lambda"] = f("a_lambda").reshape(2, 256)
    shared["a_subln"] = np.ascontiguousarray(f("a_subln").T)
    shared["kv_ada_w"] = f("kv_ada_w")
    shared["kvadab"] = np.ascontiguousarray(f("kv_ada_b").reshape(16, 128).T)
    shared["kvn"] = col8(f("kv_norm"))
    shared["w_kv"] = f("w_kv")
    shared["cmp_posT"] = np.ascontiguousarray(f("cmp_pos").transpose(0, 2, 1))
    shared["cmp_w1"] = f("cmp_w1")
    shared["cmp_w2"] = f("cmp_w2")
    shared["b_w_in"] = f("b_w_in")
    shared["b_w_out"] = f("b_w_out")
    shared["fnorm"] = col8(f("final_norm"))
    shared.update(make_consts(T))
    maps = []
    c = np.asarray(inputs["c"], np.float32)
    for b in range(B):
        m = dict(shared)
        m["xT"] = np.ascontiguousarray(x[b].T)
        m["cT"] = np.ascontiguousarray(c[b].reshape(8, 128).T)
        maps.append(m)
    return maps


_CACHE = {}


def run(inputs, T, layers=NL, debug=None):
    key = (T, layers, tuple(debug) if debug else None)
    if key not in _CACHE:
        _CACHE[key] = Builder(T, layers, debug).build()
    nc = _CACHE[key]
    maps = make_in_maps(inputs, T)
    res = run_bass_kernel_spmd(nc, maps, core_ids=list(range(len(maps))))
    return res.results


def kernel(**inputs):
    T = int(np.asarray(inputs["x"]).shape[1])
    results = run(inputs, T)
    out = np.stack([np.ascontiguousarray(r["outT"].T) for r in results], axis=0)
    return out.astype(np.float32)
```

```python
import math
from contextlib import ExitStack

import numpy as np
import ml_dtypes

import concourse.bass as bass
import concourse.mybir as mybir
from concourse.bass_utils import run_bass_kernel_spmd

F32 = mybir.dt.float32
BF16 = mybir.dt.bfloat16
AF = mybir.ActivationFunctionType
ALU = mybir.AluOpType
AX = mybir.AxisListType

D = 1024
DFF = 4096
NL = 4
EPS = 1e-6
NEG = -1e30
P = 128


class Buf:
    __slots__ = ("name", "w", "r")

    def __init__(self, name):
        self.name = name
        self.w = []
        self.r = []


class Op:
    __slots__ = ("eng", "fn", "deps", "dma", "slot", "target", "val", "waits")

    def __init__(self, eng, fn, deps, dma):
        self.eng = eng
        self.fn = fn
        self.deps = deps
        self.dma = dma
        self.slot = None
        self.target = False
        self.val = 0
        self.waits = []


class Prog:
    ENGS = ("pe", "act", "dve", "pool", "sp")
    NSLOT = {"sp": 24, "pool": 8, "act": 4}

    def __init__(self, nc):
        self.nc = nc
        self.ops = []
        self.rr = {q: 0 for q in self.NSLOT}
        self.slot_last = {}
        self.last_on_eng = {}

    def _reduce(self, ids):
        best = {}
        out = set()
        for i in ids:
            o = self.ops[i]
            if o.dma:
                out.add(i)
            else:
                if o.eng not in best or best[o.eng] < i:
                    best[o.eng] = i
        out.update(best.values())
        return out

    def _mk(self, eng, fn, reads, writes, dma):
        deps = set()
        for b in reads:
            deps.update(b.w)
        for b in writes:
            deps.update(b.w)
            deps.update(b.r)
        gid = len(self.ops)
        op = Op(eng, fn, self._reduce(deps), dma)
        if dma:
            s = self.rr[eng]
            self.rr[eng] = (s + 1) % self.NSLOT[eng]
            op.slot = (eng, s)
            prev = self.slot_last.get(op.slot)
            if prev is not None:
                op.deps.add(prev)
            self.slot_last[op.slot] = gid
        self.ops.append(op)
        wset = set(id(b) for b in writes)
        for b in writes:
            b.w = [gid]
            b.r = []
        for b in reads:
            if id(b) not in wset:
                b.r.append(gid)
                if len(b.r) > 12:
                    b.r = list(self._reduce(b.r))
        self.last_on_eng[eng if not dma else ("dma", gid)] = gid
        return gid

    def op(self, eng, fn, reads=(), writes=()):
        return self._mk(eng, fn, reads, writes, False)

    def dma(self, q, fn, reads=(), writes=()):
        return self._mk(q, fn, reads, writes, True)

    def barrier(self):
        allb = Buf("barrier")
        ids = [i for i, o in enumerate(self.ops)]
        last = {}
        dmas = []
        for i in range(len(self.ops) - 1, -1, -1):
            o = self.ops[i]
            if o.dma:
                if o.slot not in last:
                    last[o.slot] = i
                    dmas.append(i)
            elif o.eng not in last:
                last[o.eng] = i
                dmas.append(i)
        allb.w = dmas
        nc = self.nc
        z = self._bar_tiles
        self.op("pe", lambda e: e.matmul(z["ps"][0:1, 0:2], z["bf"][0:1, 0:1], z["bf"][0:1, 0:2],
                                          start=True, stop=True), reads=[allb], writes=[z["b_pe"]])
        self.op("act", lambda e: e.activation(out=z["a"][0:1, 0:1], in_=z["src"][0:1, 0:1], func=AF.Copy),
                reads=[allb], writes=[z["b_act"]])
        self.op("dve", lambda e: e.tensor_copy(out=z["v"][0:1, 0:1], in_=z["src"][0:1, 0:1]),
                reads=[allb], writes=[z["b_dve"]])
        self.op("pool", lambda e: e.tensor_copy(out=z["g"][0:1, 0:1], in_=z["src"][0:1, 0:1]),
                reads=[allb], writes=[z["b_pool"]])
        self.dma("sp", lambda e: e.dma_start(out=z["s"][0:1, 0:1], in_=z["src"][0:1, 0:1]),
                 reads=[allb], writes=[z["b_sp"]])

    def emit(self, stack):
        nc = self.nc
        ops = self.ops
        comp = ("pe", "act", "dve", "pool")
        sems = {e: stack.enter_context(nc.semaphore("sem_" + e)) for e in comp}
        dsem = {}
        for q, n in self.NSLOT.items():
            for s in range(n):
                dsem[(q, s)] = stack.enter_context(nc.semaphore("dsem_%s_%d" % (q, s)))
        for o in ops:
            for d in o.deps:
                t = ops[d]
                if t.dma:
                    continue
                if o.eng == "pe" and t.eng == "pe" and not o.dma:
                    continue
                t.target = True
        cnt = {e: 0 for e in comp}
        dcnt = {k: 0 for k in dsem}
        for o in ops:
            if o.dma:
                dcnt[o.slot] += 16
                o.val = dcnt[o.slot]
            else:
                if o.target:
                    cnt[o.eng] += 1
                o.val = cnt[o.eng]
        known = {e: {} for e in self.ENGS}
        clocks = {}
        for gid, o in enumerate(ops):
            kn = known[o.eng]
            m = {}
            for d in sorted(o.deps):
                t = ops[d]
                if t.dma:
                    key = ("d", t.slot)
                    sem = dsem[t.slot]
                else:
                    if o.eng == "pe" and t.eng == "pe" and not o.dma:
                        continue
                    key = ("e", t.eng)
                    sem = sems[t.eng]
                if kn.get(key, 0) >= t.val:
                    continue
                if key not in m or m[key][1] < t.val:
                    m[key] = (sem, t.val, d)
            for key, (sem, v, d) in sorted(m.items(), key=lambda kv: -kv[1][2]):
                if kn.get(key, 0) >= v:
                    continue
                o.waits.append((key, sem, v))
                for k2, v2 in clocks[d].items():
                    if kn.get(k2, 0) < v2:
                        kn[k2] = v2
            if o.dma or o.target:
                c = dict(kn)
                if o.dma:
                    c[("d", o.slot)] = o.val
                else:
                    k = ("e", o.eng)
                    if c.get(k, 0) < o.val:
                        c[k] = o.val
                clocks[gid] = c
        streams = {e: [] for e in self.ENGS}
        for o in ops:
            streams[o.eng].append(o)
        final = [(dsem[k], v) for k, v in dcnt.items() if v > 0]

        def run(eng_name, e):
            for o in streams[eng_name]:
                for _, sem, v in o.waits:
                    e.wait_ge(sem, v)
                ins = o.fn(e)
                if o.dma:
                    ins.then_inc(dsem[o.slot], 16)
                elif o.target:
                    ins.then_inc(sems[o.eng], 1)
            if eng_name == "sp":
                for sem, v in final:
                    e.wait_ge(sem, v)

        with nc.Block() as block:
            @block.tensor
            def _(e):
                run("pe", e)

            @block.scalar
            def _(e):
                run("act", e)

            @block.vector
            def _(e):
                run("dve", e)

            @block.gpsimd
            def _(e):
                run("pool", e)

            @block.sync
            def _(e):
                run("sp", e)


def _t5_bucket_np(dist):
    n = np.maximum(dist, 0)
    nf = np.maximum(n, 1).astype(np.float32)
    large = 16 + (np.log(nf / np.float32(16)) / np.float32(math.log(8.0)) * np.float32(16)).astype(np.int32)
    large = np.minimum(large, 31)
    return np.where(n < 16, n, large)


def make_consts(T):
    NT = T // 128
    c = {}
    oh = np.zeros((33, 384), np.float32)
    for i in range(384):
        dist = i - 127
        if dist < 0:
            oh[32, i] = 1.0
        else:
            oh[int(_t5_bucket_np(np.array(dist))), i] += 1.0
            oh[31, i] -= 1.0
    c["c_oh"] = oh
    ki = np.arange(128)[:, None]
    qi = np.arange(128)[None, :]
    c["c_w4"] = np.where(ki > qi, 0.0, NEG).astype(np.float32)
    c["c_ident"] = np.eye(128, dtype=np.float32)
    ex = np.zeros((64, NT, 128), np.float32)
    for kt in range(NT):
        for k in range(128):
            ex[2 * kt + k // 64, kt, k] = 1.0
    c["c_ex"] = ex
    n_cmp = (T - 32) // 16 + 1
    n_slc = T // 64
    n = np.arange(256)
    cs = n * 16
    ce = cs + 31
    ss = np.arange(n_slc) * 64
    ov = ((cs[:, None] < ss[None, :] + 64) & (ce[:, None] >= ss[None, :]) & (n[:, None] < n_cmp)).astype(np.float32)
    ovp = np.zeros((256, 64), np.float32)
    ovp[:, :n_slc] = ov
    c["c_ov"] = np.ascontiguousarray(ovp.reshape(2, 128, 64).transpose(1, 0, 2))
    q = np.arange(T)
    mc = ((ce[:, None] <= q[None, :]) & (n[:, None] < n_cmp)).astype(np.float32)
    c["c_maskc"] = np.ascontiguousarray(mc.reshape(2, 128, T).transpose(1, 0, 2))
    j = np.arange(64)[None, :]
    qb = (q // 64)[:, None]
    forced = (j == 0) | ((j <= qb) & (j > qb - 2))
    valid = (j <= qb) & (j < n_slc)
    keep = (~forced & valid).astype(np.float32)
    add = np.where(valid, np.where(forced, 1e4, 0.0), NEG).astype(np.float32)
    c["c_keep"] = np.ascontiguousarray(keep.reshape(NT, 128, 64).transpose(1, 0, 2))
    c["c_add"] = np.ascontiguousarray(add.reshape(NT, 128, 64).transpose(1, 0, 2))
    return c


class Builder:
    def __init__(self, T, layers=NL, debug=False):
        self.T = T
        self.NT = T // 128
        self.NC = T // 512
        self.layers = layers
        self.debug = debug
        self.nc = bass.Bass("TRN2", target_bir_lowering=False)
        self.pg = Prog(self.nc)
        self.gstack = ExitStack()
        self.pstack = None
        self.uid = 0

    def dram_in(self, name, shape, dt=F32):
        return self.nc.dram_tensor(name, list(shape), dt, kind="ExternalInput").ap()

    def dram_out(self, name, shape, dt=F32):
        return self.nc.dram_tensor(name, list(shape), dt, kind="ExternalOutput").ap()

    def dram(self, name, shape, dt):
        return self.nc.dram_tensor(name, list(shape), dt).ap()

    def sb(self, shape, dt, persistent=False, name=None):
        self.uid += 1
        st = self.gstack if persistent else self.pstack
        return st.enter_context(self.nc.sbuf_tensor("%s_%d" % (name or "t", self.uid), list(shape), dt))

    def B(self, name="b"):
        self.uid += 1
        return Buf("%s%d" % (name, self.uid))

    def phase_begin(self):
        self.pstack = ExitStack()

    def phase_end(self):
        self.pg.barrier()
        self.pstack.close()
        self.pstack = None

    def mm(self, out, lhsT, rhs, start, stop, reads, writes):
        self.pg.op("pe", lambda e: e.matmul(out, lhsT, rhs, start=start, stop=stop), reads, writes)

    def act(self, out, in_, func, reads, writes, bias=None, scale=None):
        kw = {}
        if bias is not None:
            kw["bias"] = bias
        if scale is not None:
            kw["scale"] = scale
        self.pg.op("act", lambda e: e.activation(out=out, in_=in_, func=func, **kw), reads, writes)

    def tt(self, out, in0, in1, op, reads, writes, eng="dve"):
        self.pg.op(eng, lambda e: e.tensor_tensor(out=out, in0=in0, in1=in1, op=op), reads, writes)

    def ts(self, out, in0, s1, s2, op0, op1, reads, writes, eng="dve"):
        if op1 is None:
            self.pg.op(eng, lambda e: e.tensor_scalar(out=out, in0=in0, scalar1=s1, scalar2=None, op0=op0),
                       reads, writes)
        else:
            self.pg.op(eng, lambda e: e.tensor_scalar(out=out, in0=in0, scalar1=s1, scalar2=s2, op0=op0, op1=op1),
                       reads, writes)

    def stt(self, out, in0, scalar, in1, op0, op1, reads, writes):
        self.pg.op("dve", lambda e: e.scalar_tensor_tensor(out=out, in0=in0, scalar=scalar, in1=in1,
                                                          op0=op0, op1=op1), reads, writes)

    def cp(self, out, in_, reads, writes, eng="dve"):
        self.pg.op(eng, lambda e: e.tensor_copy(out=out, in_=in_), reads, writes)

    def ld(self, out, in_, reads, writes, q="sp"):
        self.pg.dma(q, lambda e: e.dma_start(out=out, in_=in_), reads, writes)

    def build(self):
        nc, pg, T, NT, NC = self.nc, self.pg, self.T, self.NT, self.NC
        g = self.gstack
        I = {}
        I["xT"] = self.dram_in("xT", [D, T])
        I["cT"] = self.dram_in("cT", [P, 8])
        I["rel_bias"] = self.dram_in("rel_bias", [32, 16])
        I["ada_w"] = self.dram_in("ada_w", [NL, D, 6 * D])
        I["adab"] = self.dram_in("adab", [NL, P, 48])
        I["an"] = self.dram_in("an", [P, NL, 8])
        I["mn"] = self.dram_in("mn", [P, NL, 8])
        I["mlp_w1"] = self.dram_in("mlp_w1", [NL, D, DFF])
        I["mlp_w2"] = self.dram_in("mlp_w2", [NL, DFF, D])
        I["a_w_in"] = self.dram_in("a_w_in", [2, D, 3 * D])
        I["a_w_out"] = self.dram_in("a_w_out", [2, D, D])
        I["a_lambda"] = self.dram_in("a_lambda", [2, 256])
        I["a_subln"] = self.dram_in("a_subln", [P, 2])
        I["kv_ada_w"] = self.dram_in("kv_ada_w", [D, 2 * D])
        I["kvadab"] = self.dram_in("kvadab", [P, 16])
        I["kvn"] = self.dram_in("kvn", [P, 8])
        I["w_kv"] = self.dram_in("w_kv", [D, 1536])
        I["cmp_posT"] = self.dram_in("cmp_posT", [2, 64, 32])
        I["cmp_w1"] = self.dram_in("cmp_w1", [2, 2048, 256])
        I["cmp_w2"] = self.dram_in("cmp_w2", [2, 256, 64])
        I["b_w_in"] = self.dram_in("b_w_in", [2, D, 1072])
        I["b_w_out"] = self.dram_in("b_w_out", [2, D, D])
        I["fnorm"] = self.dram_in("fnorm", [P, 8])
        I["c_oh"] = self.dram_in("c_oh", [33, 384])
        I["c_w4"] = self.dram_in("c_w4", [P, 128])
        I["c_ident"] = self.dram_in("c_ident", [P, 128])
        I["c_ex"] = self.dram_in("c_ex", [64, NT, 128])
        I["c_ov"] = self.dram_in("c_ov", [P, 2, 64])
        I["c_maskc"] = self.dram_in("c_maskc", [P, 2, T])
        I["c_keep"] = self.dram_in("c_keep", [P, NT, 64])
        I["c_add"] = self.dram_in("c_add", [P, NT, 64])
        self.I = I
        outT = self.dram_out("outT", [D, T])
        S = {}
        S["xT"] = self.dram("s_xT", [D, T], F32)
        S["qT"] = self.dram("s_qT", [D, T], BF16)
        S["kT"] = self.dram("s_kT", [D, T], BF16)
        S["vtok"] = self.dram("s_vtok", [T, D], BF16)
        S["oT"] = self.dram("s_oT", [D, T], BF16)
        S["kvT"] = self.dram("s_kvT", [1536, T], BF16)
        S["vslc"] = self.dram("s_vslc", [T, 256], BF16)
        S["vwin"] = self.dram("s_vwin", [T, 256], BF16)
        S["gT"] = self.dram("s_gT", [48, T], F32)
        S["tT"] = self.dram("s_tT", [16, 384], F32)
        S["d0"] = self.dram("s_d0", [P, 16, 128], F32)
        S["d1"] = self.dram("s_d1", [P, 16, 128], F32)
        self.S = S
        SB = {k: [self.B(k) for _ in range(NC)] for k in ("xT", "qT", "kT", "vtok", "oT", "kvT", "vslc", "vwin", "gT")}
        for k in ("tT", "d0", "d1"):
            SB[k] = [self.B(k)]
        self.SB = SB
        PS = g.enter_context(nc.psum_tensor("psall", [P, 4096], F32))
        self.PS = PS
        ps = [PS[:, i * 512:(i + 1) * 512] for i in range(8)]
        self.ps = ps
        self.psB = [self.B("ps") for _ in range(8)]
        K = {}
        K["ones_bf"] = self.sb([P, 128], BF16, True, "ones")
        K["onesD"] = self.sb([P, 128], BF16, True, "onesD")
        K["onesH"] = self.sb([P, 128], BF16, True, "onesH")
        K["ones32"] = self.sb([P, 128], F32, True, "ones32")
        K["ident"] = self.sb([P, 128], F32, True, "ident")
        K["cact"] = self.sb([P, 8], F32, True, "cact")
        K["mod"] = self.sb([P, NL, 48], F32, True, "mod")
        K["kvmod"] = self.sb([P, 16], F32, True, "kvmod")
        K["g1"] = self.sb([P, NL, 8], F32, True, "g1")
        K["g2"] = self.sb([P, NL, 8], F32, True, "g2")
        K["gkv"] = self.sb([P, 8], F32, True, "gkv")
        K["fn"] = self.sb([P, 8], F32, True, "fn")
        K["zero8"] = self.sb([P, 8], F32, True, "zero8")
        K["b31"] = self.sb([P, 16], F32, True, "b31")
        K["lamneg"] = self.sb([P, 2], F32, True, "lamneg")
        K["subg"] = self.sb([P, 2], F32, True, "subg")
        K["kcmpT"] = self.sb([P, 4, 256], BF16, True, "kcmpT")
        K["vcmp"] = self.sb([P, 4, 2, 128], BF16, True, "vcmp")
        K["sel64"] = self.sb([P, 128], F32, True, "sel64")
        K["bar"] = self.sb([P, 16], F32, True, "bar")
        K["barbf"] = self.sb([P, 4], BF16, True, "barbf")
        self.K = K
        KB = {k: self.B(k) for k in K}
        self.KB = KB
        pg._bar_tiles = dict(ps=ps[6], bf=K["barbf"], src=K["bar"][:, 0:1], a=K["bar"][:, 1:2], v=K["bar"][:, 2:3],
                             g=K["bar"][:, 3:4], s=K["bar"][:, 4:5], b_pe=self.psB[6], b_act=self.B(), b_dve=self.B(),
                             b_pool=self.B(), b_sp=self.B())

        self.phase_setup()
        for l in range(self.layers):
            if l < 2:
                self.phase_a_proj(l)
                self.phase_a_attn(l)
                wo = I["a_w_out"][l]
            else:
                self.phase_b_proj(l)
                self.phase_b_attn(l)
                wo = I["b_w_out"][l - 2]
            self.phase_outproj(l, wo)
            self.phase_mlp(l)
            if l == 1:
                self.phase_kv()
                self.phase_cmp()
        self.phase_final(outT)
        if self.debug:
            dbg = {}
            for k in self.debug:
                t = S[k]
                o = self.dram_out("dbg_" + k, list(t.shape), t.dtype)
                self.ld(o, t, reads=SB[k], writes=[self.B()])
        pg.emit(g)
        return nc

    def phase_setup(self):
        nc, pg, I, K, KB, S, SB = self.nc, self.pg, self.I, self.K, self.KB, self.S, self.SB
        ps, psB = self.ps, self.psB
        self.phase_begin()
        pg.op("dve", lambda e: e.memset(K["bar"][:], 0.0), [], [KB["bar"]])
        pg.op("dve", lambda e: e.memset(K["barbf"][:], 0.0), [], [KB["barbf"]])
        pg.op("dve", lambda e: e.memset(K["ones_bf"][:], 1.0), [], [KB["ones_bf"]])
        pg.op("dve", lambda e: e.memset(K["onesD"][:], 1.0 / 1024), [], [KB["onesD"]])
        pg.op("dve", lambda e: e.memset(K["onesH"][:], 1.0 / 128), [], [KB["onesH"]])
        pg.op("dve", lambda e: e.memset(K["ones32"][:], 1.0), [], [KB["ones32"]])
        pg.op("dve", lambda e: e.memset(K["zero8"][:], 0.0), [], [KB["zero8"]])
        pg.op("dve", lambda e: e.memset(K["sel64"][:], 0.0), [], [KB["sel64"]])
        pg.op("dve", lambda e: e.memset(K["sel64"][64:65, :], 1.0), [KB["sel64"]], [KB["sel64"]])
        pg.op("dve", lambda e: e.memset(K["vcmp"][:], 0.0), [], [KB["vcmp"]])
        pg.op("dve", lambda e: e.memset(K["kcmpT"][:], 0.0), [], [KB["kcmpT"]])
        self.ld(K["ident"][:], I["c_ident"][:, :], [], [KB["ident"]])
        self.ld(K["fn"][:], I["fnorm"][:, :], [], [KB["fn"]])
        for c in range(self.NC):
            cs = slice(c * 512, (c + 1) * 512)
            self.ld(S["xT"][:, cs], I["xT"][:, cs], [], [SB["xT"][c]])
        craw = self.sb([P, 8], F32)
        b_craw = self.B()
        self.ld(craw[:], I["cT"][:, :], [], [b_craw])
        csig = self.sb([P, 8], F32)
        b_csig = self.B()
        self.act(csig[:], craw[:], AF.Sigmoid, [b_craw], [b_csig])
        self.tt(K["cact"][:], craw[:], csig[:], ALU.mult, [b_craw, b_csig], [KB["cact"]])
        wt = [self.sb([P, 8, 512], F32) for _ in range(2)]
        wtB = [self.B() for _ in range(2)]
        adab = self.sb([P, NL, 48], F32)
        b_adab = self.B()
        self.ld(adab[:], I["adab"].rearrange("l p j -> p l j"), [], [b_adab])
        kvadab = self.sb([P, 16], F32)
        b_kvadab = self.B()
        self.ld(kvadab[:], I["kvadab"][:, :], [], [b_kvadab])
        blk = 0
        jobs = [(I["ada_w"][l], 12, l) for l in range(NL)] + [(I["kv_ada_w"], 4, None)]
        for (wsrc, nblk, l) in jobs:
            pacc = ps[0]
            for bi in range(nblk):
                w = wt[blk % 2]
                wb = wtB[blk % 2]
                blk += 1
                self.ld(w[:], wsrc.rearrange("(kc p) n -> p kc n", p=P)[:, :, bi * 512:(bi + 1) * 512], [], [wb])
                for jj in range(4):
                    j = bi * 4 + jj
                    for kc in range(8):
                        self.mm(pacc[:, j:j + 1], w[:, kc, jj * 128:(jj + 1) * 128], K["cact"][:, kc:kc + 1],
                                kc == 0, kc == 7, [wb, KB["cact"]], [psB[0]])
            if l is not None:
                self.tt(K["mod"][:, l, :], pacc[:, 0:48], adab[:, l, :], ALU.add, [psB[0], b_adab], [KB["mod"]])
            else:
                self.tt(K["kvmod"][:], pacc[:, 0:16], kvadab[:], ALU.add, [psB[0], b_kvadab], [KB["kvmod"]])
        an = self.sb([P, NL, 8], F32)
        mn = self.sb([P, NL, 8], F32)
        kvn = self.sb([P, 8], F32)
        b_n = self.B()
        self.ld(an[:], I["an"][:, :, :], [], [b_n])
        b_n2 = self.B()
        self.ld(mn[:], I["mn"][:, :, :], [], [b_n2])
        b_n3 = self.B()
        self.ld(kvn[:], I["kvn"][:, :], [], [b_n3])
        tmp = self.sb([P, NL, 8], F32)
        b_tmp = self.B()
        for (dst, kb, nrm, nb, lo) in ((K["g1"], KB["g1"], an, b_n, 8), (K["g2"], KB["g2"], mn, b_n2, 32)):
            self.tt(tmp[:], K["mod"][:, :, lo:lo + 8], nrm[:], ALU.mult, [KB["mod"], nb], [b_tmp])
            self.tt(dst[:], tmp[:], nrm[:], ALU.add, [b_tmp, nb], [kb])
        tmp2 = self.sb([P, 8], F32)
        b_tmp2 = self.B()
        self.tt(tmp2[:], K["kvmod"][:, 8:16], kvn[:], ALU.mult, [KB["kvmod"], b_n3], [b_tmp2])
        self.tt(K["gkv"][:], tmp2[:], kvn[:], ALU.add, [b_tmp2, b_n3], [KB["gkv"]])
        tab = self.sb([33, 16], F32)
        b_tab = self.B()
        pg.op("dve", lambda e: e.memset(tab[32:33, :], NEG), [], [b_tab])
        b_tab2 = self.B()
        self.ld(tab[0:32, :], I["rel_bias"][:, :], [b_tab], [b_tab2])
        oh = self.sb([33, 384], F32)
        b_oh = self.B()
        self.ld(oh[:], I["c_oh"][:, :], [], [b_oh])
        self.mm(ps[1][0:16, 0:384], tab[:, :], oh[:, :], True, True, [b_tab, b_tab2, b_oh], [psB[1]])
        tsb = self.sb([16, 384], F32)
        b_tsb = self.B()
        self.cp(tsb[:], ps[1][0:16, 0:384], [psB[1]], [b_tsb])
        self.ld(S["tT"][:, :], tsb[:], [b_tsb], SB["tT"])
        for k in range(128):
            self.ld(S["d0"][k:k + 1, :, :], S["tT"][:, 127 - k:255 - k].rearrange("(o m) q -> o m q", o=1),
                    SB["tT"], [self.B()], q=("sp" if k % 2 == 0 else "act"))
            self.ld(S["d1"][k:k + 1, :, :], S["tT"][:, 255 - k:383 - k].rearrange("(o m) q -> o m q", o=1),
                    SB["tT"], [self.B()], q=("sp" if k % 2 == 0 else "act"))
        self.ld(K["b31"][:], bass.AP(I["rel_bias"].tensor, 31 * 16, [[0, P], [1, 16]]), [], [KB["b31"]])
        lam = self.sb([P, 2, 256], F32)
        b_lam = self.B()
        self.ld(lam[:], bass.AP(I["a_lambda"].tensor, 0, [[0, P], [256, 2], [1, 256]]), [], [b_lam])
        sub = self.sb([P, 2], F32)
        b_sub = self.B()
        self.ld(sub[:], I["a_subln"][:, :], [], [b_sub])
        prod = self.sb([P, 2, 2, 64], F32)
        b_prod = self.B()
        red = self.sb([P, 4], F32)
        b_red = self.B()
        for l in range(2):
            for i in range(2):
                self.tt(prod[:, l, i, :], lam[:, l, (2 * i) * 64:(2 * i + 1) * 64],
                        lam[:, l, (2 * i + 1) * 64:(2 * i + 2) * 64], ALU.mult, [b_lam], [b_prod])
        pg.op("dve", lambda e: e.tensor_reduce(out=red[:], in_=prod[:].rearrange("p l i d -> p (l i) d"),
                                               axis=AX.X, op=ALU.add), [b_prod], [b_red])
        ered = self.sb([P, 4], F32)
        b_ered = self.B()
        self.act(ered[:], red[:], AF.Exp, [b_red], [b_ered])
        for l in range(2):
            lam_init = 0.8 - 0.6 * math.exp(-0.3 * l)
            self.tt(K["lamneg"][:, l:l + 1], ered[:, 2 * l + 1:2 * l + 2], ered[:, 2 * l:2 * l + 1], ALU.subtract,
                    [b_ered], [KB["lamneg"]])
            self.ts(K["lamneg"][:, l:l + 1], K["lamneg"][:, l:l + 1], -lam_init, None, ALU.add, None,
                    [KB["lamneg"]], [KB["lamneg"]])
            self.ts(K["subg"][:, l:l + 1], sub[:, l:l + 1], 1.0 - lam_init, None, ALU.mult, None,
                    [b_sub], [KB["subg"]])
        self.phase_end()

    def norm_mod(self, xt, xb, N, gvec, shvec, gB, hout, hB, sq, sqB, rstd, rB, psi, tout=None, tB=None):
        K, KB, ps, psB = self.K, self.KB, self.ps, self.psB
        for j in range(8):
            s_, sb_ = sq[j % len(sq)], sqB[j % len(sq)]
            self.act(s_[:, 0:N], xt[:, j, 0:N], AF.Square, [xb], [sb_])
            self.mm(ps[psi][:, 0:N], K["onesD"][:, :], s_[:, 0:N], j == 0, j == 7, [KB["onesD"], sb_], [psB[psi]])
        self.act(rstd[:, 0:N], ps[psi][:, 0:N], AF.Ln, [psB[psi], self.KB["bar"]], [rB], bias=self.eps_ap)
        self.act(rstd[:, 0:N], rstd[:, 0:N], AF.Exp, [rB], [rB], scale=-0.5)
        for j in range(8):
            t_, tb_ = tout[j % len(tout)], tB[j % len(tout)]
            self.tt(t_[:, 0:N], xt[:, j, 0:N], rstd[:, 0:N], ALU.mult, [xb, rB], [tb_])
            self.act(hout[:, j, 0:N], t_[:, 0:N], AF.Identity, [tb_, gB], [hB],
                     bias=shvec[:, j:j + 1], scale=gvec[:, j:j + 1])

    def norm_rings(self, N=512):
        sq = [self.sb([P, N], BF16) for _ in range(2)]
        tt_ = [self.sb([P, N], F32) for _ in range(2)]
        return sq, [self.B() for _ in range(2)], tt_, [self.B() for _ in range(2)]

    @property
    def eps_ap(self):
        if not hasattr(self, "_eps_done"):
            self._eps_done = True
            K, KB = self.K, self.KB
            self.pg.op("dve", lambda e: e.memset(K["bar"][:, 5:6], EPS), [], [KB["bar"]])
        return self.K["bar"][:, 5:6]

    def load_w_bf16(self, dst, dstB, src_view, ncols, blk=512):
        nb = (ncols + blk - 1) // blk
        for i in range(nb):
            a, b = i * blk, min(ncols, (i + 1) * blk)
            self.ld(dst[:, :, a:b], src_view[:, :, a:b], [], [dstB[i]], q="pool")

    def phase_a_proj(self, l):
        I, K, KB, S, SB, ps, psB = self.I, self.K, self.KB, self.S, self.SB, self.ps, self.psB
        NC = self.NC
        self.phase_begin()
        w = self.sb([P, 8, 3072], BF16)
        wB = [self.B() for _ in range(6)]
        self.load_w_bf16(w, wB, I["a_w_in"][l].rearrange("(kc p) n -> p kc n", p=P), 3072)
        xt = [self.sb([P, 8, 512], F32) for _ in range(2)]
        xB = [self.B() for _ in range(2)]
        sq, sqB, tr, trB = self.norm_rings(512)
        rstd = self.sb([P, 512], F32)
        rB = self.B()
        h = [self.sb([P, 8, 512], BF16) for _ in range(2)]
        hB = [self.B() for _ in range(2)]
        qst = [self.sb([P, 16, 512], BF16) for _ in range(2)]
        qB = [self.B() for _ in range(2)]
        kB = [self.B() for _ in range(2)]
        vst = [self.sb([P, 4, 1024], BF16) for _ in range(2)]
        vB = [self.B() for _ in range(2)]
        xv = S["xT"].rearrange("(j p) t -> p j t", p=P)
        ring = 0
        for c in range(NC):
            cs = slice(c * 512, (c + 1) * 512)
            x_, xb_ = xt[c % 2], xB[c % 2]
            self.ld(x_[:], xv[:, :, cs], [SB["xT"][c]], [xb_])
            h_, hb_ = h[c % 2], hB[c % 2]
            self.norm_mod(x_, xb_, 512, K["g1"][:, l, :], K["mod"][:, l, 0:8], KB["g1"], h_, hb_, sq, sqB, rstd, rB, 2, tout=tr, tB=trB)
            q_, qb_, kb_ = qst[c % 2], qB[c % 2], kB[c % 2]
            for m in range(16):
                pi = ring % 2
                ring += 1
                for kc in range(8):
                    self.mm(ps[pi][:, :], w[:, kc, m * 128:(m + 1) * 128], h_[:, kc, :], kc == 0, kc == 7,
                            [wB[m // 4], hb_], [psB[pi]])
                if m < 8:
                    self.act(q_[:, m, :], ps[pi][:, :], AF.Copy, [psB[pi]], [qb_], scale=0.125)
                else:
                    self.cp(q_[:, m, :], ps[pi][:, :], [psB[pi]], [kb_])
            self.ld(S["qT"].rearrange("(m p) t -> p m t", p=P)[:, :, cs], q_[:, 0:8, :], [qb_], [SB["qT"][c]])
            self.ld(S["kT"].rearrange("(m p) t -> p m t", p=P)[:, :, cs], q_[:, 8:16, :], [kb_], [SB["kT"][c]])
            v_, vb_ = vst[c % 2], vB[c % 2]
            for tt in range(4):
                for half in range(2):
                    pi = ring % 2
                    ring += 1
                    for kc in range(8):
                        self.mm(ps[pi][:, :], h_[:, kc, tt * 128:(tt + 1) * 128],
                                w[:, kc, 2048 + half * 512:2048 + (half + 1) * 512], kc == 0, kc == 7,
                                [wB[4 + half], hb_], [psB[pi]])
                    self.cp(v_[:, tt, half * 512:(half + 1) * 512], ps[pi][:, :], [psB[pi]], [vb_],
                            eng=("dve" if half == 0 else "act_copy"))
            self.ld(S["vtok"].rearrange("(tt p) e -> p tt e", p=P)[:, c * 4:(c + 1) * 4, :], v_[:], [vb_],
                    [SB["vtok"][c]])
        self.phase_end()

    def attn_tiles(self, tiles, stageA, stageB, depth=1):
        n = len(tiles)
        for i in range(min(depth, n)):
            stageA(tiles[i], i)
        for i, t in enumerate(tiles):
            if i + depth < n:
                stageA(tiles[i + depth], i + depth)
            stageB(t, i)

    def load_bias_tiles(self):
        S, SB = self.S, self.SB
        d0 = self.sb([P, 16, 128], F32)
        d1 = self.sb([P, 16, 128], F32)
        w4 = self.sb([P, 128], F32)
        bd = self.B()
        self.ld(d0[:], S["d0"][:, :, :], SB["d0"], [bd])
        bd1 = self.B()
        self.ld(d1[:], S["d1"][:, :, :], SB["d1"], [bd1])
        bw = self.B()
        self.ld(w4[:], self.I["c_w4"][:, :], [], [bw])
        return d0, d1, w4, [bd, bd1, bw]

    def phase_a_attn(self, l):
        I, K, KB, S, SB, ps, psB, PS = self.I, self.K, self.KB, self.S, self.SB, self.ps, self.psB, self.PS
        NC, NT, T = self.NC, self.NT, self.T
        pg = self.pg
        self.phase_begin()
        d0, d1, w4, dB = self.load_bias_tiles()
        qh = [self.sb([P, T], BF16) for _ in range(2)]
        kA = [self.sb([P, T], BF16) for _ in range(2)]
        kBt = [self.sb([P, T], BF16) for _ in range(2)]
        vh = [self.sb([P, NT, 128], BF16) for _ in range(2)]
        lB = [[self.B() for _ in range(4)] for _ in range(2)]
        for i in range(2):
            pg.op("pool", lambda e, i=i: e.memset(kA[i][64:128, :], 0.0), [], [lB[i][1]])
            pg.op("pool", lambda e, i=i: e.memset(kBt[i][0:64, :], 0.0), [], [lB[i][3]])
        NPT = 4
        Pt = [self.sb([P, 2, 512], BF16) for _ in range(NPT)]
        PB = [self.B() for _ in range(NPT)]
        accL = [[self.sb([P, 512], F32) for _ in range(2)] for _ in range(2)]
        accB = [[self.B() for _ in range(2)] for _ in range(2)]
        rT = [[self.sb([P, 512], F32) for _ in range(2)] for _ in range(2)]
        tT = [[self.sb([P, 512], F32) for _ in range(2)] for _ in range(2)]
        rTB = [[self.B() for _ in range(2)] for _ in range(2)]
        tTB = [[self.B() for _ in range(2)] for _ in range(2)]
        osq = [self.sb([P, 512], BF16) for _ in range(2)]
        osqB = [self.B() for _ in range(2)]
        rs = [self.sb([P, 512], F32) for _ in range(2)]
        rsB = [self.B() for _ in range(2)]
        ost = [self.sb([P, 512], BF16) for _ in range(2)]
        ostB = [self.B(), self.B()]
        pairs = [0, 4]
        pairB = {0: self.B(), 4: self.B()}
        OBP = [(2, 3), (2, 3)]
        oS = [[self.sb([P, 512], F32) for _ in range(2)] for _ in range(2)]
        oSB = [[self.B() for _ in range(2)] for _ in range(2)]
        ctr = {"s": 0, "p": 0}

        def load_head(h):
            q_, ka_, kb_, v_ = qh[h % 2], kA[h % 2], kBt[h % 2], vh[h % 2]
            lb = lB[h % 2]
            self.ld(q_[:], S["qT"][h * 128:(h + 1) * 128, :], SB["qT"], [lb[0]])
            self.ld(ka_[0:64, :], S["kT"][h * 128:h * 128 + 64, :], SB["kT"], [lb[1]])
            self.ld(kb_[64:128, :], S["kT"][h * 128 + 64:(h + 1) * 128, :], SB["kT"], [lb[3]])
            self.ld(v_[:], S["vtok"].rearrange("(kt p) e -> p kt e", p=P)[:, :, h * 128:(h + 1) * 128], SB["vtok"],
                    [lb[2]])

        loops = []
        for h in range(8):
            for qc in range(NC):
                li = len(loops)
                loops.append(dict(h=h, qc=qc, li=li, par=li % 2, nk=4 * qc + 4, st={}))
        flat = [(L, kt) for L in loops for kt in range(L["nk"])]
        loaded = set()

        def ring_pair():
            pb = pairs[ctr["s"] % 2]
            ctr["s"] += 1
            return pb

        def stageA(L, kt):
            h, qc = L["h"], L["qc"]
            if h not in loaded:
                loaded.add(h)
                load_head(h)
            q_, ka_, kb_ = qh[h % 2], kA[h % 2], kBt[h % 2]
            lb = lB[h % 2]
            pb = ring_pair()
            L["st"][kt] = pb
            c0 = max(0, kt - 4 * qc) * 128
            for m in range(2):
                hm = h * 2 + m
                bank = ps[pb + m]
                fixes = []
                for ii in range(4):
                    delta = 4 * qc + ii - kt
                    if delta == 0 or delta == 1:
                        fixes.append((ii, d0 if delta == 0 else d1))
                kk = ka_ if m == 0 else kb_
                self.mm(bank[:, c0:512], kk[:, kt * 128:(kt + 1) * 128],
                        q_[:, qc * 512 + c0:(qc + 1) * 512], True, len(fixes) == 0,
                        [lb[0], lb[1], lb[3]], [pairB[pb]])
                for fi, (ii, dd) in enumerate(fixes):
                    self.mm(bank[:, ii * 128:(ii + 1) * 128], K["ident"][:, :], dd[:, hm, :], False,
                            fi == len(fixes) - 1, [KB["ident"]] + dB, [pairB[pb]])

        def stageB(L, kt):
            h, qc, par, nk = L["h"], L["qc"], L["par"], L["nk"]
            if qc == 0 and kt == 0 and h + 1 < 8 and (h + 1) not in loaded:
                loaded.add(h + 1)
                load_head(h + 1)
            v_ = vh[h % 2]
            lb = lB[h % 2]
            aL, aB = accL[par], accB[par]
            ob = OBP[par]
            pb = L["st"][kt]
            c0 = max(0, kt - 4 * qc) * 128
            pi = ctr["p"] % NPT
            ctr["p"] += 1
            pv = PS[:, pb * 512:(pb + 2) * 512].rearrange("p (m c) -> p m c", m=2)
            self.act(Pt[pi][:, :, c0:512], pv[:, :, c0:512], AF.Exp, [pairB[pb]], [PB[pi]])
            for m in range(2):
                eng = "pool" if m == 0 else "dve"
                if kt == 0:
                    self.cp(aL[m][:, c0:512], Pt[pi][:, m, c0:512], [PB[pi]], [aB[m]], eng=eng)
                else:
                    self.tt(aL[m][:, c0:512], aL[m][:, c0:512], Pt[pi][:, m, c0:512], ALU.add,
                            [PB[pi], aB[m]], [aB[m]], eng=eng)
            for m in range(2):
                self.mm(ps[ob[m]][:, c0:512], v_[:, kt, :], Pt[pi][:, m, c0:512], kt == 0, kt == nk - 1,
                        [lb[2], PB[pi]], [psB[ob[m]]])

        def epilogue_stages(L):
            h, qc, par = L["h"], L["qc"], L["par"]
            aL, aB = accL[par], accB[par]
            ob = OBP[par]
            r_, rb_, t_, tb_ = rT[par], rTB[par], tT[par], tTB[par]
            stt = {}

            def s0():
                for m in range(2):
                    self.cp(oS[par][m][:], ps[ob[m]][:, :], [psB[ob[m]]], [oSB[par][m]])

            def s1():
                run_pending(-1, upto_loop=L["li"] - 1)
                for m in range(2):
                    self.mm(ps[6 + m][:, :], K["ones32"][:, :], aL[m][:, :], True, True, [KB["ones32"], aB[m]],
                            [psB[6 + m]])

            def s2():
                for m in range(2):
                    self.act(r_[m][:], ps[6 + m][:, :], AF.Ln, [psB[6 + m]], [rb_[m]])
                    self.act(r_[m][:], r_[m][:], AF.Exp, [rb_[m]], [rb_[m]], scale=-1.0)

            def s3():
                for m in range(2):
                    self.tt(t_[m][:], oS[par][m][:], r_[m][:], ALU.mult, [oSB[par][m], rb_[m]], [tb_[m]])
                self.stt(t_[0][:], t_[1][:], K["lamneg"][:, l:l + 1], t_[0][:], ALU.mult, ALU.add,
                         [tb_[0], tb_[1], KB["lamneg"]], [tb_[0]])

            def s4():
                self.act(osq[par][:], t_[0][:], AF.Square, [tb_[0]], [osqB[par]])

            def s5():
                self.mm(ps[6][:, :], K["onesH"][:, :], osq[par][:], True, True, [KB["onesH"], osqB[par]],
                        [psB[6]])

            def s6():
                self.act(rs[par][:], ps[6][:, :], AF.Ln, [psB[6], KB["bar"]], [rsB[par]], bias=self.eps_ap)
                self.act(rs[par][:], rs[par][:], AF.Exp, [rsB[par]], [rsB[par]], scale=-0.5)

            def s7():
                self.tt(t_[0][:], t_[0][:], rs[par][:], ALU.mult, [tb_[0], rsB[par]], [tb_[0]])

            def s8():
                self.act(ost[par][:], t_[0][:], AF.Identity, [tb_[0], KB["subg"]], [ostB[par]],
                         scale=K["subg"][:, l:l + 1])
                self.ld(S["oT"][h * 128:(h + 1) * 128, qc * 512:(qc + 1) * 512], ost[par][:], [ostB[par]],
                        [SB["oT"][qc]])

            return [s0, s1, s2, s3, s4, s5, s6, s7, s8]

        pending = []

        def run_pending(i, upto_loop=None):
            j = 0
            while j < len(pending):
                due, li, fn = pending[j]
                if due <= i or (upto_loop is not None and li <= upto_loop):
                    pending.pop(j)
                    fn()
                    j = 0
                else:
                    j += 1

        nflat = len(flat)
        depth = 1
        for i in range(min(depth, nflat)):
            stageA(*flat[i])
        for i in range(nflat):
            if i + depth < nflat:
                stageA(*flat[i + depth])
            L, kt = flat[i]
            if kt == 0 and L["li"] >= 2:
                run_pending(i, upto_loop=L["li"] - 2)
            stageB(L, kt)
            if kt == L["nk"] - 1:
                stages = epilogue_stages(L)
                stages[0]()
                for k, fn in enumerate(stages[1:]):
                    pending.append((i + 2 + 2 * k, L["li"], fn))
            run_pending(i)
        run_pending(nflat + 1000)
        self.phase_end()

    def phase_outproj(self, l, wo_src):
        I, K, KB, S, SB, ps, psB = self.I, self.K, self.KB, self.S, self.SB, self.ps, self.psB
        NC = self.NC
        self.phase_begin()
        w = self.sb([P, 8, 1024], BF16)
        wB = [self.B() for _ in range(2)]
        self.load_w_bf16(w, wB, wo_src.rearrange("(kc p) n -> p kc n", p=P), 1024)
        xt = [self.sb([P, 8, 512], F32) for _ in range(2)]
        xB = [self.B() for _ in range(2)]
        ot = [self.sb([P, 8, 512], BF16) for _ in range(2)]
        oB = [self.B() for _ in range(2)]
        xv = S["xT"].rearrange("(j p) t -> p j t", p=P)
        ov = S["oT"].rearrange("(j p) t -> p j t", p=P)
        ring = 0
        for c in range(NC):
            cs = slice(c * 512, (c + 1) * 512)
            x_, xb_ = xt[c % 2], xB[c % 2]
            o_, ob_ = ot[c % 2], oB[c % 2]
            self.ld(x_[:], xv[:, :, cs], [SB["xT"][c]], [xb_])
            self.ld(o_[:], ov[:, :, cs], [SB["oT"][c]], [ob_])
            for j in range(8):
                pi = ring % 2
                ring += 1
                for hc in range(8):
                    self.mm(ps[pi][:, :], w[:, hc, j * 128:(j + 1) * 128], o_[:, hc, :], hc == 0, hc == 7,
                            [wB[j // 4], ob_], [psB[pi]])
                self.stt(x_[:, j, :], ps[pi][:, :], K["mod"][:, l, 16 + j:17 + j], x_[:, j, :], ALU.mult, ALU.add,
                         [psB[pi], xb_, KB["mod"]], [xb_])
            self.ld(xv[:, :, cs], x_[:], [xb_], [SB["xT"][c]])
        self.phase_end()

    def phase_mlp(self, l):
        I, K, KB, S, SB, ps, psB = self.I, self.K, self.KB, self.S, self.SB, self.ps, self.psB
        T = self.T
        N = 512
        self.phase_begin()
        w1 = self.sb([P, 8, DFF], BF16)
        w1B = [self.B() for _ in range(8)]
        w2 = self.sb([P, 32, D], BF16)
        w2B = [self.B() for _ in range(8)]
        self.load_w_bf16(w1, w1B, I["mlp_w1"][l].rearrange("(kc p) n -> p kc n", p=P), DFF)
        v2 = I["mlp_w2"][l].rearrange("(f p) n -> p f n", p=P)
        for i in range(8):
            self.ld(w2[:, i * 4:(i + 1) * 4, :], v2[:, i * 4:(i + 1) * 4, :], [], [w2B[i]], q="pool")
        xt = self.sb([P, 8, N], F32)
        xB = self.B()
        sq, sqB, tr, trB = self.norm_rings(N)
        rstd = self.sb([P, N], F32)
        rB = self.B()
        h = self.sb([P, 8, N], BF16)
        hB = self.B()
        hid = self.sb([P, 32, N], BF16)
        hidB = [self.B() for _ in range(8)]
        r32 = [self.sb([P, N], F32) for _ in range(2)]
        r32B = [self.B() for _ in range(2)]
        xv = S["xT"].rearrange("(j p) t -> p j t", p=P)
        ring = 0
        for c in range(T // N):
            cs = slice(c * N, (c + 1) * N)
            sbx = SB["xT"][c]
            x_, xb_ = xt, xB
            self.ld(x_[:], xv[:, :, cs], [sbx], [xb_])
            self.norm_mod(x_, xb_, N, K["g2"][:, l, :], K["mod"][:, l, 24:32], KB["g2"], h, hB, sq, sqB, rstd, rB, 2,
                          tout=tr, tB=trB)
            for f in range(32):
                pi = ring % 2
                ring += 1
                for kc in range(8):
                    self.mm(ps[pi][:, 0:N], w1[:, kc, f * 128:(f + 1) * 128], h[:, kc, :], kc == 0, kc == 7,
                            [w1B[f // 4], hB], [psB[pi]])
                ri = f % 2
                self.act(r32[ri][:], ps[pi][:, 0:N], AF.Relu, [psB[pi]], [r32B[ri]])
                self.tt(hid[:, f, :], r32[ri][:], r32[ri][:], ALU.mult, [r32B[ri]], [hidB[f // 4]])
            for j in range(8):
                pi = 3 + (ring % 2)
                ring += 1
                for f in range(32):
                    self.mm(ps[pi][:, 0:N], w2[:, f, j * 128:(j + 1) * 128], hid[:, f, :], f == 0, f == 31,
                            [w2B[f // 4], hidB[f // 4]], [psB[pi]])
                self.stt(x_[:, j, :], ps[pi][:, 0:N], K["mod"][:, l, 40 + j:41 + j], x_[:, j, :], ALU.mult, ALU.add,
                         [psB[pi], xb_, KB["mod"]], [xb_])
            self.ld(xv[:, :, cs], x_[:], [xb_], [sbx])
        self.phase_end()

    def phase_final(self, outT):
        I, K, KB, S, SB, ps, psB = self.I, self.K, self.KB, self.S, self.SB, self.ps, self.psB
        self.phase_begin()
        xt = [self.sb([P, 8, 512], F32) for _ in range(2)]
        xB = [self.B() for _ in range(2)]
        yt = [self.sb([P, 8, 512], F32) for _ in range(2)]
        yB = [self.B() for _ in range(2)]
        sq, sqB, tr, trB = self.norm_rings(512)
        rstd = self.sb([P, 512], F32)
        rB = self.B()
        xv = S["xT"].rearrange("(j p) t -> p j t", p=P)
        ov = outT.rearrange("(j p) t -> p j t", p=P)
        for c in range(self.NC):
            cs = slice(c * 512, (c + 1) * 512)
            x_, xb_ = xt[c % 2], xB[c % 2]
            self.ld(x_[:], xv[:, :, cs], [SB["xT"][c]], [xb_])
            self.norm_mod(x_, xb_, 512, K["fn"], K["zero8"], KB["fn"], yt[c % 2], yB[c % 2], sq, sqB, rstd, rB, 2, tout=tr, tB=trB)
            self.ld(ov[:, :, cs], yt[c % 2][:], [yB[c % 2]], [self.B()])
        self.phase_end()

    def phase_kv(self):
        I, K, KB, S, SB, ps, psB = self.I, self.K, self.KB, self.S, self.SB, self.ps, self.psB
        NC = self.NC
        self.phase_begin()
        w = self.sb([P, 8, 1536], BF16)
        wB = [self.B() for _ in range(3)]
        self.load_w_bf16(w, wB, I["w_kv"].rearrange("(kc p) n -> p kc n", p=P), 1536)
        xt = [self.sb([P, 8, 512], F32) for _ in range(2)]
        xB = [self.B() for _ in range(2)]
        sq, sqB, tr, trB = self.norm_rings(512)
        rstd = self.sb([P, 512], F32)
        rB = self.B()
        h = [self.sb([P, 8, 512], BF16) for _ in range(2)]
        hB = [self.B() for _ in range(2)]
        kst = [self.sb([P, 12, 512], BF16) for _ in range(2)]
        kB = [self.B() for _ in range(2)]
        vst = [self.sb([P, 4, 2, 256], BF16) for _ in range(2)]
        vB = [self.B() for _ in range(2)]
        xv = S["xT"].rearrange("(j p) t -> p j t", p=P)
        ring = 0
        for c in range(NC):
            cs = slice(c * 512, (c + 1) * 512)
            x_, xb_ = xt[c % 2], xB[c % 2]
            self.ld(x_[:], xv[:, :, cs], [SB["xT"][c]], [xb_])
            h_, hb_ = h[c % 2], hB[c % 2]
            self.norm_mod(x_, xb_, 512, K["gkv"], K["kvmod"][:, 0:8], KB["gkv"], h_, hb_, sq, sqB, rstd, rB, 2, tout=tr, tB=trB)
            k_, kb_ = kst[c % 2], kB[c % 2]
            for m in range(12):
                pi = ring % 2
                ring += 1
                for kc in range(8):
                    self.mm(ps[pi][:, :], w[:, kc, m * 128:(m + 1) * 128], h_[:, kc, :], kc == 0, kc == 7,
                            [wB[m // 4], hb_], [psB[pi]])
                self.cp(k_[:, m, :], ps[pi][:, :], [psB[pi]], [kb_], eng=("dve" if m % 2 == 0 else "act_copy"))
            self.ld(S["kvT"].rearrange("(m p) t -> p m t", p=P)[:, :, cs], k_[:], [kb_], [SB["kvT"][c]])
            v_, vb_ = vst[c % 2], vB[c % 2]
            for tt in range(4):
                pi = ring % 2
                ring += 1
                for si, s0 in enumerate((768, 1280)):
                    for kc in range(8):
                        self.mm(ps[pi][:, si * 256:(si + 1) * 256], h_[:, kc, tt * 128:(tt + 1) * 128],
                                w[:, kc, s0:s0 + 256], kc == 0, kc == 7, [wB[s0 // 512], hb_], [psB[pi]])
                self.cp(v_[:, tt, :, :], ps[pi][:, :].rearrange("p (s e) -> p s e", s=2), [psB[pi]], [vb_])
            self.ld(S["vslc"].rearrange("(tt p) e -> p tt e", p=P)[:, c * 4:(c + 1) * 4, :], v_[:, :, 0, :], [vb_],
                    [SB["vslc"][c]])
            self.ld(S["vwin"].rearrange("(tt p) e -> p tt e", p=P)[:, c * 4:(c + 1) * 4, :], v_[:, :, 1, :], [vb_],
                    [SB["vwin"][c]])
        self.phase_end()

    def phase_cmp(self):
        I, K, KB, S, SB, ps, psB = self.I, self.K, self.KB, self.S, self.SB, self.ps, self.psB
        T = self.T
        ncmp = T // 16 - 1
        self.phase_begin()
        src = [self.sb([64, T], BF16) for _ in range(2)]
        srcB = [self.B() for _ in range(2)]
        w1r = self.sb([64, 32, 256], BF16)
        w2 = self.sb([P, 2, 64], BF16)
        posT = self.sb([64, 32], BF16)
        hidT = self.sb([P, 2, 256], BF16)
        hidB = self.B()
        pre = self.sb([P, 256], F32)
        u = self.sb([P, 256], F32)
        bias = self.sb([P, 2], F32)
        bB = {k: self.B() for k in ("pre", "u", "bias")}
        pg = self.pg
        pg.op("dve", lambda e: e.memset(hidT[:], 0.0), [], [hidB])
        it = 0
        wb = [self.B(), self.B(), self.B()]
        for s in range(2):
            self.ld(w1r[:], I["cmp_w1"][s].rearrange("(t d) h -> d t h", d=64), [], [wb[0]], q="pool")
            self.ld(w2[:], I["cmp_w2"][s].rearrange("(hc p) d -> p hc d", p=P), [], [wb[1]], q="pool")
            self.ld(posT[:], I["cmp_posT"][s], [], [wb[2]], q="pool")
            for hc in range(2):
                for t in range(32):
                    self.mm(ps[6][:, hc:hc + 1], w1r[:, t, hc * 128:(hc + 1) * 128], posT[:, t:t + 1], t == 0, t == 31,
                            [wb[0], wb[2]], [psB[6]])
            self.cp(bias[:], ps[6][:, 0:2], [psB[6]], [bB["bias"]])
            for g in range(4):
                sr, srb = src[it % 2], srcB[it % 2]
                it += 1
                r0 = s * 256 + g * 64
                self.ld(sr[:], S["kvT"][r0:r0 + 64, :], SB["kvT"], [srb])
                for hc in range(2):
                    for t in range(32):
                        self.mm(ps[hc][:, 0:ncmp], w1r[:, t, hc * 128:(hc + 1) * 128],
                                sr[:, t:t + 16 * (ncmp - 1) + 1:16], t == 0, t == 31, [wb[0], srb], [psB[hc]])
                    self.act(pre[:, 0:ncmp], ps[hc][:, 0:ncmp], AF.Identity, [psB[hc], bB["bias"]], [bB["pre"]],
                             bias=bias[:, hc:hc + 1])
                    self.tt(u[:, 0:ncmp], pre[:, 0:ncmp], pre[:, 0:ncmp], ALU.mult, [bB["pre"]], [bB["u"]])
                    self.ts(u[:, 0:ncmp], u[:, 0:ncmp], 0.044715, 1.0, ALU.mult, ALU.add, [bB["u"]], [bB["u"]])
                    self.tt(u[:, 0:ncmp], u[:, 0:ncmp], pre[:, 0:ncmp], ALU.mult, [bB["u"], bB["pre"]], [bB["u"]])
                    self.act(u[:, 0:ncmp], u[:, 0:ncmp], AF.Sigmoid, [bB["u"]], [bB["u"]],
                             scale=2.0 * math.sqrt(2.0 / math.pi))
                    self.tt(hidT[:, hc, 0:ncmp], u[:, 0:ncmp], pre[:, 0:ncmp], ALU.mult, [bB["u"], bB["pre"]],
                            [hidB])
                if s == 0:
                    for hc in range(2):
                        self.mm(ps[2][0:64, 0:ncmp], w2[:, hc, :], hidT[:, hc, 0:ncmp], hc == 0, hc == 1,
                                [wb[1], hidB], [psB[2]])
                    self.cp(K["kcmpT"][0:64, g, 0:ncmp], ps[2][0:64, 0:ncmp], [psB[2]], [KB["kcmpT"]])
                else:
                    for nt in range(2):
                        nn = min(128, ncmp - nt * 128)
                        if nn <= 0:
                            continue
                        for hc in range(2):
                            self.mm(ps[3][0:nn, nt * 64:(nt + 1) * 64], hidT[:, hc, nt * 128:nt * 128 + nn],
                                    w2[:, hc, :], hc == 0, hc == 1, [wb[1], hidB], [psB[3]])
                        self.cp(K["vcmp"][0:nn, g, nt, 0:64], ps[3][0:nn, nt * 64:(nt + 1) * 64], [psB[3]],
                                [KB["vcmp"]])
        self.phase_end()

    def phase_b_proj(self, l):
        I, K, KB, S, SB, ps, psB = self.I, self.K, self.KB, self.S, self.SB, self.ps, self.psB
        NC = self.NC
        self.phase_begin()
        w = self.sb([P, 8, 1072], BF16)
        wB = [self.B() for _ in range(3)]
        self.load_w_bf16(w, wB, I["b_w_in"][l - 2].rearrange("(kc p) n -> p kc n", p=P), 1072)
        xt = [self.sb([P, 8, 512], F32) for _ in range(2)]
        xB = [self.B() for _ in range(2)]
        sq, sqB, tr, trB = self.norm_rings(512)
        rstd = self.sb([P, 512], F32)
        rB = self.B()
        h = [self.sb([P, 8, 512], BF16) for _ in range(2)]
        hB = [self.B() for _ in range(2)]
        qst = [self.sb([P, 8, 512], BF16) for _ in range(2)]
        qB = [self.B() for _ in range(2)]
        gst = [self.sb([48, 512], F32) for _ in range(2)]
        gB = [self.B() for _ in range(2)]
        xv = S["xT"].rearrange("(j p) t -> p j t", p=P)
        ring = 0
        for c in range(NC):
            cs = slice(c * 512, (c + 1) * 512)
            x_, xb_ = xt[c % 2], xB[c % 2]
            self.ld(x_[:], xv[:, :, cs], [SB["xT"][c]], [xb_])
            h_, hb_ = h[c % 2], hB[c % 2]
            self.norm_mod(x_, xb_, 512, K["g1"][:, l, :], K["mod"][:, l, 0:8], KB["g1"], h_, hb_, sq, sqB, rstd, rB, 2, tout=tr, tB=trB)
            q_, qb_ = qst[c % 2], qB[c % 2]
            for m in range(8):
                pi = ring % 2
                ring += 1
                for kc in range(8):
                    self.mm(ps[pi][:, :], w[:, kc, m * 128:(m + 1) * 128], h_[:, kc, :], kc == 0, kc == 7,
                            [wB[m // 4], hb_], [psB[pi]])
                self.act(q_[:, m, :], ps[pi][:, :], AF.Copy, [psB[pi]], [qb_], scale=0.125)
            self.ld(S["qT"].rearrange("(m p) t -> p m t", p=P)[:, :, cs], q_[:], [qb_], [SB["qT"][c]])
            pi = ring % 2
            ring += 1
            for kc in range(8):
                self.mm(ps[pi][0:48, :], w[:, kc, 1024:1072], h_[:, kc, :], kc == 0, kc == 7, [wB[2], hb_], [psB[pi]])
            self.act(gst[c % 2][:], ps[pi][0:48, :], AF.Sigmoid, [psB[pi]], [gB[c % 2]])
            self.ld(S["gT"][:, cs], gst[c % 2][:], [gB[c % 2]], [SB["gT"][c]])
        self.phase_end()

    def phase_b_attn(self, l):
        I, K, KB, S, SB, ps, psB = self.I, self.K, self.KB, self.S, self.SB, self.ps, self.psB
        NC, NT, T = self.NC, self.NT, self.T
        pg = self.pg
        self.phase_begin()
        d0, d1, w4, dB = self.load_bias_tiles()
        ex = self.sb([P, NT, 128], BF16)
        exB = self.B()
        pg.op("pool", lambda e: e.memset(ex[64:128, :, :], 0.0), [], [exB])
        self.ld(ex[0:64, :, :], I["c_ex"][:, :, :], [exB], [exB], q="pool")
        ov = self.sb([P, 2, 64], F32)
        ovB = self.B()
        self.ld(ov[:], I["c_ov"][:, :, :], [], [ovB])
        maskc = self.sb([P, 2, T], BF16)
        mcB = self.B()
        self.ld(maskc[:], I["c_maskc"][:, :, :], [], [mcB], q="pool")
        keep = self.sb([P, NT, 64], BF16)
        addm = self.sb([P, NT, 64], BF16)
        kaB = [self.B(), self.B()]
        self.ld(keep[:], I["c_keep"][:, :, :], [], [kaB[0]], q="pool")
        self.ld(addm[:], I["c_add"][:, :, :], [], [kaB[1]], q="pool")
        ksl = self.sb([P, T], BF16)
        kwn = self.sb([P, T], BF16)
        vsl = self.sb([P, NT, 65], BF16)
        vwn = self.sb([P, NT, 65], BF16)
        gB_ = [self.B() for _ in range(4)]
        pg.op("pool", lambda e: e.memset(ksl[64:128, :], 0.0), [], [gB_[0]])
        pg.op("pool", lambda e: e.memset(kwn[64:128, :], 0.0), [], [gB_[1]])
        pg.op("pool", lambda e: e.memset(vsl[:], 1.0), [], [gB_[2]])
        pg.op("pool", lambda e: e.memset(vwn[:], 1.0), [], [gB_[3]])
        lfull = [self.sb([P, 512], F32) for _ in range(2)]
        lfB = [self.B() for _ in range(2)]
        for i in range(2):
            pg.op("dve", lambda e, i=i: e.memset(lfull[i][:], 0.0), [], [lfB[i]])
        qg = [self.sb([P, 4, 512], BF16) for _ in range(2)]
        qgB = [self.B() for _ in range(2)]
        for i in range(2):
            pg.op("pool", lambda e, i=i: e.memset(qg[i][64:128, :, :], 0.0), [], [qgB[i]])
        gb = self.sb([64, 12, 512], F32)
        gbB = self.B()
        pc32 = [self.sb([P, 512], F32) for _ in range(2)]
        pn32 = [self.sb([P, 512], F32) for _ in range(2)]
        pn16 = [self.sb([P, 512], BF16) for _ in range(2)]
        pcB = [self.B() for _ in range(2)]
        pnB = [self.B() for _ in range(2)]
        pn16B = [self.B() for _ in range(2)]
        rl = self.sb([P, 512], F32)
        rlB = self.B()
        oc = self.sb([64, 4, 512], F32)
        ocB = [self.B() for _ in range(4)]
        impv = self.sb([P, 64], F32)
        impv2 = self.sb([P, 64], F32)
        m8a = self.sb([P, 8], F32)
        m8b = self.sb([P, 8], F32)
        msel = self.sb([P, 4, 128], F32)
        tkB = {k: self.B() for k in ("impv", "impv2", "m8a", "m8b", "msel")}
        pg.op("dve", lambda e: e.memset(msel[:], 0.0), [], [tkB["msel"]])
        mT = self.sb([P, 512], BF16)
        mTB = self.B()
        NPT = 6
        SR = [0, 1, 4, 5]
        OB = [2, 6]
        Pt = [self.sb([P, 512], BF16) for _ in range(NPT)]
        PB = [self.B() for _ in range(NPT)]
        rr = [self.sb([64, 512], F32) for _ in range(2)]
        rrB = [self.B() for _ in range(2)]
        acc = self.sb([64, 512], F32)
        accB = self.B()
        tmp = [self.sb([64, 512], F32) for _ in range(2)]
        tmpB = [self.B() for _ in range(2)]
        ost = [self.sb([64, 4, 512], BF16) for _ in range(2)]
        ostB = [self.B() for _ in range(2)]
        ctr = {"s": 0, "p": 0}
        it = 0
        for g in range(4):
            r0 = g * 64
            self.ld(ksl[0:64, :], S["kvT"][512 + r0:512 + r0 + 64, :], SB["kvT"], [gB_[0]])
            self.ld(kwn[0:64, :], S["kvT"][1024 + r0:1024 + r0 + 64, :], SB["kvT"], [gB_[1]])
            self.ld(vsl[:, :, 0:64], S["vslc"].rearrange("(kt p) e -> p kt e", p=P)[:, :, r0:r0 + 64], SB["vslc"],
                    [gB_[2]])
            self.ld(vwn[:, :, 0:64], S["vwin"].rearrange("(kt p) e -> p kt e", p=P)[:, :, r0:r0 + 64], SB["vwin"],
                    [gB_[3]])
            for qc in range(NC):
                cs = slice(qc * 512, (qc + 1) * 512)
                q_, qb_ = qg[it % 2], qgB[it % 2]
                o_st, o_stB = ost[it % 2], ostB[it % 2]
                it += 1
                self.ld(q_[0:64, :, :], S["qT"].rearrange("(h d) t -> d h t", d=64)[:, g * 4:(g + 1) * 4, cs],
                        [SB["qT"][qc]], [qb_])
                self.ld(gb[:], bass.AP(S["gT"].tensor, g * 12 * T + qc * 512, [[0, 64], [T, 12], [1, 512]]),
                        [SB["gT"][qc]], [gbB])
                for r in range(4):
                    for nt in range(2):
                        si = ctr["s"] % 2
                        ctr["s"] += 1
                        self.mm(ps[si][:, :], K["kcmpT"][:, g, nt * 128:(nt + 1) * 128], q_[:, r, :], True, True,
                                [KB["kcmpT"], qb_], [psB[si]])
                        self.act(pc32[nt][:], ps[si][:, :], AF.Exp, [psB[si]], [pcB[nt]])
                        self.tt(pc32[nt][:], pc32[nt][:], maskc[:, nt, cs], ALU.mult, [pcB[nt], mcB], [pcB[nt]])
                    for nt in range(2):
                        self.mm(ps[7][:, :], K["ones32"][:, :], pc32[nt][:], nt == 0, nt == 1,
                                [KB["ones32"], pcB[nt]], [psB[7]])
                    self.ts(rl[:], ps[7][:, :], 1e-18, None, ALU.max, None, [psB[7]], [rlB])
                    self.act(rl[:], rl[:], AF.Ln, [rlB], [rlB])
                    self.act(rl[:], rl[:], AF.Exp, [rlB], [rlB], scale=-1.0)
                    for nt in range(2):
                        self.tt(pn32[nt][:], pc32[nt][:], rl[:], ALU.mult, [pcB[nt], rlB], [pnB[nt]])
                        self.cp(pn16[nt][:], pn32[nt][:], [pnB[nt]], [pn16B[nt]], eng="pool")
                    for nt in range(2):
                        self.mm(ps[2][:, :], K["vcmp"][:, g, nt, :], pn16[nt][:], nt == 0, nt == 1,
                                [KB["vcmp"], pn16B[nt]], [psB[2]])
                    self.cp(oc[:, r, :], ps[2][0:64, :], [psB[2]], [ocB[r]])
                    for nt in range(2):
                        for i in range(4):
                            first = (r == 0 and nt == 0 and i == 0)
                            last = (r == 3 and nt == 1 and i == 3)
                            self.mm(ps[3][:, i * 64:(i + 1) * 64], pn32[nt][:, i * 128:(i + 1) * 128], ov[:, nt, :],
                                    first, last, [pnB[nt], ovB], [psB[3]])
                for i in range(4):
                    qb = qc * 4 + i
                    self.tt(impv[:], ps[3][:, i * 64:(i + 1) * 64], keep[:, qb, :], ALU.mult, [psB[3], kaB[0]],
                            [tkB["impv"]])
                    self.tt(impv[:], impv[:], addm[:, qb, :], ALU.add, [tkB["impv"], kaB[1]], [tkB["impv"]])
                    pg.op("dve", lambda e: e.max(out=m8a[:], in_=impv[:]), [tkB["impv"]], [tkB["m8a"]])
                    pg.op("dve", lambda e: e.match_replace(out=impv2[:], in_to_replace=m8a[:], in_values=impv[:],
                                                           imm_value=-3.0e38),
                          [tkB["impv"], tkB["m8a"]], [tkB["impv2"]])
                    pg.op("dve", lambda e: e.max(out=m8b[:], in_=impv2[:]), [tkB["impv2"]], [tkB["m8b"]])
                    self.ts(msel[:, i, 0:64], impv[:], m8b[:, 7:8], None, ALU.is_ge, None,
                            [tkB["impv"], tkB["m8b"]], [tkB["msel"]])
                for i in range(4):
                    pg.op("pe", lambda e, i=i: e.transpose(out=ps[7][:, i * 128:(i + 1) * 128], in_=msel[:, i, :],
                                                           identity=K["ident"][:, :]),
                          [tkB["msel"], KB["ident"]], [psB[7]])
                self.ts(mT[:], ps[7][:, 0:512], -1.0, 30000.0, ALU.add, ALU.mult, [psB[7]], [mTB])
                loops = []
                for r in range(4):
                    for sel in (True, False):
                        if sel:
                            tl = list(range(0, 4 * qc + 4))
                            kT_, kB_, vT_, vB_ = ksl, gB_[0], vsl, gB_[2]
                        else:
                            tl = list(range(max(0, 4 * qc - 4), 4 * qc + 4))
                            kT_, kB_, vT_, vB_ = kwn, gB_[1], vwn, gB_[3]
                        loops.append(dict(r=r, sel=sel, tiles=tl, kT=kT_, kB=kB_, vT=vT_, vB=vB_,
                                          ob=OB[len(loops) % 2], par=len(loops) % 2, st={}))

                def rng(L, kt):
                    c0 = max(0, kt - 4 * qc) * 128
                    c1 = 512 if L["sel"] else min(4, kt + 5 - 4 * qc) * 128
                    return c0, c1

                def stageA(L, kt):
                    h = g * 4 + L["r"]
                    si = SR[ctr["s"] % len(SR)]
                    ctr["s"] += 1
                    L["st"][kt] = si
                    c0, c1 = rng(L, kt)
                    extra = []
                    if L["sel"]:
                        extra.append((c0, c1, ex[:, kt, :], mT[:, c0:c1], [exB, mTB]))
                    for ii in range(c0 // 128, c1 // 128):
                        delta = 4 * qc + ii - kt
                        dd = None
                        if delta == 0:
                            dd = d0[:, h, :]
                        elif delta == 1:
                            dd = d1[:, h, :]
                        elif delta == 4 and not L["sel"]:
                            dd = w4[:, :]
                        if dd is not None:
                            extra.append((ii * 128, (ii + 1) * 128, K["ident"][:, :], dd, [KB["ident"]] + dB))
                    self.mm(ps[si][:, c0:c1], L["kT"][:, kt * 128:(kt + 1) * 128], q_[:, L["r"], c0:c1], True,
                            len(extra) == 0, [L["kB"], qb_], [psB[si]])
                    for xi, (a0, a1, lh, rh, rd) in enumerate(extra):
                        self.mm(ps[si][:, a0:a1], lh, rh, False, xi == len(extra) - 1, rd, [psB[si]])

                def stageB(L, kt, idx):
                    si = L["st"][kt]
                    c0, c1 = rng(L, kt)
                    pi = ctr["p"] % NPT
                    ctr["p"] += 1
                    n = len(L["tiles"])
                    self.act(Pt[pi][:, c0:c1], ps[si][:, c0:c1], AF.Exp, [psB[si]], [PB[pi]])
                    self.mm(ps[L["ob"]][0:65, c0:c1], L["vT"][:, kt, :], Pt[pi][:, c0:c1], idx == 0, idx == n - 1,
                            [L["vB"], PB[pi]], [psB[L["ob"]]])

                def epi1(L):
                    lf, lb_ = lfull[L["par"]], lfB[L["par"]]
                    ob = L["ob"]
                    self.act(lf[64:65, :], ps[ob][64:65, :], AF.Ln, [psB[ob]], [lb_])
                    self.act(lf[64:65, :], lf[64:65, :], AF.Exp, [lb_], [lb_], scale=-1.0)

                def epi2(L):
                    lf, lb_ = lfull[L["par"]], lfB[L["par"]]
                    ob = L["ob"]
                    r = L["r"]
                    rr_, rrb_ = rr[L["par"]], rrB[L["par"]]
                    tmp_, tmpb_ = tmp[L["par"]], tmpB[L["par"]]
                    self.mm(ps[3][:, :], K["sel64"][:, :], lf[:, :], True, True, [KB["sel64"], lb_], [psB[3]])
                    self.cp(rr_[:], ps[3][0:64, :], [psB[3]], [rrb_])
                    self.tt(tmp_[:], ps[ob][0:64, :], rr_[:], ALU.mult, [psB[ob], rrb_], [tmpb_])
                    if L["sel"]:
                        self.tt(acc[:], oc[:, r, :], gb[:, r * 3 + 0, :], ALU.mult, [ocB[r], gbB], [accB], eng="pool")
                        self.tt(tmp_[:], tmp_[:], gb[:, r * 3 + 1, :], ALU.mult, [tmpb_, gbB], [tmpb_])
                        self.tt(acc[:], acc[:], tmp_[:], ALU.add, [accB, tmpb_], [accB])
                    else:
                        self.tt(tmp_[:], tmp_[:], gb[:, r * 3 + 2, :], ALU.mult, [tmpb_, gbB], [tmpb_])
                        self.tt(o_st[:, r, :], acc[:], tmp_[:], ALU.add, [accB, tmpb_], [o_stB])

                flat = [(L, kt, idx) for L in loops for idx, kt in enumerate(L["tiles"])]
                nflat = len(flat)
                depth = 3
                pending = []
                for i in range(min(depth, nflat)):
                    stageA(flat[i][0], flat[i][1])
                for i in range(nflat):
                    if i + depth < nflat:
                        stageA(flat[i + depth][0], flat[i + depth][1])
                    L, kt, idx = flat[i]
                    stageB(L, kt, idx)
                    if idx == len(L["tiles"]) - 1:
                        epi1(L)
                        pending.append((i + 3, L))
                    while pending and pending[0][0] <= i:
                        epi2(pending.pop(0)[1])
                while pending:
                    epi2(pending.pop(0)[1])
                self.ld(S["oT"].rearrange("(h d) t -> d h t", d=64)[:, g * 4:(g + 1) * 4, cs], o_st[:], [o_stB],
                        [SB["oT"][qc]])
        self.phase_end()


_orig_op = Prog.op


def _op(self, eng, fn, reads=(), writes=()):
    if eng == "act_copy":
        return _orig_op(self, "act", fn, reads, writes)
    return _orig_op(self, eng, fn, reads, writes)


Prog.op = _op
_orig_cp = Builder.cp


def _cp(self, out, in_, reads, writes, eng="dve"):
    if eng == "act_copy":
        self.pg.op("act", lambda e: e.activation(out=out, in_=in_, func=AF.Copy), reads, writes)
    else:
        _orig_cp(self, out, in_, reads, writes, eng)


Builder.cp = _cp


def col8(v):
    v = np.asarray(v, np.float32)
    return np.ascontiguousarray(np.moveaxis(v.reshape(v.shape[:-1] + (v.shape[-1] // 128, 128)), -1, 0))


def make_in_maps(inputs, T):
    x = np.asarray(inputs["x"], np.float32)
    B = x.shape[0]
    shared = {}
    f = lambda k: np.ascontiguousarray(np.asarray(inputs[k], np.float32))
    shared["rel_bias"] = f("rel_bias")
    shared["ada_w"] = f("ada_w")
    shared["adab"] = np.ascontiguousarray(f("ada_b").reshape(NL, 48, 128).transpose(0, 2, 1))
    shared["an"] = col8(f("attn_norm"))
    shared["mn"] = col8(f("mlp_norm"))
    shared["mlp_w1"] = f("mlp_w1")
    shared["mlp_w2"] = f("mlp_w2")
    shared["a_w_in"] = f("a_w_in")
    shared["a_w_out"] = f("a_w_out")
    shared["a_lambda"] = f("a_lambda").reshape(2, 256)
    shared["a_subln"] = np.ascontiguousarray(f("a_subln").T)
    shared["kv_ada_w"] = f("kv_ada_w")
    shared["kvadab"] = np.ascontiguousarray(f("kv_ada_b").reshape(16, 128).T)
    shared["kvn"] = col8(f("kv_norm"))
    shared["w_kv"] = f("w_kv")
    shared["cmp_posT"] = np.ascontiguousarray(f("cmp_pos").transpose(0, 2, 1))
    shared["cmp_w1"] = f("cmp_w1")
    shared["cmp_w2"] = f("cmp_w2")
    shared["b_w_in"] = f("b_w_in")
    shared["b_w_out"] = f("b_w_out")
    shared["fnorm"] = col8(f("final_norm"))
    shared.update(make_consts(T))
    maps = []
    c = np.asarray(inputs["c"], np.float32)
    for b in range(B):
        m = dict(shared)
        m["xT"] = np.ascontiguousarray(x[b].T)
        m["cT"] = np.ascontiguousarray(c[b].reshape(8, 128).T)
        maps.append(m)
    return maps


_CACHE = {}


def run(inputs, T, layers=NL, debug=None):
    key = (T, layers, tuple(debug) if debug else None)
    if key not in _CACHE:
        _CACHE[key] = Builder(T, layers, debug).build()
    nc = _CACHE[key]
    maps = make_in_maps(inputs, T)
    res = run_bass_kernel_spmd(nc, maps, core_ids=list(range(len(maps))))
    return res.results


def kernel(**inputs):
    T = int(np.asarray(inputs["x"]).shape[1])
    results = run(inputs, T)
    out = np.stack([np.ascontiguousarray(r["outT"].T) for r in results], axis=0)
    return out.astype(np.float32)
```

```python
import math
from contextlib import ExitStack

import numpy as np
import ml_dtypes

import concourse.bass as bass
import concourse.mybir as mybir
from concourse.bass_utils import run_bass_kernel_spmd

F32 = mybir.dt.float32
BF16 = mybir.dt.bfloat16
AF = mybir.ActivationFunctionType
ALU = mybir.AluOpType
AX = mybir.AxisListType

D = 1024
DFF = 4096
NL = 4
EPS = 1e-6
NEG = -1e30
P = 128


class Buf:
    __slots__ = ("name", "w", "r")

    def __init__(self, name):
        self.name = name
        self.w = []
        self.r = []


class Op:
    __slots__ = ("eng", "fn", "deps", "dma", "slot", "target", "val", "waits")

    def __init__(self, eng, fn, deps, dma):
        self.eng = eng
        self.fn = fn
        self.deps = deps
        self.dma = dma
        self.slot = None
        self.target = False
        self.val = 0
        self.waits = []


class Prog:
    ENGS = ("pe", "act", "dve", "pool", "sp")
    NSLOT = {"sp": 24, "pool": 8, "act": 4}

    def __init__(self, nc):
        self.nc = nc
        self.ops = []
        self.rr = {q: 0 for q in self.NSLOT}
        self.slot_last = {}
        self.last_on_eng = {}

    def _reduce(self, ids):
        best = {}
        out = set()
        for i in ids:
            o = self.ops[i]
            if o.dma:
                out.add(i)
            else:
                if o.eng not in best or best[o.eng] < i:
                    best[o.eng] = i
        out.update(best.values())
        return out

    def _mk(self, eng, fn, reads, writes, dma):
        deps = set()
        for b in reads:
            deps.update(b.w)
        for b in writes:
            deps.update(b.w)
            deps.update(b.r)
        gid = len(self.ops)
        op = Op(eng, fn, self._reduce(deps), dma)
        if dma:
            s = self.rr[eng]
            self.rr[eng] = (s + 1) % self.NSLOT[eng]
            op.slot = (eng, s)
            prev = self.slot_last.get(op.slot)
            if prev is not None:
                op.deps.add(prev)
            self.slot_last[op.slot] = gid
        self.ops.append(op)
        wset = set(id(b) for b in writes)
        for b in writes:
            b.w = [gid]
            b.r = []
        for b in reads:
            if id(b) not in wset:
                b.r.append(gid)
                if len(b.r) > 12:
                    b.r = list(self._reduce(b.r))
        self.last_on_eng[eng if not dma else ("dma", gid)] = gid
        return gid

    def op(self, eng, fn, reads=(), writes=()):
        return self._mk(eng, fn, reads, writes, False)

    def dma(self, q, fn, reads=(), writes=()):
        return self._mk(q, fn, reads, writes, True)

    def barrier(self):
        allb = Buf("barrier")
        ids = [i for i, o in enumerate(self.ops)]
        last = {}
        dmas = []
        for i in range(len(self.ops) - 1, -1, -1):
            o = self.ops[i]
            if o.dma:
                if o.slot not in last:
                    last[o.slot] = i
                    dmas.append(i)
            elif o.eng not in last:
                last[o.eng] = i
                dmas.append(i)
        allb.w = dmas
        nc = self.nc
        z = self._bar_tiles
        self.op("pe", lambda e: e.matmul(z["ps"][0:1, 0:2], z["bf"][0:1, 0:1], z["bf"][0:1, 0:2],
                                          start=True, stop=True), reads=[allb], writes=[z["b_pe"]])
        self.op("act", lambda e: e.activation(out=z["a"][0:1, 0:1], in_=z["src"][0:1, 0:1], func=AF.Copy),
                reads=[allb], writes=[z["b_act"]])
        self.op("dve", lambda e: e.tensor_copy(out=z["v"][0:1, 0:1], in_=z["src"][0:1, 0:1]),
                reads=[allb], writes=[z["b_dve"]])
        self.op("pool", lambda e: e.tensor_copy(out=z["g"][0:1, 0:1], in_=z["src"][0:1, 0:1]),
                reads=[allb], writes=[z["b_pool"]])
        self.dma("sp", lambda e: e.dma_start(out=z["s"][0:1, 0:1], in_=z["src"][0:1, 0:1]),
                 reads=[allb], writes=[z["b_sp"]])

    def emit(self, stack):
        nc = self.nc
        ops = self.ops
        comp = ("pe", "act", "dve", "pool")
        sems = {e: stack.enter_context(nc.semaphore("sem_" + e)) for e in comp}
        dsem = {}
        for q, n in self.NSLOT.items():
            for s in range(n):
                dsem[(q, s)] = stack.enter_context(nc.semaphore("dsem_%s_%d" % (q, s)))
        for o in ops:
            for d in o.deps:
                t = ops[d]
                if t.dma:
                    continue
                if o.eng == "pe" and t.eng == "pe" and not o.dma:
                    continue
                t.target = True
        cnt = {e: 0 for e in comp}
        dcnt = {k: 0 for k in dsem}
        for o in ops:
            if o.dma:
                dcnt[o.slot] += 16
                o.val = dcnt[o.slot]
            else:
                if o.target:
                    cnt[o.eng] += 1
                o.val = cnt[o.eng]
        known = {e: {} for e in self.ENGS}
        clocks = {}
        for gid, o in enumerate(ops):
            kn = known[o.eng]
            m = {}
            for d in sorted(o.deps):
                t = ops[d]
                if t.dma:
                    key = ("d", t.slot)
                    sem = dsem[t.slot]
                else:
                    if o.eng == "pe" and t.eng == "pe" and not o.dma:
                        continue
                    key = ("e", t.eng)
                    sem = sems[t.eng]
                if kn.get(key, 0) >= t.val:
                    continue
                if key not in m or m[key][1] < t.val:
                    m[key] = (sem, t.val, d)
            for key, (sem, v, d) in sorted(m.items(), key=lambda kv: -kv[1][2]):
                if kn.get(key, 0) >= v:
                    continue
                o.waits.append((key, sem, v))
                for k2, v2 in clocks[d].items():
                    if kn.get(k2, 0) < v2:
                        kn[k2] = v2
            if o.dma or o.target:
                c = dict(kn)
                if o.dma:
                    c[("d", o.slot)] = o.val
                else:
                    k = ("e", o.eng)
                    if c.get(k, 0) < o.val:
                        c[k] = o.val
                clocks[gid] = c
        streams = {e: [] for e in self.ENGS}
        for o in ops:
            streams[o.eng].append(o)
        final = [(dsem[k], v) for k, v in dcnt.items() if v > 0]

        def run(eng_name, e):
            for o in streams[eng_name]:
                for _, sem, v in o.waits:
                    e.wait_ge(sem, v)
                ins = o.fn(e)
                if o.dma:
                    ins.then_inc(dsem[o.slot], 16)
                elif o.target:
                    ins.then_inc(sems[o.eng], 1)
            if eng_name == "sp":
                for sem, v in final:
                    e.wait_ge(sem, v)

        with nc.Block() as block:
            @block.tensor
            def _(e):
                run("pe", e)

            @block.scalar
            def _(e):
                run("act", e)

            @block.vector
            def _(e):
                run("dve", e)

            @block.gpsimd
            def _(e):
                run("pool", e)

            @block.sync
            def _(e):
                run("sp", e)


def _t5_bucket_np(dist):
    n = np.maximum(dist, 0)
    nf = np.maximum(n, 1).astype(np.float32)
    large = 16 + (np.log(nf / np.float32(16)) / np.float32(math.log(8.0)) * np.float32(16)).astype(np.int32)
    large = np.minimum(large, 31)
    return np.where(n < 16, n, large)


def make_consts(T):
    NT = T // 128
    c = {}
    oh = np.zeros((33, 384), np.float32)
    for i in range(384):
        dist = i - 127
        if dist < 0:
            oh[32, i] = 1.0
        else:
            oh[int(_t5_bucket_np(np.array(dist))), i] += 1.0
            oh[31, i] -= 1.0
    c["c_oh"] = oh
    ki = np.arange(128)[:, None]
    qi = np.arange(128)[None, :]
    c["c_w4"] = np.where(ki > qi, 0.0, NEG).astype(np.float32)
    c["c_ident"] = np.eye(128, dtype=np.float32)
    ex = np.zeros((64, NT, 128), np.float32)
    for kt in range(NT):
        for k in range(128):
            ex[2 * kt + k // 64, kt, k] = 1.0
    c["c_ex"] = ex
    n_cmp = (T - 32) // 16 + 1
    n_slc = T // 64
    n = np.arange(256)
    cs = n * 16
    ce = cs + 31
    ss = np.arange(n_slc) * 64
    ov = ((cs[:, None] < ss[None, :] + 64) & (ce[:, None] >= ss[None, :]) & (n[:, None] < n_cmp)).astype(np.float32)
    ovp = np.zeros((256, 64), np.float32)
    ovp[:, :n_slc] = ov
    c["c_ov"] = np.ascontiguousarray(ovp.reshape(2, 128, 64).transpose(1, 0, 2))
    q = np.arange(T)
    mc = ((ce[:, None] <= q[None, :]) & (n[:, None] < n_cmp)).astype(np.float32)
    c["c_maskc"] = np.ascontiguousarray(mc.reshape(2, 128, T).transpose(1, 0, 2))
    j = np.arange(64)[None, :]
    qb = (q // 64)[:, None]
    forced = (j == 0) | ((j <= qb) & (j > qb - 2))
    valid = (j <= qb) & (j < n_slc)
    keep = (~forced & valid).astype(np.float32)
    add = np.where(valid, np.where(forced, 1e4, 0.0), NEG).astype(np.float32)
    c["c_keep"] = np.ascontiguousarray(keep.reshape(NT, 128, 64).transpose(1, 0, 2))
    c["c_add"] = np.ascontiguousarray(add.reshape(NT, 128, 64).transpose(1, 0, 2))
    return c


class Builder:
    def __init__(self, T, layers=NL, debug=False):
        self.T = T
        self.NT = T // 128
        self.NC = T // 512
        self.layers = layers
        self.debug = debug
        self.nc = bass.Bass("TRN2", target_bir_lowering=False)
        self.pg = Prog(self.nc)
        self.gstack = ExitStack()
        self.pstack = None
        self.uid = 0

    def dram_in(self, name, shape, dt=F32):
        return self.nc.dram_tensor(name, list(shape), dt, kind="ExternalInput").ap()

    def dram_out(self, name, shape, dt=F32):
        return self.nc.dram_tensor(name, list(shape), dt, kind="ExternalOutput").ap()

    def dram(self, name, shape, dt):
        return self.nc.dram_tensor(name, list(shape), dt).ap()

    def sb(self, shape, dt, persistent=False, name=None):
        self.uid += 1
        st = self.gstack if persistent else self.pstack
        return st.enter_context(self.nc.sbuf_tensor("%s_%d" % (name or "t", self.uid), list(shape), dt))

    def B(self, name="b"):
        self.uid += 1
        return Buf("%s%d" % (name, self.uid))

    def phase_begin(self):
        self.pstack = ExitStack()

    def phase_end(self):
        self.pg.barrier()
        self.pstack.close()
        self.pstack = None

    def mm(self, out, lhsT, rhs, start, stop, reads, writes):
        self.pg.op("pe", lambda e: e.matmul(out, lhsT, rhs, start=start, stop=stop), reads, writes)

    def act(self, out, in_, func, reads, writes, bias=None, scale=None):
        kw = {}
        if bias is not None:
            kw["bias"] = bias
        if scale is not None:
            kw["scale"] = scale
        self.pg.op("act", lambda e: e.activation(out=out, in_=in_, func=func, **kw), reads, writes)

    def tt(self, out, in0, in1, op, reads, writes, eng="dve"):
        self.pg.op(eng, lambda e: e.tensor_tensor(out=out, in0=in0, in1=in1, op=op), reads, writes)

    def ts(self, out, in0, s1, s2, op0, op1, reads, writes, eng="dve"):
        if op1 is None:
            self.pg.op(eng, lambda e: e.tensor_scalar(out=out, in0=in0, scalar1=s1, scalar2=None, op0=op0),
                       reads, writes)
        else:
            self.pg.op(eng, lambda e: e.tensor_scalar(out=out, in0=in0, scalar1=s1, scalar2=s2, op0=op0, op1=op1),
                       reads, writes)

    def stt(self, out, in0, scalar, in1, op0, op1, reads, writes):
        self.pg.op("dve", lambda e: e.scalar_tensor_tensor(out=out, in0=in0, scalar=scalar, in1=in1,
                                                          op0=op0, op1=op1), reads, writes)

    def cp(self, out, in_, reads, writes, eng="dve"):
        self.pg.op(eng, lambda e: e.tensor_copy(out=out, in_=in_), reads, writes)

    def ld(self, out, in_, reads, writes, q="sp"):
        self.pg.dma(q, lambda e: e.dma_start(out=out, in_=in_), reads, writes)

    def build(self):
        nc, pg, T, NT, NC = self.nc, self.pg, self.T, self.NT, self.NC
        g = self.gstack
        I = {}
        I["xT"] = self.dram_in("xT", [D, T])
        I["cT"] = self.dram_in("cT", [P, 8])
        I["rel_bias"] = self.dram_in("rel_bias", [32, 16])
        I["ada_w"] = self.dram_in("ada_w", [NL, D, 6 * D])
        I["adab"] = self.dram_in("adab", [NL, P, 48])
        I["an"] = self.dram_in("an", [P, NL, 8])
        I["mn"] = self.dram_in("mn", [P, NL, 8])
        I["mlp_w1"] = self.dram_in("mlp_w1", [NL, D, DFF])
        I["mlp_w2"] = self.dram_in("mlp_w2", [NL, DFF, D])
        I["a_w_in"] = self.dram_in("a_w_in", [2, D, 3 * D])
        I["a_w_out"] = self.dram_in("a_w_out", [2, D, D])
        I["a_lambda"] = self.dram_in("a_lambda", [2, 256])
        I["a_subln"] = self.dram_in("a_subln", [P, 2])
        I["kv_ada_w"] = self.dram_in("kv_ada_w", [D, 2 * D])
        I["kvadab"] = self.dram_in("kvadab", [P, 16])
        I["kvn"] = self.dram_in("kvn", [P, 8])
        I["w_kv"] = self.dram_in("w_kv", [D, 1536])
        I["cmp_posT"] = self.dram_in("cmp_posT", [2, 64, 32])
        I["cmp_w1"] = self.dram_in("cmp_w1", [2, 2048, 256])
        I["cmp_w2"] = self.dram_in("cmp_w2", [2, 256, 64])
        I["b_w_in"] = self.dram_in("b_w_in", [2, D, 1072])
        I["b_w_out"] = self.dram_in("b_w_out", [2, D, D])
        I["fnorm"] = self.dram_in("fnorm", [P, 8])
        I["c_oh"] = self.dram_in("c_oh", [33, 384])
        I["c_w4"] = self.dram_in("c_w4", [P, 128])
        I["c_ident"] = self.dram_in("c_ident", [P, 128])
        I["c_ex"] = self.dram_in("c_ex", [64, NT, 128])
        I["c_ov"] = self.dram_in("c_ov", [P, 2, 64])
        I["c_maskc"] = self.dram_in("c_maskc", [P, 2, T])
        I["c_keep"] = self.dram_in("c_keep", [P, NT, 64])
        I["c_add"] = self.dram_in("c_add", [P, NT, 64])
        self.I = I
        outT = self.dram_out("outT", [D, T])
        S = {}
        S["xT"] = self.dram("s_xT", [D, T], F32)
        S["qT"] = self.dram("s_qT", [D, T], BF16)
        S["kT"] = self.dram("s_kT", [D, T], BF16)
        S["vtok"] = self.dram("s_vtok", [T, D], BF16)
        S["oT"] = self.dram("s_oT", [D, T], BF16)
        S["kvT"] = self.dram("s_kvT", [1536, T], BF16)
        S["vslc"] = self.dram("s_vslc", [T, 256], BF16)
        S["vwin"] = self.dram("s_vwin", [T, 256], BF16)
        S["gT"] = self.dram("s_gT", [48, T], F32)
        S["tT"] = self.dram("s_tT", [16, 384], F32)
        S["d0"] = self.dram("s_d0", [P, 16, 128], F32)
        S["d1"] = self.dram("s_d1", [P, 16, 128], F32)
        self.S = S
        SB = {k: [self.B(k) for _ in range(NC)] for k in ("xT", "qT", "kT", "vtok", "oT", "kvT", "vslc", "vwin", "gT")}
        for k in ("tT", "d0", "d1"):
            SB[k] = [self.B(k)]
        self.SB = SB
        PS = g.enter_context(nc.psum_tensor("psall", [P, 4096], F32))
        self.PS = PS
        ps = [PS[:, i * 512:(i + 1) * 512] for i in range(8)]
        self.ps = ps
        self.psB = [self.B("ps") for _ in range(8)]
        K = {}
        K["ones_bf"] = self.sb([P, 128], BF16, True, "ones")
        K["onesD"] = self.sb([P, 128], BF16, True, "onesD")
        K["onesH"] = self.sb([P, 128], BF16, True, "onesH")
        K["ones32"] = self.sb([P, 128], F32, True, "ones32")
        K["ident"] = self.sb([P, 128], F32, True, "ident")
        K["cact"] = self.sb([P, 8], F32, True, "cact")
        K["mod"] = self.sb([P, NL, 48], F32, True, "mod")
        K["kvmod"] = self.sb([P, 16], F32, True, "kvmod")
        K["g1"] = self.sb([P, NL, 8], F32, True, "g1")
        K["g2"] = self.sb([P, NL, 8], F32, True, "g2")
        K["gkv"] = self.sb([P, 8], F32, True, "gkv")
        K["fn"] = self.sb([P, 8], F32, True, "fn")
        K["zero8"] = self.sb([P, 8], F32, True, "zero8")
        K["b31"] = self.sb([P, 16], F32, True, "b31")
        K["lamneg"] = self.sb([P, 2], F32, True, "lamneg")
        K["subg"] = self.sb([P, 2], F32, True, "subg")
        K["kcmpT"] = self.sb([P, 4, 256], BF16, True, "kcmpT")
        K["vcmp"] = self.sb([P, 4, 2, 128], BF16, True, "vcmp")
        K["sel64"] = self.sb([P, 128], F32, True, "sel64")
        K["bar"] = self.sb([P, 16], F32, True, "bar")
        K["barbf"] = self.sb([P, 4], BF16, True, "barbf")
        self.K = K
        KB = {k: self.B(k) for k in K}
        self.KB = KB
        pg._bar_tiles = dict(ps=ps[6], bf=K["barbf"], src=K["bar"][:, 0:1], a=K["bar"][:, 1:2], v=K["bar"][:, 2:3],
                             g=K["bar"][:, 3:4], s=K["bar"][:, 4:5], b_pe=self.psB[6], b_act=self.B(), b_dve=self.B(),
                             b_pool=self.B(), b_sp=self.B())

        self.phase_setup()
        for l in range(self.layers):
            if l < 2:
                self.phase_a_proj(l)
                self.phase_a_attn(l)
                wo = I["a_w_out"][l]
            else:
                self.phase_b_proj(l)
                self.phase_b_attn(l)
                wo = I["b_w_out"][l - 2]
            self.phase_outproj(l, wo)
            self.phase_mlp(l)
            if l == 1:
                self.phase_kv()
                self.phase_cmp()
        self.phase_final(outT)
        if self.debug:
            dbg = {}
            for k in self.debug:
                t = S[k]
                o = self.dram_out("dbg_" + k, list(t.shape), t.dtype)
                self.ld(o, t, reads=SB[k], writes=[self.B()])
        pg.emit(g)
        return nc

    def phase_setup(self):
        nc, pg, I, K, KB, S, SB = self.nc, self.pg, self.I, self.K, self.KB, self.S, self.SB
        ps, psB = self.ps, self.psB
        self.phase_begin()
        pg.op("dve", lambda e: e.memset(K["bar"][:], 0.0), [], [KB["bar"]])
        pg.op("dve", lambda e: e.memset(K["barbf"][:], 0.0), [], [KB["barbf"]])
        pg.op("dve", lambda e: e.memset(K["ones_bf"][:], 1.0), [], [KB["ones_bf"]])
        pg.op("dve", lambda e: e.memset(K["onesD"][:], 1.0 / 1024), [], [KB["onesD"]])
        pg.op("dve", lambda e: e.memset(K["onesH"][:], 1.0 / 128), [], [KB["onesH"]])
        pg.op("dve", lambda e: e.memset(K["ones32"][:], 1.0), [], [KB["ones32"]])
        pg.op("dve", lambda e: e.memset(K["zero8"][:], 0.0), [], [KB["zero8"]])
        pg.op("dve", lambda e: e.memset(K["sel64"][:], 0.0), [], [KB["sel64"]])
        pg.op("dve", lambda e: e.memset(K["sel64"][64:65, :], 1.0), [KB["sel64"]], [KB["sel64"]])
        pg.op("dve", lambda e: e.memset(K["vcmp"][:], 0.0), [], [KB["vcmp"]])
        pg.op("dve", lambda e: e.memset(K["kcmpT"][:], 0.0), [], [KB["kcmpT"]])
        self.ld(K["ident"][:], I["c_ident"][:, :], [], [KB["ident"]])
        self.ld(K["fn"][:], I["fnorm"][:, :], [], [KB["fn"]])
        for c in range(self.NC):
            cs = slice(c * 512, (c + 1) * 512)
            self.ld(S["xT"][:, cs], I["xT"][:, cs], [], [SB["xT"][c]])
        craw = self.sb([P, 8], F32)
        b_craw = self.B()
        self.ld(craw[:], I["cT"][:, :], [], [b_craw])
        csig = self.sb([P, 8], F32)
        b_csig = self.B()
        self.act(csig[:], craw[:], AF.Sigmoid, [b_craw], [b_csig])
        self.tt(K["cact"][:], craw[:], csig[:], ALU.mult, [b_craw, b_csig], [KB["cact"]])
        NW = 6
        wt = [self.sb([P, 8, 512], F32) for _ in range(NW)]
        wtB = [self.B() for _ in range(NW)]
        adab = self.sb([P, NL, 48], F32)
        b_adab = self.B()
        self.ld(adab[:], I["adab"].rearrange("l p j -> p l j"), [], [b_adab])
        kvadab = self.sb([P, 16], F32)
        b_kvadab = self.B()
        self.ld(kvadab[:], I["kvadab"][:, :], [], [b_kvadab])
        blk = 0
        jobs = [(I["ada_w"][l], 12, l) for l in range(NL)] + [(I["kv_ada_w"], 4, None)]
        for (wsrc, nblk, l) in jobs:
            pacc = ps[0]
            for bi in range(nblk):
                w = wt[blk % NW]
                wb = wtB[blk % NW]
                blk += 1
                self.ld(w[:], wsrc.rearrange("(kc p) n -> p kc n", p=P)[:, :, bi * 512:(bi + 1) * 512], [], [wb])
                for jj in range(4):
                    j = bi * 4 + jj
                    for kc in range(8):
                        self.mm(pacc[:, j:j + 1], w[:, kc, jj * 128:(jj + 1) * 128], K["cact"][:, kc:kc + 1],
                                kc == 0, kc == 7, [wb, KB["cact"]], [psB[0]])
            if l is not None:
                self.tt(K["mod"][:, l, :], pacc[:, 0:48], adab[:, l, :], ALU.add, [psB[0], b_adab], [KB["mod"]])
            else:
                self.tt(K["kvmod"][:], pacc[:, 0:16], kvadab[:], ALU.add, [psB[0], b_kvadab], [KB["kvmod"]])
        an = self.sb([P, NL, 8], F32)
        mn = self.sb([P, NL, 8], F32)
        kvn = self.sb([P, 8], F32)
        b_n = self.B()
        self.ld(an[:], I["an"][:, :, :], [], [b_n])
        b_n2 = self.B()
        self.ld(mn[:], I["mn"][:, :, :], [], [b_n2])
        b_n3 = self.B()
        self.ld(kvn[:], I["kvn"][:, :], [], [b_n3])
        tmp = self.sb([P, NL, 8], F32)
        b_tmp = self.B()
        for (dst, kb, nrm, nb, lo) in ((K["g1"], KB["g1"], an, b_n, 8), (K["g2"], KB["g2"], mn, b_n2, 32)):
            self.tt(tmp[:], K["mod"][:, :, lo:lo + 8], nrm[:], ALU.mult, [KB["mod"], nb], [b_tmp])
            self.tt(dst[:], tmp[:], nrm[:], ALU.add, [b_tmp, nb], [kb])
        tmp2 = self.sb([P, 8], F32)
        b_tmp2 = self.B()
        self.tt(tmp2[:], K["kvmod"][:, 8:16], kvn[:], ALU.mult, [KB["kvmod"], b_n3], [b_tmp2])
        self.tt(K["gkv"][:], tmp2[:], kvn[:], ALU.add, [b_tmp2, b_n3], [KB["gkv"]])
        tab = self.sb([33, 16], F32)
        b_tab = self.B()
        pg.op("dve", lambda e: e.memset(tab[32:33, :], NEG), [], [b_tab])
        b_tab2 = self.B()
        self.ld(tab[0:32, :], I["rel_bias"][:, :], [b_tab], [b_tab2])
        oh = self.sb([33, 384], F32)
        b_oh = self.B()
        self.ld(oh[:], I["c_oh"][:, :], [], [b_oh])
        self.mm(ps[1][0:16, 0:384], tab[:, :], oh[:, :], True, True, [b_tab, b_tab2, b_oh], [psB[1]])
        tsb = self.sb([16, 384], F32)
        b_tsb = self.B()
        self.cp(tsb[:], ps[1][0:16, 0:384], [psB[1]], [b_tsb])
        self.ld(S["tT"][:, :], tsb[:], [b_tsb], SB["tT"])
        for k in range(128):
            self.ld(S["d0"][k:k + 1, :, :], S["tT"][:, 127 - k:255 - k].rearrange("(o m) q -> o m q", o=1),
                    SB["tT"], [self.B()], q=("sp" if k % 2 == 0 else "act"))
            self.ld(S["d1"][k:k + 1, :, :], S["tT"][:, 255 - k:383 - k].rearrange("(o m) q -> o m q", o=1),
                    SB["tT"], [self.B()], q=("sp" if k % 2 == 0 else "act"))
        self.ld(K["b31"][:], bass.AP(I["rel_bias"].tensor, 31 * 16, [[0, P], [1, 16]]), [], [KB["b31"]])
        lam = self.sb([P, 2, 256], F32)
        b_lam = self.B()
        self.ld(lam[:], bass.AP(I["a_lambda"].tensor, 0, [[0, P], [256, 2], [1, 256]]), [], [b_lam])
        sub = self.sb([P, 2], F32)
        b_sub = self.B()
        self.ld(sub[:], I["a_subln"][:, :], [], [b_sub])
        prod = self.sb([P, 2, 2, 64], F32)
        b_prod = self.B()
        red = self.sb([P, 4], F32)
        b_red = self.B()
        for l in range(2):
            for i in range(2):
                self.tt(prod[:, l, i, :], lam[:, l, (2 * i) * 64:(2 * i + 1) * 64],
                        lam[:, l, (2 * i + 1) * 64:(2 * i + 2) * 64], ALU.mult, [b_lam], [b_prod])
        pg.op("dve", lambda e: e.tensor_reduce(out=red[:], in_=prod[:].rearrange("p l i d -> p (l i) d"),
                                               axis=AX.X, op=ALU.add), [b_prod], [b_red])
        ered = self.sb([P, 4], F32)
        b_ered = self.B()
        self.act(ered[:], red[:], AF.Exp, [b_red], [b_ered])
        for l in range(2):
            lam_init = 0.8 - 0.6 * math.exp(-0.3 * l)
            self.tt(K["lamneg"][:, l:l + 1], ered[:, 2 * l + 1:2 * l + 2], ered[:, 2 * l:2 * l + 1], ALU.subtract,
                    [b_ered], [KB["lamneg"]])
            self.ts(K["lamneg"][:, l:l + 1], K["lamneg"][:, l:l + 1], -lam_init, None, ALU.add, None,
                    [KB["lamneg"]], [KB["lamneg"]])
            self.ts(K["subg"][:, l:l + 1], sub[:, l:l + 1], 1.0 - lam_init, None, ALU.mult, None,
                    [b_sub], [KB["subg"]])
        self.phase_end()

    def norm_mod(self, xt, xb, N, gvec, shvec, gB, hout, hB, sq, sqB, rstd, rB, psi, tout=None, tB=None):
        K, KB, ps, psB = self.K, self.KB, self.ps, self.psB
        for j in range(8):
            s_, sb_ = sq[j % len(sq)], sqB[j % len(sq)]
            self.act(s_[:, 0:N], xt[:, j, 0:N], AF.Square, [xb], [sb_])
            self.mm(ps[psi][:, 0:N], K["onesD"][:, :], s_[:, 0:N], j == 0, j == 7, [KB["onesD"], sb_], [psB[psi]])
        self.act(rstd[:, 0:N], ps[psi][:, 0:N], AF.Ln, [psB[psi], self.KB["bar"]], [rB], bias=self.eps_ap)
        self.act(rstd[:, 0:N], rstd[:, 0:N], AF.Exp, [rB], [rB], scale=-0.5)
        for j in range(8):
            t_, tb_ = tout[j % len(tout)], tB[j % len(tout)]
            self.tt(t_[:, 0:N], xt[:, j, 0:N], rstd[:, 0:N], ALU.mult, [xb, rB], [tb_])
            self.act(hout[:, j, 0:N], t_[:, 0:N], AF.Identity, [tb_, gB], [hB],
                     bias=shvec[:, j:j + 1], scale=gvec[:, j:j + 1])

    def norm_rings(self, N=512):
        sq = [self.sb([P, N], BF16) for _ in range(2)]
        tt_ = [self.sb([P, N], F32) for _ in range(2)]
        return sq, [self.B() for _ in range(2)], tt_, [self.B() for _ in range(2)]

    @property
    def eps_ap(self):
        if not hasattr(self, "_eps_done"):
            self._eps_done = True
            K, KB = self.K, self.KB
            self.pg.op("dve", lambda e: e.memset(K["bar"][:, 5:6], EPS), [], [KB["bar"]])
        return self.K["bar"][:, 5:6]

    def load_w_bf16(self, dst, dstB, src_view, ncols, blk=512):
        nb = (ncols + blk - 1) // blk
        for i in range(nb):
            a, b = i * blk, min(ncols, (i + 1) * blk)
            self.ld(dst[:, :, a:b], src_view[:, :, a:b], [], [dstB[i]], q="pool")

    def phase_a_proj(self, l):
        I, K, KB, S, SB, ps, psB = self.I, self.K, self.KB, self.S, self.SB, self.ps, self.psB
        NC = self.NC
        self.phase_begin()
        w = self.sb([P, 8, 3072], BF16)
        wB = [self.B() for _ in range(6)]
        self.load_w_bf16(w, wB, I["a_w_in"][l].rearrange("(kc p) n -> p kc n", p=P), 3072)
        xt = [self.sb([P, 8, 512], F32) for _ in range(2)]
        xB = [self.B() for _ in range(2)]
        sq, sqB, tr, trB = self.norm_rings(512)
        rstd = self.sb([P, 512], F32)
        rB = self.B()
        h = [self.sb([P, 8, 512], BF16) for _ in range(2)]
        hB = [self.B() for _ in range(2)]
        qst = [self.sb([P, 16, 512], BF16) for _ in range(2)]
        qB = [self.B() for _ in range(2)]
        kB = [self.B() for _ in range(2)]
        vst = [self.sb([P, 4, 1024], BF16) for _ in range(2)]
        vB = [self.B() for _ in range(2)]
        xv = S["xT"].rearrange("(j p) t -> p j t", p=P)
        ring = 0
        for c in range(NC):
            cs = slice(c * 512, (c + 1) * 512)
            x_, xb_ = xt[c % 2], xB[c % 2]
            self.ld(x_[:], xv[:, :, cs], [SB["xT"][c]], [xb_])
            h_, hb_ = h[c % 2], hB[c % 2]
            self.norm_mod(x_, xb_, 512, K["g1"][:, l, :], K["mod"][:, l, 0:8], KB["g1"], h_, hb_, sq, sqB, rstd, rB, 2, tout=tr, tB=trB)
            q_, qb_, kb_ = qst[c % 2], qB[c % 2], kB[c % 2]
            for m in range(16):
                pi = ring % 2
                ring += 1
                for kc in range(8):
                    self.mm(ps[pi][:, :], w[:, kc, m * 128:(m + 1) * 128], h_[:, kc, :], kc == 0, kc == 7,
                            [wB[m // 4], hb_], [psB[pi]])
                if m < 8:
                    self.act(q_[:, m, :], ps[pi][:, :], AF.Copy, [psB[pi]], [qb_], scale=0.125)
                else:
                    self.cp(q_[:, m, :], ps[pi][:, :], [psB[pi]], [kb_])
            self.ld(S["qT"].rearrange("(m p) t -> p m t", p=P)[:, :, cs], q_[:, 0:8, :], [qb_], [SB["qT"][c]])
            self.ld(S["kT"].rearrange("(m p) t -> p m t", p=P)[:, :, cs], q_[:, 8:16, :], [kb_], [SB["kT"][c]])
            v_, vb_ = vst[c % 2], vB[c % 2]
            for tt in range(4):
                for half in range(2):
                    pi = ring % 2
                    ring += 1
                    for kc in range(8):
                        self.mm(ps[pi][:, :], h_[:, kc, tt * 128:(tt + 1) * 128],
                                w[:, kc, 2048 + half * 512:2048 + (half + 1) * 512], kc == 0, kc == 7,
                                [wB[4 + half], hb_], [psB[pi]])
                    self.cp(v_[:, tt, half * 512:(half + 1) * 512], ps[pi][:, :], [psB[pi]], [vb_],
                            eng=("dve" if half == 0 else "act_copy"))
            self.ld(S["vtok"].rearrange("(tt p) e -> p tt e", p=P)[:, c * 4:(c + 1) * 4, :], v_[:], [vb_],
                    [SB["vtok"][c]])
        self.phase_end()

    def attn_tiles(self, tiles, stageA, stageB, depth=1):
        n = len(tiles)
        for i in range(min(depth, n)):
            stageA(tiles[i], i)
        for i, t in enumerate(tiles):
            if i + depth < n:
                stageA(tiles[i + depth], i + depth)
            stageB(t, i)

    def load_bias_tiles(self):
        S, SB = self.S, self.SB
        d0 = self.sb([P, 16, 128], F32)
        d1 = self.sb([P, 16, 128], F32)
        w4 = self.sb([P, 128], F32)
        bd = self.B()
        self.ld(d0[:], S["d0"][:, :, :], SB["d0"], [bd])
        bd1 = self.B()
        self.ld(d1[:], S["d1"][:, :, :], SB["d1"], [bd1])
        bw = self.B()
        self.ld(w4[:], self.I["c_w4"][:, :], [], [bw])
        return d0, d1, w4, [bd, bd1, bw]

    def phase_a_attn(self, l):
        I, K, KB, S, SB, ps, psB, PS = self.I, self.K, self.KB, self.S, self.SB, self.ps, self.psB, self.PS
        NC, NT, T = self.NC, self.NT, self.T
        pg = self.pg
        self.phase_begin()
        d0, d1, w4, dB = self.load_bias_tiles()
        qh = [self.sb([P, T], BF16) for _ in range(2)]
        kA = [self.sb([P, T], BF16) for _ in range(2)]
        kBt = [self.sb([P, T], BF16) for _ in range(2)]
        vh = [self.sb([P, NT, 128], BF16) for _ in range(2)]
        lB = [[self.B() for _ in range(4)] for _ in range(2)]
        for i in range(2):
            pg.op("pool", lambda e, i=i: e.memset(kA[i][64:128, :], 0.0), [], [lB[i][1]])
            pg.op("pool", lambda e, i=i: e.memset(kBt[i][0:64, :], 0.0), [], [lB[i][3]])
        NPT = 4
        Pt = [self.sb([P, 2, 512], BF16) for _ in range(NPT)]
        PB = [self.B() for _ in range(NPT)]
        accL = [[self.sb([P, 512], F32) for _ in range(2)] for _ in range(2)]
        accB = [[self.B() for _ in range(2)] for _ in range(2)]
        rT = [[self.sb([P, 512], F32) for _ in range(2)] for _ in range(2)]
        tT = [[self.sb([P, 512], F32) for _ in range(2)] for _ in range(2)]
        rTB = [[self.B() for _ in range(2)] for _ in range(2)]
        tTB = [[self.B() for _ in range(2)] for _ in range(2)]
        osq = [self.sb([P, 512], BF16) for _ in range(2)]
        osqB = [self.B() for _ in range(2)]
        rs = [self.sb([P, 512], F32) for _ in range(2)]
        rsB = [self.B() for _ in range(2)]
        ost = [self.sb([P, 512], BF16) for _ in range(2)]
        ostB = [self.B(), self.B()]
        pairs = [0, 4]
        pairB = {0: self.B(), 4: self.B()}
        OBP = [(2, 3), (2, 3)]
        oS = [[self.sb([P, 512], F32) for _ in range(2)] for _ in range(2)]
        oSB = [[self.B() for _ in range(2)] for _ in range(2)]
        ctr = {"s": 0, "p": 0}

        def load_head(h):
            q_, ka_, kb_, v_ = qh[h % 2], kA[h % 2], kBt[h % 2], vh[h % 2]
            lb = lB[h % 2]
            self.ld(q_[:], S["qT"][h * 128:(h + 1) * 128, :], SB["qT"], [lb[0]])
            self.ld(ka_[0:64, :], S["kT"][h * 128:h * 128 + 64, :], SB["kT"], [lb[1]])
            self.ld(kb_[64:128, :], S["kT"][h * 128 + 64:(h + 1) * 128, :], SB["kT"], [lb[3]])
            self.ld(v_[:], S["vtok"].rearrange("(kt p) e -> p kt e", p=P)[:, :, h * 128:(h + 1) * 128], SB["vtok"],
                    [lb[2]])

        loops = []
        for h in range(8):
            for qc in range(NC):
                li = len(loops)
                loops.append(dict(h=h, qc=qc, li=li, par=li % 2, nk=4 * qc + 4, st={}))
        flat = [(L, kt) for L in loops for kt in range(L["nk"])]
        loaded = set()

        def ring_pair():
            pb = pairs[ctr["s"] % 2]
            ctr["s"] += 1
            return pb

        def stageA(L, kt):
            h, qc = L["h"], L["qc"]
            if h not in loaded:
                loaded.add(h)
                load_head(h)
            q_, ka_, kb_ = qh[h % 2], kA[h % 2], kBt[h % 2]
            lb = lB[h % 2]
            pb = ring_pair()
            L["st"][kt] = pb
            c0 = max(0, kt - 4 * qc) * 128
            for m in range(2):
                hm = h * 2 + m
                bank = ps[pb + m]
                fixes = []
                for ii in range(4):
                    delta = 4 * qc + ii - kt
                    if delta == 0 or delta == 1:
                        fixes.append((ii, d0 if delta == 0 else d1))
                kk = ka_ if m == 0 else kb_
                self.mm(bank[:, c0:512], kk[:, kt * 128:(kt + 1) * 128],
                        q_[:, qc * 512 + c0:(qc + 1) * 512], True, len(fixes) == 0,
                        [lb[0], lb[1], lb[3]], [pairB[pb]])
                for fi, (ii, dd) in enumerate(fixes):
                    self.mm(bank[:, ii * 128:(ii + 1) * 128], K["ident"][:, :], dd[:, hm, :], False,
                            fi == len(fixes) - 1, [KB["ident"]] + dB, [pairB[pb]])

        def stageB(L, kt):
            h, qc, par, nk = L["h"], L["qc"], L["par"], L["nk"]
            if qc == 0 and kt == 0 and h + 1 < 8 and (h + 1) not in loaded:
                loaded.add(h + 1)
                load_head(h + 1)
            v_ = vh[h % 2]
            lb = lB[h % 2]
            aL, aB = accL[par], accB[par]
            ob = OBP[par]
            pb = L["st"][kt]
            c0 = max(0, kt - 4 * qc) * 128
            pi = ctr["p"] % NPT
            ctr["p"] += 1
            pv = PS[:, pb * 512:(pb + 2) * 512].rearrange("p (m c) -> p m c", m=2)
            self.act(Pt[pi][:, :, c0:512], pv[:, :, c0:512], AF.Exp, [pairB[pb]], [PB[pi]])
            for m in range(2):
                eng = "pool" if m == 0 else "dve"
                if kt == 0:
                    self.cp(aL[m][:, c0:512], Pt[pi][:, m, c0:512], [PB[pi]], [aB[m]], eng=eng)
                else:
                    self.tt(aL[m][:, c0:512], aL[m][:, c0:512], Pt[pi][:, m, c0:512], ALU.add,
                            [PB[pi], aB[m]], [aB[m]], eng=eng)
            for m in range(2):
                self.mm(ps[ob[m]][:, c0:512], v_[:, kt, :], Pt[pi][:, m, c0:512], kt == 0, kt == nk - 1,
                        [lb[2], PB[pi]], [psB[ob[m]]])

        def epilogue_stages(L):
            h, qc, par = L["h"], L["qc"], L["par"]
            aL, aB = accL[par], accB[par]
            ob = OBP[par]
            r_, rb_, t_, tb_ = rT[par], rTB[par], tT[par], tTB[par]
            stt = {}

            def s0():
                for m in range(2):
                    self.cp(oS[par][m][:], ps[ob[m]][:, :], [psB[ob[m]]], [oSB[par][m]])

            def s1():
                run_pending(-1, upto_loop=L["li"] - 1)
                for m in range(2):
                    self.mm(ps[6 + m][:, :], K["ones32"][:, :], aL[m][:, :], True, True, [KB["ones32"], aB[m]],
                            [psB[6 + m]])

            def s2():
                for m in range(2):
                    self.act(r_[m][:], ps[6 + m][:, :], AF.Ln, [psB[6 + m]], [rb_[m]])
                    self.act(r_[m][:], r_[m][:], AF.Exp, [rb_[m]], [rb_[m]], scale=-1.0)

            def s3():
                for m in range(2):
                    self.tt(t_[m][:], oS[par][m][:], r_[m][:], ALU.mult, [oSB[par][m], rb_[m]], [tb_[m]])
                self.stt(t_[0][:], t_[1][:], K["lamneg"][:, l:l + 1], t_[0][:], ALU.mult, ALU.add,
                         [tb_[0], tb_[1], KB["lamneg"]], [tb_[0]])

            def s4():
                self.act(osq[par][:], t_[0][:], AF.Square, [tb_[0]], [osqB[par]])

            def s5():
                self.mm(ps[6][:, :], K["onesH"][:, :], osq[par][:], True, True, [KB["onesH"], osqB[par]],
                        [psB[6]])

            def s6():
                self.act(rs[par][:], ps[6][:, :], AF.Ln, [psB[6], KB["bar"]], [rsB[par]], bias=self.eps_ap)
                self.act(rs[par][:], rs[par][:], AF.Exp, [rsB[par]], [rsB[par]], scale=-0.5)

            def s7():
                self.tt(t_[0][:], t_[0][:], rs[par][:], ALU.mult, [tb_[0], rsB[par]], [tb_[0]])

            def s8():
                self.act(ost[par][:], t_[0][:], AF.Identity, [tb_[0], KB["subg"]], [ostB[par]],
                         scale=K["subg"][:, l:l + 1])
                self.ld(S["oT"][h * 128:(h + 1) * 128, qc * 512:(qc + 1) * 512], ost[par][:], [ostB[par]],
                        [SB["oT"][qc]])

            return [s0, s1, s2, s3, s4, s5, s6, s7, s8]

        pending = []

        def run_pending(i, upto_loop=None):
            j = 0
            while j < len(pending):
                due, li, fn = pending[j]
                if due <= i or (upto_loop is not None and li <= upto_loop):
                    pending.pop(j)
                    fn()
                    j = 0
                else:
                    j += 1

        nflat = len(flat)
        depth = 1
        for i in range(min(depth, nflat)):
            stageA(*flat[i])
        for i in range(nflat):
            if i + depth < nflat:
                stageA(*flat[i + depth])
            L, kt = flat[i]
            if kt == 0 and L["li"] >= 2:
                run_pending(i, upto_loop=L["li"] - 2)
            stageB(L, kt)
            if kt == L["nk"] - 1:
                stages = epilogue_stages(L)
                stages[0]()
                for k, fn in enumerate(stages[1:]):
                    pending.append((i + 2 + 2 * k, L["li"], fn))
            run_pending(i)
        run_pending(nflat + 1000)
        self.phase_end()

    def phase_outproj(self, l, wo_src):
        I, K, KB, S, SB, ps, psB = self.I, self.K, self.KB, self.S, self.SB, self.ps, self.psB
        NC = self.NC
        self.phase_begin()
        w = self.sb([P, 8, 1024], BF16)
        wB = [self.B() for _ in range(2)]
        self.load_w_bf16(w, wB, wo_src.rearrange("(kc p) n -> p kc n", p=P), 1024)
        xt = [self.sb([P, 8, 512], F32) for _ in range(2)]
        xB = [self.B() for _ in range(2)]
        ot = [self.sb([P, 8, 512], BF16) for _ in range(2)]
        oB = [self.B() for _ in range(2)]
        xv = S["xT"].rearrange("(j p) t -> p j t", p=P)
        ov = S["oT"].rearrange("(j p) t -> p j t", p=P)
        ring = 0
        for c in range(NC):
            cs = slice(c * 512, (c + 1) * 512)
            x_, xb_ = xt[c % 2], xB[c % 2]
            o_, ob_ = ot[c % 2], oB[c % 2]
            self.ld(x_[:], xv[:, :, cs], [SB["xT"][c]], [xb_])
            self.ld(o_[:], ov[:, :, cs], [SB["oT"][c]], [ob_])
            for j in range(8):
                pi = ring % 2
                ring += 1
                for hc in range(8):
                    self.mm(ps[pi][:, :], w[:, hc, j * 128:(j + 1) * 128], o_[:, hc, :], hc == 0, hc == 7,
                            [wB[j // 4], ob_], [psB[pi]])
                self.stt(x_[:, j, :], ps[pi][:, :], K["mod"][:, l, 16 + j:17 + j], x_[:, j, :], ALU.mult, ALU.add,
                         [psB[pi], xb_, KB["mod"]], [xb_])
            self.ld(xv[:, :, cs], x_[:], [xb_], [SB["xT"][c]])
        self.phase_end()

    def phase_mlp(self, l):
        I, K, KB, S, SB, ps, psB = self.I, self.K, self.KB, self.S, self.SB, self.ps, self.psB
        T = self.T
        N = 512
        self.phase_begin()
        w1 = self.sb([P, 8, DFF], BF16)
        w1B = [self.B() for _ in range(8)]
        w2 = self.sb([P, 32, D], BF16)
        w2B = [self.B() for _ in range(8)]
        self.load_w_bf16(w1, w1B, I["mlp_w1"][l].rearrange("(kc p) n -> p kc n", p=P), DFF)
        v2 = I["mlp_w2"][l].rearrange("(f p) n -> p f n", p=P)
        for i in range(8):
            self.ld(w2[:, i * 4:(i + 1) * 4, :], v2[:, i * 4:(i + 1) * 4, :], [], [w2B[i]], q="pool")
        xt = self.sb([P, 8, N], F32)
        xB = self.B()
        sq, sqB, tr, trB = self.norm_rings(N)
        rstd = self.sb([P, N], F32)
        rB = self.B()
        h = self.sb([P, 8, N], BF16)
        hB = self.B()
        hid = self.sb([P, 32, N], BF16)
        hidB = [self.B() for _ in range(8)]
        r32 = [self.sb([P, N], F32) for _ in range(2)]
        r32B = [self.B() for _ in range(2)]
        xv = S["xT"].rearrange("(j p) t -> p j t", p=P)
        ring = 0
        for c in range(T // N):
            cs = slice(c * N, (c + 1) * N)
            sbx = SB["xT"][c]
            x_, xb_ = xt, xB
            self.ld(x_[:], xv[:, :, cs], [sbx], [xb_])
            self.norm_mod(x_, xb_, N, K["g2"][:, l, :], K["mod"][:, l, 24:32], KB["g2"], h, hB, sq, sqB, rstd, rB, 2,
                          tout=tr, tB=trB)
            for f in range(32):
                pi = ring % 2
                ring += 1
                for kc in range(8):
                    self.mm(ps[pi][:, 0:N], w1[:, kc, f * 128:(f + 1) * 128], h[:, kc, :], kc == 0, kc == 7,
                            [w1B[f // 4], hB], [psB[pi]])
                ri = f % 2
                self.act(r32[ri][:], ps[pi][:, 0:N], AF.Relu, [psB[pi]], [r32B[ri]])
                self.tt(hid[:, f, :], r32[ri][:], r32[ri][:], ALU.mult, [r32B[ri]], [hidB[f // 4]])
            for j in range(8):
                pi = 3 + (ring % 2)
                ring += 1
                for f in range(32):
                    self.mm(ps[pi][:, 0:N], w2[:, f, j * 128:(j + 1) * 128], hid[:, f, :], f == 0, f == 31,
                            [w2B[f // 4], hidB[f // 4]], [psB[pi]])
                self.stt(x_[:, j, :], ps[pi][:, 0:N], K["mod"][:, l, 40 + j:41 + j], x_[:, j, :], ALU.mult, ALU.add,
                         [psB[pi], xb_, KB["mod"]], [xb_])
            self.ld(xv[:, :, cs], x_[:], [xb_], [sbx])
        self.phase_end()

    def phase_final(self, outT):
        I, K, KB, S, SB, ps, psB = self.I, self.K, self.KB, self.S, self.SB, self.ps, self.psB
        self.phase_begin()
        xt = [self.sb([P, 8, 512], F32) for _ in range(2)]
        xB = [self.B() for _ in range(2)]
        yt = [self.sb([P, 8, 512], F32) for _ in range(2)]
        yB = [self.B() for _ in range(2)]
        sq, sqB, tr, trB = self.norm_rings(512)
        rstd = self.sb([P, 512], F32)
        rB = self.B()
        xv = S["xT"].rearrange("(j p) t -> p j t", p=P)
        ov = outT.rearrange("(j p) t -> p j t", p=P)
        for c in range(self.NC):
            cs = slice(c * 512, (c + 1) * 512)
            x_, xb_ = xt[c % 2], xB[c % 2]
            self.ld(x_[:], xv[:, :, cs], [SB["xT"][c]], [xb_])
            self.norm_mod(x_, xb_, 512, K["fn"], K["zero8"], KB["fn"], yt[c % 2], yB[c % 2], sq, sqB, rstd, rB, 2, tout=tr, tB=trB)
            self.ld(ov[:, :, cs], yt[c % 2][:], [yB[c % 2]], [self.B()])
        self.phase_end()

    def phase_kv(self):
        I, K, KB, S, SB, ps, psB = self.I, self.K, self.KB, self.S, self.SB, self.ps, self.psB
        NC = self.NC
        self.phase_begin()
        w = self.sb([P, 8, 1536], BF16)
        wB = [self.B() for _ in range(3)]
        self.load_w_bf16(w, wB, I["w_kv"].rearrange("(kc p) n -> p kc n", p=P), 1536)
        xt = [self.sb([P, 8, 512], F32) for _ in range(2)]
        xB = [self.B() for _ in range(2)]
        sq, sqB, tr, trB = self.norm_rings(512)
        rstd = self.sb([P, 512], F32)
        rB = self.B()
        h = [self.sb([P, 8, 512], BF16) for _ in range(2)]
        hB = [self.B() for _ in range(2)]
        kst = [self.sb([P, 12, 512], BF16) for _ in range(2)]
        kB = [self.B() for _ in range(2)]
        vst = [self.sb([P, 4, 2, 256], BF16) for _ in range(2)]
        vB = [self.B() for _ in range(2)]
        xv = S["xT"].rearrange("(j p) t -> p j t", p=P)
        ring = 0
        for c in range(NC):
            cs = slice(c * 512, (c + 1) * 512)
            x_, xb_ = xt[c % 2], xB[c % 2]
            self.ld(x_[:], xv[:, :, cs], [SB["xT"][c]], [xb_])
            h_, hb_ = h[c % 2], hB[c % 2]
            self.norm_mod(x_, xb_, 512, K["gkv"], K["kvmod"][:, 0:8], KB["gkv"], h_, hb_, sq, sqB, rstd, rB, 2, tout=tr, tB=trB)
            k_, kb_ = kst[c % 2], kB[c % 2]
            for m in range(12):
                pi = ring % 2
                ring += 1
                for kc in range(8):
                    self.mm(ps[pi][:, :], w[:, kc, m * 128:(m + 1) * 128], h_[:, kc, :], kc == 0, kc == 7,
                            [wB[m // 4], hb_], [psB[pi]])
                self.cp(k_[:, m, :], ps[pi][:, :], [psB[pi]], [kb_], eng=("dve" if m % 2 == 0 else "act_copy"))
            self.ld(S["kvT"].rearrange("(m p) t -> p m t", p=P)[:, :, cs], k_[:], [kb_], [SB["kvT"][c]])
            v_, vb_ = vst[c % 2], vB[c % 2]
            for tt in range(4):
                pi = ring % 2
                ring += 1
                for si, s0 in enumerate((768, 1280)):
                    for kc in range(8):
                        self.mm(ps[pi][:, si * 256:(si + 1) * 256], h_[:, kc, tt * 128:(tt + 1) * 128],
                                w[:, kc, s0:s0 + 256], kc == 0, kc == 7, [wB[s0 // 512], hb_], [psB[pi]])
                self.cp(v_[:, tt, :, :], ps[pi][:, :].rearrange("p (s e) -> p s e", s=2), [psB[pi]], [vb_])
            self.ld(S["vslc"].rearrange("(tt p) e -> p tt e", p=P)[:, c * 4:(c + 1) * 4, :], v_[:, :, 0, :], [vb_],
                    [SB["vslc"][c]])
            self.ld(S["vwin"].rearrange("(tt p) e -> p tt e", p=P)[:, c * 4:(c + 1) * 4, :], v_[:, :, 1, :], [vb_],
                    [SB["vwin"][c]])
        self.phase_end()

    def phase_cmp(self):
        I, K, KB, S, SB, ps, psB = self.I, self.K, self.KB, self.S, self.SB, self.ps, self.psB
        T = self.T
        ncmp = T // 16 - 1
        self.phase_begin()
        src = [self.sb([64, T], BF16) for _ in range(2)]
        srcB = [self.B() for _ in range(2)]
        w1r = self.sb([64, 32, 256], BF16)
        w2 = self.sb([P, 2, 64], BF16)
        posT = self.sb([64, 32], BF16)
        hidT = self.sb([P, 2, 256], BF16)
        hidB = self.B()
        pre = self.sb([P, 256], F32)
        u = self.sb([P, 256], F32)
        bias = self.sb([P, 2], F32)
        bB = {k: self.B() for k in ("pre", "u", "bias")}
        pg = self.pg
        pg.op("dve", lambda e: e.memset(hidT[:], 0.0), [], [hidB])
        it = 0
        wb = [self.B(), self.B(), self.B()]
        for s in range(2):
            self.ld(w1r[:], I["cmp_w1"][s].rearrange("(t d) h -> d t h", d=64), [], [wb[0]], q="pool")
            self.ld(w2[:], I["cmp_w2"][s].rearrange("(hc p) d -> p hc d", p=P), [], [wb[1]], q="pool")
            self.ld(posT[:], I["cmp_posT"][s], [], [wb[2]], q="pool")
            for hc in range(2):
                for t in range(32):
                    self.mm(ps[6][:, hc:hc + 1], w1r[:, t, hc * 128:(hc + 1) * 128], posT[:, t:t + 1], t == 0, t == 31,
                            [wb[0], wb[2]], [psB[6]])
            self.cp(bias[:], ps[6][:, 0:2], [psB[6]], [bB["bias"]])
            for g in range(4):
                sr, srb = src[it % 2], srcB[it % 2]
                it += 1
                r0 = s * 256 + g * 64
                self.ld(sr[:], S["kvT"][r0:r0 + 64, :], SB["kvT"], [srb])
                for hc in range(2):
                    for t in range(32):
                        self.mm(ps[hc][:, 0:ncmp], w1r[:, t, hc * 128:(hc + 1) * 128],
                                sr[:, t:t + 16 * (ncmp - 1) + 1:16], t == 0, t == 31, [wb[0], srb], [psB[hc]])
                    self.act(pre[:, 0:ncmp], ps[hc][:, 0:ncmp], AF.Identity, [psB[hc], bB["bias"]], [bB["pre"]],
                             bias=bias[:, hc:hc + 1])
                    self.tt(u[:, 0:ncmp], pre[:, 0:ncmp], pre[:, 0:ncmp], ALU.mult, [bB["pre"]], [bB["u"]])
                    self.ts(u[:, 0:ncmp], u[:, 0:ncmp], 0.044715, 1.0, ALU.mult, ALU.add, [bB["u"]], [bB["u"]])
                    self.tt(u[:, 0:ncmp], u[:, 0:ncmp], pre[:, 0:ncmp], ALU.mult, [bB["u"], bB["pre"]], [bB["u"]])
                    self.act(u[:, 0:ncmp], u[:, 0:ncmp], AF.Sigmoid, [bB["u"]], [bB["u"]],
                             scale=2.0 * math.sqrt(2.0 / math.pi))
                    self.tt(hidT[:, hc, 0:ncmp], u[:, 0:ncmp], pre[:, 0:ncmp], ALU.mult, [bB["u"], bB["pre"]],
                            [hidB])
                if s == 0:
                    for hc in range(2):
                        self.mm(ps[2][0:64, 0:ncmp], w2[:, hc, :], hidT[:, hc, 0:ncmp], hc == 0, hc == 1,
                                [wb[1], hidB], [psB[2]])
                    self.cp(K["kcmpT"][0:64, g, 0:ncmp], ps[2][0:64, 0:ncmp], [psB[2]], [KB["kcmpT"]])
                else:
                    for nt in range(2):
                        nn = min(128, ncmp - nt * 128)
                        if nn <= 0:
                            continue
                        for hc in range(2):
                            self.mm(ps[3][0:nn, nt * 64:(nt + 1) * 64], hidT[:, hc, nt * 128:nt * 128 + nn],
                                    w2[:, hc, :], hc == 0, hc == 1, [wb[1], hidB], [psB[3]])
                        self.cp(K["vcmp"][0:nn, g, nt, 0:64], ps[3][0:nn, nt * 64:(nt + 1) * 64], [psB[3]],
                                [KB["vcmp"]])
        self.phase_end()

    def phase_b_proj(self, l):
        I, K, KB, S, SB, ps, psB = self.I, self.K, self.KB, self.S, self.SB, self.ps, self.psB
        NC = self.NC
        self.phase_begin()
        w = self.sb([P, 8, 1072], BF16)
        wB = [self.B() for _ in range(3)]
        self.load_w_bf16(w, wB, I["b_w_in"][l - 2].rearrange("(kc p) n -> p kc n", p=P), 1072)
        xt = [self.sb([P, 8, 512], F32) for _ in range(2)]
        xB = [self.B() for _ in range(2)]
        sq, sqB, tr, trB = self.norm_rings(512)
        rstd = self.sb([P, 512], F32)
        rB = self.B()
        h = [self.sb([P, 8, 512], BF16) for _ in range(2)]
        hB = [self.B() for _ in range(2)]
        qst = [self.sb([P, 8, 512], BF16) for _ in range(2)]
        qB = [self.B() for _ in range(2)]
        gst = [self.sb([48, 512], F32) for _ in range(2)]
        gB = [self.B() for _ in range(2)]
        xv = S["xT"].rearrange("(j p) t -> p j t", p=P)
        ring = 0
        for c in range(NC):
            cs = slice(c * 512, (c + 1) * 512)
            x_, xb_ = xt[c % 2], xB[c % 2]
            self.ld(x_[:], xv[:, :, cs], [SB["xT"][c]], [xb_])
            h_, hb_ = h[c % 2], hB[c % 2]
            self.norm_mod(x_, xb_, 512, K["g1"][:, l, :], K["mod"][:, l, 0:8], KB["g1"], h_, hb_, sq, sqB, rstd, rB, 2, tout=tr, tB=trB)
            q_, qb_ = qst[c % 2], qB[c % 2]
            for m in range(8):
                pi = ring % 2
                ring += 1
                for kc in range(8):
                    self.mm(ps[pi][:, :], w[:, kc, m * 128:(m + 1) * 128], h_[:, kc, :], kc == 0, kc == 7,
                            [wB[m // 4], hb_], [psB[pi]])
                self.act(q_[:, m, :], ps[pi][:, :], AF.Copy, [psB[pi]], [qb_], scale=0.125)
            self.ld(S["qT"].rearrange("(m p) t -> p m t", p=P)[:, :, cs], q_[:], [qb_], [SB["qT"][c]])
            pi = ring % 2
            ring += 1
            for kc in range(8):
                self.mm(ps[pi][0:48, :], w[:, kc, 1024:1072], h_[:, kc, :], kc == 0, kc == 7, [wB[2], hb_], [psB[pi]])
            self.act(gst[c % 2][:], ps[pi][0:48, :], AF.Sigmoid, [psB[pi]], [gB[c % 2]])
            self.ld(S["gT"][:, cs], gst[c % 2][:], [gB[c % 2]], [SB["gT"][c]])
        self.phase_end()

    def phase_b_attn(self, l):
        I, K, KB, S, SB, ps, psB = self.I, self.K, self.KB, self.S, self.SB, self.ps, self.psB
        NC, NT, T = self.NC, self.NT, self.T
        pg = self.pg
        self.phase_begin()
        d0, d1, w4, dB = self.load_bias_tiles()
        ex = self.sb([P, NT, 128], BF16)
        exB = self.B()
        pg.op("pool", lambda e: e.memset(ex[64:128, :, :], 0.0), [], [exB])
        self.ld(ex[0:64, :, :], I["c_ex"][:, :, :], [exB], [exB], q="pool")
        ov = self.sb([P, 2, 64], F32)
        ovB = self.B()
        self.ld(ov[:], I["c_ov"][:, :, :], [], [ovB])
        maskc = self.sb([P, 2, T], BF16)
        mcB = self.B()
        self.ld(maskc[:], I["c_maskc"][:, :, :], [], [mcB], q="pool")
        keep = self.sb([P, NT, 64], BF16)
        addm = self.sb([P, NT, 64], BF16)
        kaB = [self.B(), self.B()]
        self.ld(keep[:], I["c_keep"][:, :, :], [], [kaB[0]], q="pool")
        self.ld(addm[:], I["c_add"][:, :, :], [], [kaB[1]], q="pool")
        ksl = self.sb([P, T], BF16)
        kwn = self.sb([P, T], BF16)
        vsl = self.sb([P, NT, 65], BF16)
        vwn = self.sb([P, NT, 65], BF16)
        gB_ = [self.B() for _ in range(4)]
        pg.op("pool", lambda e: e.memset(ksl[64:128, :], 0.0), [], [gB_[0]])
        pg.op("pool", lambda e: e.memset(kwn[64:128, :], 0.0), [], [gB_[1]])
        pg.op("pool", lambda e: e.memset(vsl[:], 1.0), [], [gB_[2]])
        pg.op("pool", lambda e: e.memset(vwn[:], 1.0), [], [gB_[3]])
        lfull = [self.sb([P, 512], F32) for _ in range(2)]
        lfB = [self.B() for _ in range(2)]
        for i in range(2):
            pg.op("dve", lambda e, i=i: e.memset(lfull[i][:], 0.0), [], [lfB[i]])
        qg = [self.sb([P, 4, 512], BF16) for _ in range(2)]
        qgB = [self.B() for _ in range(2)]
        for i in range(2):
            pg.op("pool", lambda e, i=i: e.memset(qg[i][64:128, :, :], 0.0), [], [qgB[i]])
        gb = self.sb([64, 12, 512], F32)
        gbB = self.B()
        pc32 = [self.sb([P, 512], F32) for _ in range(2)]
        pn32 = [self.sb([P, 512], F32) for _ in range(2)]
        pn16 = [self.sb([P, 512], BF16) for _ in range(2)]
        pcB = [self.B() for _ in range(2)]
        pnB = [self.B() for _ in range(2)]
        pn16B = [self.B() for _ in range(2)]
        rl = self.sb([P, 512], F32)
        rlB = self.B()
        oc = [self.sb([64, 4, 512], F32) for _ in range(2)]
        ocB = [[self.B() for _ in range(4)] for _ in range(2)]
        impv = self.sb([P, 64], F32)
        impv2 = self.sb([P, 64], F32)
        m8a = self.sb([P, 8], F32)
        m8b = self.sb([P, 8], F32)
        msel = self.sb([P, 4, 128], F32)
        tkB = {k: self.B() for k in ("impv", "impv2", "m8a", "m8b", "msel")}
        pg.op("dve", lambda e: e.memset(msel[:], 0.0), [], [tkB["msel"]])
        mT = [self.sb([P, 512], BF16) for _ in range(2)]
        mTB = [self.B() for _ in range(2)]
        NPT = 6
        SR = [0, 1, 4]
        OB = [2, 6]
        Pt = [self.sb([P, 512], BF16) for _ in range(NPT)]
        PB = [self.B() for _ in range(NPT)]
        rr = [self.sb([64, 512], F32) for _ in range(2)]
        rrB = [self.B() for _ in range(2)]
        acc = self.sb([64, 512], F32)
        accB = self.B()
        tmp = [self.sb([64, 512], F32) for _ in range(2)]
        tmpB = [self.B() for _ in range(2)]
        ost = [self.sb([64, 4, 512], BF16) for _ in range(2)]
        ostB = [self.B() for _ in range(2)]
        ctr = {"s": 0, "p": 0}
        blocks = [(g, qc) for g in range(4) for qc in range(NC)]
        CS = 5
        CI = 7

        def cmp_thunks(bi):
            g, qc = blocks[bi]
            p = bi % 2
            cs = slice(qc * 512, (qc + 1) * 512)
            q_, qb_ = qg[p], qgB[p]
            oc_, ocb_ = oc[p], ocB[p]
            mT_, mtb_ = mT[p], mTB[p]
            th = []
            th.append(lambda: self.ld(q_[0:64, :, :],
                                      S["qT"].rearrange("(h d) t -> d h t", d=64)[:, g * 4:(g + 1) * 4, cs],
                                      [SB["qT"][qc]], [qb_]))
            for r in range(4):
                for nt in range(2):
                    th.append(lambda r=r, nt=nt: self.mm(ps[CS][:, :], K["kcmpT"][:, g, nt * 128:(nt + 1) * 128],
                                                         q_[:, r, :], True, True, [KB["kcmpT"], qb_], [psB[CS]]))

                    def f_exp(r=r, nt=nt):
                        self.act(pc32[nt][:], ps[CS][:, :], AF.Exp, [psB[CS]], [pcB[nt]])
                        self.tt(pc32[nt][:], pc32[nt][:], maskc[:, nt, cs], ALU.mult, [pcB[nt], mcB], [pcB[nt]])
                    th.append(f_exp)

                def f_l(r=r):
                    for nt in range(2):
                        self.mm(ps[CS][:, :], K["ones32"][:, :], pc32[nt][:], nt == 0, nt == 1,
                                [KB["ones32"], pcB[nt]], [psB[CS]])
                th.append(f_l)
                th.append(lambda: self.ts(rl[:], ps[CS][:, :], 1e-18, None, ALU.max, None, [psB[CS]], [rlB]))
                th.append(lambda: self.act(rl[:], rl[:], AF.Ln, [rlB], [rlB]))
                th.append(lambda: self.act(rl[:], rl[:], AF.Exp, [rlB], [rlB], scale=-1.0))

                def f_pn():
                    for nt in range(2):
                        self.tt(pn32[nt][:], pc32[nt][:], rl[:], ALU.mult, [pcB[nt], rlB], [pnB[nt]])
                        self.cp(pn16[nt][:], pn32[nt][:], [pnB[nt]], [pn16B[nt]], eng="pool")
                th.append(f_pn)

                def f_imp(r=r):
                    for nt in range(2):
                        for i in range(4):
                            first = (r == 0 and nt == 0 and i == 0)
                            last = (r == 3 and nt == 1 and i == 3)
                            self.mm(ps[CI][:, i * 64:(i + 1) * 64], pn32[nt][:, i * 128:(i + 1) * 128], ov[:, nt, :],
                                    first, last, [pnB[nt], ovB], [psB[CI]])
                th.append(f_imp)

                def f_pv(r=r):
                    for nt in range(2):
                        self.mm(ps[CS][:, :], K["vcmp"][:, g, nt, :], pn16[nt][:], nt == 0, nt == 1,
                                [KB["vcmp"], pn16B[nt]], [psB[CS]])
                th.append(f_pv)
                th.append(lambda r=r: self.cp(oc_[:, r, :], ps[CS][0:64, :], [psB[CS]], [ocb_[r]]))
            for i in range(4):
                qb = qc * 4 + i

                def f_k1(i=i, qb=qb):
                    self.tt(impv[:], ps[CI][:, i * 64:(i + 1) * 64], keep[:, qb, :], ALU.mult, [psB[CI], kaB[0]],
                            [tkB["impv"]])
                    self.tt(impv[:], impv[:], addm[:, qb, :], ALU.add, [tkB["impv"], kaB[1]], [tkB["impv"]])
                th.append(f_k1)
                th.append(lambda: pg.op("dve", lambda e: e.max(out=m8a[:], in_=impv[:]), [tkB["impv"]],
                                        [tkB["m8a"]]))
                th.append(lambda: pg.op("dve", lambda e: e.match_replace(out=impv2[:], in_to_replace=m8a[:],
                                                                          in_values=impv[:], imm_value=-3.0e38),
                                        [tkB["impv"], tkB["m8a"]], [tkB["impv2"]]))
                th.append(lambda: pg.op("dve", lambda e: e.max(out=m8b[:], in_=impv2[:]), [tkB["impv2"]],
                                        [tkB["m8b"]]))
                th.append(lambda i=i: self.ts(msel[:, i, 0:64], impv[:], m8b[:, 7:8], None, ALU.is_ge, None,
                                              [tkB["impv"], tkB["m8b"]], [tkB["msel"]]))

            def f_tr():
                for i in range(4):
                    pg.op("pe", lambda e, i=i: e.transpose(out=ps[CI][:, i * 128:(i + 1) * 128], in_=msel[:, i, :],
                                                           identity=K["ident"][:, :]),
                          [tkB["msel"], KB["ident"]], [psB[CI]])
            th.append(f_tr)
            th.append(lambda: self.ts(mT_[:], ps[CI][:, 0:512], -1.0, 30000.0, ALU.add, ALU.mult, [psB[CI]], [mtb_]))
            return th

        def emit_tiles(bi, bg):
            g, qc = blocks[bi]
            p = bi % 2
            cs = slice(qc * 512, (qc + 1) * 512)
            q_, qb_ = qg[p], qgB[p]
            oc_, ocb_ = oc[p], ocB[p]
            mT_, mtb_ = mT[p], mTB[p]
            o_st, o_stB = ost[p], ostB[p]
            r0 = g * 64
            if qc == 0:
                self.ld(ksl[0:64, :], S["kvT"][512 + r0:512 + r0 + 64, :], SB["kvT"], [gB_[0]])
                self.ld(kwn[0:64, :], S["kvT"][1024 + r0:1024 + r0 + 64, :], SB["kvT"], [gB_[1]])
                self.ld(vsl[:, :, 0:64], S["vslc"].rearrange("(kt p) e -> p kt e", p=P)[:, :, r0:r0 + 64],
                        SB["vslc"], [gB_[2]])
                self.ld(vwn[:, :, 0:64], S["vwin"].rearrange("(kt p) e -> p kt e", p=P)[:, :, r0:r0 + 64],
                        SB["vwin"], [gB_[3]])
            self.ld(gb[:], bass.AP(S["gT"].tensor, g * 12 * T + qc * 512, [[0, 64], [T, 12], [1, 512]]),
                    [SB["gT"][qc]], [gbB])
            loops = []
            for r in range(4):
                for sel in (True, False):
                    if sel:
                        tl = list(range(0, 4 * qc + 4))
                        kT_, kB_, vT_, vB_ = ksl, gB_[0], vsl, gB_[2]
                    else:
                        tl = list(range(max(0, 4 * qc - 4), 4 * qc + 4))
                        kT_, kB_, vT_, vB_ = kwn, gB_[1], vwn, gB_[3]
                    loops.append(dict(r=r, sel=sel, tiles=tl, kT=kT_, kB=kB_, vT=vT_, vB=vB_,
                                      ob=OB[len(loops) % 2], par=len(loops) % 2, st={}))

            def rng(L, kt):
                c0 = max(0, kt - 4 * qc) * 128
                c1 = 512 if L["sel"] else min(4, kt + 5 - 4 * qc) * 128
                return c0, c1

            def stageA(L, kt):
                h = g * 4 + L["r"]
                si = SR[ctr["s"] % len(SR)]
                ctr["s"] += 1
                L["st"][kt] = si
                c0, c1 = rng(L, kt)
                extra = []
                if L["sel"]:
                    extra.append((c0, c1, ex[:, kt, :], mT_[:, c0:c1], [exB, mtb_]))
                for ii in range(c0 // 128, c1 // 128):
                    delta = 4 * qc + ii - kt
                    dd = None
                    if delta == 0:
                        dd = d0[:, h, :]
                    elif delta == 1:
                        dd = d1[:, h, :]
                    elif delta == 4 and not L["sel"]:
                        dd = w4[:, :]
                    if dd is not None:
                        extra.append((ii * 128, (ii + 1) * 128, K["ident"][:, :], dd, [KB["ident"]] + dB))
                self.mm(ps[si][:, c0:c1], L["kT"][:, kt * 128:(kt + 1) * 128], q_[:, L["r"], c0:c1], True,
                        len(extra) == 0, [L["kB"], qb_], [psB[si]])
                for xi, (a0, a1, lh, rh, rd) in enumerate(extra):
                    self.mm(ps[si][:, a0:a1], lh, rh, False, xi == len(extra) - 1, rd, [psB[si]])

            def stageB(L, kt, idx):
                si = L["st"][kt]
                c0, c1 = rng(L, kt)
                pi = ctr["p"] % NPT
                ctr["p"] += 1
                n = len(L["tiles"])
                self.act(Pt[pi][:, c0:c1], ps[si][:, c0:c1], AF.Exp, [psB[si]], [PB[pi]])
                self.mm(ps[L["ob"]][0:65, c0:c1], L["vT"][:, kt, :], Pt[pi][:, c0:c1], idx == 0, idx == n - 1,
                        [L["vB"], PB[pi]], [psB[L["ob"]]])

            def epi1(L):
                lf, lb_ = lfull[L["par"]], lfB[L["par"]]
                ob = L["ob"]
                self.act(lf[64:65, :], ps[ob][64:65, :], AF.Ln, [psB[ob]], [lb_])
                self.act(lf[64:65, :], lf[64:65, :], AF.Exp, [lb_], [lb_], scale=-1.0)

            def epi2(L):
                lf, lb_ = lfull[L["par"]], lfB[L["par"]]
                ob = L["ob"]
                r = L["r"]
                rr_, rrb_ = rr[L["par"]], rrB[L["par"]]
                tmp_, tmpb_ = tmp[L["par"]], tmpB[L["par"]]
                self.mm(ps[3][:, :], K["sel64"][:, :], lf[:, :], True, True, [KB["sel64"], lb_], [psB[3]])
                self.cp(rr_[:], ps[3][0:64, :], [psB[3]], [rrb_])
                self.tt(tmp_[:], ps[ob][0:64, :], rr_[:], ALU.mult, [psB[ob], rrb_], [tmpb_])
                if L["sel"]:
                    self.tt(acc[:], oc_[:, r, :], gb[:, r * 3 + 0, :], ALU.mult, [ocb_[r], gbB], [accB], eng="pool")
                    self.tt(tmp_[:], tmp_[:], gb[:, r * 3 + 1, :], ALU.mult, [tmpb_, gbB], [tmpb_])
                    self.tt(acc[:], acc[:], tmp_[:], ALU.add, [accB, tmpb_], [accB])
                else:
                    self.tt(tmp_[:], tmp_[:], gb[:, r * 3 + 2, :], ALU.mult, [tmpb_, gbB], [tmpb_])
                    self.tt(o_st[:, r, :], acc[:], tmp_[:], ALU.add, [accB, tmpb_], [o_stB])

            flat = [(L, kt, idx) for L in loops for idx, kt in enumerate(L["tiles"])]
            nflat = len(flat)
            depth = 2
            pending = []
            bgq = list(bg)
            for i in range(min(depth, nflat)):
                stageA(flat[i][0], flat[i][1])
            for i in range(nflat):
                if i + depth < nflat:
                    stageA(flat[i + depth][0], flat[i + depth][1])
                L, kt, idx = flat[i]
                stageB(L, kt, idx)
                if idx == len(L["tiles"]) - 1:
                    epi1(L)
                    pending.append((i + 3, L))
                while pending and pending[0][0] <= i:
                    epi2(pending.pop(0)[1])
                if bgq:
                    bgq.pop(0)()
            while pending:
                epi2(pending.pop(0)[1])
            for t_ in bgq:
                t_()
            self.ld(S["oT"].rearrange("(h d) t -> d h t", d=64)[:, g * 4:(g + 1) * 4, cs], o_st[:], [o_stB],
                    [SB["oT"][qc]])

        for t_ in cmp_thunks(0):
            t_()
        for bi in range(len(blocks)):
            nxt = cmp_thunks(bi + 1) if bi + 1 < len(blocks) else []
            emit_tiles(bi, nxt)
        self.phase_end()


_orig_op = Prog.op


def _op(self, eng, fn, reads=(), writes=()):
    if eng == "act_copy":
        return _orig_op(self, "act", fn, reads, writes)
    return _orig_op(self, eng, fn, reads, writes)


Prog.op = _op
_orig_cp = Builder.cp


def _cp(self, out, in_, reads, writes, eng="dve"):
    if eng == "act_copy":
        self.pg.op("act", lambda e: e.activation(out=out, in_=in_, func=AF.Copy), reads, writes)
    else:
        _orig_cp(self, out, in_, reads, writes, eng)


Builder.cp = _cp


def col8(v):
    v = np.asarray(v, np.float32)
    return np.ascontiguousarray(np.moveaxis(v.reshape(v.shape[:-1] + (v.shape[-1] // 128, 128)), -1, 0))


def make_in_maps(inputs, T):
    x = np.asarray(inputs["x"], np.float32)
    B = x.shape[0]
    shared = {}
    f = lambda k: np.ascontiguousarray(np.asarray(inputs[k], np.float32))
    shared["rel_bias"] = f("rel_bias")
    shared["ada_w"] = f("ada_w")
    shared["adab"] = np.ascontiguousarray(f("ada_b").reshape(NL, 48, 128).transpose(0, 2, 1))
    shared["an"] = col8(f("attn_norm"))
    shared["mn"] = col8(f("mlp_norm"))
    shared["mlp_w1"] = f("mlp_w1")
    shared["mlp_w2"] = f("mlp_w2")
    shared["a_w_in"] = f("a_w_in")
    shared["a_w_out"] = f("a_w_out")
    shared["a_lambda"] = f("a_lambda").reshape(2, 256)
    shared["a_subln"] = np.ascontiguousarray(f("a_subln").T)
    shared["kv_ada_w"] = f("kv_ada_w")
    shared["kvadab"] = np.ascontiguousarray(f("kv_ada_b").reshape(16, 128).T)
    shared["kvn"] = col8(f("kv_norm"))
    shared["w_kv"] = f("w_kv")
    shared["cmp_posT"] = np.ascontiguousarray(f("cmp_pos").transpose(0, 2, 1))
    shared["cmp_w1"] = f("cmp_w1")
    shared["cmp_w2"] = f("cmp_w2")
    shared["b_w_in"] = f("b_w_in")
    shared["b_w_out"] = f("b_w_out")
    shared["fnorm"] = col8(f("final_norm"))
    shared.update(make_consts(T))
    maps = []
    c = np.asarray(inputs["c"], np.float32)
    for b in range(B):
        m = dict(shared)
        m["xT"] = np.ascontiguousarray(x[b].T)
        m["cT"] = np.ascontiguousarray(c[b].reshape(8, 128).T)
        maps.append(m)
    return maps


_CACHE = {}


def run(inputs, T, layers=NL, debug=None):
    key = (T, layers, tuple(debug) if debug else None)
    if key not in _CACHE:
        _CACHE[key] = Builder(T, layers, debug).build()
    nc = _CACHE[key]
    maps = make_in_maps(inputs, T)
    res = run_bass_kernel_spmd(nc, maps, core_ids=list(range(len(maps))))
    return res.results


def kernel(**inputs):
    T = int(np.asarray(inputs["x"]).shape[1])
    results = run(inputs, T)
    out = np.stack([np.ascontiguousarray(r["outT"].T) for r in results], axis=0)
    return out.astype(np.float32)
```

```python
import math
from contextlib import ExitStack

import numpy as np
import ml_dtypes

import concourse.bass as bass
import concourse.mybir as mybir
from concourse.bass_utils import run_bass_kernel_spmd

F32 = mybir.dt.float32
BF16 = mybir.dt.bfloat16
AF = mybir.ActivationFunctionType
ALU = mybir.AluOpType
AX = mybir.AxisListType

D = 1024
DFF = 4096
NL = 4
EPS = 1e-6
NEG = -1e30
P = 128


class Buf:
    __slots__ = ("name", "w", "r")

    def __init__(self, name):
        self.name = name
        self.w = []
        self.r = []


class Op:
    __slots__ = ("eng", "fn", "deps", "dma", "slot", "target", "val", "waits")

    def __init__(self, eng, fn, deps, dma):
        self.eng = eng
        self.fn = fn
        self.deps = deps
        self.dma = dma
        self.slot = None
        self.target = False
        self.val = 0
        self.waits = []


class Prog:
    ENGS = ("pe", "act", "dve", "pool", "sp")
    NSLOT = {"sp": 24, "pool": 8, "act": 4}

    def __init__(self, nc):
        self.nc = nc
        self.ops = []
        self.rr = {q: 0 for q in self.NSLOT}
        self.slot_last = {}
        self.last_on_eng = {}

    def _reduce(self, ids):
        best = {}
        out = set()
        for i in ids:
            o = self.ops[i]
            if o.dma:
                out.add(i)
            else:
                if o.eng not in best or best[o.eng] < i:
                    best[o.eng] = i
        out.update(best.values())
        return out

    def _mk(self, eng, fn, reads, writes, dma):
        deps = set()
        for b in reads:
            deps.update(b.w)
        for b in writes:
            deps.update(b.w)
            deps.update(b.r)
        gid = len(self.ops)
        op = Op(eng, fn, self._reduce(deps), dma)
        if dma:
            s = self.rr[eng]
            self.rr[eng] = (s + 1) % self.NSLOT[eng]
            op.slot = (eng, s)
            prev = self.slot_last.get(op.slot)
            if prev is not None:
                op.deps.add(prev)
            self.slot_last[op.slot] = gid
        self.ops.append(op)
        wset = set(id(b) for b in writes)
        for b in writes:
            b.w = [gid]
            b.r = []
        for b in reads:
            if id(b) not in wset:
                b.r.append(gid)
                if len(b.r) > 12:
                    b.r = list(self._reduce(b.r))
        self.last_on_eng[eng if not dma else ("dma", gid)] = gid
        return gid

    def op(self, eng, fn, reads=(), writes=()):
        return self._mk(eng, fn, reads, writes, False)

    def dma(self, q, fn, reads=(), writes=()):
        return self._mk(q, fn, reads, writes, True)

    def barrier(self):
        allb = Buf("barrier")
        ids = [i for i, o in enumerate(self.ops)]
        last = {}
        dmas = []
        for i in range(len(self.ops) - 1, -1, -1):
            o = self.ops[i]
            if o.dma:
                if o.slot not in last:
                    last[o.slot] = i
                    dmas.append(i)
            elif o.eng not in last:
                last[o.eng] = i
                dmas.append(i)
        allb.w = dmas
        nc = self.nc
        z = self._bar_tiles
        self.op("pe", lambda e: e.matmul(z["ps"][0:1, 0:2], z["bf"][0:1, 0:1], z["bf"][0:1, 0:2],
                                          start=True, stop=True), reads=[allb], writes=[z["b_pe"]])
        self.op("act", lambda e: e.activation(out=z["a"][0:1, 0:1], in_=z["src"][0:1, 0:1], func=AF.Copy),
                reads=[allb], writes=[z["b_act"]])
        self.op("dve", lambda e: e.tensor_copy(out=z["v"][0:1, 0:1], in_=z["src"][0:1, 0:1]),
                reads=[allb], writes=[z["b_dve"]])
        self.op("pool", lambda e: e.tensor_copy(out=z["g"][0:1, 0:1], in_=z["src"][0:1, 0:1]),
                reads=[allb], writes=[z["b_pool"]])
        self.dma("sp", lambda e: e.dma_start(out=z["s"][0:1, 0:1], in_=z["src"][0:1, 0:1]),
                 reads=[allb], writes=[z["b_sp"]])

    def emit(self, stack):
        nc = self.nc
        ops = self.ops
        comp = ("pe", "act", "dve", "pool")
        sems = {e: stack.enter_context(nc.semaphore("sem_" + e)) for e in comp}
        dsem = {}
        for q, n in self.NSLOT.items():
            for s in range(n):
                dsem[(q, s)] = stack.enter_context(nc.semaphore("dsem_%s_%d" % (q, s)))
        for o in ops:
            for d in o.deps:
                t = ops[d]
                if t.dma:
                    continue
                if o.eng == "pe" and t.eng == "pe" and not o.dma:
                    continue
                t.target = True
        cnt = {e: 0 for e in comp}
        dcnt = {k: 0 for k in dsem}
        for o in ops:
            if o.dma:
                dcnt[o.slot] += 16
                o.val = dcnt[o.slot]
            else:
                if o.target:
                    cnt[o.eng] += 1
                o.val = cnt[o.eng]
        known = {e: {} for e in self.ENGS}
        clocks = {}
        for gid, o in enumerate(ops):
            kn = known[o.eng]
            m = {}
            for d in sorted(o.deps):
                t = ops[d]
                if t.dma:
                    key = ("d", t.slot)
                    sem = dsem[t.slot]
                else:
                    if o.eng == "pe" and t.eng == "pe" and not o.dma:
                        continue
                    key = ("e", t.eng)
                    sem = sems[t.eng]
                if kn.get(key, 0) >= t.val:
                    continue
                if key not in m or m[key][1] < t.val:
                    m[key] = (sem, t.val, d)
            for key, (sem, v, d) in sorted(m.items(), key=lambda kv: -kv[1][2]):
                if kn.get(key, 0) >= v:
                    continue
                o.waits.append((key, sem, v))
                for k2, v2 in clocks[d].items():
                    if kn.get(k2, 0) < v2:
                        kn[k2] = v2
            if o.dma or o.target:
                c = dict(kn)
                if o.dma:
                    c[("d", o.slot)] = o.val
                else:
                    k = ("e", o.eng)
                    if c.get(k, 0) < o.val:
                        c[k] = o.val
                clocks[gid] = c
        streams = {e: [] for e in self.ENGS}
        for o in ops:
            streams[o.eng].append(o)
        final = [(dsem[k], v) for k, v in dcnt.items() if v > 0]

        def run(eng_name, e):
            for o in streams[eng_name]:
                for _, sem, v in o.waits:
                    e.wait_ge(sem, v)
                ins = o.fn(e)
                if o.dma:
                    ins.then_inc(dsem[o.slot], 16)
                elif o.target:
                    ins.then_inc(sems[o.eng], 1)
            if eng_name == "sp":
                for sem, v in final:
                    e.wait_ge(sem, v)

        with nc.Block() as block:
            @block.tensor
            def _(e):
                run("pe", e)

            @block.scalar
            def _(e):
                run("act", e)

            @block.vector
            def _(e):
                run("dve", e)

            @block.gpsimd
            def _(e):
                run("pool", e)

            @block.sync
            def _(e):
                run("sp", e)


def _t5_bucket_np(dist):
    n = np.maximum(dist, 0)
    nf = np.maximum(n, 1).astype(np.float32)
    large = 16 + (np.log(nf / np.float32(16)) / np.float32(math.log(8.0)) * np.float32(16)).astype(np.int32)
    large = np.minimum(large, 31)
    return np.where(n < 16, n, large)


def make_consts(T):
    NT = T // 128
    c = {}
    oh = np.zeros((33, 384), np.float32)
    for i in range(384):
        dist = i - 127
        if dist < 0:
            oh[32, i] = 1.0
        else:
            oh[int(_t5_bucket_np(np.array(dist))), i] += 1.0
            oh[31, i] -= 1.0
    c["c_oh"] = oh
    ki = np.arange(128)[:, None]
    qi = np.arange(128)[None, :]
    c["c_w4"] = np.where(ki > qi, 0.0, NEG).astype(np.float32)
    c["c_ident"] = np.eye(128, dtype=np.float32)
    ex = np.zeros((64, NT, 128), np.float32)
    for kt in range(NT):
        for k in range(128):
            ex[2 * kt + k // 64, kt, k] = 1.0
    c["c_ex"] = ex
    n_cmp = (T - 32) // 16 + 1
    n_slc = T // 64
    n = np.arange(256)
    cs = n * 16
    ce = cs + 31
    ss = np.arange(n_slc) * 64
    ov = ((cs[:, None] < ss[None, :] + 64) & (ce[:, None] >= ss[None, :]) & (n[:, None] < n_cmp)).astype(np.float32)
    ovp = np.zeros((256, 64), np.float32)
    ovp[:, :n_slc] = ov
    c["c_ov"] = np.ascontiguousarray(ovp.reshape(2, 128, 64).transpose(1, 0, 2))
    q = np.arange(T)
    mc = ((ce[:, None] <= q[None, :]) & (n[:, None] < n_cmp)).astype(np.float32)
    c["c_maskc"] = np.ascontiguousarray(mc.reshape(2, 128, T).transpose(1, 0, 2))
    j = np.arange(64)[None, :]
    qb = (q // 64)[:, None]
    forced = (j == 0) | ((j <= qb) & (j > qb - 2))
    valid = (j <= qb) & (j < n_slc)
    keep = (~forced & valid).astype(np.float32)
    add = np.where(valid, np.where(forced, 1e4, 0.0), NEG).astype(np.float32)
    c["c_keep"] = np.ascontiguousarray(keep.reshape(NT, 128, 64).transpose(1, 0, 2))
    c["c_add"] = np.ascontiguousarray(add.reshape(NT, 128, 64).transpose(1, 0, 2))
    return c


class Builder:
    def __init__(self, T, layers=NL, debug=False):
        self.T = T
        self.NT = T // 128
        self.NC = T // 512
        self.layers = layers
        self.debug = debug
        self.nc = bass.Bass("TRN2", target_bir_lowering=False)
        self.pg = Prog(self.nc)
        self.gstack = ExitStack()
        self.pstack = None
        self.uid = 0

    def dram_in(self, name, shape, dt=F32):
        return self.nc.dram_tensor(name, list(shape), dt, kind="ExternalInput").ap()

    def dram_out(self, name, shape, dt=F32):
        return self.nc.dram_tensor(name, list(shape), dt, kind="ExternalOutput").ap()

    def dram(self, name, shape, dt):
        return self.nc.dram_tensor(name, list(shape), dt).ap()

    def sb(self, shape, dt, persistent=False, name=None):
        self.uid += 1
        st = self.gstack if persistent else self.pstack
        return st.enter_context(self.nc.sbuf_tensor("%s_%d" % (name or "t", self.uid), list(shape), dt))

    def B(self, name="b"):
        self.uid += 1
        return Buf("%s%d" % (name, self.uid))

    def phase_begin(self):
        self.pstack = ExitStack()

    def phase_end(self):
        self.pg.barrier()
        self.pstack.close()
        self.pstack = None

    def mm(self, out, lhsT, rhs, start, stop, reads, writes):
        self.pg.op("pe", lambda e: e.matmul(out, lhsT, rhs, start=start, stop=stop), reads, writes)

    def act(self, out, in_, func, reads, writes, bias=None, scale=None):
        kw = {}
        if bias is not None:
            kw["bias"] = bias
        if scale is not None:
            kw["scale"] = scale
        self.pg.op("act", lambda e: e.activation(out=out, in_=in_, func=func, **kw), reads, writes)

    def tt(self, out, in0, in1, op, reads, writes, eng="dve"):
        self.pg.op(eng, lambda e: e.tensor_tensor(out=out, in0=in0, in1=in1, op=op), reads, writes)

    def ts(self, out, in0, s1, s2, op0, op1, reads, writes, eng="dve"):
        if op1 is None:
            self.pg.op(eng, lambda e: e.tensor_scalar(out=out, in0=in0, scalar1=s1, scalar2=None, op0=op0),
                       reads, writes)
        else:
            self.pg.op(eng, lambda e: e.tensor_scalar(out=out, in0=in0, scalar1=s1, scalar2=s2, op0=op0, op1=op1),
                       reads, writes)

    def stt(self, out, in0, scalar, in1, op0, op1, reads, writes):
        self.pg.op("dve", lambda e: e.scalar_tensor_tensor(out=out, in0=in0, scalar=scalar, in1=in1,
                                                          op0=op0, op1=op1), reads, writes)

    def cp(self, out, in_, reads, writes, eng="dve"):
        self.pg.op(eng, lambda e: e.tensor_copy(out=out, in_=in_), reads, writes)

    def ld(self, out, in_, reads, writes, q="sp"):
        self.pg.dma(q, lambda e: e.dma_start(out=out, in_=in_), reads, writes)

    def build(self):
        nc, pg, T, NT, NC = self.nc, self.pg, self.T, self.NT, self.NC
        g = self.gstack
        I = {}
        I["xT"] = self.dram_in("xT", [D, T])
        I["cT"] = self.dram_in("cT", [P, 8])
        I["rel_bias"] = self.dram_in("rel_bias", [32, 16])
        I["ada_w"] = self.dram_in("ada_w", [NL, D, 6 * D])
        I["adab"] = self.dram_in("adab", [NL, P, 48])
        I["an"] = self.dram_in("an", [P, NL, 8])
        I["mn"] = self.dram_in("mn", [P, NL, 8])
        I["mlp_w1"] = self.dram_in("mlp_w1", [NL, D, DFF])
        I["mlp_w2"] = self.dram_in("mlp_w2", [NL, DFF, D])
        I["a_w_in"] = self.dram_in("a_w_in", [2, D, 3 * D])
        I["a_w_out"] = self.dram_in("a_w_out", [2, D, D])
        I["a_lambda"] = self.dram_in("a_lambda", [2, 256])
        I["a_subln"] = self.dram_in("a_subln", [P, 2])
        I["kv_ada_w"] = self.dram_in("kv_ada_w", [D, 2 * D])
        I["kvadab"] = self.dram_in("kvadab", [P, 16])
        I["kvn"] = self.dram_in("kvn", [P, 8])
        I["w_kv"] = self.dram_in("w_kv", [D, 1536])
        I["cmp_posT"] = self.dram_in("cmp_posT", [2, 64, 32])
        I["cmp_w1"] = self.dram_in("cmp_w1", [2, 2048, 256])
        I["cmp_w2"] = self.dram_in("cmp_w2", [2, 256, 64])
        I["b_w_in"] = self.dram_in("b_w_in", [2, D, 1072])
        I["b_w_out"] = self.dram_in("b_w_out", [2, D, D])
        I["fnorm"] = self.dram_in("fnorm", [P, 8])
        I["c_oh"] = self.dram_in("c_oh", [33, 384])
        I["c_w4"] = self.dram_in("c_w4", [P, 128])
        I["c_ident"] = self.dram_in("c_ident", [P, 128])
        I["c_ex"] = self.dram_in("c_ex", [64, NT, 128])
        I["c_ov"] = self.dram_in("c_ov", [P, 2, 64])
        I["c_maskc"] = self.dram_in("c_maskc", [P, 2, T])
        I["c_keep"] = self.dram_in("c_keep", [P, NT, 64])
        I["c_add"] = self.dram_in("c_add", [P, NT, 64])
        self.I = I
        outT = self.dram_out("outT", [D, T])
        S = {}
        S["xT"] = self.dram("s_xT", [D, T], F32)
        S["qT"] = self.dram("s_qT", [D, T], BF16)
        S["kT"] = self.dram("s_kT", [D, T], BF16)
        S["vtok"] = self.dram("s_vtok", [T, D], BF16)
        S["oT"] = self.dram("s_oT", [D, T], BF16)
        S["kvT"] = self.dram("s_kvT", [1536, T], BF16)
        S["vslc"] = self.dram("s_vslc", [T, 256], BF16)
        S["vwin"] = self.dram("s_vwin", [T, 256], BF16)
        S["gT"] = self.dram("s_gT", [48, T], F32)
        S["tT"] = self.dram("s_tT", [16, 384], F32)
        S["d0"] = self.dram("s_d0", [P, 16, 128], F32)
        S["d1"] = self.dram("s_d1", [P, 16, 128], F32)
        self.S = S
        SB = {k: [self.B(k) for _ in range(NC)] for k in ("xT", "qT", "kT", "vtok", "oT", "kvT", "vslc", "vwin", "gT")}
        for k in ("tT", "d0", "d1"):
            SB[k] = [self.B(k)]
        self.SB = SB
        PS = g.enter_context(nc.psum_tensor("psall", [P, 4096], F32))
        self.PS = PS
        ps = [PS[:, i * 512:(i + 1) * 512] for i in range(8)]
        self.ps = ps
        self.psB = [self.B("ps") for _ in range(8)]
        K = {}
        K["ones_bf"] = self.sb([P, 128], BF16, True, "ones")
        K["onesD"] = self.sb([P, 128], BF16, True, "onesD")
        K["onesH"] = self.sb([P, 128], BF16, True, "onesH")
        K["ones32"] = self.sb([P, 128], F32, True, "ones32")
        K["ident"] = self.sb([P, 128], F32, True, "ident")
        K["cact"] = self.sb([P, 8], F32, True, "cact")
        K["mod"] = self.sb([P, NL, 48], F32, True, "mod")
        K["kvmod"] = self.sb([P, 16], F32, True, "kvmod")
        K["g1"] = self.sb([P, NL, 8], F32, True, "g1")
        K["g2"] = self.sb([P, NL, 8], F32, True, "g2")
        K["gkv"] = self.sb([P, 8], F32, True, "gkv")
        K["fn"] = self.sb([P, 8], F32, True, "fn")
        K["zero8"] = self.sb([P, 8], F32, True, "zero8")
        K["b31"] = self.sb([P, 16], F32, True, "b31")
        K["lamneg"] = self.sb([P, 2], F32, True, "lamneg")
        K["subg"] = self.sb([P, 2], F32, True, "subg")
        K["kcmpT"] = self.sb([P, 4, 256], BF16, True, "kcmpT")
        K["vcmp"] = self.sb([P, 4, 2, 128], BF16, True, "vcmp")
        K["sel64"] = self.sb([P, 128], F32, True, "sel64")
        K["bar"] = self.sb([P, 16], F32, True, "bar")
        K["barbf"] = self.sb([P, 4], BF16, True, "barbf")
        self.K = K
        KB = {k: self.B(k) for k in K}
        self.KB = KB
        pg._bar_tiles = dict(ps=ps[6], bf=K["barbf"], src=K["bar"][:, 0:1], a=K["bar"][:, 1:2], v=K["bar"][:, 2:3],
                             g=K["bar"][:, 3:4], s=K["bar"][:, 4:5], b_pe=self.psB[6], b_act=self.B(), b_dve=self.B(),
                             b_pool=self.B(), b_sp=self.B())

        self.phase_setup()
        for l in range(self.layers):
            if l < 2:
                self.phase_a_proj(l)
                self.phase_a_attn(l)
                wo = I["a_w_out"][l]
            else:
                self.phase_b_proj(l)
                self.phase_b_attn(l)
                wo = I["b_w_out"][l - 2]
            self.phase_outproj(l, wo)
            self.phase_mlp(l)
            if l == 1:
                self.phase_kv()
                self.phase_cmp()
        self.phase_final(outT)
        if self.debug:
            dbg = {}
            for k in self.debug:
                t = S[k]
                o = self.dram_out("dbg_" + k, list(t.shape), t.dtype)
                self.ld(o, t, reads=SB[k], writes=[self.B()])
        pg.emit(g)
        return nc

    def phase_setup(self):
        nc, pg, I, K, KB, S, SB = self.nc, self.pg, self.I, self.K, self.KB, self.S, self.SB
        ps, psB = self.ps, self.psB
        self.phase_begin()
        pg.op("dve", lambda e: e.memset(K["bar"][:], 0.0), [], [KB["bar"]])
        pg.op("dve", lambda e: e.memset(K["barbf"][:], 0.0), [], [KB["barbf"]])
        pg.op("dve", lambda e: e.memset(K["ones_bf"][:], 1.0), [], [KB["ones_bf"]])
        pg.op("dve", lambda e: e.memset(K["onesD"][:], 1.0 / 1024), [], [KB["onesD"]])
        pg.op("dve", lambda e: e.memset(K["onesH"][:], 1.0 / 128), [], [KB["onesH"]])
        pg.op("dve", lambda e: e.memset(K["ones32"][:], 1.0), [], [KB["ones32"]])
        pg.op("dve", lambda e: e.memset(K["zero8"][:], 0.0), [], [KB["zero8"]])
        pg.op("dve", lambda e: e.memset(K["sel64"][:], 0.0), [], [KB["sel64"]])
        pg.op("dve", lambda e: e.memset(K["sel64"][64:65, :], 1.0), [KB["sel64"]], [KB["sel64"]])
        pg.op("dve", lambda e: e.memset(K["vcmp"][:], 0.0), [], [KB["vcmp"]])
        pg.op("dve", lambda e: e.memset(K["kcmpT"][:], 0.0), [], [KB["kcmpT"]])
        self.ld(K["ident"][:], I["c_ident"][:, :], [], [KB["ident"]])
        self.ld(K["fn"][:], I["fnorm"][:, :], [], [KB["fn"]])
        for c in range(self.NC):
            cs = slice(c * 512, (c + 1) * 512)
            self.ld(S["xT"][:, cs], I["xT"][:, cs], [], [SB["xT"][c]])
        craw = self.sb([P, 8], F32)
        b_craw = self.B()
        self.ld(craw[:], I["cT"][:, :], [], [b_craw])
        csig = self.sb([P, 8], F32)
        b_csig = self.B()
        self.act(csig[:], craw[:], AF.Sigmoid, [b_craw], [b_csig])
        self.tt(K["cact"][:], craw[:], csig[:], ALU.mult, [b_craw, b_csig], [KB["cact"]])
        NW = 8
        wt = [self.sb([P, 8, 512], BF16) for _ in range(NW)]
        wtB = [self.B() for _ in range(NW)]
        cact16 = self.sb([P, 8], BF16)
        b_c16 = self.B()
        self.cp(cact16[:], K["cact"][:], [KB["cact"]], [b_c16])
        adab = self.sb([P, NL, 48], F32)
        b_adab = self.B()
        self.ld(adab[:], I["adab"].rearrange("l p j -> p l j"), [], [b_adab])
        kvadab = self.sb([P, 16], F32)
        b_kvadab = self.B()
        self.ld(kvadab[:], I["kvadab"][:, :], [], [b_kvadab])
        blk = 0
        jobs = [(I["ada_w"][l], 12, l) for l in range(NL)] + [(I["kv_ada_w"], 4, None)]
        for (wsrc, nblk, l) in jobs:
            pacc = ps[0]
            for bi in range(nblk):
                w = wt[blk % NW]
                wb = wtB[blk % NW]
                blk += 1
                self.ld(w[:], wsrc.rearrange("(kc p) n -> p kc n", p=P)[:, :, bi * 512:(bi + 1) * 512], [], [wb],
                        q="pool")
                for jj in range(4):
                    j = bi * 4 + jj
                    for kc in range(8):
                        self.mm(pacc[:, j:j + 1], w[:, kc, jj * 128:(jj + 1) * 128], cact16[:, kc:kc + 1],
                                kc == 0, kc == 7, [wb, b_c16], [psB[0]])
            if l is not None:
                self.tt(K["mod"][:, l, :], pacc[:, 0:48], adab[:, l, :], ALU.add, [psB[0], b_adab], [KB["mod"]])
            else:
                self.tt(K["kvmod"][:], pacc[:, 0:16], kvadab[:], ALU.add, [psB[0], b_kvadab], [KB["kvmod"]])
        an = self.sb([P, NL, 8], F32)
        mn = self.sb([P, NL, 8], F32)
        kvn = self.sb([P, 8], F32)
        b_n = self.B()
        self.ld(an[:], I["an"][:, :, :], [], [b_n])
        b_n2 = self.B()
        self.ld(mn[:], I["mn"][:, :, :], [], [b_n2])
        b_n3 = self.B()
        self.ld(kvn[:], I["kvn"][:, :], [], [b_n3])
        tmp = self.sb([P, NL, 8], F32)
        b_tmp = self.B()
        for (dst, kb, nrm, nb, lo) in ((K["g1"], KB["g1"], an, b_n, 8), (K["g2"], KB["g2"], mn, b_n2, 32)):
            self.tt(tmp[:], K["mod"][:, :, lo:lo + 8], nrm[:], ALU.mult, [KB["mod"], nb], [b_tmp])
            self.tt(dst[:], tmp[:], nrm[:], ALU.add, [b_tmp, nb], [kb])
        tmp2 = self.sb([P, 8], F32)
        b_tmp2 = self.B()
        self.tt(tmp2[:], K["kvmod"][:, 8:16], kvn[:], ALU.mult, [KB["kvmod"], b_n3], [b_tmp2])
        self.tt(K["gkv"][:], tmp2[:], kvn[:], ALU.add, [b_tmp2, b_n3], [KB["gkv"]])
        tab = self.sb([33, 16], F32)
        b_tab = self.B()
        pg.op("dve", lambda e: e.memset(tab[32:33, :], NEG), [], [b_tab])
        b_tab2 = self.B()
        self.ld(tab[0:32, :], I["rel_bias"][:, :], [b_tab], [b_tab2])
        oh = self.sb([33, 384], F32)
        b_oh = self.B()
        self.ld(oh[:], I["c_oh"][:, :], [], [b_oh])
        self.mm(ps[1][0:16, 0:384], tab[:, :], oh[:, :], True, True, [b_tab, b_tab2, b_oh], [psB[1]])
        tsb = self.sb([16, 384], F32)
        b_tsb = self.B()
        self.cp(tsb[:], ps[1][0:16, 0:384], [psB[1]], [b_tsb])
        self.ld(S["tT"][:, :], tsb[:], [b_tsb], SB["tT"])
        for k in range(128):
            self.ld(S["d0"][k:k + 1, :, :], S["tT"][:, 127 - k:255 - k].rearrange("(o m) q -> o m q", o=1),
                    SB["tT"], [self.B()], q=("sp" if k % 2 == 0 else "act"))
            self.ld(S["d1"][k:k + 1, :, :], S["tT"][:, 255 - k:383 - k].rearrange("(o m) q -> o m q", o=1),
                    SB["tT"], [self.B()], q=("sp" if k % 2 == 0 else "act"))
        self.ld(K["b31"][:], bass.AP(I["rel_bias"].tensor, 31 * 16, [[0, P], [1, 16]]), [], [KB["b31"]])
        lam = self.sb([P, 2, 256], F32)
        b_lam = self.B()
        self.ld(lam[:], bass.AP(I["a_lambda"].tensor, 0, [[0, P], [256, 2], [1, 256]]), [], [b_lam])
        sub = self.sb([P, 2], F32)
        b_sub = self.B()
        self.ld(sub[:], I["a_subln"][:, :], [], [b_sub])
        prod = self.sb([P, 2, 2, 64], F32)
        b_prod = self.B()
        red = self.sb([P, 4], F32)
        b_red = self.B()
        for l in range(2):
            for i in range(2):
                self.tt(prod[:, l, i, :], lam[:, l, (2 * i) * 64:(2 * i + 1) * 64],
                        lam[:, l, (2 * i + 1) * 64:(2 * i + 2) * 64], ALU.mult, [b_lam], [b_prod])
        pg.op("dve", lambda e: e.tensor_reduce(out=red[:], in_=prod[:].rearrange("p l i d -> p (l i) d"),
                                               axis=AX.X, op=ALU.add), [b_prod], [b_red])
        ered = self.sb([P, 4], F32)
        b_ered = self.B()
        self.act(ered[:], red[:], AF.Exp, [b_red], [b_ered])
        for l in range(2):
            lam_init = 0.8 - 0.6 * math.exp(-0.3 * l)
            self.tt(K["lamneg"][:, l:l + 1], ered[:, 2 * l + 1:2 * l + 2], ered[:, 2 * l:2 * l + 1], ALU.subtract,
                    [b_ered], [KB["lamneg"]])
            self.ts(K["lamneg"][:, l:l + 1], K["lamneg"][:, l:l + 1], -lam_init, None, ALU.add, None,
                    [KB["lamneg"]], [KB["lamneg"]])
            self.ts(K["subg"][:, l:l + 1], sub[:, l:l + 1], 1.0 - lam_init, None, ALU.mult, None,
                    [b_sub], [KB["subg"]])
        self.phase_end()

    def norm_mod(self, xt, xb, N, gvec, shvec, gB, hout, hB, sq, sqB, rstd, rB, psi, tout=None, tB=None):
        K, KB, ps, psB = self.K, self.KB, self.ps, self.psB
        for j in range(8):
            s_, sb_ = sq[j % len(sq)], sqB[j % len(sq)]
            self.act(s_[:, 0:N], xt[:, j, 0:N], AF.Square, [xb], [sb_])
            self.mm(ps[psi][:, 0:N], K["onesD"][:, :], s_[:, 0:N], j == 0, j == 7, [KB["onesD"], sb_], [psB[psi]])
        self.act(rstd[:, 0:N], ps[psi][:, 0:N], AF.Ln, [psB[psi], self.KB["bar"]], [rB], bias=self.eps_ap)
        self.act(rstd[:, 0:N], rstd[:, 0:N], AF.Exp, [rB], [rB], scale=-0.5)
        for j in range(8):
            t_, tb_ = tout[j % len(tout)], tB[j % len(tout)]
            self.tt(t_[:, 0:N], xt[:, j, 0:N], rstd[:, 0:N], ALU.mult, [xb, rB], [tb_])
            self.act(hout[:, j, 0:N], t_[:, 0:N], AF.Identity, [tb_, gB], [hB],
                     bias=shvec[:, j:j + 1], scale=gvec[:, j:j + 1])

    def norm_rings(self, N=512):
        sq = [self.sb([P, N], BF16) for _ in range(2)]
        tt_ = [self.sb([P, N], F32) for _ in range(2)]
        return sq, [self.B() for _ in range(2)], tt_, [self.B() for _ in range(2)]

    @property
    def eps_ap(self):
        if not hasattr(self, "_eps_done"):
            self._eps_done = True
            K, KB = self.K, self.KB
            self.pg.op("dve", lambda e: e.memset(K["bar"][:, 5:6], EPS), [], [KB["bar"]])
        return self.K["bar"][:, 5:6]

    def load_w_bf16(self, dst, dstB, src_view, ncols, blk=512):
        nb = (ncols + blk - 1) // blk
        for i in range(nb):
            a, b = i * blk, min(ncols, (i + 1) * blk)
            self.ld(dst[:, :, a:b], src_view[:, :, a:b], [], [dstB[i]], q="pool")

    def phase_a_proj(self, l):
        I, K, KB, S, SB, ps, psB = self.I, self.K, self.KB, self.S, self.SB, self.ps, self.psB
        NC = self.NC
        self.phase_begin()
        w = self.sb([P, 8, 3072], BF16)
        wB = [self.B() for _ in range(6)]
        self.load_w_bf16(w, wB, I["a_w_in"][l].rearrange("(kc p) n -> p kc n", p=P), 3072)
        xt = [self.sb([P, 8, 512], F32) for _ in range(2)]
        xB = [self.B() for _ in range(2)]
        sq, sqB, tr, trB = self.norm_rings(512)
        rstd = self.sb([P, 512], F32)
        rB = self.B()
        h = [self.sb([P, 8, 512], BF16) for _ in range(2)]
        hB = [self.B() for _ in range(2)]
        qst = [self.sb([P, 16, 512], BF16) for _ in range(2)]
        qB = [self.B() for _ in range(2)]
        kB = [self.B() for _ in range(2)]
        vst = [self.sb([P, 4, 1024], BF16) for _ in range(2)]
        vB = [self.B() for _ in range(2)]
        xv = S["xT"].rearrange("(j p) t -> p j t", p=P)
        ring = 0
        for c in range(NC):
            cs = slice(c * 512, (c + 1) * 512)
            x_, xb_ = xt[c % 2], xB[c % 2]
            self.ld(x_[:], xv[:, :, cs], [SB["xT"][c]], [xb_])
            h_, hb_ = h[c % 2], hB[c % 2]
            self.norm_mod(x_, xb_, 512, K["g1"][:, l, :], K["mod"][:, l, 0:8], KB["g1"], h_, hb_, sq, sqB, rstd, rB, 2, tout=tr, tB=trB)
            q_, qb_, kb_ = qst[c % 2], qB[c % 2], kB[c % 2]
            for m in range(16):
                pi = ring % 2
                ring += 1
                for kc in range(8):
                    self.mm(ps[pi][:, :], w[:, kc, m * 128:(m + 1) * 128], h_[:, kc, :], kc == 0, kc == 7,
                            [wB[m // 4], hb_], [psB[pi]])
                if m < 8:
                    self.act(q_[:, m, :], ps[pi][:, :], AF.Copy, [psB[pi]], [qb_], scale=0.125)
                else:
                    self.cp(q_[:, m, :], ps[pi][:, :], [psB[pi]], [kb_])
            self.ld(S["qT"].rearrange("(m p) t -> p m t", p=P)[:, :, cs], q_[:, 0:8, :], [qb_], [SB["qT"][c]])
            self.ld(S["kT"].rearrange("(m p) t -> p m t", p=P)[:, :, cs], q_[:, 8:16, :], [kb_], [SB["kT"][c]])
            v_, vb_ = vst[c % 2], vB[c % 2]
            for tt in range(4):
                for half in range(2):
                    pi = ring % 2
                    ring += 1
                    for kc in range(8):
                        self.mm(ps[pi][:, :], h_[:, kc, tt * 128:(tt + 1) * 128],
                                w[:, kc, 2048 + half * 512:2048 + (half + 1) * 512], kc == 0, kc == 7,
                                [wB[4 + half], hb_], [psB[pi]])
                    self.cp(v_[:, tt, half * 512:(half + 1) * 512], ps[pi][:, :], [psB[pi]], [vb_],
                            eng=("dve" if half == 0 else "act_copy"))
            self.ld(S["vtok"].rearrange("(tt p) e -> p tt e", p=P)[:, c * 4:(c + 1) * 4, :], v_[:], [vb_],
                    [SB["vtok"][c]])
        self.phase_end()

    def attn_tiles(self, tiles, stageA, stageB, depth=1):
        n = len(tiles)
        for i in range(min(depth, n)):
            stageA(tiles[i], i)
        for i, t in enumerate(tiles):
            if i + depth < n:
                stageA(tiles[i + depth], i + depth)
            stageB(t, i)

    def load_bias_tiles(self):
        S, SB = self.S, self.SB
        d0 = self.sb([P, 16, 128], F32)
        d1 = self.sb([P, 16, 128], F32)
        w4 = self.sb([P, 128], F32)
        bd = self.B()
        self.ld(d0[:], S["d0"][:, :, :], SB["d0"], [bd])
        bd1 = self.B()
        self.ld(d1[:], S["d1"][:, :, :], SB["d1"], [bd1])
        bw = self.B()
        self.ld(w4[:], self.I["c_w4"][:, :], [], [bw])
        return d0, d1, w4, [bd, bd1, bw]

    def phase_a_attn(self, l):
        I, K, KB, S, SB, ps, psB, PS = self.I, self.K, self.KB, self.S, self.SB, self.ps, self.psB, self.PS
        NC, NT, T = self.NC, self.NT, self.T
        pg = self.pg
        self.phase_begin()
        d0, d1, w4, dB = self.load_bias_tiles()
        qh = [self.sb([P, T], BF16) for _ in range(2)]
        kA = [self.sb([P, T], BF16) for _ in range(2)]
        kBt = [self.sb([P, T], BF16) for _ in range(2)]
        vh = [self.sb([P, NT, 128], BF16) for _ in range(2)]
        lB = [[self.B() for _ in range(4)] for _ in range(2)]
        for i in range(2):
            pg.op("pool", lambda e, i=i: e.memset(kA[i][64:128, :], 0.0), [], [lB[i][1]])
            pg.op("pool", lambda e, i=i: e.memset(kBt[i][0:64, :], 0.0), [], [lB[i][3]])
        NPT = 4
        Pt = [self.sb([P, 2, 512], BF16) for _ in range(NPT)]
        PB = [self.B() for _ in range(NPT)]
        accL = [[self.sb([P, 512], F32) for _ in range(2)] for _ in range(2)]
        accB = [[self.B() for _ in range(2)] for _ in range(2)]
        rT = [[self.sb([P, 512], F32) for _ in range(2)] for _ in range(2)]
        tT = [[self.sb([P, 512], F32) for _ in range(2)] for _ in range(2)]
        rTB = [[self.B() for _ in range(2)] for _ in range(2)]
        tTB = [[self.B() for _ in range(2)] for _ in range(2)]
        osq = [self.sb([P, 512], BF16) for _ in range(2)]
        osqB = [self.B() for _ in range(2)]
        rs = [self.sb([P, 512], F32) for _ in range(2)]
        rsB = [self.B() for _ in range(2)]
        ost = [self.sb([P, 512], BF16) for _ in range(2)]
        ostB = [self.B(), self.B()]
        pairs = [0, 4]
        pairB = {0: self.B(), 4: self.B()}
        OBP = [(2, 3), (2, 3)]
        oS = [[self.sb([P, 512], F32) for _ in range(2)] for _ in range(2)]
        oSB = [[self.B() for _ in range(2)] for _ in range(2)]
        ctr = {"s": 0, "p": 0}

        def load_head(h):
            q_, ka_, kb_, v_ = qh[h % 2], kA[h % 2], kBt[h % 2], vh[h % 2]
            lb = lB[h % 2]
            self.ld(q_[:], S["qT"][h * 128:(h + 1) * 128, :], SB["qT"], [lb[0]])
            self.ld(ka_[0:64, :], S["kT"][h * 128:h * 128 + 64, :], SB["kT"], [lb[1]])
            self.ld(kb_[64:128, :], S["kT"][h * 128 + 64:(h + 1) * 128, :], SB["kT"], [lb[3]])
            self.ld(v_[:], S["vtok"].rearrange("(kt p) e -> p kt e", p=P)[:, :, h * 128:(h + 1) * 128], SB["vtok"],
                    [lb[2]])

        loops = []
        for h in range(8):
            for qc in range(NC):
                li = len(loops)
                loops.append(dict(h=h, qc=qc, li=li, par=li % 2, nk=4 * qc + 4, st={}))
        flat = [(L, kt) for L in loops for kt in range(L["nk"])]
        loaded = set()

        def ring_pair():
            pb = pairs[ctr["s"] % 2]
            ctr["s"] += 1
            return pb

        def stageA(L, kt):
            h, qc = L["h"], L["qc"]
            if h not in loaded:
                loaded.add(h)
                load_head(h)
            q_, ka_, kb_ = qh[h % 2], kA[h % 2], kBt[h % 2]
            lb = lB[h % 2]
            pb = ring_pair()
            L["st"][kt] = pb
            c0 = max(0, kt - 4 * qc) * 128
            for m in range(2):
                hm = h * 2 + m
                bank = ps[pb + m]
                fixes = []
                for ii in range(4):
                    delta = 4 * qc + ii - kt
                    if delta == 0 or delta == 1:
                        fixes.append((ii, d0 if delta == 0 else d1))
                kk = ka_ if m == 0 else kb_
                self.mm(bank[:, c0:512], kk[:, kt * 128:(kt + 1) * 128],
                        q_[:, qc * 512 + c0:(qc + 1) * 512], True, len(fixes) == 0,
                        [lb[0], lb[1], lb[3]], [pairB[pb]])
                for fi, (ii, dd) in enumerate(fixes):
                    self.mm(bank[:, ii * 128:(ii + 1) * 128], K["ident"][:, :], dd[:, hm, :], False,
                            fi == len(fixes) - 1, [KB["ident"]] + dB, [pairB[pb]])

        def stageB(L, kt):
            h, qc, par, nk = L["h"], L["qc"], L["par"], L["nk"]
            if qc == 0 and kt == 0 and h + 1 < 8 and (h + 1) not in loaded:
                loaded.add(h + 1)
                load_head(h + 1)
            v_ = vh[h % 2]
            lb = lB[h % 2]
            aL, aB = accL[par], accB[par]
            ob = OBP[par]
            pb = L["st"][kt]
            c0 = max(0, kt - 4 * qc) * 128
            pi = ctr["p"] % NPT
            ctr["p"] += 1
            pv = PS[:, pb * 512:(pb + 2) * 512].rearrange("p (m c) -> p m c", m=2)
            self.act(Pt[pi][:, :, c0:512], pv[:, :, c0:512], AF.Exp, [pairB[pb]], [PB[pi]])
            for m in range(2):
                self.mm(ps[ob[m]][:, c0:512], v_[:, kt, :], Pt[pi][:, m, c0:512], kt == 0, kt == nk - 1,
                        [lb[2], PB[pi]], [psB[ob[m]]])
            for m in range(2):
                self.mm(ps[6 + m][:, c0:512], K["ones_bf"][:, :], Pt[pi][:, m, c0:512], kt == 0, kt == nk - 1,
                        [KB["ones_bf"], PB[pi]], [psB[6 + m]])

        def epilogue_stages(L):
            h, qc, par = L["h"], L["qc"], L["par"]
            aL, aB = accL[par], accB[par]
            ob = OBP[par]
            r_, rb_, t_, tb_ = rT[par], rTB[par], tT[par], tTB[par]
            stt = {}

            def s0():
                for m in range(2):
                    self.cp(oS[par][m][:], ps[ob[m]][:, :], [psB[ob[m]]], [oSB[par][m]])
                for m in range(2):
                    self.act(r_[m][:], ps[6 + m][:, :], AF.Ln, [psB[6 + m]], [rb_[m]])
                    self.act(r_[m][:], r_[m][:], AF.Exp, [rb_[m]], [rb_[m]], scale=-1.0)

            def s3():
                for m in range(2):
                    self.tt(t_[m][:], oS[par][m][:], r_[m][:], ALU.mult, [oSB[par][m], rb_[m]], [tb_[m]])
                self.stt(t_[0][:], t_[1][:], K["lamneg"][:, l:l + 1], t_[0][:], ALU.mult, ALU.add,
                         [tb_[0], tb_[1], KB["lamneg"]], [tb_[0]])

            def s4():
                self.act(osq[par][:], t_[0][:], AF.Square, [tb_[0]], [osqB[par]])

            def s56():
                pb = pairs[ctr["s"] % 2]
                self.mm(ps[pb][:, :], K["onesH"][:, :], osq[par][:], True, True, [KB["onesH"], osqB[par]],
                        [pairB[pb]])
                self.act(rs[par][:], ps[pb][:, :], AF.Ln, [pairB[pb], KB["bar"]], [rsB[par]], bias=self.eps_ap)
                self.act(rs[par][:], rs[par][:], AF.Exp, [rsB[par]], [rsB[par]], scale=-0.5)

            def s7():
                self.tt(t_[0][:], t_[0][:], rs[par][:], ALU.mult, [tb_[0], rsB[par]], [tb_[0]])

            def s8():
                self.act(ost[par][:], t_[0][:], AF.Identity, [tb_[0], KB["subg"]], [ostB[par]],
                         scale=K["subg"][:, l:l + 1])
                self.ld(S["oT"][h * 128:(h + 1) * 128, qc * 512:(qc + 1) * 512], ost[par][:], [ostB[par]],
                        [SB["oT"][qc]])

            return [s0, s3, s4, s56, s7, s8]

        pending = []

        def run_pending(i, upto_loop=None):
            j = 0
            while j < len(pending):
                due, li, fn = pending[j]
                if due <= i or (upto_loop is not None and li <= upto_loop):
                    pending.pop(j)
                    fn()
                    j = 0
                else:
                    j += 1

        nflat = len(flat)
        depth = 1
        for i in range(min(depth, nflat)):
            stageA(*flat[i])
        for i in range(nflat):
            if i + depth < nflat:
                stageA(*flat[i + depth])
            L, kt = flat[i]
            stageB(L, kt)
            if kt == L["nk"] - 1:
                run_pending(i, upto_loop=L["li"] - 2)
                stages = epilogue_stages(L)
                stages[0]()
                for k, fn in enumerate(stages[1:]):
                    pending.append((i + 2 + 2 * k, L["li"], fn))
            run_pending(i)
        run_pending(nflat + 1000)
        self.phase_end()

    def phase_outproj(self, l, wo_src):
        I, K, KB, S, SB, ps, psB = self.I, self.K, self.KB, self.S, self.SB, self.ps, self.psB
        NC = self.NC
        self.phase_begin()
        w = self.sb([P, 8, 1024], BF16)
        wB = [self.B() for _ in range(2)]
        self.load_w_bf16(w, wB, wo_src.rearrange("(kc p) n -> p kc n", p=P), 1024)
        xt = [self.sb([P, 8, 512], F32) for _ in range(2)]
        xB = [self.B() for _ in range(2)]
        ot = [self.sb([P, 8, 512], BF16) for _ in range(2)]
        oB = [self.B() for _ in range(2)]
        xv = S["xT"].rearrange("(j p) t -> p j t", p=P)
        ov = S["oT"].rearrange("(j p) t -> p j t", p=P)
        ring = 0
        for c in range(NC):
            cs = slice(c * 512, (c + 1) * 512)
            x_, xb_ = xt[c % 2], xB[c % 2]
            o_, ob_ = ot[c % 2], oB[c % 2]
            self.ld(x_[:], xv[:, :, cs], [SB["xT"][c]], [xb_])
            self.ld(o_[:], ov[:, :, cs], [SB["oT"][c]], [ob_])
            for j in range(8):
                pi = ring % 2
                ring += 1
                for hc in range(8):
                    self.mm(ps[pi][:, :], w[:, hc, j * 128:(j + 1) * 128], o_[:, hc, :], hc == 0, hc == 7,
                            [wB[j // 4], ob_], [psB[pi]])
                self.stt(x_[:, j, :], ps[pi][:, :], K["mod"][:, l, 16 + j:17 + j], x_[:, j, :], ALU.mult, ALU.add,
                         [psB[pi], xb_, KB["mod"]], [xb_])
            self.ld(xv[:, :, cs], x_[:], [xb_], [SB["xT"][c]])
        self.phase_end()

    def phase_mlp(self, l):
        I, K, KB, S, SB, ps, psB = self.I, self.K, self.KB, self.S, self.SB, self.ps, self.psB
        T = self.T
        N = 512
        self.phase_begin()
        w1 = self.sb([P, 8, DFF], BF16)
        w1B = [self.B() for _ in range(8)]
        w2 = self.sb([P, 32, D], BF16)
        w2B = [self.B() for _ in range(8)]
        self.load_w_bf16(w1, w1B, I["mlp_w1"][l].rearrange("(kc p) n -> p kc n", p=P), DFF)
        v2 = I["mlp_w2"][l].rearrange("(f p) n -> p f n", p=P)
        for i in range(8):
            self.ld(w2[:, i * 4:(i + 1) * 4, :], v2[:, i * 4:(i + 1) * 4, :], [], [w2B[i]], q="pool")
        xt = self.sb([P, 8, N], F32)
        xB = self.B()
        sq, sqB, tr, trB = self.norm_rings(N)
        rstd = self.sb([P, N], F32)
        rB = self.B()
        h = self.sb([P, 8, N], BF16)
        hB = self.B()
        hid = self.sb([P, 32, N], BF16)
        hidB = [self.B() for _ in range(8)]
        r32 = [self.sb([P, N], F32) for _ in range(2)]
        r32B = [self.B() for _ in range(2)]
        xv = S["xT"].rearrange("(j p) t -> p j t", p=P)
        ring = 0
        for c in range(T // N):
            cs = slice(c * N, (c + 1) * N)
            sbx = SB["xT"][c]
            x_, xb_ = xt, xB
            self.ld(x_[:], xv[:, :, cs], [sbx], [xb_])
            self.norm_mod(x_, xb_, N, K["g2"][:, l, :], K["mod"][:, l, 24:32], KB["g2"], h, hB, sq, sqB, rstd, rB, 2,
                          tout=tr, tB=trB)
            for f in range(32):
                pi = ring % 2
                ring += 1
                for kc in range(8):
                    self.mm(ps[pi][:, 0:N], w1[:, kc, f * 128:(f + 1) * 128], h[:, kc, :], kc == 0, kc == 7,
                            [w1B[f // 4], hB], [psB[pi]])
                ri = f % 2
                self.act(r32[ri][:], ps[pi][:, 0:N], AF.Relu, [psB[pi]], [r32B[ri]])
                self.tt(hid[:, f, :], r32[ri][:], r32[ri][:], ALU.mult, [r32B[ri]], [hidB[f // 4]])
            for j in range(8):
                pi = 3 + (ring % 2)
                ring += 1
                for f in range(32):
                    self.mm(ps[pi][:, 0:N], w2[:, f, j * 128:(j + 1) * 128], hid[:, f, :], f == 0, f == 31,
                            [w2B[f // 4], hidB[f // 4]], [psB[pi]])
                self.stt(x_[:, j, :], ps[pi][:, 0:N], K["mod"][:, l, 40 + j:41 + j], x_[:, j, :], ALU.mult, ALU.add,
                         [psB[pi], xb_, KB["mod"]], [xb_])
            self.ld(xv[:, :, cs], x_[:], [xb_], [sbx])
        self.phase_end()

    def phase_final(self, outT):
        I, K, KB, S, SB, ps, psB = self.I, self.K, self.KB, self.S, self.SB, self.ps, self.psB
        self.phase_begin()
        xt = [self.sb([P, 8, 512], F32) for _ in range(2)]
        xB = [self.B() for _ in range(2)]
        yt = [self.sb([P, 8, 512], F32) for _ in range(2)]
        yB = [self.B() for _ in range(2)]
        sq, sqB, tr, trB = self.norm_rings(512)
        rstd = self.sb([P, 512], F32)
        rB = self.B()
        xv = S["xT"].rearrange("(j p) t -> p j t", p=P)
        ov = outT.rearrange("(j p) t -> p j t", p=P)
        for c in range(self.NC):
            cs = slice(c * 512, (c + 1) * 512)
            x_, xb_ = xt[c % 2], xB[c % 2]
            self.ld(x_[:], xv[:, :, cs], [SB["xT"][c]], [xb_])
            self.norm_mod(x_, xb_, 512, K["fn"], K["zero8"], KB["fn"], yt[c % 2], yB[c % 2], sq, sqB, rstd, rB, 2, tout=tr, tB=trB)
            self.ld(ov[:, :, cs], yt[c % 2][:], [yB[c % 2]], [self.B()])
        self.phase_end()

    def phase_kv(self):
        I, K, KB, S, SB, ps, psB = self.I, self.K, self.KB, self.S, self.SB, self.ps, self.psB
        NC = self.NC
        self.phase_begin()
        w = self.sb([P, 8, 1536], BF16)
        wB = [self.B() for _ in range(3)]
        self.load_w_bf16(w, wB, I["w_kv"].rearrange("(kc p) n -> p kc n", p=P), 1536)
        xt = [self.sb([P, 8, 512], F32) for _ in range(2)]
        xB = [self.B() for _ in range(2)]
        sq, sqB, tr, trB = self.norm_rings(512)
        rstd = self.sb([P, 512], F32)
        rB = self.B()
        h = [self.sb([P, 8, 512], BF16) for _ in range(2)]
        hB = [self.B() for _ in range(2)]
        kst = [self.sb([P, 12, 512], BF16) for _ in range(2)]
        kB = [self.B() for _ in range(2)]
        vst = [self.sb([P, 4, 2, 256], BF16) for _ in range(2)]
        vB = [self.B() for _ in range(2)]
        xv = S["xT"].rearrange("(j p) t -> p j t", p=P)
        ring = 0
        for c in range(NC):
            cs = slice(c * 512, (c + 1) * 512)
            x_, xb_ = xt[c % 2], xB[c % 2]
            self.ld(x_[:], xv[:, :, cs], [SB["xT"][c]], [xb_])
            h_, hb_ = h[c % 2], hB[c % 2]
            self.norm_mod(x_, xb_, 512, K["gkv"], K["kvmod"][:, 0:8], KB["gkv"], h_, hb_, sq, sqB, rstd, rB, 2, tout=tr, tB=trB)
            k_, kb_ = kst[c % 2], kB[c % 2]
            for m in range(12):
                pi = ring % 2
                ring += 1
                for kc in range(8):
                    self.mm(ps[pi][:, :], w[:, kc, m * 128:(m + 1) * 128], h_[:, kc, :], kc == 0, kc == 7,
                            [wB[m // 4], hb_], [psB[pi]])
                self.cp(k_[:, m, :], ps[pi][:, :], [psB[pi]], [kb_], eng=("dve" if m % 2 == 0 else "act_copy"))
            self.ld(S["kvT"].rearrange("(m p) t -> p m t", p=P)[:, :, cs], k_[:], [kb_], [SB["kvT"][c]])
            v_, vb_ = vst[c % 2], vB[c % 2]
            for tt in range(4):
                pi = ring % 2
                ring += 1
                for si, s0 in enumerate((768, 1280)):
                    for kc in range(8):
                        self.mm(ps[pi][:, si * 256:(si + 1) * 256], h_[:, kc, tt * 128:(tt + 1) * 128],
                                w[:, kc, s0:s0 + 256], kc == 0, kc == 7, [wB[s0 // 512], hb_], [psB[pi]])
                self.cp(v_[:, tt, :, :], ps[pi][:, :].rearrange("p (s e) -> p s e", s=2), [psB[pi]], [vb_])
            self.ld(S["vslc"].rearrange("(tt p) e -> p tt e", p=P)[:, c * 4:(c + 1) * 4, :], v_[:, :, 0, :], [vb_],
                    [SB["vslc"][c]])
            self.ld(S["vwin"].rearrange("(tt p) e -> p tt e", p=P)[:, c * 4:(c + 1) * 4, :], v_[:, :, 1, :], [vb_],
                    [SB["vwin"][c]])
        self.phase_end()

    def phase_cmp(self):
        I, K, KB, S, SB, ps, psB = self.I, self.K, self.KB, self.S, self.SB, self.ps, self.psB
        T = self.T
        ncmp = T // 16 - 1
        self.phase_begin()
        src = [self.sb([64, T], BF16) for _ in range(2)]
        srcB = [self.B() for _ in range(2)]
        w1r = self.sb([64, 32, 256], BF16)
        w2 = self.sb([P, 2, 64], BF16)
        posT = self.sb([64, 32], BF16)
        hidT = self.sb([P, 2, 256], BF16)
        hidB = self.B()
        pre = self.sb([P, 256], F32)
        u = self.sb([P, 256], F32)
        bias = self.sb([P, 2], F32)
        bB = {k: self.B() for k in ("pre", "u", "bias")}
        pg = self.pg
        pg.op("dve", lambda e: e.memset(hidT[:], 0.0), [], [hidB])
        it = 0
        wb = [self.B(), self.B(), self.B()]
        for s in range(2):
            self.ld(w1r[:], I["cmp_w1"][s].rearrange("(t d) h -> d t h", d=64), [], [wb[0]], q="pool")
            self.ld(w2[:], I["cmp_w2"][s].rearrange("(hc p) d -> p hc d", p=P), [], [wb[1]], q="pool")
            self.ld(posT[:], I["cmp_posT"][s], [], [wb[2]], q="pool")
            for hc in range(2):
                for t in range(32):
                    self.mm(ps[6][:, hc:hc + 1], w1r[:, t, hc * 128:(hc + 1) * 128], posT[:, t:t + 1], t == 0, t == 31,
                            [wb[0], wb[2]], [psB[6]])
            self.cp(bias[:], ps[6][:, 0:2], [psB[6]], [bB["bias"]])
            for g in range(4):
                sr, srb = src[it % 2], srcB[it % 2]
                it += 1
                r0 = s * 256 + g * 64
                self.ld(sr[:], S["kvT"][r0:r0 + 64, :], SB["kvT"], [srb])
                for hc in range(2):
                    for t in range(32):
                        self.mm(ps[hc][:, 0:ncmp], w1r[:, t, hc * 128:(hc + 1) * 128],
                                sr[:, t:t + 16 * (ncmp - 1) + 1:16], t == 0, t == 31, [wb[0], srb], [psB[hc]])
                    self.act(pre[:, 0:ncmp], ps[hc][:, 0:ncmp], AF.Identity, [psB[hc], bB["bias"]], [bB["pre"]],
                             bias=bias[:, hc:hc + 1])
                    self.tt(u[:, 0:ncmp], pre[:, 0:ncmp], pre[:, 0:ncmp], ALU.mult, [bB["pre"]], [bB["u"]])
                    self.ts(u[:, 0:ncmp], u[:, 0:ncmp], 0.044715, 1.0, ALU.mult, ALU.add, [bB["u"]], [bB["u"]])
                    self.tt(u[:, 0:ncmp], u[:, 0:ncmp], pre[:, 0:ncmp], ALU.mult, [bB["u"], bB["pre"]], [bB["u"]])
                    self.act(u[:, 0:ncmp], u[:, 0:ncmp], AF.Sigmoid, [bB["u"]], [bB["u"]],
                             scale=2.0 * math.sqrt(2.0 / math.pi))
                    self.tt(hidT[:, hc, 0:ncmp], u[:, 0:ncmp], pre[:, 0:ncmp], ALU.mult, [bB["u"], bB["pre"]],
                            [hidB])
                if s == 0:
                    for hc in range(2):
                        self.mm(ps[2][0:64, 0:ncmp], w2[:, hc, :], hidT[:, hc, 0:ncmp], hc == 0, hc == 1,
                                [wb[1], hidB], [psB[2]])
                    self.cp(K["kcmpT"][0:64, g, 0:ncmp], ps[2][0:64, 0:ncmp], [psB[2]], [KB["kcmpT"]])
                else:
                    for nt in range(2):
                        nn = min(128, ncmp - nt * 128)
                        if nn <= 0:
                            continue
                        for hc in range(2):
                            self.mm(ps[3][0:nn, nt * 64:(nt + 1) * 64], hidT[:, hc, nt * 128:nt * 128 + nn],
                                    w2[:, hc, :], hc == 0, hc == 1, [wb[1], hidB], [psB[3]])
                        self.cp(K["vcmp"][0:nn, g, nt, 0:64], ps[3][0:nn, nt * 64:(nt + 1) * 64], [psB[3]],
                                [KB["vcmp"]])
        self.phase_end()

    def phase_b_proj(self, l):
        I, K, KB, S, SB, ps, psB = self.I, self.K, self.KB, self.S, self.SB, self.ps, self.psB
        NC = self.NC
        self.phase_begin()
        w = self.sb([P, 8, 1072], BF16)
        wB = [self.B() for _ in range(3)]
        self.load_w_bf16(w, wB, I["b_w_in"][l - 2].rearrange("(kc p) n -> p kc n", p=P), 1072)
        xt = [self.sb([P, 8, 512], F32) for _ in range(2)]
        xB = [self.B() for _ in range(2)]
        sq, sqB, tr, trB = self.norm_rings(512)
        rstd = self.sb([P, 512], F32)
        rB = self.B()
        h = [self.sb([P, 8, 512], BF16) for _ in range(2)]
        hB = [self.B() for _ in range(2)]
        qst = [self.sb([P, 8, 512], BF16) for _ in range(2)]
        qB = [self.B() for _ in range(2)]
        gst = [self.sb([48, 512], F32) for _ in range(2)]
        gB = [self.B() for _ in range(2)]
        xv = S["xT"].rearrange("(j p) t -> p j t", p=P)
        ring = 0
        for c in range(NC):
            cs = slice(c * 512, (c + 1) * 512)
            x_, xb_ = xt[c % 2], xB[c % 2]
            self.ld(x_[:], xv[:, :, cs], [SB["xT"][c]], [xb_])
            h_, hb_ = h[c % 2], hB[c % 2]
            self.norm_mod(x_, xb_, 512, K["g1"][:, l, :], K["mod"][:, l, 0:8], KB["g1"], h_, hb_, sq, sqB, rstd, rB, 2, tout=tr, tB=trB)
            q_, qb_ = qst[c % 2], qB[c % 2]
            for m in range(8):
                pi = ring % 2
                ring += 1
                for kc in range(8):
                    self.mm(ps[pi][:, :], w[:, kc, m * 128:(m + 1) * 128], h_[:, kc, :], kc == 0, kc == 7,
                            [wB[m // 4], hb_], [psB[pi]])
                self.act(q_[:, m, :], ps[pi][:, :], AF.Copy, [psB[pi]], [qb_], scale=0.125)
            self.ld(S["qT"].rearrange("(m p) t -> p m t", p=P)[:, :, cs], q_[:], [qb_], [SB["qT"][c]])
            pi = ring % 2
            ring += 1
            for kc in range(8):
                self.mm(ps[pi][0:48, :], w[:, kc, 1024:1072], h_[:, kc, :], kc == 0, kc == 7, [wB[2], hb_], [psB[pi]])
            self.act(gst[c % 2][:], ps[pi][0:48, :], AF.Sigmoid, [psB[pi]], [gB[c % 2]])
            self.ld(S["gT"][:, cs], gst[c % 2][:], [gB[c % 2]], [SB["gT"][c]])
        self.phase_end()

    def phase_b_attn(self, l):
        I, K, KB, S, SB, ps, psB = self.I, self.K, self.KB, self.S, self.SB, self.ps, self.psB
        NC, NT, T = self.NC, self.NT, self.T
        pg = self.pg
        self.phase_begin()
        d0, d1, w4, dB = self.load_bias_tiles()
        ex = self.sb([P, NT, 128], BF16)
        exB = self.B()
        pg.op("pool", lambda e: e.memset(ex[64:128, :, :], 0.0), [], [exB])
        self.ld(ex[0:64, :, :], I["c_ex"][:, :, :], [exB], [exB], q="pool")
        ov = self.sb([P, 2, 64], F32)
        ovB = self.B()
        self.ld(ov[:], I["c_ov"][:, :, :], [], [ovB])
        maskc = self.sb([P, 2, T], BF16)
        mcB = self.B()
        self.ld(maskc[:], I["c_maskc"][:, :, :], [], [mcB], q="pool")
        keep = self.sb([P, NT, 64], BF16)
        addm = self.sb([P, NT, 64], BF16)
        kaB = [self.B(), self.B()]
        self.ld(keep[:], I["c_keep"][:, :, :], [], [kaB[0]], q="pool")
        self.ld(addm[:], I["c_add"][:, :, :], [], [kaB[1]], q="pool")
        ksl = self.sb([P, T], BF16)
        kwn = self.sb([P, T], BF16)
        vsl = self.sb([P, NT, 65], BF16)
        vwn = self.sb([P, NT, 65], BF16)
        gB_ = [self.B() for _ in range(4)]
        pg.op("pool", lambda e: e.memset(ksl[64:128, :], 0.0), [], [gB_[0]])
        pg.op("pool", lambda e: e.memset(kwn[64:128, :], 0.0), [], [gB_[1]])
        pg.op("pool", lambda e: e.memset(vsl[:], 1.0), [], [gB_[2]])
        pg.op("pool", lambda e: e.memset(vwn[:], 1.0), [], [gB_[3]])
        lfull = [self.sb([P, 512], F32) for _ in range(2)]
        lfB = [self.B() for _ in range(2)]
        for i in range(2):
            pg.op("dve", lambda e, i=i: e.memset(lfull[i][:], 0.0), [], [lfB[i]])
        qg = [self.sb([P, 4, 512], BF16) for _ in range(2)]
        qgB = [self.B() for _ in range(2)]
        for i in range(2):
            pg.op("pool", lambda e, i=i: e.memset(qg[i][64:128, :, :], 0.0), [], [qgB[i]])
        gb = self.sb([64, 12, 512], F32)
        gbB = self.B()
        pc32 = [self.sb([P, 512], F32) for _ in range(2)]
        pn32 = [self.sb([P, 512], F32) for _ in range(2)]
        pn16 = [self.sb([P, 512], BF16) for _ in range(2)]
        pcB = [self.B() for _ in range(2)]
        pnB = [self.B() for _ in range(2)]
        pn16B = [self.B() for _ in range(2)]
        rl = self.sb([P, 512], F32)
        rlB = self.B()
        oc = [self.sb([64, 4, 512], F32) for _ in range(2)]
        ocB = [[self.B() for _ in range(4)] for _ in range(2)]
        impv = self.sb([P, 64], F32)
        impv2 = self.sb([P, 64], F32)
        m8a = self.sb([P, 8], F32)
        m8b = self.sb([P, 8], F32)
        msel = self.sb([P, 4, 128], F32)
        tkB = {k: self.B() for k in ("impv", "impv2", "m8a", "m8b", "msel")}
        pg.op("dve", lambda e: e.memset(msel[:], 0.0), [], [tkB["msel"]])
        mT = [self.sb([P, 512], BF16) for _ in range(2)]
        mTB = [self.B() for _ in range(2)]
        NPT = 6
        SR = [0, 1, 4]
        OB = [2, 6]
        Pt = [self.sb([P, 512], BF16) for _ in range(NPT)]
        PB = [self.B() for _ in range(NPT)]
        rr = [self.sb([64, 512], F32) for _ in range(2)]
        rrB = [self.B() for _ in range(2)]
        acc = self.sb([64, 512], F32)
        accB = self.B()
        tmp = [self.sb([64, 512], F32) for _ in range(2)]
        tmpB = [self.B() for _ in range(2)]
        ost = [self.sb([64, 4, 512], BF16) for _ in range(2)]
        ostB = [self.B() for _ in range(2)]
        ctr = {"s": 0, "p": 0}
        blocks = [(g, qc) for g in range(4) for qc in range(NC)]
        CS = 5
        CI = 7

        def cmp_thunks(bi):
            g, qc = blocks[bi]
            p = bi % 2
            cs = slice(qc * 512, (qc + 1) * 512)
            q_, qb_ = qg[p], qgB[p]
            oc_, ocb_ = oc[p], ocB[p]
            mT_, mtb_ = mT[p], mTB[p]
            th = []
            th.append(lambda: self.ld(q_[0:64, :, :],
                                      S["qT"].rearrange("(h d) t -> d h t", d=64)[:, g * 4:(g + 1) * 4, cs],
                                      [SB["qT"][qc]], [qb_]))
            for r in range(4):
                for nt in range(2):
                    th.append(lambda r=r, nt=nt: self.mm(ps[CS][:, :], K["kcmpT"][:, g, nt * 128:(nt + 1) * 128],
                                                         q_[:, r, :], True, True, [KB["kcmpT"], qb_], [psB[CS]]))

                    def f_exp(r=r, nt=nt):
                        self.act(pc32[nt][:], ps[CS][:, :], AF.Exp, [psB[CS]], [pcB[nt]])
                        self.tt(pc32[nt][:], pc32[nt][:], maskc[:, nt, cs], ALU.mult, [pcB[nt], mcB], [pcB[nt]])
                    th.append(f_exp)

                def f_l(r=r):
                    for nt in range(2):
                        self.mm(ps[CS][:, :], K["ones32"][:, :], pc32[nt][:], nt == 0, nt == 1,
                                [KB["ones32"], pcB[nt]], [psB[CS]])
                th.append(f_l)
                th.append(lambda: self.ts(rl[:], ps[CS][:, :], 1e-18, None, ALU.max, None, [psB[CS]], [rlB]))
                th.append(lambda: self.act(rl[:], rl[:], AF.Ln, [rlB], [rlB]))
                th.append(lambda: self.act(rl[:], rl[:], AF.Exp, [rlB], [rlB], scale=-1.0))

                def f_pn():
                    for nt in range(2):
                        self.tt(pn32[nt][:], pc32[nt][:], rl[:], ALU.mult, [pcB[nt], rlB], [pnB[nt]])
                        self.cp(pn16[nt][:], pn32[nt][:], [pnB[nt]], [pn16B[nt]], eng="pool")
                th.append(f_pn)

                def f_imp(r=r):
                    for nt in range(2):
                        for i in range(4):
                            first = (r == 0 and nt == 0 and i == 0)
                            last = (r == 3 and nt == 1 and i == 3)
                            self.mm(ps[CI][:, i * 64:(i + 1) * 64], pn32[nt][:, i * 128:(i + 1) * 128], ov[:, nt, :],
                                    first, last, [pnB[nt], ovB], [psB[CI]])
                th.append(f_imp)

                def f_pv(r=r):
                    for nt in range(2):
                        self.mm(ps[CS][:, :], K["vcmp"][:, g, nt, :], pn16[nt][:], nt == 0, nt == 1,
                                [KB["vcmp"], pn16B[nt]], [psB[CS]])
                th.append(f_pv)
                th.append(lambda r=r: self.cp(oc_[:, r, :], ps[CS][0:64, :], [psB[CS]], [ocb_[r]]))
            for i in range(4):
                qb = qc * 4 + i

                def f_k1(i=i, qb=qb):
                    self.tt(impv[:], ps[CI][:, i * 64:(i + 1) * 64], keep[:, qb, :], ALU.mult, [psB[CI], kaB[0]],
                            [tkB["impv"]])
                    self.tt(impv[:], impv[:], addm[:, qb, :], ALU.add, [tkB["impv"], kaB[1]], [tkB["impv"]])
                th.append(f_k1)
                th.append(lambda: pg.op("dve", lambda e: e.max(out=m8a[:], in_=impv[:]), [tkB["impv"]],
                                        [tkB["m8a"]]))
                th.append(lambda: pg.op("dve", lambda e: e.match_replace(out=impv2[:], in_to_replace=m8a[:],
                                                                          in_values=impv[:], imm_value=-3.0e38),
                                        [tkB["impv"], tkB["m8a"]], [tkB["impv2"]]))
                th.append(lambda: pg.op("dve", lambda e: e.max(out=m8b[:], in_=impv2[:]), [tkB["impv2"]],
                                        [tkB["m8b"]]))
                th.append(lambda i=i: self.ts(msel[:, i, 0:64], impv[:], m8b[:, 7:8], None, ALU.is_ge, None,
                                              [tkB["impv"], tkB["m8b"]], [tkB["msel"]]))

            def f_tr():
                for i in range(4):
                    pg.op("pe", lambda e, i=i: e.transpose(out=ps[CI][:, i * 128:(i + 1) * 128], in_=msel[:, i, :],
                                                           identity=K["ident"][:, :]),
                          [tkB["msel"], KB["ident"]], [psB[CI]])
            th.append(f_tr)
            th.append(lambda: self.ts(mT_[:], ps[CI][:, 0:512], -1.0, 30000.0, ALU.add, ALU.mult, [psB[CI]], [mtb_]))
            return th

        def emit_tiles(bi, bg):
            g, qc = blocks[bi]
            p = bi % 2
            cs = slice(qc * 512, (qc + 1) * 512)
            q_, qb_ = qg[p], qgB[p]
            oc_, ocb_ = oc[p], ocB[p]
            mT_, mtb_ = mT[p], mTB[p]
            o_st, o_stB = ost[p], ostB[p]
            r0 = g * 64
            if qc == 0:
                self.ld(ksl[0:64, :], S["kvT"][512 + r0:512 + r0 + 64, :], SB["kvT"], [gB_[0]])
                self.ld(kwn[0:64, :], S["kvT"][1024 + r0:1024 + r0 + 64, :], SB["kvT"], [gB_[1]])
                self.ld(vsl[:, :, 0:64], S["vslc"].rearrange("(kt p) e -> p kt e", p=P)[:, :, r0:r0 + 64],
                        SB["vslc"], [gB_[2]])
                self.ld(vwn[:, :, 0:64], S["vwin"].rearrange("(kt p) e -> p kt e", p=P)[:, :, r0:r0 + 64],
                        SB["vwin"], [gB_[3]])
            self.ld(gb[:], bass.AP(S["gT"].tensor, g * 12 * T + qc * 512, [[0, 64], [T, 12], [1, 512]]),
                    [SB["gT"][qc]], [gbB])
            loops = []
            for r in range(4):
                for sel in (True, False):
                    if sel:
                        tl = list(range(0, 4 * qc + 4))
                        kT_, kB_, vT_, vB_ = ksl, gB_[0], vsl, gB_[2]
                    else:
                        tl = list(range(max(0, 4 * qc - 4), 4 * qc + 4))
                        kT_, kB_, vT_, vB_ = kwn, gB_[1], vwn, gB_[3]
                    loops.append(dict(r=r, sel=sel, tiles=tl, kT=kT_, kB=kB_, vT=vT_, vB=vB_,
                                      ob=OB[len(loops) % 2], par=len(loops) % 2, st={}))

            def rng(L, kt):
                c0 = max(0, kt - 4 * qc) * 128
                c1 = 512 if L["sel"] else min(4, kt + 5 - 4 * qc) * 128
                return c0, c1

            def stageA(L, kt):
                h = g * 4 + L["r"]
                si = SR[ctr["s"] % len(SR)]
                ctr["s"] += 1
                L["st"][kt] = si
                c0, c1 = rng(L, kt)
                extra = []
                if L["sel"]:
                    extra.append((c0, c1, ex[:, kt, :], mT_[:, c0:c1], [exB, mtb_]))
                for ii in range(c0 // 128, c1 // 128):
                    delta = 4 * qc + ii - kt
                    dd = None
                    if delta == 0:
                        dd = d0[:, h, :]
                    elif delta == 1:
                        dd = d1[:, h, :]
                    elif delta == 4 and not L["sel"]:
                        dd = w4[:, :]
                    if dd is not None:
                        extra.append((ii * 128, (ii + 1) * 128, K["ident"][:, :], dd, [KB["ident"]] + dB))
                self.mm(ps[si][:, c0:c1], L["kT"][:, kt * 128:(kt + 1) * 128], q_[:, L["r"], c0:c1], True,
                        len(extra) == 0, [L["kB"], qb_], [psB[si]])
                for xi, (a0, a1, lh, rh, rd) in enumerate(extra):
                    self.mm(ps[si][:, a0:a1], lh, rh, False, xi == len(extra) - 1, rd, [psB[si]])

            def stageB(L, kt, idx):
                si = L["st"][kt]
                c0, c1 = rng(L, kt)
                pi = ctr["p"] % NPT
                ctr["p"] += 1
                n = len(L["tiles"])
                self.act(Pt[pi][:, c0:c1], ps[si][:, c0:c1], AF.Exp, [psB[si]], [PB[pi]])
                self.mm(ps[L["ob"]][0:65, c0:c1], L["vT"][:, kt, :], Pt[pi][:, c0:c1], idx == 0, idx == n - 1,
                        [L["vB"], PB[pi]], [psB[L["ob"]]])

            def epi1(L):
                lf, lb_ = lfull[L["par"]], lfB[L["par"]]
                ob = L["ob"]
                self.act(lf[64:65, :], ps[ob][64:65, :], AF.Ln, [psB[ob]], [lb_])
                self.act(lf[64:65, :], lf[64:65, :], AF.Exp, [lb_], [lb_], scale=-1.0)

            def epi2(L):
                lf, lb_ = lfull[L["par"]], lfB[L["par"]]
                ob = L["ob"]
                r = L["r"]
                rr_, rrb_ = rr[L["par"]], rrB[L["par"]]
                tmp_, tmpb_ = tmp[L["par"]], tmpB[L["par"]]
                self.mm(ps[3][:, :], K["sel64"][:, :], lf[:, :], True, True, [KB["sel64"], lb_], [psB[3]])
                self.cp(rr_[:], ps[3][0:64, :], [psB[3]], [rrb_])
                self.tt(tmp_[:], ps[ob][0:64, :], rr_[:], ALU.mult, [psB[ob], rrb_], [tmpb_])
                if L["sel"]:
                    self.tt(acc[:], oc_[:, r, :], gb[:, r * 3 + 0, :], ALU.mult, [ocb_[r], gbB], [accB], eng="pool")
                    self.tt(tmp_[:], tmp_[:], gb[:, r * 3 + 1, :], ALU.mult, [tmpb_, gbB], [tmpb_])
                    self.tt(acc[:], acc[:], tmp_[:], ALU.add, [accB, tmpb_], [accB])
                else:
                    self.tt(tmp_[:], tmp_[:], gb[:, r * 3 + 2, :], ALU.mult, [tmpb_, gbB], [tmpb_])
                    self.tt(o_st[:, r, :], acc[:], tmp_[:], ALU.add, [accB, tmpb_], [o_stB])

            flat = [(L, kt, idx) for L in loops for idx, kt in enumerate(L["tiles"])]
            nflat = len(flat)
            depth = 2
            pending = []
            bgq = list(bg)
            for i in range(min(depth, nflat)):
                stageA(flat[i][0], flat[i][1])
            for i in range(nflat):
                if i + depth < nflat:
                    stageA(flat[i + depth][0], flat[i + depth][1])
                L, kt, idx = flat[i]
                stageB(L, kt, idx)
                if idx == len(L["tiles"]) - 1:
                    epi1(L)
                    pending.append((i + 3, L))
                while pending and pending[0][0] <= i:
                    epi2(pending.pop(0)[1])
                if bgq:
                    bgq.pop(0)()
            while pending:
                epi2(pending.pop(0)[1])
            for t_ in bgq:
                t_()
            self.ld(S["oT"].rearrange("(h d) t -> d h t", d=64)[:, g * 4:(g + 1) * 4, cs], o_st[:], [o_stB],
                    [SB["oT"][qc]])

        for t_ in cmp_thunks(0):
            t_()
        for bi in range(len(blocks)):
            nxt = cmp_thunks(bi + 1) if bi + 1 < len(blocks) else []
            emit_tiles(bi, nxt)
        self.phase_end()


_orig_op = Prog.op


def _op(self, eng, fn, reads=(), writes=()):
    if eng == "act_copy":
        return _orig_op(self, "act", fn, reads, writes)
    return _orig_op(self, eng, fn, reads, writes)


Prog.op = _op
_orig_cp = Builder.cp


def _cp(self, out, in_, reads, writes, eng="dve"):
    if eng == "act_copy":
        self.pg.op("act", lambda e: e.activation(out=out, in_=in_, func=AF.Copy), reads, writes)
    else:
        _orig_cp(self, out, in_, reads, writes, eng)


Builder.cp = _cp


def col8(v):
    v = np.asarray(v, np.float32)
    return np.ascontiguousarray(np.moveaxis(v.reshape(v.shape[:-1] + (v.shape[-1] // 128, 128)), -1, 0))


def make_in_maps(inputs, T):
    x = np.asarray(inputs["x"], np.float32)
    B = x.shape[0]
    shared = {}
    f = lambda k: np.ascontiguousarray(np.asarray(inputs[k], np.float32))
    shared["rel_bias"] = f("rel_bias")
    shared["ada_w"] = f("ada_w")
    shared["adab"] = np.ascontiguousarray(f("ada_b").reshape(NL, 48, 128).transpose(0, 2, 1))
    shared["an"] = col8(f("attn_norm"))
    shared["mn"] = col8(f("mlp_norm"))
    shared["mlp_w1"] = f("mlp_w1")
    shared["mlp_w2"] = f("mlp_w2")
    shared["a_w_in"] = f("a_w_in")
    shared["a_w_out"] = f("a_w_out")
    shared["a_lambda"] = f("a_lambda").reshape(2, 256)
    shared["a_subln"] = np.ascontiguousarray(f("a_subln").T)
    shared["kv_ada_w"] = f("kv_ada_w")
    shared["kvadab"] = np.ascontiguousarray(f("kv_ada_b").reshape(16, 128).T)
    shared["kvn"] = col8(f("kv_norm"))
    shared["w_kv"] = f("w_kv")
    shared["cmp_posT"] = np.ascontiguousarray(f("cmp_pos").transpose(0, 2, 1))
    shared["cmp_w1"] = f("cmp_w1")
    shared["cmp_w2"] = f("cmp_w2")
    shared["b_w_in"] = f("b_w_in")
    shared["b_w_out"] = f("b_w_out")
    shared["fnorm"] = col8(f("final_norm"))
    shared.update(make_consts(T))
    maps = []
    c = np.asarray(inputs["c"], np.float32)
    for b in range(B):
        m = dict(shared)
        m["xT"] = np.ascontiguousarray(x[b].T)
        m["cT"] = np.ascontiguousarray(c[b].reshape(8, 128).T)
        maps.append(m)
    return maps


_CACHE = {}


def run(inputs, T, layers=NL, debug=None):
    key = (T, layers, tuple(debug) if debug else None)
    if key not in _CACHE:
        _CACHE[key] = Builder(T, layers, debug).build()
    nc = _CACHE[key]
    maps = make_in_maps(inputs, T)
    res = run_bass_kernel_spmd(nc, maps, core_ids=list(range(len(maps))))
    return res.results


def kernel(**inputs):
    T = int(np.asarray(inputs["x"]).shape[1])
    results = run(inputs, T)
    out = np.stack([np.ascontiguousarray(r["outT"].T) for r in results], axis=0)
    return out.astype(np.float32)
```
